# Optimizing a Trainium2 kernel written in Bass

```python
import math
import jax, jax.numpy as jnp
from jax import lax
import numpy as np

D_MODEL = 1024
BATCH = 8
SEQ = 4096
DEPTH = 2

N_EVEN = (DEPTH + 1) // 2
N_ODD = DEPTH // 2
DN_ALPHA = (2.0 * DEPTH) ** 0.25
DN_BETA = (8.0 * DEPTH) ** -0.25

RNN_WIDTH = D_MODEL // 2
RNN_HEADS = 8
RNN_HEAD_DIM = RNN_WIDTH // RNN_HEADS
CONV_WIDTH = 4
RG_C = 8.0

MLA_HEADS = 8
MLA_NOPE = 64
MLA_ROPE = 32
MLA_V = 64
MLA_Q_RANK = D_MODEL // 4
MLA_KV_RANK = D_MODEL // 8
MLA_WIDTH = MLA_HEADS * MLA_V
ROPE_THETA = 10000.0
Q_BLOCK = 128

AB_WIDTH = RNN_WIDTH + MLA_WIDTH
AB_IN = RNN_WIDTH + AB_WIDTH + MLA_Q_RANK + MLA_KV_RANK + MLA_ROPE

SSD_INNER = 2 * D_MODEL
SSD_HEAD_DIM = 64
SSD_HEADS = SSD_INNER // SSD_HEAD_DIM
SSD_GROUPS = 4
SSD_STATE = 128
SSD_CHUNK = 128
SSD_CONV_DIM = SSD_INNER + 2 * SSD_GROUPS * SSD_STATE
SSD_IN = SSD_INNER + SSD_CONV_DIM + SSD_HEADS

kernel_name = "hybrid_rglru_mla_ssd_deepnorm"


def _rmsnorm(x, g, eps=1e-6):
    xf = x.astype(jnp.float32)
    y = xf * lax.rsqrt(jnp.mean(xf * xf, axis=-1, keepdims=True) + eps)
    return (y * g.astype(jnp.float32)).astype(x.dtype)


def _layernorm(x, g, b, eps=1e-5):
    xf = x.astype(jnp.float32)
    mu = jnp.mean(xf, axis=-1, keepdims=True)
    xc = xf - mu
    var = jnp.mean(xc * xc, axis=-1, keepdims=True)
    y = xc * lax.rsqrt(var + eps) * g.astype(jnp.float32) + b.astype(jnp.float32)
    return y.astype(x.dtype)


def _causal_conv(x, w, b):
    k_taps = w.shape[0]
    seqlen = x.shape[1]
    xp = jnp.pad(x, ((0, 0), (k_taps - 1, 0), (0, 0)))
    return sum(xp[:, k:k + seqlen] * w[k] for k in range(k_taps)) + b


def _rope(x, cos, sin):
    half = x.shape[-1] // 2
    x1, x2 = x[..., :half], x[..., half:]
    return jnp.concatenate([x1 * cos - x2 * sin, x2 * cos + x1 * sin], axis=-1)


def _rg_lru(x, w_a, b_a, w_x, b_x, lam):
    bsz, seqlen, _ = x.shape
    xh = x.reshape(bsz, seqlen, RNN_HEADS, RNN_HEAD_DIM)
    r = jax.nn.sigmoid(jnp.einsum('bshi,hij->bshj', xh, w_a).reshape(bsz, seqlen, RNN_WIDTH) + b_a)
    i = jax.nn.sigmoid(jnp.einsum('bshi,hij->bshj', xh, w_x).reshape(bsz, seqlen, RNN_WIDTH) + b_x)
    log_a = (-RG_C * r.astype(jnp.float32)) * jax.nn.softplus(-lam.astype(jnp.float32))
    a = jnp.exp(log_a)
    mult = jnp.sqrt(-jnp.expm1(2.0 * log_a))
    u = mult * (i * x).astype(jnp.float32)

    def combine(c1, c2):
        a1, b1 = c1
        a2, b2 = c2
        return a1 * a2, a2 * b1 + b2

    _, h = lax.associative_scan(combine, (a, u), axis=1)
    return h.astype(x.dtype)


def _mla_attention(q_nope, q_rope, k_nope, k_rope, v):
    seqlen = q_nope.shape[1]
    scale = (MLA_NOPE + MLA_ROPE) ** -0.5
    outs = []
    for blk in range(seqlen // Q_BLOCK):
        q0 = blk * Q_BLOCK
        kend = q0 + Q_BLOCK
        s = (jnp.einsum('bqhd,bkhd->bhqk', q_nope[:, q0:kend], k_nope[:, :kend])
             + jnp.einsum('bqhr,bkr->bhqk', q_rope[:, q0:kend], k_rope[:, :kend]))
        s = s.astype(jnp.float32) * scale
        mask = jnp.arange(kend)[None, :] <= (q0 + jnp.arange(Q_BLOCK))[:, None]
        s = jnp.where(mask, s, -jnp.inf)
        p = jax.nn.softmax(s, axis=-1).astype(v.dtype)
        outs.append(jnp.einsum('bhqk,bkhd->bqhd', p, v[:, :kend]))
    return jnp.concatenate(outs, axis=1)


def _rglru_mla_layer(x, cos, sin, w_in, conv_w, conv_b, gate_a_w, gate_a_b, gate_x_w,
                     gate_x_b, lam, q_norm, kv_norm, w_uq, w_ukv, w_out):
    bsz, seqlen, _ = x.shape
    proj = jnp.einsum('bsd,de->bse', x, w_in)
    x_rnn, gate, c_q, c_kv, k_rope = jnp.split(
        proj, [RNN_WIDTH, RNN_WIDTH + AB_WIDTH, RNN_WIDTH + AB_WIDTH + MLA_Q_RANK,
               RNN_WIDTH + AB_WIDTH + MLA_Q_RANK + MLA_KV_RANK], axis=-1)
    x_rnn = _causal_conv(x_rnn, conv_w, conv_b)
    y_rnn = _rg_lru(x_rnn, gate_a_w, gate_a_b, gate_x_w, gate_x_b, lam)
    q = jnp.einsum('bsr,re->bse', _rmsnorm(c_q, q_norm), w_uq)
    q = q.reshape(bsz, seqlen, MLA_HEADS, MLA_NOPE + MLA_ROPE)
    q_nope = q[..., :MLA_NOPE]
    q_rope = _rope(q[..., MLA_NOPE:], cos[:, :, None], sin[:, :, None])
    kv = jnp.einsum('bsr,re->bse', _rmsnorm(c_kv, kv_norm), w_ukv)
    kv = kv.reshape(bsz, seqlen, MLA_HEADS, MLA_NOPE + MLA_V)
    k_nope, v = kv[..., :MLA_NOPE], kv[..., MLA_NOPE:]
    k_rope = _rope(k_rope, cos, sin)
    y_mla = _mla_attention(q_nope, q_rope, k_nope, k_rope, v).reshape(bsz, seqlen, MLA_WIDTH)
    y = jnp.concatenate([y_rnn, y_mla], axis=-1) * jax.nn.silu(gate)
    return jnp.einsum('bse,ed->bsd', y, w_out)


def _ssd_scan(x, dt, a_neg, bm, cm):
    bsz, seqlen, n_heads, p_dim = x.shape
    n_groups, n_state = bm.shape[2], bm.shape[3]
    hpg = n_heads // n_groups
    nc, L = seqlen // SSD_CHUNK, SSD_CHUNK
    xf = (x.astype(jnp.float32) * dt[..., None]).reshape(bsz, nc, L, n_groups, hpg, p_dim)
    a_dt = (dt * a_neg).reshape(bsz, nc, L, n_groups, hpg)
    bc = bm.astype(jnp.float32).reshape(bsz, nc, L, n_groups, n_state)
    cc = cm.astype(jnp.float32).reshape(bsz, nc, L, n_groups, n_state)
    cs = jnp.cumsum(a_dt, axis=2)
    cs_h = jnp.transpose(cs, (0, 1, 3, 4, 2))
    seg = cs_h[..., :, None] - cs_h[..., None, :]
    causal = jnp.tril(jnp.ones((L, L), dtype=bool))
    decay = jnp.where(causal, jnp.exp(jnp.where(causal, seg, 0.0)), 0.0)
    cb = jnp.einsum('bclgn,bcsgn->bcgls', cc, bc)
    y_diag = jnp.einsum('bcghls,bcsghp->bclghp', cb[:, :, :, None] * decay, xf)
    decay_states = jnp.exp(cs[:, :, -1:] - cs)
    states = jnp.einsum('bclgn,bclghp->bcghpn', bc, xf * decay_states[..., None])
    chunk_decay = jnp.exp(cs[:, :, -1])

    def step(h, inp):
        dec, st = inp
        return h * dec[..., None, None] + st, h

    h0 = jnp.zeros((bsz, n_groups, hpg, p_dim, n_state), jnp.float32)
    _, prev = lax.scan(step, h0, (jnp.moveaxis(chunk_decay, 1, 0), jnp.moveaxis(states, 1, 0)))
    prev = jnp.moveaxis(prev, 0, 1)
    y_off = jnp.einsum('bclgn,bcghpn->bclghp', cc, prev) * jnp.exp(cs)[..., None]
    return (y_diag + y_off).reshape(bsz, seqlen, n_heads, p_dim)


def _ssd_layer(x, w_in, conv_w, conv_b, dt_bias, a_log, d_skip, norm_w, w_out):
    bsz, seqlen, _ = x.shape
    proj = jnp.einsum('bsd,de->bse', x, w_in)
    z, xbc, dt = jnp.split(proj, [SSD_INNER, SSD_INNER + SSD_CONV_DIM], axis=-1)
    xbc = jax.nn.silu(_causal_conv(xbc, conv_w, conv_b))
    xs, bm, cm = jnp.split(xbc, [SSD_INNER, SSD_INNER + SSD_GROUPS * SSD_STATE], axis=-1)
    xs = xs.reshape(bsz, seqlen, SSD_HEADS, SSD_HEAD_DIM)
    bm = bm.reshape(bsz, seqlen, SSD_GROUPS, SSD_STATE)
    cm = cm.reshape(bsz, seqlen, SSD_GROUPS, SSD_STATE)
    dt = jax.nn.softplus(dt.astype(jnp.float32) + dt_bias.astype(jnp.float32))
    a_neg = -jnp.exp(a_log.astype(jnp.float32))
    y = _ssd_scan(xs, dt, a_neg, bm, cm) + d_skip.astype(jnp.float32)[:, None] * xs.astype(jnp.float32)
    y = y.reshape(bsz, seqlen, SSD_INNER) * jax.nn.silu(z.astype(jnp.float32))
    yg = y.reshape(bsz, seqlen, SSD_GROUPS, SSD_INNER // SSD_GROUPS)
    yg = yg * lax.rsqrt(jnp.mean(yg * yg, axis=-1, keepdims=True) + 1e-6)
    y = (yg.reshape(bsz, seqlen, SSD_INNER) * norm_w.astype(jnp.float32)).astype(x.dtype)
    return jnp.einsum('bse,ed->bsd', y, w_out)


def setup_inputs(seed: int = 0) -> dict:
    key = jax.random.key(seed)
    ks = jax.random.split(key, 32)
    f32 = jnp.float32
    nrm = lambda k, shape, s: jax.random.normal(k, shape, f32) * s

    x = jax.random.normal(ks[0], (BATCH, SEQ, D_MODEL), f32)
    offset = jax.random.randint(ks[1], (BATCH, 1), 0, 1024, dtype=jnp.int32)
    positions = offset + jnp.arange(SEQ, dtype=jnp.int32)[None, :]

    ab_w_in = nrm(ks[2], (N_EVEN, D_MODEL, AB_IN), D_MODEL ** -0.5)
    ab_conv_w = nrm(ks[3], (N_EVEN, CONV_WIDTH, RNN_WIDTH), CONV_WIDTH ** -0.5)
    ab_conv_b = nrm(ks[4], (N_EVEN, RNN_WIDTH), 0.02)
    ab_gate_a_w = nrm(ks[5], (N_EVEN, RNN_HEADS, RNN_HEAD_DIM, RNN_HEAD_DIM), RNN_HEAD_DIM ** -0.5)
    ab_gate_a_b = nrm(ks[6], (N_EVEN, RNN_WIDTH), 0.1)
    ab_gate_x_w = nrm(ks[7], (N_EVEN, RNN_HEADS, RNN_HEAD_DIM, RNN_HEAD_DIM), RNN_HEAD_DIM ** -0.5)
    ab_gate_x_b = nrm(ks[8], (N_EVEN, RNN_WIDTH), 0.1)
    u = jax.random.uniform(ks[9], (N_EVEN, RNN_WIDTH), f32, 0.9, 0.999)
    a0 = u ** (1.0 / RG_C)
    ab_lambda = jnp.log(a0) - jnp.log1p(-a0)
    mla_q_norm = 1.0 + nrm(ks[10], (N_EVEN, MLA_Q_RANK), 0.05)
    mla_kv_norm = 1.0 + nrm(ks[11], (N_EVEN, MLA_KV_RANK), 0.05)
    mla_w_uq = nrm(ks[12], (N_EVEN, MLA_Q_RANK, MLA_HEADS * (MLA_NOPE + MLA_ROPE)), MLA_Q_RANK ** -0.5)
    mla_w_ukv = nrm(ks[13], (N_EVEN, MLA_KV_RANK, MLA_HEADS * (MLA_NOPE + MLA_V)), MLA_KV_RANK ** -0.5)
    ab_w_out = nrm(ks[14], (N_EVEN, AB_WIDTH, D_MODEL), DN_BETA * math.sqrt(2.0 / (AB_WIDTH + D_MODEL)))
    ab_ln_g = 1.0 + nrm(ks[15], (N_EVEN, D_MODEL), 0.05)
    ab_ln_b = nrm(ks[16], (N_EVEN, D_MODEL), 0.02)

    ssd_w_in = nrm(ks[17], (N_ODD, D_MODEL, SSD_IN), D_MODEL ** -0.5)
    ssd_conv_w = nrm(ks[18], (N_ODD, CONV_WIDTH, SSD_CONV_DIM), CONV_WIDTH ** -0.5)
    ssd_conv_b = nrm(ks[19], (N_ODD, SSD_CONV_DIM), 0.02)
    dt0 = jnp.exp(jax.random.uniform(ks[20], (N_ODD, SSD_HEADS), f32, math.log(1e-3), math.log(1e-1)))
    ssd_dt_bias = dt0 + jnp.log(-jnp.expm1(-dt0))
    ssd_a_log = jnp.log(jax.random.uniform(ks[21], (N_ODD, SSD_HEADS), f32, 1.0, 16.0))
    ssd_d = 1.0 + nrm(ks[22], (N_ODD, SSD_HEADS), 0.1)
    ssd_norm = 1.0 + nrm(ks[23], (N_ODD, SSD_INNER), 0.05)
    ssd_w_out = nrm(ks[24], (N_ODD, SSD_INNER, D_MODEL), DN_BETA * math.sqrt(2.0 / (SSD_INNER + D_MODEL)))
    ssd_ln_g = 1.0 + nrm(ks[25], (N_ODD, D_MODEL), 0.05)
    ssd_ln_b = nrm(ks[26], (N_ODD, D_MODEL), 0.02)

    return {"x": x, "positions": positions,
            "ab_w_in": ab_w_in, "ab_conv_w": ab_conv_w, "ab_conv_b": ab_conv_b,
            "ab_gate_a_w": ab_gate_a_w, "ab_gate_a_b": ab_gate_a_b,
            "ab_gate_x_w": ab_gate_x_w, "ab_gate_x_b": ab_gate_x_b,
            "ab_lambda": ab_lambda, "mla_q_norm": mla_q_norm, "mla_kv_norm": mla_kv_norm,
            "mla_w_uq": mla_w_uq, "mla_w_ukv": mla_w_ukv, "ab_w_out": ab_w_out,
            "ab_ln_g": ab_ln_g, "ab_ln_b": ab_ln_b,
            "ssd_w_in": ssd_w_in, "ssd_conv_w": ssd_conv_w, "ssd_conv_b": ssd_conv_b,
            "ssd_dt_bias": ssd_dt_bias, "ssd_a_log": ssd_a_log, "ssd_d": ssd_d,
            "ssd_norm": ssd_norm, "ssd_w_out": ssd_w_out,
            "ssd_ln_g": ssd_ln_g, "ssd_ln_b": ssd_ln_b}


def reference(x, positions, ab_w_in, ab_conv_w, ab_conv_b, ab_gate_a_w, ab_gate_a_b,
              ab_gate_x_w, ab_gate_x_b, ab_lambda, mla_q_norm, mla_kv_norm, mla_w_uq,
              mla_w_ukv, ab_w_out, ab_ln_g, ab_ln_b, ssd_w_in, ssd_conv_w, ssd_conv_b,
              ssd_dt_bias, ssd_a_log, ssd_d, ssd_norm, ssd_w_out, ssd_ln_g, ssd_ln_b):
    inv_freq = ROPE_THETA ** (-jnp.arange(0, MLA_ROPE, 2, dtype=jnp.float32) / MLA_ROPE)
    ang = positions.astype(jnp.float32)[..., None] * inv_freq
    cos = jnp.cos(ang).astype(x.dtype)
    sin = jnp.sin(ang).astype(x.dtype)
    for layer in range(DEPTH):
        j = layer // 2
        if layer % 2 == 0:
            y = _rglru_mla_layer(x, cos, sin, ab_w_in[j], ab_conv_w[j], ab_conv_b[j],
                                 ab_gate_a_w[j], ab_gate_a_b[j], ab_gate_x_w[j], ab_gate_x_b[j],
                                 ab_lambda[j], mla_q_norm[j], mla_kv_norm[j], mla_w_uq[j],
                                 mla_w_ukv[j], ab_w_out[j])
            x = _layernorm(DN_ALPHA * x + y, ab_ln_g[j], ab_ln_b[j])
        else:
            y = _ssd_layer(x, ssd_w_in[j], ssd_conv_w[j], ssd_conv_b[j], ssd_dt_bias[j],
                           ssd_a_log[j], ssd_d[j], ssd_norm[j], ssd_w_out[j])
            x = _layernorm(DN_ALPHA * x + y, ssd_ln_g[j], ssd_ln_b[j])
    return x
```

```python
import math
import numpy as np
import concourse.bass as bass
import concourse.mybir as mybir
from concourse.bass_utils import run_bass_kernel_spmd

F32 = mybir.dt.float32
BF16 = mybir.dt.bfloat16
I32 = mybir.dt.int32
AF = mybir.ActivationFunctionType
ALU = mybir.AluOpType

D = 1024
NCORES = 8
SEQ = 4096
ALPHA = 4.0 ** 0.25
MAGIC = 12582912.0
C1 = 6.28125
C2 = 2.0 * math.pi - 6.28125


class Op:
    __slots__ = ("eng", "fn", "deps", "is_dma", "sem", "val", "marked", "dma_wait")

    def __init__(self, eng, fn, is_dma=False, sem=None):
        self.eng = eng
        self.fn = fn
        self.deps = []
        self.is_dma = is_dma
        self.sem = sem
        self.val = None
        self.marked = False
        self.dma_wait = {}


class Prog:
    ENGS = ("pe", "act", "dve", "pool", "sp")

    def __init__(self, nc):
        self.nc = nc
        self.eobj = {"pe": nc.tensor, "act": nc.scalar, "dve": nc.vector,
                     "pool": nc.gpsimd, "sp": nc.sync}
        self.ops = []
        self.last_writer = {}
        self.readers = {}
        self.dma_count = {}
        self.dma_last = {}
        self.last_on = {}
        self.bar = {}

    def _add(self, op, reads, writes):
        deps = op.deps

        def need(p):
            if p is None or p is op:
                return
            if p.is_dma:
                op.dma_wait[p.sem] = self.dma_count[p.sem]
            else:
                deps.append(p)

        b = self.bar.pop(op.eng, None)
        if b is not None:
            for p in b[0]:
                need(p)
            for s, c in b[1].items():
                op.dma_wait[s] = c
        for k in reads:
            w = self.last_writer.get(k)
            if w is not None:
                if (not w.is_dma) and (not op.is_dma) and w.eng == op.eng == "pe":
                    continue
                need(w)
        strict = op.eng != "pe"
        for k in writes:
            w = self.last_writer.get(k)
            if w is not None:
                if w.is_dma or op.is_dma or w.eng != op.eng or strict:
                    need(w)
            for r in self.readers.get(k, ()):
                if r.is_dma or op.is_dma or r.eng != op.eng or strict:
                    need(r)
        for k in writes:
            self.last_writer[k] = op
            self.readers[k] = []
        for k in reads:
            self.readers.setdefault(k, []).append(op)
        self.ops.append(op)
        if not op.is_dma:
            self.last_on[op.eng] = op
        return op

    def op(self, eng, fn, reads=(), writes=()):
        return self._add(Op(eng, fn), reads, writes)

    def dma(self, eng, fn, sem, reads=(), writes=(), chain=True):
        o = Op(eng, fn, is_dma=True, sem=sem)
        if chain and sem in self.dma_last:
            o.dma_wait[sem] = self.dma_count[sem]
        self.dma_count.setdefault(sem, 0)
        self._add(o, reads, writes)
        self.dma_count[sem] += 1
        o.val = self.dma_count[sem]
        self.dma_last[sem] = o
        return o

    def pe(self, fn, reads=(), writes=()):
        return self.op("pe", fn, reads, writes)

    def act(self, fn, reads=(), writes=()):
        return self.op("act", fn, reads, writes)

    def dve(self, fn, reads=(), writes=()):
        return self.op("dve", fn, reads, writes)

    def pool(self, fn, reads=(), writes=()):
        return self.op("pool", fn, reads, writes)

    def barrier(self):
        lasts = list(self.last_on.values())
        dm = dict(self.dma_count)
        for e in self.ENGS:
            self.bar[e] = (lasts, dm)
        self.last_writer = {}
        self.readers = {}

    def emit(self):
        nc = self.nc
        for o in self.ops:
            for d in o.deps:
                d.marked = True
        cnt = {e: 0 for e in self.ENGS}
        for o in self.ops:
            if not o.is_dma and o.marked:
                cnt[o.eng] += 1
                o.val = cnt[o.eng]
        ctr = {e: nc.alloc_semaphore(name="ctr_" + e) for e in self.ENGS}
        dsem = {s: nc.alloc_semaphore(name="dma_%d" % i) for i, s in enumerate(self.dma_count)}
        waited = {e: {} for e in self.ENGS}
        nwait = 0
        for o in self.ops:
            eng = self.eobj[o.eng]
            need = {}
            for d in o.deps:
                key = ("c", d.eng)
                if need.get(key, (None, 0))[1] < d.val:
                    need[key] = (ctr[d.eng], d.val)
            for s, c in o.dma_wait.items():
                key = ("d", s)
                if need.get(key, (None, 0))[1] < 16 * c:
                    need[key] = (dsem[s], 16 * c)
            w = waited[o.eng]
            for key, (h, v) in need.items():
                if w.get(key, 0) >= v:
                    continue
                eng.wait_ge(h, v)
                nwait += 1
                w[key] = v
            ins = o.fn(eng)
            if o.is_dma:
                ins.then_inc(dsem[o.sem], 16)
            elif o.marked:
                ins.then_inc(ctr[o.eng], 1)
        eng = self.eobj["sp"]
        for s, c in self.dma_count.items():
            eng.wait_ge(dsem[s], 16 * c)
        return dict(n_ops=len(self.ops), n_wait=nwait, marked=cnt)


class Alloc:
    def __init__(self, nc):
        self.nc = nc
        self.guards = []

    def sb(self, name, shape, dt=F32):
        g = self.nc.sbuf_tensor("s_" + name, list(shape), dt)
        t = g.__enter__()
        self.guards.append(g)
        return t

    def ps(self, name, shape, dt=F32):
        g = self.nc.psum_tensor("p_" + name, list(shape), dt)
        t = g.__enter__()
        self.guards.append(g)
        return t

    def free(self):
        for g in reversed(self.guards):
            g.__exit__(None, None, None)
        self.guards = []


def make_consts(nc, P, A, pfx):
    ident = A.sb(pfx + "ident", [128, 128], BF16)
    maskT = A.sb(pfx + "maskT", [128, 128], BF16)
    ones32 = A.sb(pfx + "ones32", [128, 128], F32)
    P.pool(lambda e: e.memset(ident[:], 1.0), writes=["ident"])
    P.pool(lambda e: e.affine_select(out=ident[:], in_=ident[:], pattern=[[-1, 128]],
                                     compare_op=ALU.is_equal, fill=0.0, base=0,
                                     channel_multiplier=1), reads=["ident"], writes=["ident"])
    P.pool(lambda e: e.memset(maskT[:], 1.0), writes=["maskT"])
    P.pool(lambda e: e.affine_select(out=maskT[:], in_=maskT[:], pattern=[[1, 128]],
                                     compare_op=ALU.is_ge, fill=0.0, base=0,
                                     channel_multiplier=-1), reads=["maskT"], writes=["maskT"])
    P.pool(lambda e: e.memset(ones32[:], 1.0), writes=["ones32"])
    return ident, maskT, ones32


V_CW, V_CB, V_BA, V_BX, V_LAM, V_QN, V_KVN, V_INVF, V_SGN, NV = 0, 16, 20, 24, 28, 32, 34, 35, 36, 40
ATT_SCALE = 96.0 ** -0.5


def build_A(nc, P, S, d, x_in, x_out):
    T = S // 128
    A = Alloc(nc)
    sb, ps = A.sb, A.ps
    ident, maskT, ones32 = make_consts(nc, P, A, "a_")
    bank = [ps("a_bank%d" % i, [128, 512], F32) for i in range(8)]

    def bk(i):
        return ("bank", i)

    w_in = sb("a_w_in", [128, 8, 2048], BF16)
    w_out = sb("a_w_out", [128, 8, 1024], BF16)
    w_uq = sb("a_w_uq", [128, 2, 1280], BF16)
    w_kn = sb("a_w_kn", [128, 512], BF16)
    w_v = sb("a_w_v", [128, 512], BF16)
    gA = sb("a_gA", [128, 4, 128], BF16)
    gX = sb("a_gX", [128, 4, 128], BF16)
    vec = sb("a_vec", [128, NV], F32)
    lng = sb("a_lng", [128, 512], F32)
    lnb = sb("a_lnb", [128, 512], F32)
    cst = sb("a_cst", [128, 8], F32)
    for i, v in enumerate([1e-6, 1e-5, math.pi / 2, 0.0]):
        P.pool(lambda e, i=i, v=v: e.memset(cst[:, i:i + 1], v), writes=["cst"])
    for c in range(0, 8, 2):
        P.dma("pool", lambda e, c=c: e.dma_start(out=w_in[:, c:c + 2, :],
                                                 in_=d["w_in"][c * 128:(c + 2) * 128, :].rearrange("(c p) n -> p c n", p=128)),
              "wl", writes=["w_in"], chain=False)
    for c in range(0, 8, 4):
        P.dma("pool", lambda e, c=c: e.dma_start(out=w_out[:, c:c + 4, :],
                                                 in_=d["w_out"][c * 128:(c + 4) * 128, :].rearrange("(c p) n -> p c n", p=128)),
              "wl", writes=["w_out"], chain=False)
    P.dma("pool", lambda e: e.dma_start(out=w_uq[:], in_=d["w_uq"].rearrange("(c p) n -> p c n", p=128)),
          "wl", writes=["w_uq"], chain=False)
    P.dma("pool", lambda e: e.dma_start(out=w_kn[:], in_=d["w_kn"]), "wl", writes=["w_kn"], chain=False)
    P.dma("pool", lambda e: e.dma_start(out=w_v[:], in_=d["w_v"]), "wl", writes=["w_v"], chain=False)
    P.dma("pool", lambda e: e.dma_start(out=gA[:], in_=d["gA"]), "wl", writes=["gA"], chain=False)
    P.dma("pool", lambda e: e.dma_start(out=gX[:], in_=d["gX"]), "wl", writes=["gX"], chain=False)
    P.dma("sp", lambda e: e.dma_start(out=vec[:], in_=d["vec"]), "wl2", writes=["vec"])

    der = sb("a_der", [128, 16], F32)
    tmp4 = sb("a_tmp4", [128, 4], F32)
    P.act(lambda e: e.activation(out=tmp4[:], in_=vec[:, V_LAM:V_LAM + 4], func=AF.Exp, scale=-1.0),
          reads=["vec"], writes=["tmp4"])
    P.act(lambda e: e.activation(out=tmp4[:], in_=tmp4[:], func=AF.Ln, bias=1.0), reads=["tmp4"], writes=["tmp4"])
    P.dve(lambda e: e.tensor_scalar(out=der[:, 0:4], in0=tmp4[:], scalar1=4.0, scalar2=None, op0=ALU.mult),
          reads=["tmp4"], writes=["der"])
    P.dve(lambda e: e.tensor_scalar(out=der[:, 4:8], in0=tmp4[:], scalar1=-4.0, scalar2=None, op0=ALU.mult),
          reads=["tmp4"], writes=["der"])
    P.dve(lambda e: e.tensor_scalar(out=der[:, 8:16], in0=vec[:, V_BA:V_BA + 8], scalar1=0.5, scalar2=None,
                                    op0=ALU.mult), reads=["vec"], writes=["der"])

    trig_d = nc.dram_tensor("a_trig", [128, 2, S], F32).ap()
    CB = min(2048, S)
    A2 = Alloc(nc)
    pi_t = A2.sb("a_pi", [128, CB], I32)
    tA = A2.sb("a_tA", [128, CB], F32)
    tB = A2.sb("a_tB", [128, CB], F32)
    tC = A2.sb("a_tC", [128, 2, CB], F32)
    for blk in range(S // CB):
        cs = slice(blk * CB, (blk + 1) * CB)
        P.dma("sp", lambda e, cs=cs: e.dma_start(out=pi_t[:], in_=d["pos"][:, cs].partition_broadcast(128)),
              "trg", writes=["pi"])
        P.dve(lambda e: e.tensor_copy(out=tA[:], in_=pi_t[:]), reads=["pi"], writes=["tA"])
        P.dve(lambda e: e.tensor_scalar(out=tA[:], in0=tA[:], scalar1=vec[:, V_INVF:V_INVF + 1], scalar2=None,
                                        op0=ALU.mult), reads=["tA", "vec"], writes=["tA"])
        P.dve(lambda e: e.tensor_scalar(out=tB[:], in0=tA[:], scalar1=1.0 / (2 * math.pi), scalar2=MAGIC,
                                        op0=ALU.mult, op1=ALU.add), reads=["tA"], writes=["tB"])
        P.dve(lambda e: e.tensor_scalar(out=tB[:], in0=tB[:], scalar1=-MAGIC, scalar2=None, op0=ALU.add),
              reads=["tB"], writes=["tB"])
        P.dve(lambda e: e.scalar_tensor_tensor(out=tA[:], in0=tB[:], scalar=-C1, in1=tA[:], op0=ALU.mult,
                                               op1=ALU.add), reads=["tA", "tB"], writes=["tA"])
        P.dve(lambda e: e.scalar_tensor_tensor(out=tA[:], in0=tB[:], scalar=-C2, in1=tA[:], op0=ALU.mult,
                                               op1=ALU.add), reads=["tA", "tB"], writes=["tA"])
        P.dve(lambda e: e.tensor_scalar(out=tA[:], in0=tA[:], scalar1=3.1415925, scalar2=-3.1415925,
                                        op0=ALU.min, op1=ALU.max), reads=["tA"], writes=["tA"])
        P.act(lambda e: e.activation(out=tB[:], in_=tA[:], func=AF.Abs), reads=["tA"], writes=["tB"])
        P.act(lambda e: e.activation(out=tC[:, 0, :], in_=tB[:], func=AF.Sin, bias=cst[:, 2:3], scale=-1.0),
              reads=["tB", "cst"], writes=["tC"])
        P.act(lambda e: e.activation(out=tC[:, 1, :], in_=tA[:], func=AF.Sin, scale=vec[:, V_SGN:V_SGN + 1]),
              reads=["tA", "vec"], writes=["tC"])
        P.dma("sp", lambda e, cs=cs: e.dma_start(out=trig_d[:, :, cs], in_=tC[:]), "trg",
              reads=["tC"], writes=["trig_d"])

    keepd = {k: v for k, v in P.last_writer.items() if k == "trig_d"}
    P.barrier()
    P.last_writer.update(keepd)
    A2.free()
    KnT = sb("a_KnT", [128, 4, S], BF16)
    KrT = sb("a_KrT", [128, S], BF16)
    Vaug = sb("a_Vaug", [128, T, 8, 96], BF16)
    P.pool(lambda e: e.memset(Vaug[:], 1.0), writes=["Vall"])
    xin2 = [sb("a_xin%d" % i, [128, 1024], F32) for i in range(2)]
    xb = sb("a_xb", [128, 1024], BF16)
    xT = sb("a_xT", [128, 8, 128], BF16)
    xr = sb("a_xr", [128, 4, 131], F32)
    s12 = [sb("a_s1%d" % i, [128, 8, 128], F32) for i in range(2)]
    sq = sb("a_sq", [128, 3, 128], F32)
    sr = sb("a_sr", [128, 2, 128], F32)
    cqn = sb("a_cqn", [128, 3, 128], BF16)
    trg = sb("a_trg", [128, 2, 128], F32)
    kr1 = sb("a_kr1", [128, 128], F32)
    kr2 = sb("a_kr2", [128, 128], F32)
    acc = sb("a_acc", [128, 4, 128], F32)
    xcb = sb("a_xcb", [128, 4, 128], BF16)
    ta = sb("a_ta", [128, 4, 128], F32)
    ti = sb("a_ti", [128, 4, 128], F32)
    aa = sb("a_aa", [128, 4, 128], F32)
    hh2 = [sb("a_hh%d" % i, [128, 4, 128], F32) for i in range(2)]
    hc = sb("a_hc", [128, 4], F32)
    qsb = sb("a_qsb", [128, 6, 128], F32)
    th = qsb[:, 0:4, :]
    QnT2 = [sb("a_QnT%d" % i, [128, 4, 128], BF16) for i in range(2)]
    QrT2 = [sb("a_QrT%d" % i, [128, 3, 128], BF16) for i in range(2)]
    PT = [sb("a_PT%d" % i, [128, 4, 128], BF16) for i in range(2)]
    rl = sb("a_rl", [128, 128], F32)
    ym = sb("a_ym", [128, 4, 128], F32)
    yT = sb("a_yT", [128, 8, 128], BF16)
    st = sb("a_st", [128, 12], F32)
    mv = sb("a_mv", [128, 4], F32)
    P.pool(lambda e: e.memset(xr[:], 0.0), writes=["xr"])
    P.pool(lambda e: e.memset(hc[:], 0.0), writes=["hc"])

    def b3(i):
        return bank[i][:].rearrange("p (a b) -> p a b", a=4)

    def front(t):
        ts_ = slice(t * 128, (t + 1) * 128)
        pp = t % 2
        xin, s1, hh, QnT, QrT = xin2[pp], s12[pp], hh2[pp], QnT2[pp], QrT2[pp]
        kx, ks1, khh, kqn, kqr = ("xin", pp), ("s1", pp), ("hh", pp), ("QnT", pp), ("QrT", pp)
        P.dma("sp", lambda e: e.dma_start(out=xin[:], in_=x_in[ts_, :]), "xin%d" % pp, writes=[kx])
        P.dma("sp", lambda e: e.dma_start(out=trg[:], in_=trig_d[:, :, ts_]), "trl", reads=["trig_d"], writes=["trg"])
        P.act(lambda e: e.activation(out=xb[:], in_=xin[:], func=AF.Copy), reads=[kx], writes=["xb"])
        yield
        b0 = bank[0][:].bitcast(BF16)
        for c in range(8):
            P.pe(lambda e, c=c: e.transpose(b0[:, c * 128:(c + 1) * 128], xb[:, c * 128:(c + 1) * 128], ident[:]),
                 reads=["xb", "ident"], writes=[bk(0)])
        P.dve(lambda e: e.tensor_copy(out=xT[:].rearrange("p a b -> p (a b)"), in_=b0), reads=[bk(0)], writes=["xT"])
        yield
        yield
        ibank = [1, 2, 3, 0]
        for j in range(16):
            bi = ibank[j // 4]
            for c in range(8):
                P.pe(lambda e, j=j, c=c, bi=bi: e.matmul(b3(bi)[:, j % 4, :], lhsT=w_in[:, c, j * 128:(j + 1) * 128],
                                                        rhs=xT[:, c, :], start=(c == 0), stop=(c == 7)),
                     reads=["w_in", "xT"], writes=[bk(bi)])
            if j % 4 == 3:
                yield
        P.act(lambda e: e.activation(out=xr[:, :, 3:131], in_=b3(1), func=AF.Copy), reads=[bk(1)], writes=["xr"])
        yield
        for i in range(2):
            P.act(lambda e, i=i: e.activation(out=s1[:, 4 * i:4 * i + 4, :], in_=b3(2 + i), func=AF.Tanh, scale=0.5),
                  reads=[bk(2 + i)], writes=[ks1])
            P.dve(lambda e, i=i: e.scalar_tensor_tensor(out=s1[:, 4 * i:4 * i + 4, :], in0=s1[:, 4 * i:4 * i + 4, :],
                                                         scalar=1.0, in1=b3(2 + i), op0=ALU.add, op1=ALU.mult),
                  reads=[ks1, bk(2 + i)], writes=[ks1])
        P.act(lambda e: e.activation(out=sq[:], in_=b3(0)[:, 0:3, :], func=AF.Square), reads=[bk(0)], writes=["sq"])
        yield
        b5 = b3(1)
        P.pe(lambda e: e.matmul(b5[:, 0, :], lhsT=ones32[:], rhs=sq[:, 0, :], start=True, stop=False),
             reads=["sq", "ones32"], writes=[bk(1)])
        yield
        P.pe(lambda e: e.matmul(b5[:, 0, :], lhsT=ones32[:], rhs=sq[:, 1, :], start=False, stop=True),
             reads=["sq", "ones32"], writes=[bk(1)])
        yield
        P.pe(lambda e: e.matmul(b5[:, 1, :], lhsT=ones32[:], rhs=sq[:, 2, :], start=True, stop=True),
             reads=["sq", "ones32"], writes=[bk(1)])
        yield
        P.act(lambda e: e.activation(out=sr[:, 0, :], in_=b5[:, 0, :], func=AF.Sqrt, bias=cst[:, 0:1], scale=1.0 / 256),
              reads=[bk(1), "cst"], writes=["sr"])
        yield
        P.act(lambda e: e.activation(out=sr[:, 1, :], in_=b5[:, 1, :], func=AF.Sqrt, bias=cst[:, 0:1], scale=1.0 / 128),
              reads=[bk(1), "cst"], writes=["sr"])
        yield
        P.dve(lambda e: e.reciprocal(out=sr[:], in_=sr[:]), reads=["sr"], writes=["sr"])
        yield
        for c in range(3):
            P.dve(lambda e, c=c: e.scalar_tensor_tensor(out=cqn[:, c, :], in0=b3(0)[:, c, :],
                                                         scalar=vec[:, V_QN + c:V_QN + c + 1],
                                                         in1=sr[:, min(c, 2) // 2, :], op0=ALU.mult, op1=ALU.mult),
                  reads=[bk(0), "vec", "sr"], writes=["cqn"])
        P.dve(lambda e: e.tensor_tensor(out=kr1[0:32, :], in0=b3(0)[0:32, 3, :], in1=trg[0:32, 0, :], op=ALU.mult),
              reads=[bk(0), "trg"], writes=["kr1"])
        yield
        P.dve(lambda e: e.tensor_tensor(out=kr2[0:32, :], in0=b3(0)[32:64, 3, :], in1=trg[32:64, 1, :], op=ALU.mult),
              reads=[bk(0), "trg"], writes=["kr2"])
        yield
        for i in range(3):
            P.pool(lambda e, i=i: e.tensor_tensor(out=KrT[32 * i:32 * i + 32, ts_], in0=kr1[0:32, :],
                                                  in1=kr2[0:32, :], op=ALU.add),
                   reads=["kr1", "kr2"], writes=[("Kr", t)])
        yield
        for c in range(4):
            P.dve(lambda e, c=c: e.tensor_scalar(out=acc[:, c, :], in0=xr[:, c, 3:131],
                                                 scalar1=vec[:, V_CW + 4 * c + 3:V_CW + 4 * c + 4],
                                                 scalar2=vec[:, V_CB + c:V_CB + c + 1], op0=ALU.mult, op1=ALU.add),
                  reads=["xr", "vec"], writes=[("acc", c)])
            for k in range(3):
                P.dve(lambda e, c=c, k=k: e.scalar_tensor_tensor(out=acc[:, c, :], in0=xr[:, c, k:k + 128],
                                                                  scalar=vec[:, V_CW + 4 * c + k:V_CW + 4 * c + k + 1],
                                                                  in1=acc[:, c, :], op0=ALU.mult, op1=ALU.add),
                      reads=["xr", "vec", ("acc", c)], writes=[("acc", c)])
            if c % 2 == 1:
                yield
        P.pool(lambda e: e.tensor_copy(out=xr[:, :, 0:3], in_=xr[:, :, 128:131]), reads=["xr"], writes=["xr"])
        yield
        accs = [("acc", c) for c in range(4)]
        P.pool(lambda e: e.tensor_copy(out=xcb[:], in_=acc[:]), reads=accs, writes=["xcb"])
        yield
        for c in range(4):
            P.pe(lambda e, c=c: e.matmul(b3(2)[:, c, :], lhsT=gA[:, c, :], rhs=xcb[:, c, :], start=True, stop=True),
                 reads=["gA", "xcb"], writes=[bk(2)])
        for c in range(4):
            P.pe(lambda e, c=c: e.matmul(b3(3)[:, c, :], lhsT=gX[:, c, :], rhs=xcb[:, c, :], start=True, stop=True),
                 reads=["gX", "xcb"], writes=[bk(3)])
        for c in range(4):
            P.act(lambda e, c=c: e.activation(out=ta[:, c, :], in_=b3(2)[:, c, :], func=AF.Tanh,
                                              bias=der[:, 8 + c:9 + c], scale=0.5), reads=[bk(2), "der"], writes=["ta"])
            P.act(lambda e, c=c: e.activation(out=ti[:, c, :], in_=b3(3)[:, c, :], func=AF.Tanh,
                                              bias=der[:, 12 + c:13 + c], scale=0.5), reads=[bk(3), "der"], writes=["ti"])
        yield
        for c in range(4):
            P.act(lambda e, c=c: e.activation(out=aa[:, c, :], in_=ta[:, c, :], func=AF.Exp,
                                              bias=der[:, 4 + c:5 + c], scale=der[:, 4 + c:5 + c]),
                  reads=["ta", "der"], writes=["aa"])
            P.act(lambda e, c=c: e.activation(out=th[:, c, :], in_=ta[:, c, :], func=AF.Tanh,
                                              bias=der[:, c:c + 1], scale=der[:, c:c + 1]),
                  reads=["ta", "der"], writes=["qsb"])
        P.dve(lambda e: e.tensor_tensor(out=ta[:], in0=aa[:], in1=aa[:], op=ALU.mult), reads=["aa"], writes=["ta"])
        yield
        P.dve(lambda e: e.scalar_tensor_tensor(out=ta[:], in0=ta[:], scalar=1.0, in1=th, op0=ALU.add, op1=ALU.mult),
              reads=["ta", "qsb"], writes=["ta"])
        yield
        P.act(lambda e: e.activation(out=ta[:], in_=ta[:], func=AF.Sqrt), reads=["ta"], writes=["ta"])
        yield
        P.dve(lambda e: e.scalar_tensor_tensor(out=ti[:], in0=ti[:], scalar=1.0, in1=acc[:], op0=ALU.add, op1=ALU.mult),
              reads=["ti"] + accs, writes=["ti"])
        yield
        P.dve(lambda e: e.scalar_tensor_tensor(out=ti[:], in0=ti[:], scalar=0.25, in1=ta[:], op0=ALU.mult, op1=ALU.mult),
              reads=["ti", "ta"], writes=["ti"])
        yield
        yield
        for c in range(4):
            P.dve(lambda e, c=c: e.tensor_tensor_scan(out=hh[:, c, :], data0=aa[:, c, :], data1=ti[:, c, :],
                                                       initial=hc[:, c:c + 1], op0=ALU.mult, op1=ALU.add),
                  reads=["aa", "ti", "hc"], writes=[khh])
        P.pool(lambda e: e.tensor_copy(out=hc[:], in_=hh[:, :, 127]), reads=[khh], writes=["hc"])
        yield
        yield
        for j in range(10):
            bi, jj = (j // 4, j % 4)
            for c in range(2):
                P.pe(lambda e, j=j, c=c, bi=bi, jj=jj: e.matmul(b3(bi)[:, jj, :], lhsT=w_uq[:, c, j * 128:(j + 1) * 128],
                                                                rhs=cqn[:, c, :], start=(c == 0), stop=(c == 1)),
                     reads=["w_uq", "cqn"], writes=[bk(bi)])
        for j in range(4):
            P.pe(lambda e, j=j: e.matmul(b3(3)[:, j, :], lhsT=w_kn[:, j * 128:(j + 1) * 128], rhs=cqn[:, 2, :],
                                         start=True, stop=True), reads=["w_kn", "cqn"], writes=[bk(3)])
        P.act(lambda e: e.activation(out=QnT[:], in_=b3(0), func=AF.Copy), reads=[bk(0)], writes=[kqn])
        yield
        P.act(lambda e: e.activation(out=qsb[:, 0:4, :], in_=b3(1), func=AF.Copy), reads=[bk(1)], writes=["qsb"])
        yield
        P.act(lambda e: e.activation(out=qsb[:, 4:6, :], in_=b3(2)[:, 0:2, :], func=AF.Copy), reads=[bk(2)], writes=["qsb"])
        yield
        P.pe(lambda e: e.matmul(bank[0][:], lhsT=cqn[:, 2, :], rhs=w_v[:], start=True, stop=True),
             reads=["w_v", "cqn"], writes=[bk(0)])
        yield
        yield
        cosb = trg[:, 0, :].unsqueeze(1).broadcast_to([128, 3, 128])
        sinb = trg[:, 1, :].unsqueeze(1).broadcast_to([128, 3, 128])
        P.pool(lambda e: e.tensor_tensor(out=qsb[:, 0:3, :], in0=qsb[:, 0:3, :], in1=cosb, op=ALU.mult),
               reads=["qsb", "trg"], writes=["qsb"])
        yield
        P.pool(lambda e: e.tensor_tensor(out=qsb[:, 3:6, :], in0=qsb[:, 3:6, :], in1=sinb, op=ALU.mult),
               reads=["qsb", "trg"], writes=["qsb"])
        yield
        P.pool(lambda e: e.tensor_tensor(out=QrT[:], in0=qsb[:, 0:3, :], in1=qsb[:, 3:6, :], op=ALU.add),
               reads=["qsb"], writes=[kqr])
        yield
        P.act(lambda e: e.activation(out=KnT[:, :, ts_], in_=b3(3), func=AF.Copy), reads=[bk(3)],
              writes=[("Kn", t)])
        yield
        P.dve(lambda e: e.tensor_copy(out=Vaug[:, t, :, 0:64], in_=bank[0][:].rearrange("p (h v) -> p h v", h=8)),
              reads=[bk(0), "Vall"], writes=[("V", t)])
        yield
        yield

    def attention(t, gen):
        pp = t % 2
        QnT, QrT = QnT2[pp], QrT2[pp]
        kqn, kqr = ("QnT", pp), ("QrT", pp)
        items = []
        for h in range(8):
            nb = (t + 4) // 4
            for b in range(nb):
                items.append((h, b, list(range(4 * b, min(4 * b + 4, t + 1)))))

        def qk(i):
            h, b, kts = items[i]
            sbk = 4 + (i % 2)
            j2, hr = h // 2, (h % 2) * 64
            c3, r3 = h // 3, (h % 3) * 32
            for jj, kt in enumerate(kts):
                ks = slice(kt * 128, (kt + 1) * 128)
                P.pe(lambda e, jj=jj, ks=ks: e.matmul(b3(sbk)[:, jj, :], lhsT=KnT[hr:hr + 64, j2, ks],
                                                       rhs=QnT[hr:hr + 64, j2, :], start=True, stop=False),
                     reads=[("Kn", kt), kqn], writes=[bk(sbk)])
                P.pe(lambda e, jj=jj, ks=ks: e.matmul(b3(sbk)[:, jj, :], lhsT=KrT[r3:r3 + 32, ks],
                                                       rhs=QrT[r3:r3 + 32, c3, :], start=False, stop=True),
                     reads=[("Kr", kt), kqr], writes=[bk(sbk)])

        def rest(i):
            h, b, kts = items[i]
            sbk = 4 + (i % 2)
            obk = 6 + (h % 2)
            n = len(kts)
            pt = PT[i % 2]
            ptk = ("PT", i % 2)
            P.act(lambda e: e.activation(out=pt[:, 0:n, :], in_=b3(sbk)[:, 0:n, :], func=AF.Exp, scale=ATT_SCALE),
                  reads=[bk(sbk)], writes=[ptk])
            if t in kts:
                jd = t - 4 * b
                P.pool(lambda e: e.tensor_tensor(out=pt[:, jd, :], in0=pt[:, jd, :], in1=maskT[:], op=ALU.mult),
                       reads=[ptk, "maskT"], writes=[ptk])
            for jj, kt in enumerate(kts):
                P.pe(lambda e, jj=jj, kt=kt: e.matmul(bank[obk][0:96, 0:128], lhsT=Vaug[:, kt, h, :], rhs=pt[:, jj, :],
                                                      start=(kt == 0), stop=(kt == t)),
                     reads=[("V", kt), "Vall", ptk], writes=[bk(obk)])
            if kts[-1] == t:
                j2, hr = h // 2, (h % 2) * 64
                for hf in range(2):
                    P.dve(lambda e, hf=hf: e.reciprocal(out=rl[32 * hf:32 * hf + 32, :], in_=bank[obk][64:96, 0:128]),
                          reads=[bk(obk)], writes=["rl"])
                P.dve(lambda e: e.scalar_tensor_tensor(
                    out=ym[hr:hr + 64, j2, :], in0=bank[obk][0:64, 0:128],
                    scalar=0.5, in1=rl[0:64, :], op0=ALU.mult, op1=ALU.mult),
                    reads=[bk(obk), "rl"], writes=["ym"])

        state = {"k": 0}

        def adv():
            while state["k"] < len(gen):
                g_ = gen[state["k"]]
                if g_ is None:
                    state["k"] += 1
                    continue
                try:
                    next(g_)
                    return
                except StopIteration:
                    state["k"] += 1

        n_adv = max(1, -(-72 // len(items)))
        for i in range(len(items)):
            qk(i)
            if i > 0:
                rest(i - 1)
            for _ in range(n_adv):
                adv()
        rest(len(items) - 1)

    def tail_a(t):
        pp = t % 2
        s1, hh = s12[pp], hh2[pp]
        ks1, khh = ("s1", pp), ("hh", pp)
        P.dve(lambda e: e.tensor_tensor(out=yT[:, 0:4, :], in0=s1[:, 0:4, :], in1=hh[:], op=ALU.mult),
              reads=[ks1, khh], writes=["yT"])
        P.dve(lambda e: e.tensor_tensor(out=yT[:, 4:8, :], in0=s1[:, 4:8, :], in1=ym[:], op=ALU.mult),
              reads=[ks1, "ym"], writes=["yT"])

    def tail(t):
        ts_ = slice(t * 128, (t + 1) * 128)
        pp = t % 2
        xin = xin2[pp]
        kx = ("xin", pp)
        for hf in range(2):
            for c in range(8):
                P.pe(lambda e, hf=hf, c=c: e.matmul(bank[hf][:], lhsT=yT[:, c, :], rhs=w_out[:, c, hf * 512:(hf + 1) * 512],
                                                    start=(c == 0), stop=(c == 7)),
                     reads=["yT", "w_out"], writes=[bk(hf)])
        for hf in range(2):
            hs = slice(hf * 512, (hf + 1) * 512)
            P.dve(lambda e, hf=hf, hs=hs: e.scalar_tensor_tensor(out=xin[:, hs], in0=xin[:, hs], scalar=ALPHA, in1=bank[hf][:],
                                                                  op0=ALU.mult, op1=ALU.add),
                  reads=[kx, bk(hf)], writes=[kx])
            yield
            P.dve(lambda e, hf=hf, hs=hs: e.bn_stats(out=st[:, 6 * hf:6 * hf + 6], in_=xin[:, hs]), reads=[kx],
                  writes=["st"])
            yield
        layer_norm_tail(P, xin, st, mv, cst, None, None, None, [kx])
        yield
        for hf in range(2):
            hs = slice(hf * 512, (hf + 1) * 512)
            P.dma("sp", lambda e, hf=hf: e.dma_start(out=lng[:], in_=d["lng"][:, hf * 512:(hf + 1) * 512].partition_broadcast(128)),
                  "lgl", writes=["lng"])
            P.dma("sp", lambda e, hf=hf: e.dma_start(out=lnb[:], in_=d["lnb"][:, hf * 512:(hf + 1) * 512].partition_broadcast(128)),
                  "lbl", writes=["lnb"])
            P.pool(lambda e, hs=hs: e.tensor_tensor(out=xin[:, hs], in0=xin[:, hs], in1=lng[:], op=ALU.mult),
                   reads=[kx, "lng"], writes=[kx])
            yield
            P.pool(lambda e, hs=hs: e.tensor_tensor(out=xin[:, hs], in0=xin[:, hs], in1=lnb[:], op=ALU.add),
                   reads=[kx, "lnb"], writes=[kx])
            yield
        P.dma("sp", lambda e: e.dma_start(out=x_out[ts_, :], in_=xin[:]), "xo%d" % pp, reads=[kx], writes=[("x1d", t)])

    def exhaust(g_):
        if g_ is not None:
            for _ in g_:
                pass

    exhaust(front(0))
    ptail = None
    for t in range(T):
        gfront = front(t + 1) if t + 1 < T else None
        attention(t, [ptail, gfront])
        exhaust(ptail)
        exhaust(gfront)
        tail_a(t)
        ptail = tail(t)
    exhaust(ptail)
    return A


def layer_norm_tail(P, z, st, mv, cst, lng, lnb, _unused, zk):
    P.dve(lambda e: e.bn_aggr(out=mv[:, 0:2], in_=st[:]), reads=["st"], writes=["mv"])
    P.act(lambda e: e.activation(out=mv[:, 2:3], in_=mv[:, 1:2], func=AF.Sqrt, bias=cst[:, 1:2], scale=1.0),
          reads=["mv", "cst"], writes=["mv2"])
    P.dve(lambda e: e.reciprocal(out=mv[:, 2:3], in_=mv[:, 2:3]), reads=["mv2"], writes=["mv2"])
    P.dve(lambda e: e.scalar_tensor_tensor(out=mv[:, 3:4], in0=mv[:, 0:1], scalar=-1.0, in1=mv[:, 2:3],
                                           op0=ALU.mult, op1=ALU.mult), reads=["mv", "mv2"], writes=["mv3"])
    P.act(lambda e: e.activation(out=z[:], in_=z[:], func=AF.Identity, bias=mv[:, 3:4], scale=mv[:, 2:3]),
          reads=zk + ["mv2", "mv3"], writes=zk)
    if lng is not None:
        P.pool(lambda e: e.tensor_tensor(out=z[:], in0=z[:], in1=lng[:], op=ALU.mult), reads=zk + ["lng"], writes=zk)
        P.pool(lambda e: e.tensor_tensor(out=z[:], in0=z[:], in1=lnb[:], op=ALU.add), reads=zk + ["lnb"], writes=zk)


A_SPECS = [("w_in", [1024, 2048], F32), ("w_out", [1024, 1024], F32), ("w_uq", [256, 1280], F32),
           ("w_kn", [128, 512], F32), ("w_v", [128, 512], F32), ("gA", [128, 4, 128], F32),
           ("gX", [128, 4, 128], F32), ("vec", [128, NV], F32), ("lng", [1, 1024], F32), ("lnb", [1, 1024], F32)]


def prep_A(inp):
    f = np.float32
    w_in = np.asarray(inp["ab_w_in"][0], f)
    kr = w_in[:, 1920:1952]
    kr_sw = np.concatenate([kr[:, 16:32], kr[:, 0:16]], axis=1)
    w_inA = np.concatenate([w_in[:, :1920], kr, kr_sw, np.zeros((1024, 64), f)], axis=1)
    uq = np.asarray(inp["mla_w_uq"][0], f).reshape(256, 8, 96)
    nope = uq[:, :, :64].reshape(256, 512)
    rope = uq[:, :, 64:96]
    rsw = np.concatenate([rope[:, :, 16:32], rope[:, :, 0:16]], axis=2)
    z32 = np.zeros((256, 1, 32), f)
    rope9 = np.concatenate([rope, z32], axis=1).reshape(256, 288)
    rsw9 = np.concatenate([rsw, z32], axis=1).reshape(256, 288)
    pad = np.zeros((256, 96), f)
    w_uqA = np.concatenate([nope, rope9, pad, rsw9, pad], axis=1)
    w_uqA = np.concatenate([nope,
                            np.concatenate([rope9[:, 0:96], np.zeros((256, 32), f)], 1),
                            np.concatenate([rope9[:, 96:192], np.zeros((256, 32), f)], 1),
                            np.concatenate([rope9[:, 192:288], np.zeros((256, 32), f)], 1),
                            np.concatenate([rsw9[:, 0:96], np.zeros((256, 32), f)], 1),
                            np.concatenate([rsw9[:, 96:192], np.zeros((256, 32), f)], 1),
                            np.concatenate([rsw9[:, 192:288], np.zeros((256, 32), f)], 1)], axis=1)
    ukv = np.asarray(inp["mla_w_ukv"][0], f).reshape(128, 8, 128)
    w_kn = np.ascontiguousarray(ukv[:, :, :64]).reshape(128, 512)
    w_v = np.ascontiguousarray(ukv[:, :, 64:]).reshape(128, 512)

    def blockdiag(w):
        w = np.asarray(w[0], f)
        o = np.zeros((128, 4, 128), f)
        for h in range(8):
            r = (h % 2) * 64
            o[r:r + 64, h // 2, r:r + 64] = w[h]
        return o

    vec = np.zeros((128, NV), f)
    cw = np.asarray(inp["ab_conv_w"][0], f)
    for c in range(4):
        for k in range(4):
            vec[:, V_CW + 4 * c + k] = cw[k, c * 128:(c + 1) * 128]
    for nm, col in (("ab_conv_b", V_CB), ("ab_gate_a_b", V_BA), ("ab_gate_x_b", V_BX), ("ab_lambda", V_LAM)):
        vec[:, col:col + 4] = np.asarray(inp[nm][0], f).reshape(4, 128).T
    vec[:, V_QN:V_QN + 2] = np.asarray(inp["mla_q_norm"][0], f).reshape(2, 128).T
    vec[:, V_KVN] = np.asarray(inp["mla_kv_norm"][0], f)
    j = np.arange(128) % 16
    vec[:, V_INVF] = (10000.0 ** (-(2.0 * j) / 32.0)).astype(f)
    vec[:, V_SGN] = np.where((np.arange(128) % 32) < 16, -1.0, 1.0)
    return {"w_in": w_inA, "w_out": np.asarray(inp["ab_w_out"][0], f), "w_uq": w_uqA, "w_kn": w_kn, "w_v": w_v,
            "gA": blockdiag(inp["ab_gate_a_w"]), "gX": blockdiag(inp["ab_gate_x_w"]), "vec": vec,
            "lng": np.asarray(inp["ab_ln_g"], f).reshape(1, 1024), "lnb": np.asarray(inp["ab_ln_b"], f).reshape(1, 1024)}


def declare(nc, specs, pfx):
    return {nm: nc.dram_tensor(pfx + nm, shape, dt, kind="ExternalInput").ap() for nm, shape, dt in specs}


def build_program_A(S):
    nc = bass.Bass("TRN2", target_bir_lowering=False)
    P = Prog(nc)
    d = declare(nc, A_SPECS, "a_")
    d["pos"] = nc.dram_tensor("pos", [1, S], I32, kind="ExternalInput").ap()
    x_in = nc.dram_tensor("x", [S, D], F32, kind="ExternalInput").ap()
    x_out = nc.dram_tensor("x1", [S, D], F32, kind="ExternalOutput").ap()
    build_A(nc, P, S, d, x_in, x_out)
    st = P.emit()
    return nc, st


VB_CW, VB_CB, NVB = 0, 96, 120
RB_DTB, RB_ALOG, RB_D, RB_NW, RB_LNG, RB_LNB, RB_CB, NRB = 0, 32, 64, 96, 2144, 3168, 4192, 7264


def build_B(nc, P, S, d, x_in, x_out):
    T = S // 128
    A = Alloc(nc)
    sb, ps = A.sb, A.ps
    ident, maskT, ones32 = make_consts(nc, P, A, "b_")
    bank = [ps("b_bank%d" % i, [128, 512], F32) for i in range(8)]
    rr_state = [0]

    def rr():
        rr_state[0] = (rr_state[0] + 1) % 8
        return rr_state[0]

    def bk(i):
        return ("bank", i)

    def b3(i):
        return bank[i][:].rearrange("p (a b) -> p a b", a=4)

    tri = sb("b_tri", [128, 128], F32)
    u2 = sb("b_u2", [128, 128], F32)
    m025 = sb("b_m025", [128, 128], F32)
    onesb = sb("b_onesb", [128, 128], BF16)
    cst = sb("b_cst", [128, 8], F32)
    P.pool(lambda e: e.memset(tri[:], 1.0), writes=["tri"])
    P.pool(lambda e: e.affine_select(out=tri[:], in_=tri[:], pattern=[[1, 128]], compare_op=ALU.is_ge, fill=0.0,
                                     base=0, channel_multiplier=-1), reads=["tri"], writes=["tri"])
    P.pool(lambda e: e.memset(u2[:], 1.0), writes=["u2"])
    P.pool(lambda e: e.affine_select(out=u2[:], in_=u2[:], pattern=[[-1, 128]], compare_op=ALU.is_gt, fill=0.0,
                                     base=0, channel_multiplier=1), reads=["u2"], writes=["u2"])
    P.pool(lambda e: e.tensor_scalar(out=m025[:], in0=tri[:], scalar1=0.25, scalar2=None, op0=ALU.mult),
           reads=["tri"], writes=["m025"])
    P.pool(lambda e: e.memset(onesb[:], 1.0), writes=["onesb"])
    for i, v in enumerate([1e-6, 1e-5, 0.0, 1.0]):
        P.pool(lambda e, i=i, v=v: e.memset(cst[:, i:i + 1], v), writes=["cst"])
    w_in = sb("b_w_in", [128, 8, 5152], BF16)
    w_out = sb("b_w_out", [128, 16, 1024], BF16)
    vecb = sb("b_vec", [128, NVB], F32)
    cbrow = sb("b_cbrow", [128, 8, 128], BF16)
    dtb = sb("b_dtb", [128, 32], F32)
    aneg = sb("b_aneg", [128, 32], F32)
    cd = sb("b_cd", [128, 32], F32)
    normw = sb("b_normw", [128, 512], F32)
    dg = sb("b_dg", [128, 96, 128], BF16)
    for c in range(0, 8, 2):
        P.dma("pool", lambda e, c=c: e.dma_start(out=w_in[:, c:c + 2, :],
                                                 in_=d["w_in"][c * 128:(c + 2) * 128, :].rearrange("(c p) n -> p c n", p=128)),
              "wl", writes=["w_in"], chain=False)
    for c in range(0, 16, 4):
        P.dma("pool", lambda e, c=c: e.dma_start(out=w_out[:, c:c + 4, :],
                                                 in_=d["w_out"][c * 128:(c + 4) * 128, :].rearrange("(c p) n -> p c n", p=128)),
              "wl", writes=["w_out"], chain=False)
    cb3 = d["row"][:, RB_CB:RB_CB + 3072].rearrange("o (i r c) -> o i r c", r=3, c=128)
    for r_ in range(3):
        P.dma("pool", lambda e, r_=r_: e.dma_start(out=cbrow[32 * r_:32 * r_ + 1, :, :], in_=cb3[:, :, r_, :]),
              "wl", writes=["cbrow"], chain=False)
    P.dma("sp", lambda e: e.dma_start(out=vecb[:], in_=d["vec"]), "wl2", writes=["vecb"])
    for tl, off, n, key in ((dtb, RB_DTB, 32, "dtb"), (aneg, RB_ALOG, 32, "aneg"), (cd, RB_D, 32, "cd")):
        P.dma("sp", lambda e, tl=tl, off=off, n=n: e.dma_start(out=tl[:], in_=d["row"][:, off:off + n].partition_broadcast(128)),
              "wl2", writes=[key])
    P.act(lambda e: e.activation(out=aneg[:], in_=aneg[:], func=AF.Exp), reads=["aneg"], writes=["aneg"])
    P.dve(lambda e: e.tensor_scalar(out=aneg[:], in0=aneg[:], scalar1=-1.0, scalar2=None, op0=ALU.mult),
          reads=["aneg"], writes=["aneg"])
    P.dve(lambda e: e.tensor_scalar(out=cd[:], in0=cd[:], scalar1=0.5, scalar2=None, op0=ALU.mult),
          reads=["cd"], writes=["cd"])
    for jk in range(96):
        P.dve(lambda e, jk=jk: e.tensor_scalar(out=dg[:, jk, :], in0=ident[:], scalar1=vecb[:, jk:jk + 1], scalar2=None,
                                               op0=ALU.mult), reads=["ident", "vecb"], writes=["dg"])
    hT = sb("b_hT", [128, 2048], F32)
    hTb = sb("b_hTb", [128, 2048], BF16)
    P.pool(lambda e: e.memset(hT[:], 0.0), writes=["hT"])
    P.pool(lambda e: e.memset(hTb[:], 0.0), writes=["hTb"])
    xrb = sb("b_xrb", [128, 24, 131], BF16)
    P.pool(lambda e: e.memset(xrb[:], 0.0), writes=["xrb"])
    xin2 = [sb("b_xin%d" % i, [128, 1024], F32) for i in range(2)]
    xb = sb("b_xb", [128, 1024], BF16)
    xT = sb("b_xT", [128, 8, 128], BF16)
    sm = sb("b_sm", [128, 12, 32], F32)
    Rg = sb("b_Rg", [128, 4, 128], F32)
    es = sb("b_es", [128, 4, 128], F32)
    MT2 = [sb("b_MT%d" % i, [128, 8, 128], BF16) for i in range(2)]
    szg2 = [sb("b_szg%d" % i, [128, 512], F32) for i in range(2)]
    xs2 = sb("b_xs2", [128, 512], F32)
    xf2 = [sb("b_xf%d" % i, [128, 512], BF16) for i in range(2)]
    xsd2 = [sb("b_xsd%d" % i, [128, 512], BF16) for i in range(2)]
    xfd2 = [sb("b_xfd%d" % i, [128, 512], BF16) for i in range(2)]
    tnh = sb("b_tnh", [128, 512], F32)
    lng = normw
    lnb = tnh
    B2 = sb("b_B2", [128, 512], BF16)
    BCT = sb("b_BCT", [128, 8, 128], BF16)
    GTm = sb("b_GTm", [128, 128], F32)
    yb = sb("b_yb", [128, 512], F32)
    ssq = sb("b_ssq", [128, 4], F32)
    yn = xb[:, 0:512]
    junk = xb[:, 512:1024]
    ynT = sb("b_ynT", [128, 16, 128], BF16)
    st = sb("b_st", [128, 12], F32)
    mv = sb("b_mv", [128, 4], F32)

    def silu2(src_bank_ap, out_ap, key_in, key_out, shape3=None):
        P.act(lambda e: e.activation(out=tnh[:] if shape3 is None else tnh[:].rearrange("p (a b) -> p a b", a=4),
                                     in_=src_bank_ap, func=AF.Tanh, scale=0.5), reads=[key_in], writes=["tnh"])
        P.dve(lambda e: e.scalar_tensor_tensor(out=out_ap, in0=tnh[:] if shape3 is None else tnh[:].rearrange("p (a b) -> p a b", a=4),
                                               scalar=1.0, in1=src_bank_ap, op0=ALU.add, op1=ALU.mult),
              reads=["tnh", key_in], writes=[key_out])

    def silu2g(src_bank_ap, out_ap, key_in, key_out, shape3=None):
        tv = tnh[:] if shape3 is None else tnh[:].rearrange("p (a b) -> p a b", a=4)
        P.act(lambda e: e.activation(out=tv, in_=src_bank_ap, func=AF.Tanh, scale=0.5), reads=[key_in], writes=["tnh"])
        yield
        P.dve(lambda e: e.scalar_tensor_tensor(out=out_ap, in0=tv, scalar=1.0, in1=src_bank_ap, op0=ALU.add, op1=ALU.mult),
              reads=["tnh", key_in], writes=[key_out])
        yield

    def lockstep(*gens):
        gens = [g_ for g_ in gens if g_ is not None]
        while gens:
            for g_ in list(gens):
                try:
                    next(g_)
                except StopIteration:
                    gens.remove(g_)

    def load_x(t):
        if t >= T:
            return
        tsl = slice(t * 128, (t + 1) * 128)
        xi = xin2[t % 2]
        P.dma("sp", lambda e: e.dma_start(out=xi[:], in_=x_in[tsl, :]), "xin%d" % (t % 2), reads=[("x1d", t)],
              writes=[("xin", t % 2)])

    load_x(0)
    for t in range(T):
        ts_ = slice(t * 128, (t + 1) * 128)
        xin = xin2[t % 2]
        kx = ("xin", t % 2)
        P.act(lambda e, xin=xin: e.activation(out=xb[:], in_=xin[:], func=AF.Copy), reads=[kx], writes=["xb"])
        r0 = rr()
        b0 = bank[r0][:].bitcast(BF16)
        for c in range(8):
            P.pe(lambda e, c=c, b0=b0: e.transpose(b0[:, c * 128:(c + 1) * 128], xb[:, c * 128:(c + 1) * 128], ident[:]),
                 reads=["xb", "ident"], writes=[bk(r0)])
        P.dve(lambda e, b0=b0: e.tensor_copy(out=xT[:].rearrange("p a b -> p (a b)"), in_=b0), reads=[bk(r0)], writes=["xT"])
        r = rr()
        for c in range(8):
            P.pe(lambda e, c=c, r=r: e.matmul(bank[r][:, 0:32], lhsT=xT[:, c, :], rhs=w_in[:, c, 5120:5152],
                                              start=(c == 0), stop=(c == 7)), reads=["xT", "w_in"], writes=[bk(r)])
        P.dve(lambda e, r=r: e.tensor_tensor(out=sm[:, 0, :], in0=bank[r][:, 0:32], in1=dtb[:], op=ALU.add),
              reads=[bk(r), "dtb"], writes=["sm0"])
        P.act(lambda e: e.activation(out=sm[:, 1, :], in_=sm[:, 0, :], func=AF.Abs), reads=["sm0"], writes=["sm1"])
        P.act(lambda e: e.activation(out=sm[:, 1, :], in_=sm[:, 1, :], func=AF.Exp, scale=-1.0), reads=["sm1"], writes=["sm1"])
        P.act(lambda e: e.activation(out=sm[:, 1, :], in_=sm[:, 1, :], func=AF.Ln, bias=1.0), reads=["sm1"], writes=["sm1"])
        P.dve(lambda e: e.scalar_tensor_tensor(out=sm[:, 2, :], in0=sm[:, 0, :], scalar=0.0, in1=sm[:, 1, :],
                                               op0=ALU.max, op1=ALU.add), reads=["sm0", "sm1"], writes=["dt"])
        P.dve(lambda e: e.tensor_tensor(out=sm[:, 3, :], in0=sm[:, 2, :], in1=aneg[:], op=ALU.mult),
              reads=["dt", "aneg"], writes=["adt"])
        P.dve(lambda e: e.tensor_scalar(out=sm[:, 4, :], in0=sm[:, 2, :], scalar1=0.5, scalar2=None, op0=ALU.mult),
              reads=["dt"], writes=["cxf"])
        r = rr()
        for i, lh in enumerate((tri, u2, ones32)):
            P.pe(lambda e, i=i, lh=lh, r=r: e.matmul(bank[r][:, 32 * i:32 * i + 32], lhsT=lh[:], rhs=sm[:, 3, :],
                                                     start=True, stop=True), reads=["adt", "tri", "u2", "ones32"],
                 writes=[bk(r)])
        P.act(lambda e, r=r: e.activation(out=sm[:, 7:10, :].rearrange("p a b -> p (a b)"), in_=bank[r][:, 0:96], func=AF.Exp),
              reads=[bk(r)], writes=["e3"])
        P.dve(lambda e: e.scalar_tensor_tensor(out=sm[:, 5, :], in0=sm[:, 4, :], scalar=0.5, in1=sm[:, 8, :],
                                               op0=ALU.mult, op1=ALU.mult), reads=["cxf", "e3"], writes=["cxfd"])
        P.dve(lambda e: e.tensor_scalar(out=sm[:, 6, :], in0=sm[:, 7, :], scalar1=0.5, scalar2=None, op0=ALU.mult),
              reads=["e3"], writes=["eoff"])
        for q in range(6):
            r = rr()
            for jj in range(4):
                j = 4 * q + jj
                for c in range(8):
                    P.pe(lambda e, j=j, jj=jj, c=c, r=r: e.matmul(b3(r)[:, jj, :], lhsT=w_in[:, c, 2048 + j * 128:2048 + (j + 1) * 128],
                                                                  rhs=xT[:, c, :], start=(c == 0), stop=(c == 7)),
                         reads=["w_in", "xT"], writes=[bk(r)])
            P.act(lambda e, q=q, r=r: e.activation(out=xrb[:, 4 * q:4 * q + 4, 3:131], in_=b3(r), func=AF.Copy),
                  reads=[bk(r)], writes=["xrb"])

        def conv_tok(r, j0):
            for jj in range(4):
                j = j0 + jj
                osl = bank[r][:, jj * 128:(jj + 1) * 128]
                for k in range(4):
                    P.pe(lambda e, j=j, k=k, osl=osl: e.matmul(osl, lhsT=xrb[:, j, k:k + 128], rhs=dg[:, 4 * j + k, :],
                                                               start=(k == 0), stop=False), reads=["xrb", "dg"], writes=[bk(r)])
                P.pe(lambda e, j=j, osl=osl: e.matmul(osl, lhsT=onesb[32 * (j % 3):32 * (j % 3) + 1, :], rhs=cbrow[32 * (j % 3):32 * (j % 3) + 1, j // 3, :],
                                                      start=False, stop=True), reads=["onesb", "cbrow"], writes=[bk(r)])

        def conv_feat(r, j0):
            for jj in range(4):
                j = j0 + jj
                osl = bank[r][:, jj * 128:(jj + 1) * 128]
                for k in range(4):
                    P.pe(lambda e, j=j, k=k, osl=osl: e.matmul(osl, lhsT=dg[:, 4 * j + k, :], rhs=xrb[:, j, k:k + 128],
                                                               start=(k == 0), stop=False), reads=["xrb", "dg"], writes=[bk(r)])
                P.pe(lambda e, j=j, osl=osl: e.matmul(osl, lhsT=cbrow[32 * (j % 3):32 * (j % 3) + 1, j // 3, :], rhs=onesb[32 * (j % 3):32 * (j % 3) + 1, :],
                                                      start=False, stop=True), reads=["onesb", "cbrow"], writes=[bk(r)])

        r = rr()
        conv_tok(r, 16)
        silu2(bank[r][:], B2[:], bk(r), "B2")
        for i in range(2):
            r = rr()
            conv_feat(r, 16 + 4 * i)
            silu2(b3(r), BCT[:, 4 * i:4 * i + 4, :], bk(r), "BCT", shape3=True)
        def stage1(g):
            gs = slice(g * 512, (g + 1) * 512)
            hs8 = slice(g * 8, (g + 1) * 8)
            pp = g % 2
            szg, xf, xsd, xfd, MT = szg2[pp], xf2[pp], xsd2[pp], xfd2[pp], MT2[pp]
            tv = tnh[:]

            def bc(i, hs8=hs8):
                return sm[:, i, hs8].unsqueeze(2).broadcast_to([128, 8, 64])

            def mkR(hf):
                h4 = slice(g * 8 + hf * 4, g * 8 + hf * 4 + 4)
                P.pool(lambda e: e.tensor_tensor(out=Rg[:], in0=tri[:].unsqueeze(1).broadcast_to([128, 4, 128]),
                                                 in1=sm[:, 3, h4].unsqueeze(2).broadcast_to([128, 4, 128]), op=ALU.mult),
                       reads=["tri", "adt"], writes=["Rg"])

            def seg_mm(r):
                P.pe(lambda e: e.matmul(bank[r][:], lhsT=u2[:], rhs=Rg[:].rearrange("p a b -> p (a b)"), start=True, stop=True),
                     reads=["u2", "Rg"], writes=[bk(r)])

            def seg_exp(r):
                P.act(lambda e: e.activation(out=es[:].rearrange("p a b -> p (a b)"), in_=bank[r][:], func=AF.Exp),
                      reads=[bk(r)], writes=["es"])

            def mk_mt(hf):
                P.dve(lambda e: e.tensor_tensor(out=MT[:, 4 * hf:4 * hf + 4, :], in0=es[:],
                                                in1=GTm[:].unsqueeze(1).broadcast_to([128, 4, 128]), op=ALU.mult),
                      reads=["es", "GTm"], writes=[("MT", pp, hf)])

            mkR(0)
            rg = rr()
            P.pe(lambda e: e.matmul(bank[rg][:, 0:128], lhsT=BCT[:, g, :], rhs=BCT[:, 4 + g, :], start=True, stop=True),
                 reads=["BCT"], writes=[bk(rg)])
            yield
            rc = rr()
            conv_tok(rc, 4 * g)
            yield
            rs0 = rr()
            seg_mm(rs0)
            P.dve(lambda e: e.tensor_tensor(out=GTm[:], in0=bank[rg][:, 0:128], in1=m025[:], op=ALU.mult),
                  reads=[bk(rg), "m025"], writes=["GTm"])
            yield
            P.act(lambda e: e.activation(out=tv, in_=bank[rc][:], func=AF.Tanh, scale=0.5), reads=[bk(rc)], writes=["tnh"])
            rz = rr()
            for c in range(8):
                P.pe(lambda e, c=c: e.matmul(bank[rz][:], lhsT=xT[:, c, :], rhs=w_in[:, c, gs], start=(c == 0),
                                             stop=(c == 7)), reads=["xT", "w_in"], writes=[bk(rz)])
            yield
            seg_exp(rs0)
            yield
            P.dve(lambda e: e.scalar_tensor_tensor(out=xs2[:], in0=tv, scalar=1.0, in1=bank[rc][:], op0=ALU.add, op1=ALU.mult),
                  reads=["tnh", bk(rc)], writes=["xs2"])
            mkR(1)
            yield
            mk_mt(0)
            rs1 = rr()
            seg_mm(rs1)
            yield
            x3 = xs2[:].rearrange("p (h v) -> p h v", h=8)
            P.act(lambda e: e.activation(out=tv, in_=bank[rz][:], func=AF.Tanh, scale=0.5), reads=[bk(rz)], writes=["tnh"])
            P.dve(lambda e: e.tensor_tensor(out=xf[:].rearrange("p (h v) -> p h v", h=8), in0=x3, in1=bc(4),
                                            op=ALU.mult), reads=["xs2", "cxf"], writes=[("xf", pp)])
            P.pool(lambda e: e.tensor_tensor(out=xsd[:].rearrange("p (h v) -> p h v", h=8), in0=x3,
                                             in1=cd[:, hs8].unsqueeze(2).broadcast_to([128, 8, 64]), op=ALU.mult),
                   reads=["xs2", "cd"], writes=[("xsd", pp)])
            yield
            seg_exp(rs1)
            P.pool(lambda e: e.tensor_tensor(out=xfd[:].rearrange("p (h v) -> p h v", h=8), in0=x3, in1=bc(5),
                                             op=ALU.mult), reads=["xs2", "cxfd"], writes=[("xfd", pp)])
            yield
            P.dve(lambda e: e.scalar_tensor_tensor(out=szg[:], in0=tv, scalar=1.0, in1=bank[rz][:], op0=ALU.add, op1=ALU.mult),
                  reads=["tnh", bk(rz)], writes=[("szg", pp)])
            yield
            mk_mt(1)
            yield

        def stage2(g):
            gs = slice(g * 512, (g + 1) * 512)
            hs8 = slice(g * 8, (g + 1) * 8)
            pp = g % 2
            szg, xf, xsd, xfd, MT = szg2[pp], xf2[pp], xsd2[pp], xfd2[pp], MT2[pp]

            def bc(i, hs8=hs8):
                return sm[:, i, hs8].unsqueeze(2).broadcast_to([128, 8, 64])
            P.dma("sp", lambda e, g=g: e.dma_start(out=normw[:], in_=d["row"][:, RB_NW + g * 512:RB_NW + (g + 1) * 512].partition_broadcast(128)),
                  "nwl", writes=["normw"])
            ry = rr()
            P.pe(lambda e, ry=ry: e.matmul(bank[ry][:], lhsT=ident[:], rhs=xsd[:], start=True, stop=False),
                 reads=["ident", ("xsd", pp)], writes=[bk(ry)])
            yield
            for hh in range(8):
                P.pe(lambda e, ry=ry, hh=hh: e.matmul(bank[ry][:, hh * 64:(hh + 1) * 64], lhsT=MT[:, hh, :],
                                                      rhs=xf[:, hh * 64:(hh + 1) * 64], start=False, stop=(hh == 7)),
                     reads=[("MT", pp, hh // 4), ("xf", pp)], writes=[bk(ry)])
                yield
            ro = rr()
            P.pe(lambda e, ro=ro, g=g, gs=gs: e.matmul(bank[ro][:], lhsT=BCT[:, 4 + g, :], rhs=hTb[:, gs], start=True, stop=True),
                 reads=["BCT", ("hTb", g)], writes=[bk(ro)])
            yield
            P.dve(lambda e, ro=ro, bc=bc: e.tensor_tensor(out=yb[:].rearrange("p (h v) -> p h v", h=8),
                                                          in0=bank[ro][:].rearrange("p (h v) -> p h v", h=8), in1=bc(6), op=ALU.mult),
                  reads=[bk(ro), "eoff"], writes=["yb"])
            yield
            P.dve(lambda e, ry=ry: e.tensor_tensor(out=yb[:], in0=yb[:], in1=bank[ry][:], op=ALU.add),
                  reads=["yb", bk(ry)], writes=["yb"])
            yield
            P.dve(lambda e: e.tensor_tensor(out=yb[:], in0=yb[:], in1=szg[:], op=ALU.mult), reads=["yb", ("szg", pp)], writes=["yb"])
            yield
            P.act(lambda e, g=g: e.activation(out=junk, in_=yb[:], func=AF.Square, accum_out=ssq[:, g:g + 1]),
                  reads=["yb"], writes=["junk", "ssq"])
            yield
            P.act(lambda e, g=g: e.activation(out=ssq[:, g:g + 1], in_=ssq[:, g:g + 1], func=AF.Sqrt, bias=cst[:, 0:1],
                                              scale=0.25 / 512), reads=["ssq", "cst"], writes=["ssq"])
            yield
            P.dve(lambda e, g=g: e.reciprocal(out=ssq[:, g:g + 1], in_=ssq[:, g:g + 1]), reads=["ssq"], writes=["ssq"])
            yield
            P.dve(lambda e, g=g: e.tensor_scalar(out=ssq[:, g:g + 1], in0=ssq[:, g:g + 1], scalar1=0.5, scalar2=None,
                                                 op0=ALU.mult), reads=["ssq"], writes=["ssq"])
            yield
            P.dve(lambda e, g=g: e.scalar_tensor_tensor(out=yn, in0=yb[:], scalar=ssq[:, g:g + 1], in1=normw[:],
                                                        op0=ALU.mult, op1=ALU.mult), reads=["yb", "ssq", "normw"], writes=["yn"])
            yield
            r = rr()
            bt = bank[r][:].bitcast(BF16)
            for c in range(4):
                P.pe(lambda e, c=c, bt=bt: e.transpose(bt[:, c * 128:(c + 1) * 128], xb[:, c * 128:(c + 1) * 128], ident[:]),
                     reads=["yn", "ident"], writes=[bk(r)])
                yield
            P.act(lambda e, g=g, bt=bt: e.activation(out=ynT[:, 4 * g:4 * g + 4, :].rearrange("p a b -> p (a b)"), in_=bt[:, 0:512],
                                                     func=AF.Copy), reads=[bk(r)], writes=[("ynT", g)])
            yield
            r = rr()
            P.pe(lambda e, r=r, g=g: e.matmul(bank[r][:], lhsT=B2[:, g * 128:(g + 1) * 128], rhs=xfd[:], start=True, stop=True),
                 reads=["B2", ("xfd", pp)], writes=[bk(r)])
            yield
            h3 = hT[:, gs].rearrange("p (h v) -> p h v", h=8)
            P.dve(lambda e, h3=h3, bc=bc: e.tensor_tensor(out=h3, in0=h3, in1=bc(9), op=ALU.mult),
                  reads=[("hT", g), "e3"], writes=[("hT", g)])
            yield
            P.dve(lambda e, r=r, gs=gs: e.tensor_tensor(out=hT[:, gs], in0=hT[:, gs], in1=bank[r][:], op=ALU.add),
                  reads=[("hT", g), bk(r)], writes=[("hT", g)])
            yield
            P.pool(lambda e, gs=gs: e.tensor_copy(out=hTb[:, gs], in_=hT[:, gs]), reads=[("hT", g)], writes=[("hTb", g)])
            yield
        lockstep(stage1(0))
        for g in range(4):
            lockstep(stage1(g + 1) if g + 1 < 4 else None, stage2(g))
        P.pool(lambda e: e.tensor_copy(out=xrb[:, :, 0:3], in_=xrb[:, :, 128:131]), reads=["xrb"], writes=["xrb"])
        load_x(t + 1)
        ynk = [("ynT", g) for g in range(4)]
        rs = [rr(), rr()]
        for hf in range(2):
            for c in range(16):
                P.pe(lambda e, hf=hf, c=c, r=rs[hf]: e.matmul(bank[r][:], lhsT=ynT[:, c, :], rhs=w_out[:, c, hf * 512:(hf + 1) * 512],
                                                              start=(c == 0), stop=(c == 15)), reads=ynk + ["w_out"], writes=[bk(rs[hf])])
        for hf in range(2):
            hs = slice(hf * 512, (hf + 1) * 512)
            P.dve(lambda e, hs=hs, r=rs[hf], xin=xin: e.scalar_tensor_tensor(out=xin[:, hs], in0=xin[:, hs], scalar=ALPHA, in1=bank[r][:],
                                                                             op0=ALU.mult, op1=ALU.add), reads=[kx, bk(rs[hf])], writes=[kx])
            P.dve(lambda e, hf=hf, hs=hs, xin=xin: e.bn_stats(out=st[:, 6 * hf:6 * hf + 6], in_=xin[:, hs]), reads=[kx], writes=["st"])
        layer_norm_tail(P, xin, st, mv, cst, None, None, None, [kx])
        for hf in range(2):
            hs = slice(hf * 512, (hf + 1) * 512)
            P.dma("sp", lambda e, hf=hf: e.dma_start(out=lng[:], in_=d["row"][:, RB_LNG + hf * 512:RB_LNG + (hf + 1) * 512].partition_broadcast(128)),
                  "lgl", writes=["normw"])
            P.dma("sp", lambda e, hf=hf: e.dma_start(out=lnb[:], in_=d["row"][:, RB_LNB + hf * 512:RB_LNB + (hf + 1) * 512].partition_broadcast(128)),
                  "lbl", writes=["tnh"])
            P.pool(lambda e, hs=hs, xin=xin: e.tensor_tensor(out=xin[:, hs], in0=xin[:, hs], in1=lng[:], op=ALU.mult), reads=[kx, "normw"], writes=[kx])
            P.pool(lambda e, hs=hs, xin=xin: e.tensor_tensor(out=xin[:, hs], in0=xin[:, hs], in1=lnb[:], op=ALU.add), reads=[kx, "tnh"], writes=[kx])
        P.dma("sp", lambda e, ts_=ts_, xin=xin: e.dma_start(out=x_out[ts_, :], in_=xin[:]), "xo%d" % (t % 2), reads=[kx], writes=[("outd", t)])
    return A


B_SPECS = [("w_in", [1024, 5152], F32), ("w_out", [2048, 1024], F32), ("vec", [128, NVB], F32), ("row", [1, NRB], F32)]


def prep_B(inp):
    f = np.float32
    vec = np.zeros((128, NVB), f)
    cw = np.asarray(inp["ssd_conv_w"][0], f)
    for j in range(24):
        for k in range(4):
            vec[:, VB_CW + 4 * j + k] = cw[k, j * 128:(j + 1) * 128]
    cb = np.asarray(inp["ssd_conv_b"][0], f)
    vec[:, VB_CB:VB_CB + 24] = cb.reshape(24, 128).T
    row = np.zeros((1, NRB), f)
    row[0, RB_DTB:RB_DTB + 32] = np.asarray(inp["ssd_dt_bias"][0], f)
    row[0, RB_ALOG:RB_ALOG + 32] = np.asarray(inp["ssd_a_log"][0], f)
    row[0, RB_D:RB_D + 32] = np.asarray(inp["ssd_d"][0], f)
    row[0, RB_NW:RB_NW + 2048] = np.asarray(inp["ssd_norm"][0], f)
    row[0, RB_LNG:RB_LNG + 1024] = np.asarray(inp["ssd_ln_g"][0], f)
    row[0, RB_LNB:RB_LNB + 1024] = np.asarray(inp["ssd_ln_b"][0], f)
    row[0, RB_CB:RB_CB + 3072] = cb
    return {"w_in": np.asarray(inp["ssd_w_in"][0], f), "w_out": np.asarray(inp["ssd_w_out"][0], f), "vec": vec, "row": row}


def build_program_B(S):
    nc = bass.Bass("TRN2", target_bir_lowering=False)
    P = Prog(nc)
    d = declare(nc, B_SPECS, "b_")
    x_in = nc.dram_tensor("x1", [S, D], F32, kind="ExternalInput").ap()
    x_out = nc.dram_tensor("out", [S, D], F32, kind="ExternalOutput").ap()
    build_B(nc, P, S, d, x_in, x_out)
    st = P.emit()
    return nc, st


_CACHE = {}


def build_program_fused(S):
    nc = bass.Bass("TRN2", target_bir_lowering=False)
    P = Prog(nc)
    dA = declare(nc, A_SPECS, "a_")
    dA["pos"] = nc.dram_tensor("pos", [1, S], I32, kind="ExternalInput").ap()
    dB = declare(nc, B_SPECS, "b_")
    x_in = nc.dram_tensor("x", [S, D], F32, kind="ExternalInput").ap()
    x1 = nc.dram_tensor("x1_scratch", [S, D], F32).ap()
    x_out = nc.dram_tensor("out", [S, D], F32, kind="ExternalOutput").ap()
    A = build_A(nc, P, S, dA, x_in, x1)
    keep = {k: v for k, v in P.last_writer.items() if isinstance(k, tuple) and k[0] == "x1d"}
    P.barrier()
    P.last_writer.update(keep)
    A.free()
    build_B(nc, P, S, dB, x1, x_out)
    st = P.emit()
    return nc, st


def kernel(**inputs):
    S = SEQ
    x = np.ascontiguousarray(np.asarray(inputs["x"], np.float32))
    pos = np.ascontiguousarray(np.asarray(inputs["positions"], np.int32))
    hpA = prep_A(inputs)
    hpB = prep_B(inputs)
    mode = "fused"
    if mode == "split":
        if "A" not in _CACHE:
            _CACHE["A"] = build_program_A(S)[0]
            _CACHE["B"] = build_program_B(S)[0]
        mapsA = []
        for b in range(NCORES):
            m = {"a_" + k: v for k, v in hpA.items()}
            m["x"] = x[b]
            m["pos"] = pos[b:b + 1]
            mapsA.append(m)
        resA = run_bass_kernel_spmd(_CACHE["A"], mapsA, core_ids=list(range(NCORES)))
        mapsB = []
        for b in range(NCORES):
            m = {"b_" + k: v for k, v in hpB.items()}
            m["x1"] = np.ascontiguousarray(np.asarray(resA.results[b]["x1"], np.float32))
            mapsB.append(m)
        resB = run_bass_kernel_spmd(_CACHE["B"], mapsB, core_ids=list(range(NCORES)))
        return np.stack([np.asarray(resB.results[b]["out"], np.float32) for b in range(NCORES)], axis=0)
    if "F" not in _CACHE:
        _CACHE["F"] = build_program_fused(S)[0]
    maps = []
    for b in range(NCORES):
        m = {"a_" + k: v for k, v in hpA.items()}
        m.update({"b_" + k: v for k, v in hpB.items()})
        m["x"] = x[b]
        m["pos"] = pos[b:b + 1]
        maps.append(m)
    res = run_bass_kernel_spmd(_CACHE["F"], maps, core_ids=list(range(NCORES)))
    return np.stack([np.asarray(res.results[b]["out"], np.float32) for b in range(NCORES)], axis=0)
```

```python
import math
import numpy as np
import concourse.bass as bass
import concourse.mybir as mybir
from concourse.bass_utils import run_bass_kernel_spmd

F32 = mybir.dt.float32
BF16 = mybir.dt.bfloat16
I32 = mybir.dt.int32
AF = mybir.ActivationFunctionType
ALU = mybir.AluOpType

D = 1024
NCORES = 8
SEQ = 4096
ALPHA = 4.0 ** 0.25
MAGIC = 12582912.0
C1 = 6.28125
C2 = 2.0 * math.pi - 6.28125


class Op:
    __slots__ = ("eng", "fn", "deps", "is_dma", "sem", "val", "marked", "dma_wait")

    def __init__(self, eng, fn, is_dma=False, sem=None):
        self.eng = eng
        self.fn = fn
        self.deps = []
        self.is_dma = is_dma
        self.sem = sem
        self.val = None
        self.marked = False
        self.dma_wait = {}


class Prog:
    ENGS = ("pe", "act", "dve", "pool", "sp")

    def __init__(self, nc):
        self.nc = nc
        self.eobj = {"pe": nc.tensor, "act": nc.scalar, "dve": nc.vector,
                     "pool": nc.gpsimd, "sp": nc.sync}
        self.ops = []
        self.last_writer = {}
        self.readers = {}
        self.dma_count = {}
        self.dma_last = {}
        self.last_on = {}
        self.bar = {}

    def _add(self, op, reads, writes):
        deps = op.deps

        def need(p):
            if p is None or p is op:
                return
            if p.is_dma:
                op.dma_wait[p.sem] = self.dma_count[p.sem]
            else:
                deps.append(p)

        b = self.bar.pop(op.eng, None)
        if b is not None:
            for p in b[0]:
                need(p)
            for s, c in b[1].items():
                op.dma_wait[s] = c
        for k in reads:
            w = self.last_writer.get(k)
            if w is not None:
                if (not w.is_dma) and (not op.is_dma) and w.eng == op.eng == "pe":
                    continue
                need(w)
        strict = op.eng != "pe"
        for k in writes:
            w = self.last_writer.get(k)
            if w is not None:
                if w.is_dma or op.is_dma or w.eng != op.eng or strict:
                    need(w)
            for r in self.readers.get(k, ()):
                if r.is_dma or op.is_dma or r.eng != op.eng or strict:
                    need(r)
        for k in writes:
            self.last_writer[k] = op
            self.readers[k] = []
        for k in reads:
            self.readers.setdefault(k, []).append(op)
        self.ops.append(op)
        if not op.is_dma:
            self.last_on[op.eng] = op
        return op

    def op(self, eng, fn, reads=(), writes=()):
        return self._add(Op(eng, fn), reads, writes)

    def dma(self, eng, fn, sem, reads=(), writes=(), chain=True):
        o = Op(eng, fn, is_dma=True, sem=sem)
        if chain and sem in self.dma_last:
            o.dma_wait[sem] = self.dma_count[sem]
        self.dma_count.setdefault(sem, 0)
        self._add(o, reads, writes)
        self.dma_count[sem] += 1
        o.val = self.dma_count[sem]
        self.dma_last[sem] = o
        return o

    def pe(self, fn, reads=(), writes=()):
        return self.op("pe", fn, reads, writes)

    def act(self, fn, reads=(), writes=()):
        return self.op("act", fn, reads, writes)

    def dve(self, fn, reads=(), writes=()):
        return self.op("dve", fn, reads, writes)

    def pool(self, fn, reads=(), writes=()):
        return self.op("pool", fn, reads, writes)

    def barrier(self):
        lasts = list(self.last_on.values())
        dm = dict(self.dma_count)
        for e in self.ENGS:
            self.bar[e] = (lasts, dm)
        self.last_writer = {}
        self.readers = {}

    def emit(self):
        nc = self.nc
        for o in self.ops:
            for d in o.deps:
                d.marked = True
        cnt = {e: 0 for e in self.ENGS}
        for o in self.ops:
            if not o.is_dma and o.marked:
                cnt[o.eng] += 1
                o.val = cnt[o.eng]
        ctr = {e: nc.alloc_semaphore(name="ctr_" + e) for e in self.ENGS}
        dsem = {s: nc.alloc_semaphore(name="dma_%d" % i) for i, s in enumerate(self.dma_count)}
        waited = {e: {} for e in self.ENGS}
        nwait = 0
        for o in self.ops:
            eng = self.eobj[o.eng]
            need = {}
            for d in o.deps:
                key = ("c", d.eng)
                if need.get(key, (None, 0))[1] < d.val:
                    need[key] = (ctr[d.eng], d.val)
            for s, c in o.dma_wait.items():
                key = ("d", s)
                if need.get(key, (None, 0))[1] < 16 * c:
                    need[key] = (dsem[s], 16 * c)
            w = waited[o.eng]
            for key, (h, v) in need.items():
                if w.get(key, 0) >= v:
                    continue
                eng.wait_ge(h, v)
                nwait += 1
                w[key] = v
            ins = o.fn(eng)
            if o.is_dma:
                ins.then_inc(dsem[o.sem], 16)
            elif o.marked:
                ins.then_inc(ctr[o.eng], 1)
        eng = self.eobj["sp"]
        for s, c in self.dma_count.items():
            eng.wait_ge(dsem[s], 16 * c)
        return dict(n_ops=len(self.ops), n_wait=nwait, marked=cnt)


class Alloc:
    def __init__(self, nc):
        self.nc = nc
        self.guards = []

    def sb(self, name, shape, dt=F32):
        g = self.nc.sbuf_tensor("s_" + name, list(shape), dt)
        t = g.__enter__()
        self.guards.append(g)
        return t

    def ps(self, name, shape, dt=F32):
        g = self.nc.psum_tensor("p_" + name, list(shape), dt)
        t = g.__enter__()
        self.guards.append(g)
        return t

    def free(self):
        for g in reversed(self.guards):
            g.__exit__(None, None, None)
        self.guards = []


def make_consts(nc, P, A, pfx):
    ident = A.sb(pfx + "ident", [128, 128], BF16)
    maskT = A.sb(pfx + "maskT", [128, 128], BF16)
    ones32 = A.sb(pfx + "ones32", [128, 128], F32)
    P.pool(lambda e: e.memset(ident[:], 1.0), writes=["ident"])
    P.pool(lambda e: e.affine_select(out=ident[:], in_=ident[:], pattern=[[-1, 128]],
                                     compare_op=ALU.is_equal, fill=0.0, base=0,
                                     channel_multiplier=1), reads=["ident"], writes=["ident"])
    P.pool(lambda e: e.memset(maskT[:], 1.0), writes=["maskT"])
    P.pool(lambda e: e.affine_select(out=maskT[:], in_=maskT[:], pattern=[[1, 128]],
                                     compare_op=ALU.is_ge, fill=0.0, base=0,
                                     channel_multiplier=-1), reads=["maskT"], writes=["maskT"])
    P.pool(lambda e: e.memset(ones32[:], 1.0), writes=["ones32"])
    return ident, maskT, ones32


V_CW, V_CB, V_BA, V_BX, V_LAM, V_QN, V_KVN, V_INVF, V_SGN, NV = 0, 16, 20, 24, 28, 32, 34, 35, 36, 40
ATT_SCALE = 96.0 ** -0.5


def build_A(nc, P, S, d, x_in, x_out):
    T = S // 128
    A = Alloc(nc)
    sb, ps = A.sb, A.ps
    ident, maskT, ones32 = make_consts(nc, P, A, "a_")
    bank = [ps("a_bank%d" % i, [128, 512], F32) for i in range(8)]

    def bk(i):
        return ("bank", i)

    w_in = sb("a_w_in", [128, 8, 2048], BF16)
    w_out = sb("a_w_out", [128, 8, 1024], BF16)
    w_uq = sb("a_w_uq", [128, 2, 1280], BF16)
    w_kn = sb("a_w_kn", [128, 512], BF16)
    w_v = sb("a_w_v", [128, 512], BF16)
    gA = sb("a_gA", [128, 4, 128], BF16)
    gX = sb("a_gX", [128, 4, 128], BF16)
    vec = sb("a_vec", [128, NV], F32)
    lng = sb("a_lng", [128, 512], F32)
    lnb = sb("a_lnb", [128, 512], F32)
    cst = sb("a_cst", [128, 8], F32)
    for i, v in enumerate([1e-6, 1e-5, math.pi / 2, 0.0]):
        P.pool(lambda e, i=i, v=v: e.memset(cst[:, i:i + 1], v), writes=["cst"])
    for c in range(0, 8, 2):
        P.dma("pool", lambda e, c=c: e.dma_start(out=w_in[:, c:c + 2, :],
                                                 in_=d["w_in"][c * 128:(c + 2) * 128, :].rearrange("(c p) n -> p c n", p=128)),
              "wl", writes=["w_in"], chain=False)
    for c in range(0, 8, 4):
        P.dma("pool", lambda e, c=c: e.dma_start(out=w_out[:, c:c + 4, :],
                                                 in_=d["w_out"][c * 128:(c + 4) * 128, :].rearrange("(c p) n -> p c n", p=128)),
              "wl", writes=["w_out"], chain=False)
    P.dma("pool", lambda e: e.dma_start(out=w_uq[:], in_=d["w_uq"].rearrange("(c p) n -> p c n", p=128)),
          "wl", writes=["w_uq"], chain=False)
    P.dma("pool", lambda e: e.dma_start(out=w_kn[:], in_=d["w_kn"]), "wl", writes=["w_kn"], chain=False)
    P.dma("pool", lambda e: e.dma_start(out=w_v[:], in_=d["w_v"]), "wl", writes=["w_v"], chain=False)
    P.dma("pool", lambda e: e.dma_start(out=gA[:], in_=d["gA"]), "wl", writes=["gA"], chain=False)
    P.dma("pool", lambda e: e.dma_start(out=gX[:], in_=d["gX"]), "wl", writes=["gX"], chain=False)
    P.dma("sp", lambda e: e.dma_start(out=vec[:], in_=d["vec"]), "wl2", writes=["vec"])

    der = sb("a_der", [128, 16], F32)
    tmp4 = sb("a_tmp4", [128, 4], F32)
    P.act(lambda e: e.activation(out=tmp4[:], in_=vec[:, V_LAM:V_LAM + 4], func=AF.Exp, scale=-1.0),
          reads=["vec"], writes=["tmp4"])
    P.act(lambda e: e.activation(out=tmp4[:], in_=tmp4[:], func=AF.Ln, bias=1.0), reads=["tmp4"], writes=["tmp4"])
    P.dve(lambda e: e.tensor_scalar(out=der[:, 0:4], in0=tmp4[:], scalar1=4.0, scalar2=None, op0=ALU.mult),
          reads=["tmp4"], writes=["der"])
    P.dve(lambda e: e.tensor_scalar(out=der[:, 4:8], in0=tmp4[:], scalar1=-4.0, scalar2=None, op0=ALU.mult),
          reads=["tmp4"], writes=["der"])
    P.dve(lambda e: e.tensor_scalar(out=der[:, 8:16], in0=vec[:, V_BA:V_BA + 8], scalar1=0.5, scalar2=None,
                                    op0=ALU.mult), reads=["vec"], writes=["der"])

    trig_d = nc.dram_tensor("a_trig", [128, 2, S], F32).ap()
    CB = min(2048, S)
    A2 = Alloc(nc)
    pi_t = A2.sb("a_pi", [128, CB], I32)
    tA = A2.sb("a_tA", [128, CB], F32)
    tB = A2.sb("a_tB", [128, CB], F32)
    tC = A2.sb("a_tC", [128, 2, CB], F32)
    for blk in range(S // CB):
        cs = slice(blk * CB, (blk + 1) * CB)
        P.dma("sp", lambda e, cs=cs: e.dma_start(out=pi_t[:], in_=d["pos"][:, cs].partition_broadcast(128)),
              "trg", writes=["pi"])
        P.dve(lambda e: e.tensor_copy(out=tA[:], in_=pi_t[:]), reads=["pi"], writes=["tA"])
        P.dve(lambda e: e.tensor_scalar(out=tA[:], in0=tA[:], scalar1=vec[:, V_INVF:V_INVF + 1], scalar2=None,
                                        op0=ALU.mult), reads=["tA", "vec"], writes=["tA"])
        P.dve(lambda e: e.tensor_scalar(out=tB[:], in0=tA[:], scalar1=1.0 / (2 * math.pi), scalar2=MAGIC,
                                        op0=ALU.mult, op1=ALU.add), reads=["tA"], writes=["tB"])
        P.dve(lambda e: e.tensor_scalar(out=tB[:], in0=tB[:], scalar1=-MAGIC, scalar2=None, op0=ALU.add),
              reads=["tB"], writes=["tB"])
        P.dve(lambda e: e.scalar_tensor_tensor(out=tA[:], in0=tB[:], scalar=-C1, in1=tA[:], op0=ALU.mult,
                                               op1=ALU.add), reads=["tA", "tB"], writes=["tA"])
        P.dve(lambda e: e.scalar_tensor_tensor(out=tA[:], in0=tB[:], scalar=-C2, in1=tA[:], op0=ALU.mult,
                                               op1=ALU.add), reads=["tA", "tB"], writes=["tA"])
        P.dve(lambda e: e.tensor_scalar(out=tA[:], in0=tA[:], scalar1=3.1415925, scalar2=-3.1415925,
                                        op0=ALU.min, op1=ALU.max), reads=["tA"], writes=["tA"])
        P.act(lambda e: e.activation(out=tB[:], in_=tA[:], func=AF.Abs), reads=["tA"], writes=["tB"])
        P.act(lambda e: e.activation(out=tC[:, 0, :], in_=tB[:], func=AF.Sin, bias=cst[:, 2:3], scale=-1.0),
              reads=["tB", "cst"], writes=["tC"])
        P.act(lambda e: e.activation(out=tC[:, 1, :], in_=tA[:], func=AF.Sin, scale=vec[:, V_SGN:V_SGN + 1]),
              reads=["tA", "vec"], writes=["tC"])
        P.dma("sp", lambda e, cs=cs: e.dma_start(out=trig_d[:, :, cs], in_=tC[:]), "trg",
              reads=["tC"], writes=["trig_d"])

    keepd = {k: v for k, v in P.last_writer.items() if k == "trig_d"}
    P.barrier()
    P.last_writer.update(keepd)
    A2.free()
    KnT = sb("a_KnT", [128, 4, S], BF16)
    KrT = sb("a_KrT", [128, S], BF16)
    Vaug = sb("a_Vaug", [128, T, 8, 96], BF16)
    P.pool(lambda e: e.memset(Vaug[:], 1.0), writes=["Vall"])
    xin2 = [sb("a_xin%d" % i, [128, 1024], F32) for i in range(2)]
    xb = sb("a_xb", [128, 1024], BF16)
    xT = sb("a_xT", [128, 8, 128], BF16)
    xr = sb("a_xr", [128, 4, 131], F32)
    s12 = [sb("a_s1%d" % i, [128, 8, 128], F32) for i in range(2)]
    sq = sb("a_sq", [128, 3, 128], F32)
    sr = sb("a_sr", [128, 2, 128], F32)
    cqn = sb("a_cqn", [128, 3, 128], BF16)
    trg = sb("a_trg", [128, 2, 128], F32)
    kr1 = sb("a_kr1", [128, 128], F32)
    kr2 = sb("a_kr2", [128, 128], F32)
    acc = sb("a_acc", [128, 4, 128], F32)
    xcb = sb("a_xcb", [128, 4, 128], BF16)
    ta = sb("a_ta", [128, 4, 128], F32)
    ti = sb("a_ti", [128, 4, 128], F32)
    aa = sb("a_aa", [128, 4, 128], F32)
    hh2 = [sb("a_hh%d" % i, [128, 4, 128], F32) for i in range(2)]
    hc = sb("a_hc", [128, 4], F32)
    qsb = sb("a_qsb", [128, 6, 128], F32)
    th = qsb[:, 0:4, :]
    QnT2 = [sb("a_QnT%d" % i, [128, 4, 128], BF16) for i in range(2)]
    QrT2 = [sb("a_QrT%d" % i, [128, 3, 128], BF16) for i in range(2)]
    PT = [sb("a_PT%d" % i, [128, 4, 128], BF16) for i in range(2)]
    rl = sb("a_rl", [128, 128], F32)
    ym = sb("a_ym", [128, 4, 128], F32)
    yT = sb("a_yT", [128, 8, 128], BF16)
    st = sb("a_st", [128, 12], F32)
    mv = sb("a_mv", [128, 4], F32)
    P.pool(lambda e: e.memset(xr[:], 0.0), writes=["xr"])
    P.pool(lambda e: e.memset(hc[:], 0.0), writes=["hc"])

    def b3(i):
        return bank[i][:].rearrange("p (a b) -> p a b", a=4)

    def front(t):
        ts_ = slice(t * 128, (t + 1) * 128)
        pp = t % 2
        xin, s1, hh, QnT, QrT = xin2[pp], s12[pp], hh2[pp], QnT2[pp], QrT2[pp]
        kx, ks1, khh, kqn, kqr = ("xin", pp), ("s1", pp), ("hh", pp), ("QnT", pp), ("QrT", pp)
        P.dma("sp", lambda e: e.dma_start(out=xin[:], in_=x_in[ts_, :]), "xin%d" % pp, writes=[kx])
        P.dma("sp", lambda e: e.dma_start(out=trg[:], in_=trig_d[:, :, ts_]), "trl", reads=["trig_d"], writes=["trg"])
        P.act(lambda e: e.activation(out=xb[:], in_=xin[:], func=AF.Copy), reads=[kx], writes=["xb"])
        yield
        b0 = bank[0][:].bitcast(BF16)
        for c in range(8):
            P.pe(lambda e, c=c: e.transpose(b0[:, c * 128:(c + 1) * 128], xb[:, c * 128:(c + 1) * 128], ident[:]),
                 reads=["xb", "ident"], writes=[bk(0)])
        P.dve(lambda e: e.tensor_copy(out=xT[:].rearrange("p a b -> p (a b)"), in_=b0), reads=[bk(0)], writes=["xT"])
        yield
        yield
        ibank = [1, 2, 3, 0]
        for j in range(16):
            bi = ibank[j // 4]
            for c in range(8):
                P.pe(lambda e, j=j, c=c, bi=bi: e.matmul(b3(bi)[:, j % 4, :], lhsT=w_in[:, c, j * 128:(j + 1) * 128],
                                                        rhs=xT[:, c, :], start=(c == 0), stop=(c == 7)),
                     reads=["w_in", "xT"], writes=[bk(bi)])
            if j % 4 == 3:
                yield
        P.act(lambda e: e.activation(out=xr[:, :, 3:131], in_=b3(1), func=AF.Copy), reads=[bk(1)], writes=["xr"])
        yield
        for i in range(2):
            P.act(lambda e, i=i: e.activation(out=s1[:, 4 * i:4 * i + 4, :], in_=b3(2 + i), func=AF.Tanh, scale=0.5),
                  reads=[bk(2 + i)], writes=[ks1])
            P.dve(lambda e, i=i: e.scalar_tensor_tensor(out=s1[:, 4 * i:4 * i + 4, :], in0=s1[:, 4 * i:4 * i + 4, :],
                                                         scalar=1.0, in1=b3(2 + i), op0=ALU.add, op1=ALU.mult),
                  reads=[ks1, bk(2 + i)], writes=[ks1])
        P.act(lambda e: e.activation(out=sq[:], in_=b3(0)[:, 0:3, :], func=AF.Square), reads=[bk(0)], writes=["sq"])
        yield
        b5 = b3(1)
        P.pe(lambda e: e.matmul(b5[:, 0, :], lhsT=ones32[:], rhs=sq[:, 0, :], start=True, stop=False),
             reads=["sq", "ones32"], writes=[bk(1)])
        yield
        P.pe(lambda e: e.matmul(b5[:, 0, :], lhsT=ones32[:], rhs=sq[:, 1, :], start=False, stop=True),
             reads=["sq", "ones32"], writes=[bk(1)])
        yield
        P.pe(lambda e: e.matmul(b5[:, 1, :], lhsT=ones32[:], rhs=sq[:, 2, :], start=True, stop=True),
             reads=["sq", "ones32"], writes=[bk(1)])
        yield
        P.act(lambda e: e.activation(out=sr[:, 0, :], in_=b5[:, 0, :], func=AF.Sqrt, bias=cst[:, 0:1], scale=1.0 / 256),
              reads=[bk(1), "cst"], writes=["sr"])
        yield
        P.act(lambda e: e.activation(out=sr[:, 1, :], in_=b5[:, 1, :], func=AF.Sqrt, bias=cst[:, 0:1], scale=1.0 / 128),
              reads=[bk(1), "cst"], writes=["sr"])
        yield
        P.dve(lambda e: e.reciprocal(out=sr[:], in_=sr[:]), reads=["sr"], writes=["sr"])
        yield
        for c in range(3):
            P.dve(lambda e, c=c: e.scalar_tensor_tensor(out=cqn[:, c, :], in0=b3(0)[:, c, :],
                                                         scalar=vec[:, V_QN + c:V_QN + c + 1],
                                                         in1=sr[:, min(c, 2) // 2, :], op0=ALU.mult, op1=ALU.mult),
                  reads=[bk(0), "vec", "sr"], writes=["cqn"])
        P.dve(lambda e: e.tensor_tensor(out=kr1[0:32, :], in0=b3(0)[0:32, 3, :], in1=trg[0:32, 0, :], op=ALU.mult),
              reads=[bk(0), "trg"], writes=["kr1"])
        yield
        P.dve(lambda e: e.tensor_tensor(out=kr2[0:32, :], in0=b3(0)[32:64, 3, :], in1=trg[32:64, 1, :], op=ALU.mult),
              reads=[bk(0), "trg"], writes=["kr2"])
        yield
        for i in range(3):
            P.pool(lambda e, i=i: e.tensor_tensor(out=KrT[32 * i:32 * i + 32, ts_], in0=kr1[0:32, :],
                                                  in1=kr2[0:32, :], op=ALU.add),
                   reads=["kr1", "kr2"], writes=[("Kr", t)])
        yield
        for c in range(4):
            P.dve(lambda e, c=c: e.tensor_scalar(out=acc[:, c, :], in0=xr[:, c, 3:131],
                                                 scalar1=vec[:, V_CW + 4 * c + 3:V_CW + 4 * c + 4],
                                                 scalar2=vec[:, V_CB + c:V_CB + c + 1], op0=ALU.mult, op1=ALU.add),
                  reads=["xr", "vec"], writes=[("acc", c)])
            for k in range(3):
                P.dve(lambda e, c=c, k=k: e.scalar_tensor_tensor(out=acc[:, c, :], in0=xr[:, c, k:k + 128],
                                                                  scalar=vec[:, V_CW + 4 * c + k:V_CW + 4 * c + k + 1],
                                                                  in1=acc[:, c, :], op0=ALU.mult, op1=ALU.add),
                      reads=["xr", "vec", ("acc", c)], writes=[("acc", c)])
            if c % 2 == 1:
                yield
        P.pool(lambda e: e.tensor_copy(out=xr[:, :, 0:3], in_=xr[:, :, 128:131]), reads=["xr"], writes=["xr"])
        yield
        accs = [("acc", c) for c in range(4)]
        P.pool(lambda e: e.tensor_copy(out=xcb[:], in_=acc[:]), reads=accs, writes=["xcb"])
        yield
        for c in range(4):
            P.pe(lambda e, c=c: e.matmul(b3(2)[:, c, :], lhsT=gA[:, c, :], rhs=xcb[:, c, :], start=True, stop=True),
                 reads=["gA", "xcb"], writes=[bk(2)])
        for c in range(4):
            P.pe(lambda e, c=c: e.matmul(b3(3)[:, c, :], lhsT=gX[:, c, :], rhs=xcb[:, c, :], start=True, stop=True),
                 reads=["gX", "xcb"], writes=[bk(3)])
        for c in range(4):
            P.act(lambda e, c=c: e.activation(out=ta[:, c, :], in_=b3(2)[:, c, :], func=AF.Tanh,
                                              bias=der[:, 8 + c:9 + c], scale=0.5), reads=[bk(2), "der"], writes=["ta"])
            P.act(lambda e, c=c: e.activation(out=ti[:, c, :], in_=b3(3)[:, c, :], func=AF.Tanh,
                                              bias=der[:, 12 + c:13 + c], scale=0.5), reads=[bk(3), "der"], writes=["ti"])
        yield
        for c in range(4):
            P.act(lambda e, c=c: e.activation(out=aa[:, c, :], in_=ta[:, c, :], func=AF.Exp,
                                              bias=der[:, 4 + c:5 + c], scale=der[:, 4 + c:5 + c]),
                  reads=["ta", "der"], writes=["aa"])
            P.act(lambda e, c=c: e.activation(out=th[:, c, :], in_=ta[:, c, :], func=AF.Tanh,
                                              bias=der[:, c:c + 1], scale=der[:, c:c + 1]),
                  reads=["ta", "der"], writes=["qsb"])
        P.dve(lambda e: e.tensor_tensor(out=ta[:], in0=aa[:], in1=aa[:], op=ALU.mult), reads=["aa"], writes=["ta"])
        yield
        P.dve(lambda e: e.scalar_tensor_tensor(out=ta[:], in0=ta[:], scalar=1.0, in1=th, op0=ALU.add, op1=ALU.mult),
              reads=["ta", "qsb"], writes=["ta"])
        yield
        P.act(lambda e: e.activation(out=ta[:], in_=ta[:], func=AF.Sqrt), reads=["ta"], writes=["ta"])
        yield
        P.dve(lambda e: e.scalar_tensor_tensor(out=ti[:], in0=ti[:], scalar=1.0, in1=acc[:], op0=ALU.add, op1=ALU.mult),
              reads=["ti"] + accs, writes=["ti"])
        yield
        P.dve(lambda e: e.scalar_tensor_tensor(out=ti[:], in0=ti[:], scalar=0.25, in1=ta[:], op0=ALU.mult, op1=ALU.mult),
              reads=["ti", "ta"], writes=["ti"])
        yield
        yield
        for c in range(4):
            P.dve(lambda e, c=c: e.tensor_tensor_scan(out=hh[:, c, :], data0=aa[:, c, :], data1=ti[:, c, :],
                                                       initial=hc[:, c:c + 1], op0=ALU.mult, op1=ALU.add),
                  reads=["aa", "ti", "hc"], writes=[khh])
        P.pool(lambda e: e.tensor_copy(out=hc[:], in_=hh[:, :, 127]), reads=[khh], writes=["hc"])
        yield
        yield
        for j in range(10):
            bi, jj = (j // 4, j % 4)
            for c in range(2):
                P.pe(lambda e, j=j, c=c, bi=bi, jj=jj: e.matmul(b3(bi)[:, jj, :], lhsT=w_uq[:, c, j * 128:(j + 1) * 128],
                                                                rhs=cqn[:, c, :], start=(c == 0), stop=(c == 1)),
                     reads=["w_uq", "cqn"], writes=[bk(bi)])
        for j in range(4):
            P.pe(lambda e, j=j: e.matmul(b3(3)[:, j, :], lhsT=w_kn[:, j * 128:(j + 1) * 128], rhs=cqn[:, 2, :],
                                         start=True, stop=True), reads=["w_kn", "cqn"], writes=[bk(3)])
        P.act(lambda e: e.activation(out=QnT[:], in_=b3(0), func=AF.Copy), reads=[bk(0)], writes=[kqn])
        yield
        P.act(lambda e: e.activation(out=qsb[:, 0:4, :], in_=b3(1), func=AF.Copy), reads=[bk(1)], writes=["qsb"])
        yield
        P.act(lambda e: e.activation(out=qsb[:, 4:6, :], in_=b3(2)[:, 0:2, :], func=AF.Copy), reads=[bk(2)], writes=["qsb"])
        yield
        P.pe(lambda e: e.matmul(bank[0][:], lhsT=cqn[:, 2, :], rhs=w_v[:], start=True, stop=True),
             reads=["w_v", "cqn"], writes=[bk(0)])
        yield
        yield
        cosb = trg[:, 0, :].unsqueeze(1).broadcast_to([128, 3, 128])
        sinb = trg[:, 1, :].unsqueeze(1).broadcast_to([128, 3, 128])
        P.pool(lambda e: e.tensor_tensor(out=qsb[:, 0:3, :], in0=qsb[:, 0:3, :], in1=cosb, op=ALU.mult),
               reads=["qsb", "trg"], writes=["qsb"])
        yield
        P.pool(lambda e: e.tensor_tensor(out=qsb[:, 3:6, :], in0=qsb[:, 3:6, :], in1=sinb, op=ALU.mult),
               reads=["qsb", "trg"], writes=["qsb"])
        yield
        P.pool(lambda e: e.tensor_tensor(out=QrT[:], in0=qsb[:, 0:3, :], in1=qsb[:, 3:6, :], op=ALU.add),
               reads=["qsb"], writes=[kqr])
        yield
        P.act(lambda e: e.activation(out=KnT[:, :, ts_], in_=b3(3), func=AF.Copy), reads=[bk(3)],
              writes=[("Kn", t)])
        yield
        P.dve(lambda e: e.tensor_copy(out=Vaug[:, t, :, 0:64], in_=bank[0][:].rearrange("p (h v) -> p h v", h=8)),
              reads=[bk(0), "Vall"], writes=[("V", t)])
        yield
        yield

    def attention(t, gen):
        pp = t % 2
        QnT, QrT = QnT2[pp], QrT2[pp]
        kqn, kqr = ("QnT", pp), ("QrT", pp)
        items = []
        for h in range(8):
            nb = (t + 4) // 4
            for b in range(nb):
                items.append((h, b, list(range(4 * b, min(4 * b + 4, t + 1)))))

        def qk(i):
            h, b, kts = items[i]
            sbk = 4 + (i % 2)
            j2, hr = h // 2, (h % 2) * 64
            c3, r3 = h // 3, (h % 3) * 32
            for jj, kt in enumerate(kts):
                ks = slice(kt * 128, (kt + 1) * 128)
                P.pe(lambda e, jj=jj, ks=ks: e.matmul(b3(sbk)[:, jj, :], lhsT=KnT[hr:hr + 64, j2, ks],
                                                       rhs=QnT[hr:hr + 64, j2, :], start=True, stop=False),
                     reads=[("Kn", kt), kqn], writes=[bk(sbk)])
                P.pe(lambda e, jj=jj, ks=ks: e.matmul(b3(sbk)[:, jj, :], lhsT=KrT[r3:r3 + 32, ks],
                                                       rhs=QrT[r3:r3 + 32, c3, :], start=False, stop=True),
                     reads=[("Kr", kt), kqr], writes=[bk(sbk)])

        def rest(i):
            h, b, kts = items[i]
            sbk = 4 + (i % 2)
            obk = 6 + (h % 2)
            n = len(kts)
            pt = PT[i % 2]
            ptk = ("PT", i % 2)
            P.act(lambda e: e.activation(out=pt[:, 0:n, :], in_=b3(sbk)[:, 0:n, :], func=AF.Exp, scale=ATT_SCALE),
                  reads=[bk(sbk)], writes=[ptk])
            if t in kts:
                jd = t - 4 * b
                P.pool(lambda e: e.tensor_tensor(out=pt[:, jd, :], in0=pt[:, jd, :], in1=maskT[:], op=ALU.mult),
                       reads=[ptk, "maskT"], writes=[ptk])
            for jj, kt in enumerate(kts):
                P.pe(lambda e, jj=jj, kt=kt: e.matmul(bank[obk][0:96, 0:128], lhsT=Vaug[:, kt, h, :], rhs=pt[:, jj, :],
                                                      start=(kt == 0), stop=(kt == t)),
                     reads=[("V", kt), "Vall", ptk], writes=[bk(obk)])
            if kts[-1] == t:
                j2, hr = h // 2, (h % 2) * 64
                for hf in range(2):
                    P.dve(lambda e, hf=hf: e.reciprocal(out=rl[32 * hf:32 * hf + 32, :], in_=bank[obk][64:96, 0:128]),
                          reads=[bk(obk)], writes=["rl"])
                P.dve(lambda e: e.scalar_tensor_tensor(
                    out=ym[hr:hr + 64, j2, :], in0=bank[obk][0:64, 0:128],
                    scalar=0.5, in1=rl[0:64, :], op0=ALU.mult, op1=ALU.mult),
                    reads=[bk(obk), "rl"], writes=["ym"])

        state = {"k": 0}

        def adv():
            while state["k"] < len(gen):
                g_ = gen[state["k"]]
                if g_ is None:
                    state["k"] += 1
                    continue
                try:
                    next(g_)
                    return
                except StopIteration:
                    state["k"] += 1

        for i in range(len(items)):
            qk(i)
            if i > 0:
                rest(i - 1)
            adv()
        rest(len(items) - 1)

    def tail_a(t):
        pp = t % 2
        s1, hh = s12[pp], hh2[pp]
        ks1, khh = ("s1", pp), ("hh", pp)
        P.dve(lambda e: e.tensor_tensor(out=yT[:, 0:4, :], in0=s1[:, 0:4, :], in1=hh[:], op=ALU.mult),
              reads=[ks1, khh], writes=["yT"])
        P.dve(lambda e: e.tensor_tensor(out=yT[:, 4:8, :], in0=s1[:, 4:8, :], in1=ym[:], op=ALU.mult),
              reads=[ks1, "ym"], writes=["yT"])

    def tail(t):
        ts_ = slice(t * 128, (t + 1) * 128)
        pp = t % 2
        xin = xin2[pp]
        kx = ("xin", pp)
        for hf in range(2):
            for c in range(8):
                P.pe(lambda e, hf=hf, c=c: e.matmul(bank[hf][:], lhsT=yT[:, c, :], rhs=w_out[:, c, hf * 512:(hf + 1) * 512],
                                                    start=(c == 0), stop=(c == 7)),
                     reads=["yT", "w_out"], writes=[bk(hf)])
        for hf in range(2):
            hs = slice(hf * 512, (hf + 1) * 512)
            P.dve(lambda e, hf=hf, hs=hs: e.scalar_tensor_tensor(out=xin[:, hs], in0=xin[:, hs], scalar=ALPHA, in1=bank[hf][:],
                                                                  op0=ALU.mult, op1=ALU.add),
                  reads=[kx, bk(hf)], writes=[kx])
            yield
            P.dve(lambda e, hf=hf, hs=hs: e.bn_stats(out=st[:, 6 * hf:6 * hf + 6], in_=xin[:, hs]), reads=[kx],
                  writes=["st"])
            yield
        layer_norm_tail(P, xin, st, mv, cst, None, None, None, [kx])
        yield
        for hf in range(2):
            hs = slice(hf * 512, (hf + 1) * 512)
            P.dma("sp", lambda e, hf=hf: e.dma_start(out=lng[:], in_=d["lng"][:, hf * 512:(hf + 1) * 512].partition_broadcast(128)),
                  "lgl", writes=["lng"])
            P.dma("sp", lambda e, hf=hf: e.dma_start(out=lnb[:], in_=d["lnb"][:, hf * 512:(hf + 1) * 512].partition_broadcast(128)),
                  "lbl", writes=["lnb"])
            P.pool(lambda e, hs=hs: e.tensor_tensor(out=xin[:, hs], in0=xin[:, hs], in1=lng[:], op=ALU.mult),
                   reads=[kx, "lng"], writes=[kx])
            yield
            P.pool(lambda e, hs=hs: e.tensor_tensor(out=xin[:, hs], in0=xin[:, hs], in1=lnb[:], op=ALU.add),
                   reads=[kx, "lnb"], writes=[kx])
            yield
        P.dma("sp", lambda e: e.dma_start(out=x_out[ts_, :], in_=xin[:]), "xo%d" % pp, reads=[kx], writes=[("x1d", t)])

    def exhaust(g_):
        if g_ is not None:
            for _ in g_:
                pass

    exhaust(front(0))
    ptail = None
    for t in range(T):
        gfront = front(t + 1) if t + 1 < T else None
        attention(t, [ptail, gfront])
        exhaust(ptail)
        exhaust(gfront)
        tail_a(t)
        ptail = tail(t)
    exhaust(ptail)
    return A


def layer_norm_tail(P, z, st, mv, cst, lng, lnb, _unused, zk):
    P.dve(lambda e: e.bn_aggr(out=mv[:, 0:2], in_=st[:]), reads=["st"], writes=["mv"])
    P.act(lambda e: e.activation(out=mv[:, 2:3], in_=mv[:, 1:2], func=AF.Sqrt, bias=cst[:, 1:2], scale=1.0),
          reads=["mv", "cst"], writes=["mv2"])
    P.dve(lambda e: e.reciprocal(out=mv[:, 2:3], in_=mv[:, 2:3]), reads=["mv2"], writes=["mv2"])
    P.dve(lambda e: e.scalar_tensor_tensor(out=mv[:, 3:4], in0=mv[:, 0:1], scalar=-1.0, in1=mv[:, 2:3],
                                           op0=ALU.mult, op1=ALU.mult), reads=["mv", "mv2"], writes=["mv3"])
    P.act(lambda e: e.activation(out=z[:], in_=z[:], func=AF.Identity, bias=mv[:, 3:4], scale=mv[:, 2:3]),
          reads=zk + ["mv2", "mv3"], writes=zk)
    if lng is not None:
        P.pool(lambda e: e.tensor_tensor(out=z[:], in0=z[:], in1=lng[:], op=ALU.mult), reads=zk + ["lng"], writes=zk)
        P.pool(lambda e: e.tensor_tensor(out=z[:], in0=z[:], in1=lnb[:], op=ALU.add), reads=zk + ["lnb"], writes=zk)


A_SPECS = [("w_in", [1024, 2048], F32), ("w_out", [1024, 1024], F32), ("w_uq", [256, 1280], F32),
           ("w_kn", [128, 512], F32), ("w_v", [128, 512], F32), ("gA", [128, 4, 128], F32),
           ("gX", [128, 4, 128], F32), ("vec", [128, NV], F32), ("lng", [1, 1024], F32), ("lnb", [1, 1024], F32)]


def prep_A(inp):
    f = np.float32
    w_in = np.asarray(inp["ab_w_in"][0], f)
    kr = w_in[:, 1920:1952]
    kr_sw = np.concatenate([kr[:, 16:32], kr[:, 0:16]], axis=1)
    w_inA = np.concatenate([w_in[:, :1920], kr, kr_sw, np.zeros((1024, 64), f)], axis=1)
    uq = np.asarray(inp["mla_w_uq"][0], f).reshape(256, 8, 96)
    nope = uq[:, :, :64].reshape(256, 512)
    rope = uq[:, :, 64:96]
    rsw = np.concatenate([rope[:, :, 16:32], rope[:, :, 0:16]], axis=2)
    z32 = np.zeros((256, 1, 32), f)
    rope9 = np.concatenate([rope, z32], axis=1).reshape(256, 288)
    rsw9 = np.concatenate([rsw, z32], axis=1).reshape(256, 288)
    pad = np.zeros((256, 96), f)
    w_uqA = np.concatenate([nope, rope9, pad, rsw9, pad], axis=1)
    w_uqA = np.concatenate([nope,
                            np.concatenate([rope9[:, 0:96], np.zeros((256, 32), f)], 1),
                            np.concatenate([rope9[:, 96:192], np.zeros((256, 32), f)], 1),
                            np.concatenate([rope9[:, 192:288], np.zeros((256, 32), f)], 1),
                            np.concatenate([rsw9[:, 0:96], np.zeros((256, 32), f)], 1),
                            np.concatenate([rsw9[:, 96:192], np.zeros((256, 32), f)], 1),
                            np.concatenate([rsw9[:, 192:288], np.zeros((256, 32), f)], 1)], axis=1)
    ukv = np.asarray(inp["mla_w_ukv"][0], f).reshape(128, 8, 128)
    w_kn = np.ascontiguousarray(ukv[:, :, :64]).reshape(128, 512)
    w_v = np.ascontiguousarray(ukv[:, :, 64:]).reshape(128, 512)

    def blockdiag(w):
        w = np.asarray(w[0], f)
        o = np.zeros((128, 4, 128), f)
        for h in range(8):
            r = (h % 2) * 64
            o[r:r + 64, h // 2, r:r + 64] = w[h]
        return o

    vec = np.zeros((128, NV), f)
    cw = np.asarray(inp["ab_conv_w"][0], f)
    for c in range(4):
        for k in range(4):
            vec[:, V_CW + 4 * c + k] = cw[k, c * 128:(c + 1) * 128]
    for nm, col in (("ab_conv_b", V_CB), ("ab_gate_a_b", V_BA), ("ab_gate_x_b", V_BX), ("ab_lambda", V_LAM)):
        vec[:, col:col + 4] = np.asarray(inp[nm][0], f).reshape(4, 128).T
    vec[:, V_QN:V_QN + 2] = np.asarray(inp["mla_q_norm"][0], f).reshape(2, 128).T
    vec[:, V_KVN] = np.asarray(inp["mla_kv_norm"][0], f)
    j = np.arange(128) % 16
    vec[:, V_INVF] = (10000.0 ** (-(2.0 * j) / 32.0)).astype(f)
    vec[:, V_SGN] = np.where((np.arange(128) % 32) < 16, -1.0, 1.0)
    return {"w_in": w_inA, "w_out": np.asarray(inp["ab_w_out"][0], f), "w_uq": w_uqA, "w_kn": w_kn, "w_v": w_v,
            "gA": blockdiag(inp["ab_gate_a_w"]), "gX": blockdiag(inp["ab_gate_x_w"]), "vec": vec,
            "lng": np.asarray(inp["ab_ln_g"], f).reshape(1, 1024), "lnb": np.asarray(inp["ab_ln_b"], f).reshape(1, 1024)}


def declare(nc, specs, pfx):
    return {nm: nc.dram_tensor(pfx + nm, shape, dt, kind="ExternalInput").ap() for nm, shape, dt in specs}


def build_program_A(S):
    nc = bass.Bass("TRN2", target_bir_lowering=False)
    P = Prog(nc)
    d = declare(nc, A_SPECS, "a_")
    d["pos"] = nc.dram_tensor("pos", [1, S], I32, kind="ExternalInput").ap()
    x_in = nc.dram_tensor("x", [S, D], F32, kind="ExternalInput").ap()
    x_out = nc.dram_tensor("x1", [S, D], F32, kind="ExternalOutput").ap()
    build_A(nc, P, S, d, x_in, x_out)
    st = P.emit()
    return nc, st


VB_CW, VB_CB, NVB = 0, 96, 120
RB_DTB, RB_ALOG, RB_D, RB_NW, RB_LNG, RB_LNB, RB_CB, NRB = 0, 32, 64, 96, 2144, 3168, 4192, 7264


def build_B(nc, P, S, d, x_in, x_out):
    T = S // 128
    A = Alloc(nc)
    sb, ps = A.sb, A.ps
    ident, maskT, ones32 = make_consts(nc, P, A, "b_")
    bank = [ps("b_bank%d" % i, [128, 512], F32) for i in range(8)]
    rr_state = [0]

    def rr():
        rr_state[0] = (rr_state[0] + 1) % 6
        return rr_state[0]

    def bk(i):
        return ("bank", i)

    def b3(i):
        return bank[i][:].rearrange("p (a b) -> p a b", a=4)

    tri = sb("b_tri", [128, 128], F32)
    u2 = sb("b_u2", [128, 128], F32)
    m025 = sb("b_m025", [128, 128], F32)
    onesb = sb("b_onesb", [128, 128], BF16)
    cst = sb("b_cst", [128, 8], F32)
    P.pool(lambda e: e.memset(tri[:], 1.0), writes=["tri"])
    P.pool(lambda e: e.affine_select(out=tri[:], in_=tri[:], pattern=[[1, 128]], compare_op=ALU.is_ge, fill=0.0,
                                     base=0, channel_multiplier=-1), reads=["tri"], writes=["tri"])
    P.pool(lambda e: e.memset(u2[:], 1.0), writes=["u2"])
    P.pool(lambda e: e.affine_select(out=u2[:], in_=u2[:], pattern=[[-1, 128]], compare_op=ALU.is_gt, fill=0.0,
                                     base=0, channel_multiplier=1), reads=["u2"], writes=["u2"])
    P.pool(lambda e: e.tensor_scalar(out=m025[:], in0=tri[:], scalar1=0.25, scalar2=None, op0=ALU.mult),
           reads=["tri"], writes=["m025"])
    P.pool(lambda e: e.memset(onesb[:], 1.0), writes=["onesb"])
    for i, v in enumerate([1e-6, 1e-5, 0.0, 1.0]):
        P.pool(lambda e, i=i, v=v: e.memset(cst[:, i:i + 1], v), writes=["cst"])
    w_in = sb("b_w_in", [128, 8, 5152], BF16)
    w_out = sb("b_w_out", [128, 16, 1024], BF16)
    vecb = sb("b_vec", [128, NVB], F32)
    cbrow = sb("b_cbrow", [128, 8, 128], BF16)
    dtb = sb("b_dtb", [128, 32], F32)
    aneg = sb("b_aneg", [128, 32], F32)
    cd = sb("b_cd", [128, 32], F32)
    normw = sb("b_normw", [128, 512], F32)
    dg = sb("b_dg", [128, 96, 128], BF16)
    for c in range(0, 8, 2):
        P.dma("pool", lambda e, c=c: e.dma_start(out=w_in[:, c:c + 2, :],
                                                 in_=d["w_in"][c * 128:(c + 2) * 128, :].rearrange("(c p) n -> p c n", p=128)),
              "wl", writes=["w_in"], chain=False)
    for c in range(0, 16, 4):
        P.dma("pool", lambda e, c=c: e.dma_start(out=w_out[:, c:c + 4, :],
                                                 in_=d["w_out"][c * 128:(c + 4) * 128, :].rearrange("(c p) n -> p c n", p=128)),
              "wl", writes=["w_out"], chain=False)
    cb3 = d["row"][:, RB_CB:RB_CB + 3072].rearrange("o (i r c) -> o i r c", r=3, c=128)
    for r_ in range(3):
        P.dma("pool", lambda e, r_=r_: e.dma_start(out=cbrow[32 * r_:32 * r_ + 1, :, :], in_=cb3[:, :, r_, :]),
              "wl", writes=["cbrow"], chain=False)
    P.dma("sp", lambda e: e.dma_start(out=vecb[:], in_=d["vec"]), "wl2", writes=["vecb"])
    for tl, off, n, key in ((dtb, RB_DTB, 32, "dtb"), (aneg, RB_ALOG, 32, "aneg"), (cd, RB_D, 32, "cd")):
        P.dma("sp", lambda e, tl=tl, off=off, n=n: e.dma_start(out=tl[:], in_=d["row"][:, off:off + n].partition_broadcast(128)),
              "wl2", writes=[key])
    P.act(lambda e: e.activation(out=aneg[:], in_=aneg[:], func=AF.Exp), reads=["aneg"], writes=["aneg"])
    P.dve(lambda e: e.tensor_scalar(out=aneg[:], in0=aneg[:], scalar1=-1.0, scalar2=None, op0=ALU.mult),
          reads=["aneg"], writes=["aneg"])
    P.dve(lambda e: e.tensor_scalar(out=cd[:], in0=cd[:], scalar1=0.5, scalar2=None, op0=ALU.mult),
          reads=["cd"], writes=["cd"])
    for jk in range(96):
        P.dve(lambda e, jk=jk: e.tensor_scalar(out=dg[:, jk, :], in0=ident[:], scalar1=vecb[:, jk:jk + 1], scalar2=None,
                                               op0=ALU.mult), reads=["ident", "vecb"], writes=["dg"])
    hT = sb("b_hT", [128, 2048], F32)
    hTb = sb("b_hTb", [128, 2048], BF16)
    P.pool(lambda e: e.memset(hT[:], 0.0), writes=["hT"])
    P.pool(lambda e: e.memset(hTb[:], 0.0), writes=["hTb"])
    xrb = sb("b_xrb", [128, 24, 131], BF16)
    P.pool(lambda e: e.memset(xrb[:], 0.0), writes=["xrb"])
    xin2 = [sb("b_xin%d" % i, [128, 1024], F32) for i in range(2)]
    xb = sb("b_xb", [128, 1024], BF16)
    xT = sb("b_xT", [128, 8, 128], BF16)
    sm = sb("b_sm", [128, 12, 32], F32)
    Rg = sb("b_Rg", [128, 4, 128], F32)
    es = sb("b_es", [128, 4, 128], F32)
    MT2 = [sb("b_MT%d" % i, [128, 8, 128], BF16) for i in range(2)]
    szg2 = [sb("b_szg%d" % i, [128, 512], F32) for i in range(2)]
    xs2 = sb("b_xs2", [128, 512], F32)
    xf2 = [sb("b_xf%d" % i, [128, 512], BF16) for i in range(2)]
    xsd2 = [sb("b_xsd%d" % i, [128, 512], BF16) for i in range(2)]
    xfd2 = [sb("b_xfd%d" % i, [128, 512], BF16) for i in range(2)]
    tnh = sb("b_tnh", [128, 512], F32)
    lng = normw
    lnb = tnh
    B2 = sb("b_B2", [128, 512], BF16)
    BCT = sb("b_BCT", [128, 8, 128], BF16)
    GTm = sb("b_GTm", [128, 128], F32)
    yb = sb("b_yb", [128, 512], F32)
    ssq = sb("b_ssq", [128, 4], F32)
    yn = xb[:, 0:512]
    junk = xb[:, 512:1024]
    ynT = sb("b_ynT", [128, 16, 128], BF16)
    st = sb("b_st", [128, 12], F32)
    mv = sb("b_mv", [128, 4], F32)

    def silu2(src_bank_ap, out_ap, key_in, key_out, shape3=None):
        P.act(lambda e: e.activation(out=tnh[:] if shape3 is None else tnh[:].rearrange("p (a b) -> p a b", a=4),
                                     in_=src_bank_ap, func=AF.Tanh, scale=0.5), reads=[key_in], writes=["tnh"])
        P.dve(lambda e: e.scalar_tensor_tensor(out=out_ap, in0=tnh[:] if shape3 is None else tnh[:].rearrange("p (a b) -> p a b", a=4),
                                               scalar=1.0, in1=src_bank_ap, op0=ALU.add, op1=ALU.mult),
              reads=["tnh", key_in], writes=[key_out])

    def silu2g(src_bank_ap, out_ap, key_in, key_out, shape3=None):
        tv = tnh[:] if shape3 is None else tnh[:].rearrange("p (a b) -> p a b", a=4)
        P.act(lambda e: e.activation(out=tv, in_=src_bank_ap, func=AF.Tanh, scale=0.5), reads=[key_in], writes=["tnh"])
        yield
        P.dve(lambda e: e.scalar_tensor_tensor(out=out_ap, in0=tv, scalar=1.0, in1=src_bank_ap, op0=ALU.add, op1=ALU.mult),
              reads=["tnh", key_in], writes=[key_out])
        yield

    def lockstep(*gens):
        gens = [g_ for g_ in gens if g_ is not None]
        while gens:
            for g_ in list(gens):
                try:
                    next(g_)
                except StopIteration:
                    gens.remove(g_)

    def load_x(t):
        if t >= T:
            return
        tsl = slice(t * 128, (t + 1) * 128)
        xi = xin2[t % 2]
        P.dma("sp", lambda e: e.dma_start(out=xi[:], in_=x_in[tsl, :]), "xin%d" % (t % 2), reads=[("x1d", t)],
              writes=[("xin", t % 2)])

    load_x(0)
    for t in range(T):
        ts_ = slice(t * 128, (t + 1) * 128)
        xin = xin2[t % 2]
        kx = ("xin", t % 2)
        P.act(lambda e, xin=xin: e.activation(out=xb[:], in_=xin[:], func=AF.Copy), reads=[kx], writes=["xb"])
        r0 = rr()
        b0 = bank[r0][:].bitcast(BF16)
        for c in range(8):
            P.pe(lambda e, c=c, b0=b0: e.transpose(b0[:, c * 128:(c + 1) * 128], xb[:, c * 128:(c + 1) * 128], ident[:]),
                 reads=["xb", "ident"], writes=[bk(r0)])
        P.dve(lambda e, b0=b0: e.tensor_copy(out=xT[:].rearrange("p a b -> p (a b)"), in_=b0), reads=[bk(r0)], writes=["xT"])
        r = rr()
        for c in range(8):
            P.pe(lambda e, c=c, r=r: e.matmul(bank[r][:, 0:32], lhsT=xT[:, c, :], rhs=w_in[:, c, 5120:5152],
                                              start=(c == 0), stop=(c == 7)), reads=["xT", "w_in"], writes=[bk(r)])
        P.dve(lambda e, r=r: e.tensor_tensor(out=sm[:, 0, :], in0=bank[r][:, 0:32], in1=dtb[:], op=ALU.add),
              reads=[bk(r), "dtb"], writes=["sm0"])
        P.act(lambda e: e.activation(out=sm[:, 1, :], in_=sm[:, 0, :], func=AF.Abs), reads=["sm0"], writes=["sm1"])
        P.act(lambda e: e.activation(out=sm[:, 1, :], in_=sm[:, 1, :], func=AF.Exp, scale=-1.0), reads=["sm1"], writes=["sm1"])
        P.act(lambda e: e.activation(out=sm[:, 1, :], in_=sm[:, 1, :], func=AF.Ln, bias=1.0), reads=["sm1"], writes=["sm1"])
        P.dve(lambda e: e.scalar_tensor_tensor(out=sm[:, 2, :], in0=sm[:, 0, :], scalar=0.0, in1=sm[:, 1, :],
                                               op0=ALU.max, op1=ALU.add), reads=["sm0", "sm1"], writes=["dt"])
        P.dve(lambda e: e.tensor_tensor(out=sm[:, 3, :], in0=sm[:, 2, :], in1=aneg[:], op=ALU.mult),
              reads=["dt", "aneg"], writes=["adt"])
        P.dve(lambda e: e.tensor_scalar(out=sm[:, 4, :], in0=sm[:, 2, :], scalar1=0.5, scalar2=None, op0=ALU.mult),
              reads=["dt"], writes=["cxf"])
        r = rr()
        for i, lh in enumerate((tri, u2, ones32)):
            P.pe(lambda e, i=i, lh=lh, r=r: e.matmul(bank[r][:, 32 * i:32 * i + 32], lhsT=lh[:], rhs=sm[:, 3, :],
                                                     start=True, stop=True), reads=["adt", "tri", "u2", "ones32"],
                 writes=[bk(r)])
        P.act(lambda e, r=r: e.activation(out=sm[:, 7:10, :].rearrange("p a b -> p (a b)"), in_=bank[r][:, 0:96], func=AF.Exp),
              reads=[bk(r)], writes=["e3"])
        P.dve(lambda e: e.scalar_tensor_tensor(out=sm[:, 5, :], in0=sm[:, 4, :], scalar=0.5, in1=sm[:, 8, :],
                                               op0=ALU.mult, op1=ALU.mult), reads=["cxf", "e3"], writes=["cxfd"])
        P.dve(lambda e: e.tensor_scalar(out=sm[:, 6, :], in0=sm[:, 7, :], scalar1=0.5, scalar2=None, op0=ALU.mult),
              reads=["e3"], writes=["eoff"])
        for q in range(6):
            r = rr()
            for jj in range(4):
                j = 4 * q + jj
                for c in range(8):
                    P.pe(lambda e, j=j, jj=jj, c=c, r=r: e.matmul(b3(r)[:, jj, :], lhsT=w_in[:, c, 2048 + j * 128:2048 + (j + 1) * 128],
                                                                  rhs=xT[:, c, :], start=(c == 0), stop=(c == 7)),
                         reads=["w_in", "xT"], writes=[bk(r)])
            P.act(lambda e, q=q, r=r: e.activation(out=xrb[:, 4 * q:4 * q + 4, 3:131], in_=b3(r), func=AF.Copy),
                  reads=[bk(r)], writes=["xrb"])

        def conv_tok(r, j0):
            for jj in range(4):
                j = j0 + jj
                osl = bank[r][:, jj * 128:(jj + 1) * 128]
                for k in range(4):
                    P.pe(lambda e, j=j, k=k, osl=osl: e.matmul(osl, lhsT=xrb[:, j, k:k + 128], rhs=dg[:, 4 * j + k, :],
                                                               start=(k == 0), stop=False), reads=["xrb", "dg"], writes=[bk(r)])
                P.pe(lambda e, j=j, osl=osl: e.matmul(osl, lhsT=onesb[32 * (j % 3):32 * (j % 3) + 1, :], rhs=cbrow[32 * (j % 3):32 * (j % 3) + 1, j // 3, :],
                                                      start=False, stop=True), reads=["onesb", "cbrow"], writes=[bk(r)])

        def conv_feat(r, j0):
            for jj in range(4):
                j = j0 + jj
                osl = bank[r][:, jj * 128:(jj + 1) * 128]
                for k in range(4):
                    P.pe(lambda e, j=j, k=k, osl=osl: e.matmul(osl, lhsT=dg[:, 4 * j + k, :], rhs=xrb[:, j, k:k + 128],
                                                               start=(k == 0), stop=False), reads=["xrb", "dg"], writes=[bk(r)])
                P.pe(lambda e, j=j, osl=osl: e.matmul(osl, lhsT=cbrow[32 * (j % 3):32 * (j % 3) + 1, j // 3, :], rhs=onesb[32 * (j % 3):32 * (j % 3) + 1, :],
                                                      start=False, stop=True), reads=["onesb", "cbrow"], writes=[bk(r)])

        r = rr()
        conv_tok(r, 16)
        silu2(bank[r][:], B2[:], bk(r), "B2")
        for i in range(2):
            r = rr()
            conv_feat(r, 16 + 4 * i)
            silu2(b3(r), BCT[:, 4 * i:4 * i + 4, :], bk(r), "BCT", shape3=True)
        def stage1(g):
            gs = slice(g * 512, (g + 1) * 512)
            hs8 = slice(g * 8, (g + 1) * 8)
            pp = g % 2
            szg, xf, xsd, xfd, MT = szg2[pp], xf2[pp], xsd2[pp], xfd2[pp], MT2[pp]
            tv = tnh[:]

            def bc(i, hs8=hs8):
                return sm[:, i, hs8].unsqueeze(2).broadcast_to([128, 8, 64])

            def mkR(hf):
                h4 = slice(g * 8 + hf * 4, g * 8 + hf * 4 + 4)
                P.pool(lambda e: e.tensor_tensor(out=Rg[:], in0=tri[:].unsqueeze(1).broadcast_to([128, 4, 128]),
                                                 in1=sm[:, 3, h4].unsqueeze(2).broadcast_to([128, 4, 128]), op=ALU.mult),
                       reads=["tri", "adt"], writes=["Rg"])

            def seg_mm(r):
                P.pe(lambda e: e.matmul(bank[r][:], lhsT=u2[:], rhs=Rg[:].rearrange("p a b -> p (a b)"), start=True, stop=True),
                     reads=["u2", "Rg"], writes=[bk(r)])

            def seg_exp(r):
                P.act(lambda e: e.activation(out=es[:].rearrange("p a b -> p (a b)"), in_=bank[r][:], func=AF.Exp),
                      reads=[bk(r)], writes=["es"])

            def mk_mt(hf):
                P.dve(lambda e: e.tensor_tensor(out=MT[:, 4 * hf:4 * hf + 4, :], in0=es[:],
                                                in1=GTm[:].unsqueeze(1).broadcast_to([128, 4, 128]), op=ALU.mult),
                      reads=["es", "GTm"], writes=[("MT", pp, hf)])

            mkR(0)
            rg = rr()
            P.pe(lambda e: e.matmul(bank[rg][:, 0:128], lhsT=BCT[:, g, :], rhs=BCT[:, 4 + g, :], start=True, stop=True),
                 reads=["BCT"], writes=[bk(rg)])
            yield
            rc = rr()
            conv_tok(rc, 4 * g)
            yield
            rs0 = rr()
            seg_mm(rs0)
            P.dve(lambda e: e.tensor_tensor(out=GTm[:], in0=bank[rg][:, 0:128], in1=m025[:], op=ALU.mult),
                  reads=[bk(rg), "m025"], writes=["GTm"])
            yield
            P.act(lambda e: e.activation(out=tv, in_=bank[rc][:], func=AF.Tanh, scale=0.5), reads=[bk(rc)], writes=["tnh"])
            rz = rr()
            for c in range(8):
                P.pe(lambda e, c=c: e.matmul(bank[rz][:], lhsT=xT[:, c, :], rhs=w_in[:, c, gs], start=(c == 0),
                                             stop=(c == 7)), reads=["xT", "w_in"], writes=[bk(rz)])
            yield
            seg_exp(rs0)
            yield
            P.dve(lambda e: e.scalar_tensor_tensor(out=xs2[:], in0=tv, scalar=1.0, in1=bank[rc][:], op0=ALU.add, op1=ALU.mult),
                  reads=["tnh", bk(rc)], writes=["xs2"])
            mkR(1)
            yield
            mk_mt(0)
            rs1 = rr()
            seg_mm(rs1)
            yield
            x3 = xs2[:].rearrange("p (h v) -> p h v", h=8)
            P.act(lambda e: e.activation(out=tv, in_=bank[rz][:], func=AF.Tanh, scale=0.5), reads=[bk(rz)], writes=["tnh"])
            P.dve(lambda e: e.tensor_tensor(out=xf[:].rearrange("p (h v) -> p h v", h=8), in0=x3, in1=bc(4),
                                            op=ALU.mult), reads=["xs2", "cxf"], writes=[("xf", pp)])
            P.pool(lambda e: e.tensor_tensor(out=xsd[:].rearrange("p (h v) -> p h v", h=8), in0=x3,
                                             in1=cd[:, hs8].unsqueeze(2).broadcast_to([128, 8, 64]), op=ALU.mult),
                   reads=["xs2", "cd"], writes=[("xsd", pp)])
            yield
            seg_exp(rs1)
            P.pool(lambda e: e.tensor_tensor(out=xfd[:].rearrange("p (h v) -> p h v", h=8), in0=x3, in1=bc(5),
                                             op=ALU.mult), reads=["xs2", "cxfd"], writes=[("xfd", pp)])
            yield
            P.dve(lambda e: e.scalar_tensor_tensor(out=szg[:], in0=tv, scalar=1.0, in1=bank[rz][:], op0=ALU.add, op1=ALU.mult),
                  reads=["tnh", bk(rz)], writes=[("szg", pp)])
            yield
            mk_mt(1)
            yield

        def stage2(g):
            gs = slice(g * 512, (g + 1) * 512)
            hs8 = slice(g * 8, (g + 1) * 8)
            pp = g % 2
            szg, xf, xsd, xfd, MT = szg2[pp], xf2[pp], xsd2[pp], xfd2[pp], MT2[pp]

            def bc(i, hs8=hs8):
                return sm[:, i, hs8].unsqueeze(2).broadcast_to([128, 8, 64])
            P.dma("sp", lambda e, g=g: e.dma_start(out=normw[:], in_=d["row"][:, RB_NW + g * 512:RB_NW + (g + 1) * 512].partition_broadcast(128)),
                  "nwl", writes=["normw"])
            ry = rr()
            P.pe(lambda e, ry=ry: e.matmul(bank[ry][:], lhsT=ident[:], rhs=xsd[:], start=True, stop=False),
                 reads=["ident", ("xsd", pp)], writes=[bk(ry)])
            yield
            for hh in range(8):
                P.pe(lambda e, ry=ry, hh=hh: e.matmul(bank[ry][:, hh * 64:(hh + 1) * 64], lhsT=MT[:, hh, :],
                                                      rhs=xf[:, hh * 64:(hh + 1) * 64], start=False, stop=(hh == 7)),
                     reads=[("MT", pp, hh // 4), ("xf", pp)], writes=[bk(ry)])
                yield
            ro = rr()
            P.pe(lambda e, ro=ro, g=g, gs=gs: e.matmul(bank[ro][:], lhsT=BCT[:, 4 + g, :], rhs=hTb[:, gs], start=True, stop=True),
                 reads=["BCT", ("hTb", g)], writes=[bk(ro)])
            yield
            P.dve(lambda e, ro=ro, bc=bc: e.tensor_tensor(out=yb[:].rearrange("p (h v) -> p h v", h=8),
                                                          in0=bank[ro][:].rearrange("p (h v) -> p h v", h=8), in1=bc(6), op=ALU.mult),
                  reads=[bk(ro), "eoff"], writes=["yb"])
            yield
            P.dve(lambda e, ry=ry: e.tensor_tensor(out=yb[:], in0=yb[:], in1=bank[ry][:], op=ALU.add),
                  reads=["yb", bk(ry)], writes=["yb"])
            yield
            P.dve(lambda e: e.tensor_tensor(out=yb[:], in0=yb[:], in1=szg[:], op=ALU.mult), reads=["yb", ("szg", pp)], writes=["yb"])
            yield
            P.act(lambda e, g=g: e.activation(out=junk, in_=yb[:], func=AF.Square, accum_out=ssq[:, g:g + 1]),
                  reads=["yb"], writes=["junk", "ssq"])
            yield
            P.act(lambda e, g=g: e.activation(out=ssq[:, g:g + 1], in_=ssq[:, g:g + 1], func=AF.Sqrt, bias=cst[:, 0:1],
                                              scale=0.25 / 512), reads=["ssq", "cst"], writes=["ssq"])
            yield
            P.dve(lambda e, g=g: e.reciprocal(out=ssq[:, g:g + 1], in_=ssq[:, g:g + 1]), reads=["ssq"], writes=["ssq"])
            yield
            P.dve(lambda e, g=g: e.tensor_scalar(out=ssq[:, g:g + 1], in0=ssq[:, g:g + 1], scalar1=0.5, scalar2=None,
                                                 op0=ALU.mult), reads=["ssq"], writes=["ssq"])
            yield
            P.dve(lambda e, g=g: e.scalar_tensor_tensor(out=yn, in0=yb[:], scalar=ssq[:, g:g + 1], in1=normw[:],
                                                        op0=ALU.mult, op1=ALU.mult), reads=["yb", "ssq", "normw"], writes=["yn"])
            yield
            r = rr()
            bt = bank[r][:].bitcast(BF16)
            for c in range(4):
                P.pe(lambda e, c=c, bt=bt: e.transpose(bt[:, c * 128:(c + 1) * 128], xb[:, c * 128:(c + 1) * 128], ident[:]),
                     reads=["yn", "ident"], writes=[bk(r)])
                yield
            P.act(lambda e, g=g, bt=bt: e.activation(out=ynT[:, 4 * g:4 * g + 4, :].rearrange("p a b -> p (a b)"), in_=bt[:, 0:512],
                                                     func=AF.Copy), reads=[bk(r)], writes=[("ynT", g)])
            yield
            for hf in range(2):
                for c in range(4):
                    P.pe(lambda e, hf=hf, c=c, g=g: e.matmul(bank[6 + hf][:], lhsT=ynT[:, 4 * g + c, :],
                                                             rhs=w_out[:, 4 * g + c, hf * 512:(hf + 1) * 512],
                                                             start=(g == 0 and c == 0), stop=(g == 3 and c == 3)),
                         reads=[("ynT", g), "w_out"], writes=[bk(6 + hf)])
                yield
            r = rr()
            P.pe(lambda e, r=r, g=g: e.matmul(bank[r][:], lhsT=B2[:, g * 128:(g + 1) * 128], rhs=xfd[:], start=True, stop=True),
                 reads=["B2", ("xfd", pp)], writes=[bk(r)])
            yield
            h3 = hT[:, gs].rearrange("p (h v) -> p h v", h=8)
            P.dve(lambda e, h3=h3, bc=bc: e.tensor_tensor(out=h3, in0=h3, in1=bc(9), op=ALU.mult),
                  reads=[("hT", g), "e3"], writes=[("hT", g)])
            yield
            P.dve(lambda e, r=r, gs=gs: e.tensor_tensor(out=hT[:, gs], in0=hT[:, gs], in1=bank[r][:], op=ALU.add),
                  reads=[("hT", g), bk(r)], writes=[("hT", g)])
            yield
            P.pool(lambda e, gs=gs: e.tensor_copy(out=hTb[:, gs], in_=hT[:, gs]), reads=[("hT", g)], writes=[("hTb", g)])
            yield
        lockstep(stage1(0))
        for g in range(4):
            lockstep(stage1(g + 1) if g + 1 < 4 else None, stage2(g))
        P.pool(lambda e: e.tensor_copy(out=xrb[:, :, 0:3], in_=xrb[:, :, 128:131]), reads=["xrb"], writes=["xrb"])
        load_x(t + 1)
        rs = [6, 7]
        for hf in range(2):
            hs = slice(hf * 512, (hf + 1) * 512)
            P.dve(lambda e, hs=hs, r=rs[hf], xin=xin: e.scalar_tensor_tensor(out=xin[:, hs], in0=xin[:, hs], scalar=ALPHA, in1=bank[r][:],
                                                                             op0=ALU.mult, op1=ALU.add), reads=[kx, bk(rs[hf])], writes=[kx])
            P.dve(lambda e, hf=hf, hs=hs, xin=xin: e.bn_stats(out=st[:, 6 * hf:6 * hf + 6], in_=xin[:, hs]), reads=[kx], writes=["st"])
        layer_norm_tail(P, xin, st, mv, cst, None, None, None, [kx])
        for hf in range(2):
            hs = slice(hf * 512, (hf + 1) * 512)
            P.dma("sp", lambda e, hf=hf: e.dma_start(out=lng[:], in_=d["row"][:, RB_LNG + hf * 512:RB_LNG + (hf + 1) * 512].partition_broadcast(128)),
                  "lgl", writes=["normw"])
            P.dma("sp", lambda e, hf=hf: e.dma_start(out=lnb[:], in_=d["row"][:, RB_LNB + hf * 512:RB_LNB + (hf + 1) * 512].partition_broadcast(128)),
                  "lbl", writes=["tnh"])
            P.pool(lambda e, hs=hs, xin=xin: e.tensor_tensor(out=xin[:, hs], in0=xin[:, hs], in1=lng[:], op=ALU.mult), reads=[kx, "normw"], writes=[kx])
            P.pool(lambda e, hs=hs, xin=xin: e.tensor_tensor(out=xin[:, hs], in0=xin[:, hs], in1=lnb[:], op=ALU.add), reads=[kx, "tnh"], writes=[kx])
        P.dma("sp", lambda e, ts_=ts_, xin=xin: e.dma_start(out=x_out[ts_, :], in_=xin[:]), "xo%d" % (t % 2), reads=[kx], writes=[("outd", t)])
    return A


B_SPECS = [("w_in", [1024, 5152], F32), ("w_out", [2048, 1024], F32), ("vec", [128, NVB], F32), ("row", [1, NRB], F32)]


def prep_B(inp):
    f = np.float32
    vec = np.zeros((128, NVB), f)
    cw = np.asarray(inp["ssd_conv_w"][0], f)
    for j in range(24):
        for k in range(4):
            vec[:, VB_CW + 4 * j + k] = cw[k, j * 128:(j + 1) * 128]
    cb = np.asarray(inp["ssd_conv_b"][0], f)
    vec[:, VB_CB:VB_CB + 24] = cb.reshape(24, 128).T
    row = np.zeros((1, NRB), f)
    row[0, RB_DTB:RB_DTB + 32] = np.asarray(inp["ssd_dt_bias"][0], f)
    row[0, RB_ALOG:RB_ALOG + 32] = np.asarray(inp["ssd_a_log"][0], f)
    row[0, RB_D:RB_D + 32] = np.asarray(inp["ssd_d"][0], f)
    row[0, RB_NW:RB_NW + 2048] = np.asarray(inp["ssd_norm"][0], f)
    row[0, RB_LNG:RB_LNG + 1024] = np.asarray(inp["ssd_ln_g"][0], f)
    row[0, RB_LNB:RB_LNB + 1024] = np.asarray(inp["ssd_ln_b"][0], f)
    row[0, RB_CB:RB_CB + 3072] = cb
    return {"w_in": np.asarray(inp["ssd_w_in"][0], f), "w_out": np.asarray(inp["ssd_w_out"][0], f), "vec": vec, "row": row}


def build_program_B(S):
    nc = bass.Bass("TRN2", target_bir_lowering=False)
    P = Prog(nc)
    d = declare(nc, B_SPECS, "b_")
    x_in = nc.dram_tensor("x1", [S, D], F32, kind="ExternalInput").ap()
    x_out = nc.dram_tensor("out", [S, D], F32, kind="ExternalOutput").ap()
    build_B(nc, P, S, d, x_in, x_out)
    st = P.emit()
    return nc, st


_CACHE = {}


def build_program_fused(S):
    nc = bass.Bass("TRN2", target_bir_lowering=False)
    P = Prog(nc)
    dA = declare(nc, A_SPECS, "a_")
    dA["pos"] = nc.dram_tensor("pos", [1, S], I32, kind="ExternalInput").ap()
    dB = declare(nc, B_SPECS, "b_")
    x_in = nc.dram_tensor("x", [S, D], F32, kind="ExternalInput").ap()
    x1 = nc.dram_tensor("x1_scratch", [S, D], F32).ap()
    x_out = nc.dram_tensor("out", [S, D], F32, kind="ExternalOutput").ap()
    A = build_A(nc, P, S, dA, x_in, x1)
    keep = {k: v for k, v in P.last_writer.items() if isinstance(k, tuple) and k[0] == "x1d"}
    P.barrier()
    P.last_writer.update(keep)
    A.free()
    build_B(nc, P, S, dB, x1, x_out)
    st = P.emit()
    return nc, st


def kernel(**inputs):
    S = SEQ
    x = np.ascontiguousarray(np.asarray(inputs["x"], np.float32))
    pos = np.ascontiguousarray(np.asarray(inputs["positions"], np.int32))
    hpA = prep_A(inputs)
    hpB = prep_B(inputs)
    mode = "fused"
    if mode == "split":
        if "A" not in _CACHE:
            _CACHE["A"] = build_program_A(S)[0]
            _CACHE["B"] = build_program_B(S)[0]
        mapsA = []
        for b in range(NCORES):
            m = {"a_" + k: v for k, v in hpA.items()}
            m["x"] = x[b]
            m["pos"] = pos[b:b + 1]
            mapsA.append(m)
        resA = run_bass_kernel_spmd(_CACHE["A"], mapsA, core_ids=list(range(NCORES)))
        mapsB = []
        for b in range(NCORES):
            m = {"b_" + k: v for k, v in hpB.items()}
            m["x1"] = np.ascontiguousarray(np.asarray(resA.results[b]["x1"], np.float32))
            mapsB.append(m)
        resB = run_bass_kernel_spmd(_CACHE["B"], mapsB, core_ids=list(range(NCORES)))
        return np.stack([np.asarray(resB.results[b]["out"], np.float32) for b in range(NCORES)], axis=0)
    if "F" not in _CACHE:
        _CACHE["F"] = build_program_fused(S)[0]
    maps = []
    for b in range(NCORES):
        m = {"a_" + k: v for k, v in hpA.items()}
        m.update({"b_" + k: v for k, v in hpB.items()})
        m["x"] = x[b]
        m["pos"] = pos[b:b + 1]
        maps.append(m)
    res = run_bass_kernel_spmd(_CACHE["F"], maps, core_ids=list(range(NCORES)))
    return np.stack([np.asarray(res.results[b]["out"], np.float32) for b in range(NCORES)], axis=0)
```

```python
import math
import numpy as np
import concourse.bass as bass
import concourse.mybir as mybir
from concourse.bass_utils import run_bass_kernel_spmd

F32 = mybir.dt.float32
BF16 = mybir.dt.bfloat16
I32 = mybir.dt.int32
AF = mybir.ActivationFunctionType
ALU = mybir.AluOpType

D = 1024
NCORES = 8
SEQ = 4096
ALPHA = 4.0 ** 0.25
MAGIC = 12582912.0
C1 = 6.28125
C2 = 2.0 * math.pi - 6.28125


class Op:
    __slots__ = ("eng", "fn", "deps", "is_dma", "sem", "val", "marked", "dma_wait")

    def __init__(self, eng, fn, is_dma=False, sem=None):
        self.eng = eng
        self.fn = fn
        self.deps = []
        self.is_dma = is_dma
        self.sem = sem
        self.val = None
        self.marked = False
        self.dma_wait = {}


class Prog:
    ENGS = ("pe", "act", "dve", "pool", "sp")

    def __init__(self, nc):
        self.nc = nc
        self.eobj = {"pe": nc.tensor, "act": nc.scalar, "dve": nc.vector,
                     "pool": nc.gpsimd, "sp": nc.sync}
        self.ops = []
        self.last_writer = {}
        self.readers = {}
        self.dma_count = {}
        self.dma_last = {}
        self.last_on = {}
        self.bar = {}

    def _add(self, op, reads, writes):
        deps = op.deps

        def need(p):
            if p is None or p is op:
                return
            if p.is_dma:
                op.dma_wait[p.sem] = self.dma_count[p.sem]
            else:
                deps.append(p)

        b = self.bar.pop(op.eng, None)
        if b is not None:
            for p in b[0]:
                need(p)
            for s, c in b[1].items():
                op.dma_wait[s] = c
        for k in reads:
            w = self.last_writer.get(k)
            if w is not None:
                if (not w.is_dma) and (not op.is_dma) and w.eng == op.eng == "pe":
                    continue
                need(w)
        strict = op.eng != "pe"
        for k in writes:
            w = self.last_writer.get(k)
            if w is not None:
                if w.is_dma or op.is_dma or w.eng != op.eng or strict:
                    need(w)
            for r in self.readers.get(k, ()):
                if r.is_dma or op.is_dma or r.eng != op.eng or strict:
                    need(r)
        for k in writes:
            self.last_writer[k] = op
            self.readers[k] = []
        for k in reads:
            self.readers.setdefault(k, []).append(op)
        self.ops.append(op)
        if not op.is_dma:
            self.last_on[op.eng] = op
        return op

    def op(self, eng, fn, reads=(), writes=()):
        return self._add(Op(eng, fn), reads, writes)

    def dma(self, eng, fn, sem, reads=(), writes=(), chain=True):
        o = Op(eng, fn, is_dma=True, sem=sem)
        if chain and sem in self.dma_last:
            o.dma_wait[sem] = self.dma_count[sem]
        self.dma_count.setdefault(sem, 0)
        self._add(o, reads, writes)
        self.dma_count[sem] += 1
        o.val = self.dma_count[sem]
        self.dma_last[sem] = o
        return o

    def pe(self, fn, reads=(), writes=()):
        return self.op("pe", fn, reads, writes)

    def act(self, fn, reads=(), writes=()):
        return self.op("act", fn, reads, writes)

    def dve(self, fn, reads=(), writes=()):
        return self.op("dve", fn, reads, writes)

    def pool(self, fn, reads=(), writes=()):
        return self.op("pool", fn, reads, writes)

    def barrier(self):
        lasts = list(self.last_on.values())
        dm = dict(self.dma_count)
        for e in self.ENGS:
            self.bar[e] = (lasts, dm)
        self.last_writer = {}
        self.readers = {}

    def emit(self):
        nc = self.nc
        for o in self.ops:
            for d in o.deps:
                d.marked = True
        cnt = {e: 0 for e in self.ENGS}
        for o in self.ops:
            if not o.is_dma and o.marked:
                cnt[o.eng] += 1
                o.val = cnt[o.eng]
        ctr = {e: nc.alloc_semaphore(name="ctr_" + e) for e in self.ENGS}
        dsem = {s: nc.alloc_semaphore(name="dma_%d" % i) for i, s in enumerate(self.dma_count)}
        waited = {e: {} for e in self.ENGS}
        nwait = 0
        for o in self.ops:
            eng = self.eobj[o.eng]
            need = {}
            for d in o.deps:
                key = ("c", d.eng)
                if need.get(key, (None, 0))[1] < d.val:
                    need[key] = (ctr[d.eng], d.val)
            for s, c in o.dma_wait.items():
                key = ("d", s)
                if need.get(key, (None, 0))[1] < 16 * c:
                    need[key] = (dsem[s], 16 * c)
            w = waited[o.eng]
            for key, (h, v) in need.items():
                if w.get(key, 0) >= v:
                    continue
                eng.wait_ge(h, v)
                nwait += 1
                w[key] = v
            ins = o.fn(eng)
            if o.is_dma:
                ins.then_inc(dsem[o.sem], 16)
            elif o.marked:
                ins.then_inc(ctr[o.eng], 1)
        eng = self.eobj["sp"]
        for s, c in self.dma_count.items():
            eng.wait_ge(dsem[s], 16 * c)
        return dict(n_ops=len(self.ops), n_wait=nwait, marked=cnt)


class Alloc:
    def __init__(self, nc):
        self.nc = nc
        self.guards = []

    def sb(self, name, shape, dt=F32):
        g = self.nc.sbuf_tensor("s_" + name, list(shape), dt)
        t = g.__enter__()
        self.guards.append(g)
        return t

    def ps(self, name, shape, dt=F32):
        g = self.nc.psum_tensor("p_" + name, list(shape), dt)
        t = g.__enter__()
        self.guards.append(g)
        return t

    def free(self):
        for g in reversed(self.guards):
            g.__exit__(None, None, None)
        self.guards = []


def make_consts(nc, P, A, pfx):
    ident = A.sb(pfx + "ident", [128, 128], BF16)
    maskT = A.sb(pfx + "maskT", [128, 128], BF16)
    ones32 = A.sb(pfx + "ones32", [128, 128], F32)
    P.pool(lambda e: e.memset(ident[:], 1.0), writes=["ident"])
    P.pool(lambda e: e.affine_select(out=ident[:], in_=ident[:], pattern=[[-1, 128]],
                                     compare_op=ALU.is_equal, fill=0.0, base=0,
                                     channel_multiplier=1), reads=["ident"], writes=["ident"])
    P.pool(lambda e: e.memset(maskT[:], 1.0), writes=["maskT"])
    P.pool(lambda e: e.affine_select(out=maskT[:], in_=maskT[:], pattern=[[1, 128]],
                                     compare_op=ALU.is_ge, fill=0.0, base=0,
                                     channel_multiplier=-1), reads=["maskT"], writes=["maskT"])
    P.pool(lambda e: e.memset(ones32[:], 1.0), writes=["ones32"])
    return ident, maskT, ones32


V_CW, V_CB, V_BA, V_BX, V_LAM, V_QN, V_KVN, V_INVF, V_SGN, NV = 0, 16, 20, 24, 28, 32, 34, 35, 36, 40
ATT_SCALE = 96.0 ** -0.5


def build_A(nc, P, S, d, x_in, x_out):
    T = S // 128
    A = Alloc(nc)
    sb, ps = A.sb, A.ps
    ident, maskT, ones32 = make_consts(nc, P, A, "a_")
    bank = [ps("a_bank%d" % i, [128, 512], F32) for i in range(8)]

    def bk(i):
        return ("bank", i)

    w_in = sb("a_w_in", [128, 8, 2048], BF16)
    w_out = sb("a_w_out", [128, 8, 1024], BF16)
    w_uq = sb("a_w_uq", [128, 2, 1280], BF16)
    w_kn = sb("a_w_kn", [128, 512], BF16)
    w_v = sb("a_w_v", [128, 512], BF16)
    gA = sb("a_gA", [128, 4, 128], BF16)
    gX = sb("a_gX", [128, 4, 128], BF16)
    vec = sb("a_vec", [128, NV], F32)
    lng = sb("a_lng", [128, 512], F32)
    lnb = sb("a_lnb", [128, 512], F32)
    cst = sb("a_cst", [128, 8], F32)
    for i, v in enumerate([1e-6, 1e-5, math.pi / 2, 0.0]):
        P.pool(lambda e, i=i, v=v: e.memset(cst[:, i:i + 1], v), writes=["cst"])
    for c in range(0, 8, 2):
        P.dma("pool", lambda e, c=c: e.dma_start(out=w_in[:, c:c + 2, :],
                                                 in_=d["w_in"][c * 128:(c + 2) * 128, :].rearrange("(c p) n -> p c n", p=128)),
              "wl", writes=["w_in"], chain=False)
    for c in range(0, 8, 4):
        P.dma("pool", lambda e, c=c: e.dma_start(out=w_out[:, c:c + 4, :],
                                                 in_=d["w_out"][c * 128:(c + 4) * 128, :].rearrange("(c p) n -> p c n", p=128)),
              "wl", writes=["w_out"], chain=False)
    P.dma("pool", lambda e: e.dma_start(out=w_uq[:], in_=d["w_uq"].rearrange("(c p) n -> p c n", p=128)),
          "wl", writes=["w_uq"], chain=False)
    P.dma("pool", lambda e: e.dma_start(out=w_kn[:], in_=d["w_kn"]), "wl", writes=["w_kn"], chain=False)
    P.dma("pool", lambda e: e.dma_start(out=w_v[:], in_=d["w_v"]), "wl", writes=["w_v"], chain=False)
    P.dma("pool", lambda e: e.dma_start(out=gA[:], in_=d["gA"]), "wl", writes=["gA"], chain=False)
    P.dma("pool", lambda e: e.dma_start(out=gX[:], in_=d["gX"]), "wl", writes=["gX"], chain=False)
    P.dma("sp", lambda e: e.dma_start(out=vec[:], in_=d["vec"]), "wl2", writes=["vec"])

    der = sb("a_der", [128, 16], F32)
    tmp4 = sb("a_tmp4", [128, 4], F32)
    P.act(lambda e: e.activation(out=tmp4[:], in_=vec[:, V_LAM:V_LAM + 4], func=AF.Exp, scale=-1.0),
          reads=["vec"], writes=["tmp4"])
    P.act(lambda e: e.activation(out=tmp4[:], in_=tmp4[:], func=AF.Ln, bias=1.0), reads=["tmp4"], writes=["tmp4"])
    P.dve(lambda e: e.tensor_scalar(out=der[:, 0:4], in0=tmp4[:], scalar1=4.0, scalar2=None, op0=ALU.mult),
          reads=["tmp4"], writes=["der"])
    P.dve(lambda e: e.tensor_scalar(out=der[:, 4:8], in0=tmp4[:], scalar1=-4.0, scalar2=None, op0=ALU.mult),
          reads=["tmp4"], writes=["der"])
    P.dve(lambda e: e.tensor_scalar(out=der[:, 8:16], in0=vec[:, V_BA:V_BA + 8], scalar1=0.5, scalar2=None,
                                    op0=ALU.mult), reads=["vec"], writes=["der"])

    trig_d = nc.dram_tensor("a_trig", [128, 2, S], F32).ap()
    CB = min(4096, S)
    A2 = Alloc(nc)
    pi_t = A2.sb("a_pi", [128, CB], I32)
    tA = A2.sb("a_tA", [128, CB], F32)
    tB = A2.sb("a_tB", [128, CB], F32)
    tC = A2.sb("a_tC", [128, 2, CB], F32)
    for blk in range(S // CB):
        cs = slice(blk * CB, (blk + 1) * CB)
        P.dma("sp", lambda e, cs=cs: e.dma_start(out=pi_t[:], in_=d["pos"][:, cs].partition_broadcast(128)),
              "trg", writes=["pi"])
        P.dve(lambda e: e.tensor_copy(out=tA[:], in_=pi_t[:]), reads=["pi"], writes=["tA"])
        P.dve(lambda e: e.tensor_scalar(out=tA[:], in0=tA[:], scalar1=vec[:, V_INVF:V_INVF + 1], scalar2=None,
                                        op0=ALU.mult), reads=["tA", "vec"], writes=["tA"])
        P.dve(lambda e: e.tensor_scalar(out=tB[:], in0=tA[:], scalar1=1.0 / (2 * math.pi), scalar2=MAGIC,
                                        op0=ALU.mult, op1=ALU.add), reads=["tA"], writes=["tB"])
        P.dve(lambda e: e.tensor_scalar(out=tB[:], in0=tB[:], scalar1=-MAGIC, scalar2=None, op0=ALU.add),
              reads=["tB"], writes=["tB"])
        P.dve(lambda e: e.scalar_tensor_tensor(out=tA[:], in0=tB[:], scalar=-C1, in1=tA[:], op0=ALU.mult,
                                               op1=ALU.add), reads=["tA", "tB"], writes=["tA"])
        P.dve(lambda e: e.scalar_tensor_tensor(out=tA[:], in0=tB[:], scalar=-C2, in1=tA[:], op0=ALU.mult,
                                               op1=ALU.add), reads=["tA", "tB"], writes=["tA"])
        P.dve(lambda e: e.tensor_scalar(out=tA[:], in0=tA[:], scalar1=3.1415925, scalar2=-3.1415925,
                                        op0=ALU.min, op1=ALU.max), reads=["tA"], writes=["tA"])
        P.act(lambda e: e.activation(out=tB[:], in_=tA[:], func=AF.Abs), reads=["tA"], writes=["tB"])
        P.act(lambda e: e.activation(out=tC[:, 0, :], in_=tB[:], func=AF.Sin, bias=cst[:, 2:3], scale=-1.0),
              reads=["tB", "cst"], writes=["tC"])
        P.act(lambda e: e.activation(out=tC[:, 1, :], in_=tA[:], func=AF.Sin, scale=vec[:, V_SGN:V_SGN + 1]),
              reads=["tA", "vec"], writes=["tC"])
        P.dma("sp", lambda e, cs=cs: e.dma_start(out=trig_d[:, :, cs], in_=tC[:]), "trg",
              reads=["tC"], writes=["trig_d"])

    keepd = {k: v for k, v in P.last_writer.items() if k == "trig_d"}
    P.barrier()
    P.last_writer.update(keepd)
    A2.free()
    KnT = sb("a_KnT", [128, 4, S], BF16)
    KrT = sb("a_KrT", [128, S], BF16)
    Vaug = sb("a_Vaug", [128, T, 8, 96], BF16)
    P.pool(lambda e: e.memset(Vaug[:], 1.0), writes=["Vall"])
    xin2 = [sb("a_xin%d" % i, [128, 1024], F32) for i in range(2)]
    xb = sb("a_xb", [128, 1024], BF16)
    xT = sb("a_xT", [128, 8, 128], BF16)
    xr = sb("a_xr", [128, 4, 131], F32)
    s12 = [sb("a_s1%d" % i, [128, 8, 128], F32) for i in range(2)]
    sq = sb("a_sq", [128, 3, 128], F32)
    sr = sb("a_sr", [128, 2, 128], F32)
    cqn = sb("a_cqn", [128, 3, 128], BF16)
    trg = sb("a_trg", [128, 2, 128], F32)
    kr1 = sb("a_kr1", [128, 128], F32)
    kr2 = sb("a_kr2", [128, 128], F32)
    acc = sb("a_acc", [128, 4, 128], F32)
    xcb = sb("a_xcb", [128, 4, 128], BF16)
    ta = sb("a_ta", [128, 4, 128], F32)
    ti = sb("a_ti", [128, 4, 128], F32)
    aa = sb("a_aa", [128, 4, 128], F32)
    hh2 = [sb("a_hh%d" % i, [128, 4, 128], F32) for i in range(2)]
    hc = sb("a_hc", [128, 4], F32)
    qsb = sb("a_qsb", [128, 6, 128], F32)
    th = qsb[:, 0:4, :]
    QnT2 = [sb("a_QnT%d" % i, [128, 4, 128], BF16) for i in range(2)]
    QrT2 = [sb("a_QrT%d" % i, [128, 3, 128], BF16) for i in range(2)]
    PT = [sb("a_PT%d" % i, [128, 4, 128], BF16) for i in range(2)]
    rl = sb("a_rl", [128, 128], F32)
    ym = sb("a_ym", [128, 4, 128], F32)
    yT = sb("a_yT", [128, 8, 128], BF16)
    st = sb("a_st", [128, 12], F32)
    mv = sb("a_mv", [128, 4], F32)
    P.pool(lambda e: e.memset(xr[:], 0.0), writes=["xr"])
    P.pool(lambda e: e.memset(hc[:], 0.0), writes=["hc"])

    def b3(i):
        return bank[i][:].rearrange("p (a b) -> p a b", a=4)

    def front(t):
        ts_ = slice(t * 128, (t + 1) * 128)
        pp = t % 2
        xin, s1, hh, QnT, QrT = xin2[pp], s12[pp], hh2[pp], QnT2[pp], QrT2[pp]
        kx, ks1, khh, kqn, kqr = ("xin", pp), ("s1", pp), ("hh", pp), ("QnT", pp), ("QrT", pp)
        P.dma("sp", lambda e: e.dma_start(out=xin[:], in_=x_in[ts_, :]), "xin%d" % pp, writes=[kx])
        P.dma("sp", lambda e: e.dma_start(out=trg[:], in_=trig_d[:, :, ts_]), "trl", reads=["trig_d"], writes=["trg"])
        P.act(lambda e: e.activation(out=xb[:], in_=xin[:], func=AF.Copy), reads=[kx], writes=["xb"])
        yield
        b0 = bank[0][:].bitcast(BF16)
        for c in range(8):
            P.pe(lambda e, c=c: e.transpose(b0[:, c * 128:(c + 1) * 128], xb[:, c * 128:(c + 1) * 128], ident[:]),
                 reads=["xb", "ident"], writes=[bk(0)])
        P.dve(lambda e: e.tensor_copy(out=xT[:].rearrange("p a b -> p (a b)"), in_=b0), reads=[bk(0)], writes=["xT"])
        yield
        yield
        ibank = [1, 2, 3, 0]
        for j in range(16):
            bi = ibank[j // 4]
            for c in range(8):
                P.pe(lambda e, j=j, c=c, bi=bi: e.matmul(b3(bi)[:, j % 4, :], lhsT=w_in[:, c, j * 128:(j + 1) * 128],
                                                        rhs=xT[:, c, :], start=(c == 0), stop=(c == 7)),
                     reads=["w_in", "xT"], writes=[bk(bi)])
            if j % 4 == 3:
                yield
        P.act(lambda e: e.activation(out=xr[:, :, 3:131], in_=b3(1), func=AF.Copy), reads=[bk(1)], writes=["xr"])
        yield
        for i in range(2):
            P.act(lambda e, i=i: e.activation(out=s1[:, 4 * i:4 * i + 4, :], in_=b3(2 + i), func=AF.Tanh, scale=0.5),
                  reads=[bk(2 + i)], writes=[ks1])
            P.dve(lambda e, i=i: e.scalar_tensor_tensor(out=s1[:, 4 * i:4 * i + 4, :], in0=s1[:, 4 * i:4 * i + 4, :],
                                                         scalar=1.0, in1=b3(2 + i), op0=ALU.add, op1=ALU.mult),
                  reads=[ks1, bk(2 + i)], writes=[ks1])
        P.act(lambda e: e.activation(out=sq[:], in_=b3(0)[:, 0:3, :], func=AF.Square), reads=[bk(0)], writes=["sq"])
        yield
        b5 = b3(1)
        P.pe(lambda e: e.matmul(b5[:, 0, :], lhsT=ones32[:], rhs=sq[:, 0, :], start=True, stop=False),
             reads=["sq", "ones32"], writes=[bk(1)])
        yield
        P.pe(lambda e: e.matmul(b5[:, 0, :], lhsT=ones32[:], rhs=sq[:, 1, :], start=False, stop=True),
             reads=["sq", "ones32"], writes=[bk(1)])
        yield
        P.pe(lambda e: e.matmul(b5[:, 1, :], lhsT=ones32[:], rhs=sq[:, 2, :], start=True, stop=True),
             reads=["sq", "ones32"], writes=[bk(1)])
        yield
        P.act(lambda e: e.activation(out=sr[:, 0, :], in_=b5[:, 0, :], func=AF.Sqrt, bias=cst[:, 0:1], scale=1.0 / 256),
              reads=[bk(1), "cst"], writes=["sr"])
        yield
        P.act(lambda e: e.activation(out=sr[:, 1, :], in_=b5[:, 1, :], func=AF.Sqrt, bias=cst[:, 0:1], scale=1.0 / 128),
              reads=[bk(1), "cst"], writes=["sr"])
        yield
        P.dve(lambda e: e.reciprocal(out=sr[:], in_=sr[:]), reads=["sr"], writes=["sr"])
        yield
        for c in range(3):
            P.dve(lambda e, c=c: e.scalar_tensor_tensor(out=cqn[:, c, :], in0=b3(0)[:, c, :],
                                                         scalar=vec[:, V_QN + c:V_QN + c + 1],
                                                         in1=sr[:, min(c, 2) // 2, :], op0=ALU.mult, op1=ALU.mult),
                  reads=[bk(0), "vec", "sr"], writes=["cqn"])
        P.dve(lambda e: e.tensor_tensor(out=kr1[0:32, :], in0=b3(0)[0:32, 3, :], in1=trg[0:32, 0, :], op=ALU.mult),
              reads=[bk(0), "trg"], writes=["kr1"])
        yield
        P.dve(lambda e: e.tensor_tensor(out=kr2[0:32, :], in0=b3(0)[32:64, 3, :], in1=trg[32:64, 1, :], op=ALU.mult),
              reads=[bk(0), "trg"], writes=["kr2"])
        yield
        for i in range(3):
            P.pool(lambda e, i=i: e.tensor_tensor(out=KrT[32 * i:32 * i + 32, ts_], in0=kr1[0:32, :],
                                                  in1=kr2[0:32, :], op=ALU.add),
                   reads=["kr1", "kr2"], writes=[("Kr", t)])
        yield
        for c in range(4):
            P.dve(lambda e, c=c: e.tensor_scalar(out=acc[:, c, :], in0=xr[:, c, 3:131],
                                                 scalar1=vec[:, V_CW + 4 * c + 3:V_CW + 4 * c + 4],
                                                 scalar2=vec[:, V_CB + c:V_CB + c + 1], op0=ALU.mult, op1=ALU.add),
                  reads=["xr", "vec"], writes=[("acc", c)])
            for k in range(3):
                P.dve(lambda e, c=c, k=k: e.scalar_tensor_tensor(out=acc[:, c, :], in0=xr[:, c, k:k + 128],
                                                                  scalar=vec[:, V_CW + 4 * c + k:V_CW + 4 * c + k + 1],
                                                                  in1=acc[:, c, :], op0=ALU.mult, op1=ALU.add),
                      reads=["xr", "vec", ("acc", c)], writes=[("acc", c)])
            if c % 2 == 1:
                yield
        P.pool(lambda e: e.tensor_copy(out=xr[:, :, 0:3], in_=xr[:, :, 128:131]), reads=["xr"], writes=["xr"])
        yield
        accs = [("acc", c) for c in range(4)]
        P.pool(lambda e: e.tensor_copy(out=xcb[:], in_=acc[:]), reads=accs, writes=["xcb"])
        yield
        for c in range(4):
            P.pe(lambda e, c=c: e.matmul(b3(2)[:, c, :], lhsT=gA[:, c, :], rhs=xcb[:, c, :], start=True, stop=True),
                 reads=["gA", "xcb"], writes=[bk(2)])
        for c in range(4):
            P.pe(lambda e, c=c: e.matmul(b3(3)[:, c, :], lhsT=gX[:, c, :], rhs=xcb[:, c, :], start=True, stop=True),
                 reads=["gX", "xcb"], writes=[bk(3)])
        for c in range(4):
            P.act(lambda e, c=c: e.activation(out=ta[:, c, :], in_=b3(2)[:, c, :], func=AF.Tanh,
                                              bias=der[:, 8 + c:9 + c], scale=0.5), reads=[bk(2), "der"], writes=["ta"])
            P.act(lambda e, c=c: e.activation(out=ti[:, c, :], in_=b3(3)[:, c, :], func=AF.Tanh,
                                              bias=der[:, 12 + c:13 + c], scale=0.5), reads=[bk(3), "der"], writes=["ti"])
        yield
        for c in range(4):
            P.act(lambda e, c=c: e.activation(out=aa[:, c, :], in_=ta[:, c, :], func=AF.Exp,
                                              bias=der[:, 4 + c:5 + c], scale=der[:, 4 + c:5 + c]),
                  reads=["ta", "der"], writes=["aa"])
            P.act(lambda e, c=c: e.activation(out=th[:, c, :], in_=ta[:, c, :], func=AF.Tanh,
                                              bias=der[:, c:c + 1], scale=der[:, c:c + 1]),
                  reads=["ta", "der"], writes=["qsb"])
        P.dve(lambda e: e.tensor_tensor(out=ta[:], in0=aa[:], in1=aa[:], op=ALU.mult), reads=["aa"], writes=["ta"])
        yield
        P.dve(lambda e: e.scalar_tensor_tensor(out=ta[:], in0=ta[:], scalar=1.0, in1=th, op0=ALU.add, op1=ALU.mult),
              reads=["ta", "qsb"], writes=["ta"])
        yield
        P.act(lambda e: e.activation(out=ta[:], in_=ta[:], func=AF.Sqrt), reads=["ta"], writes=["ta"])
        yield
        P.dve(lambda e: e.scalar_tensor_tensor(out=ti[:], in0=ti[:], scalar=1.0, in1=acc[:], op0=ALU.add, op1=ALU.mult),
              reads=["ti"] + accs, writes=["ti"])
        yield
        P.dve(lambda e: e.scalar_tensor_tensor(out=ti[:], in0=ti[:], scalar=0.25, in1=ta[:], op0=ALU.mult, op1=ALU.mult),
              reads=["ti", "ta"], writes=["ti"])
        yield
        yield
        for c in range(4):
            P.dve(lambda e, c=c: e.tensor_tensor_scan(out=hh[:, c, :], data0=aa[:, c, :], data1=ti[:, c, :],
                                                       initial=hc[:, c:c + 1], op0=ALU.mult, op1=ALU.add),
                  reads=["aa", "ti", "hc"], writes=[khh])
        P.pool(lambda e: e.tensor_copy(out=hc[:], in_=hh[:, :, 127]), reads=[khh], writes=["hc"])
        yield
        yield
        for j in range(10):
            bi, jj = (j // 4, j % 4)
            for c in range(2):
                P.pe(lambda e, j=j, c=c, bi=bi, jj=jj: e.matmul(b3(bi)[:, jj, :], lhsT=w_uq[:, c, j * 128:(j + 1) * 128],
                                                                rhs=cqn[:, c, :], start=(c == 0), stop=(c == 1)),
                     reads=["w_uq", "cqn"], writes=[bk(bi)])
        for j in range(4):
            P.pe(lambda e, j=j: e.matmul(b3(3)[:, j, :], lhsT=w_kn[:, j * 128:(j + 1) * 128], rhs=cqn[:, 2, :],
                                         start=True, stop=True), reads=["w_kn", "cqn"], writes=[bk(3)])
        P.act(lambda e: e.activation(out=QnT[:], in_=b3(0), func=AF.Copy), reads=[bk(0)], writes=[kqn])
        yield
        P.act(lambda e: e.activation(out=qsb[:, 0:4, :], in_=b3(1), func=AF.Copy), reads=[bk(1)], writes=["qsb"])
        yield
        P.act(lambda e: e.activation(out=qsb[:, 4:6, :], in_=b3(2)[:, 0:2, :], func=AF.Copy), reads=[bk(2)], writes=["qsb"])
        yield
        P.pe(lambda e: e.matmul(bank[0][:], lhsT=cqn[:, 2, :], rhs=w_v[:], start=True, stop=True),
             reads=["w_v", "cqn"], writes=[bk(0)])
        yield
        yield
        cosb = trg[:, 0, :].unsqueeze(1).broadcast_to([128, 3, 128])
        sinb = trg[:, 1, :].unsqueeze(1).broadcast_to([128, 3, 128])
        P.pool(lambda e: e.tensor_tensor(out=qsb[:, 0:3, :], in0=qsb[:, 0:3, :], in1=cosb, op=ALU.mult),
               reads=["qsb", "trg"], writes=["qsb"])
        yield
        P.pool(lambda e: e.tensor_tensor(out=qsb[:, 3:6, :], in0=qsb[:, 3:6, :], in1=sinb, op=ALU.mult),
               reads=["qsb", "trg"], writes=["qsb"])
        yield
        P.pool(lambda e: e.tensor_tensor(out=QrT[:], in0=qsb[:, 0:3, :], in1=qsb[:, 3:6, :], op=ALU.add),
               reads=["qsb"], writes=[kqr])
        yield
        P.act(lambda e: e.activation(out=KnT[:, :, ts_], in_=b3(3), func=AF.Copy), reads=[bk(3)],
              writes=[("Kn", t)])
        yield
        P.dve(lambda e: e.tensor_copy(out=Vaug[:, t, :, 0:64], in_=bank[0][:].rearrange("p (h v) -> p h v", h=8)),
              reads=[bk(0), "Vall"], writes=[("V", t)])
        yield
        yield

    def attention(t, gen):
        pp = t % 2
        QnT, QrT = QnT2[pp], QrT2[pp]
        kqn, kqr = ("QnT", pp), ("QrT", pp)
        items = []
        for h in range(8):
            nb = (t + 4) // 4
            for b in range(nb):
                items.append((h, b, list(range(4 * b, min(4 * b + 4, t + 1)))))

        def qk(i):
            h, b, kts = items[i]
            sbk = 4 + (i % 2)
            j2, hr = h // 2, (h % 2) * 64
            c3, r3 = h // 3, (h % 3) * 32
            for jj, kt in enumerate(kts):
                ks = slice(kt * 128, (kt + 1) * 128)
                P.pe(lambda e, jj=jj, ks=ks: e.matmul(b3(sbk)[:, jj, :], lhsT=KnT[hr:hr + 64, j2, ks],
                                                       rhs=QnT[hr:hr + 64, j2, :], start=True, stop=False),
                     reads=[("Kn", kt), kqn], writes=[bk(sbk)])
                P.pe(lambda e, jj=jj, ks=ks: e.matmul(b3(sbk)[:, jj, :], lhsT=KrT[r3:r3 + 32, ks],
                                                       rhs=QrT[r3:r3 + 32, c3, :], start=False, stop=True),
                     reads=[("Kr", kt), kqr], writes=[bk(sbk)])

        def rest(i):
            h, b, kts = items[i]
            sbk = 4 + (i % 2)
            obk = 6 + (h % 2)
            n = len(kts)
            pt = PT[i % 2]
            ptk = ("PT", i % 2)
            P.act(lambda e: e.activation(out=pt[:, 0:n, :], in_=b3(sbk)[:, 0:n, :], func=AF.Exp, scale=ATT_SCALE),
                  reads=[bk(sbk)], writes=[ptk])
            if t in kts:
                jd = t - 4 * b
                P.pool(lambda e: e.tensor_tensor(out=pt[:, jd, :], in0=pt[:, jd, :], in1=maskT[:], op=ALU.mult),
                       reads=[ptk, "maskT"], writes=[ptk])
            for jj, kt in enumerate(kts):
                P.pe(lambda e, jj=jj, kt=kt: e.matmul(bank[obk][0:96, 0:128], lhsT=Vaug[:, kt, h, :], rhs=pt[:, jj, :],
                                                      start=(kt == 0), stop=(kt == t)),
                     reads=[("V", kt), "Vall", ptk], writes=[bk(obk)])
            if kts[-1] == t:
                j2, hr = h // 2, (h % 2) * 64
                for hf in range(2):
                    P.dve(lambda e, hf=hf: e.reciprocal(out=rl[32 * hf:32 * hf + 32, :], in_=bank[obk][64:96, 0:128]),
                          reads=[bk(obk)], writes=["rl"])
                P.dve(lambda e: e.scalar_tensor_tensor(
                    out=ym[hr:hr + 64, j2, :], in0=bank[obk][0:64, 0:128],
                    scalar=0.5, in1=rl[0:64, :], op0=ALU.mult, op1=ALU.mult),
                    reads=[bk(obk), "rl"], writes=["ym"])

        state = {"k": 0}

        def adv():
            while state["k"] < len(gen):
                g_ = gen[state["k"]]
                if g_ is None:
                    state["k"] += 1
                    continue
                try:
                    next(g_)
                    return
                except StopIteration:
                    state["k"] += 1

        for i in range(len(items)):
            qk(i)
            if i > 0:
                rest(i - 1)
            adv()
        rest(len(items) - 1)

    def tail_a(t):
        pp = t % 2
        s1, hh = s12[pp], hh2[pp]
        ks1, khh = ("s1", pp), ("hh", pp)
        P.dve(lambda e: e.tensor_tensor(out=yT[:, 0:4, :], in0=s1[:, 0:4, :], in1=hh[:], op=ALU.mult),
              reads=[ks1, khh], writes=["yT"])
        P.dve(lambda e: e.tensor_tensor(out=yT[:, 4:8, :], in0=s1[:, 4:8, :], in1=ym[:], op=ALU.mult),
              reads=[ks1, "ym"], writes=["yT"])

    def tail(t):
        ts_ = slice(t * 128, (t + 1) * 128)
        pp = t % 2
        xin = xin2[pp]
        kx = ("xin", pp)
        for hf in range(2):
            for c in range(8):
                P.pe(lambda e, hf=hf, c=c: e.matmul(bank[hf][:], lhsT=yT[:, c, :], rhs=w_out[:, c, hf * 512:(hf + 1) * 512],
                                                    start=(c == 0), stop=(c == 7)),
                     reads=["yT", "w_out"], writes=[bk(hf)])
        for hf in range(2):
            hs = slice(hf * 512, (hf + 1) * 512)
            P.dve(lambda e, hf=hf, hs=hs: e.scalar_tensor_tensor(out=xin[:, hs], in0=xin[:, hs], scalar=ALPHA, in1=bank[hf][:],
                                                                  op0=ALU.mult, op1=ALU.add),
                  reads=[kx, bk(hf)], writes=[kx])
            yield
            P.dve(lambda e, hf=hf, hs=hs: e.bn_stats(out=st[:, 6 * hf:6 * hf + 6], in_=xin[:, hs]), reads=[kx],
                  writes=["st"])
            yield
        layer_norm_tail(P, xin, st, mv, cst, None, None, None, [kx])
        yield
        for hf in range(2):
            hs = slice(hf * 512, (hf + 1) * 512)
            P.dma("sp", lambda e, hf=hf: e.dma_start(out=lng[:], in_=d["lng"][:, hf * 512:(hf + 1) * 512].partition_broadcast(128)),
                  "lgl", writes=["lng"])
            P.dma("sp", lambda e, hf=hf: e.dma_start(out=lnb[:], in_=d["lnb"][:, hf * 512:(hf + 1) * 512].partition_broadcast(128)),
                  "lbl", writes=["lnb"])
            P.pool(lambda e, hs=hs: e.tensor_tensor(out=xin[:, hs], in0=xin[:, hs], in1=lng[:], op=ALU.mult),
                   reads=[kx, "lng"], writes=[kx])
            yield
            P.pool(lambda e, hs=hs: e.tensor_tensor(out=xin[:, hs], in0=xin[:, hs], in1=lnb[:], op=ALU.add),
                   reads=[kx, "lnb"], writes=[kx])
            yield
        P.dma("sp", lambda e: e.dma_start(out=x_out[ts_, :], in_=xin[:]), "xo%d" % pp, reads=[kx], writes=[("x1d", t)])

    def exhaust(g_):
        if g_ is not None:
            for _ in g_:
                pass

    exhaust(front(0))
    ptail = None
    for t in range(T):
        gfront = front(t + 1) if t + 1 < T else None
        attention(t, [ptail, gfront])
        exhaust(ptail)
        exhaust(gfront)
        tail_a(t)
        ptail = tail(t)
    exhaust(ptail)
    return A


def layer_norm_tail(P, z, st, mv, cst, lng, lnb, _unused, zk):
    P.dve(lambda e: e.bn_aggr(out=mv[:, 0:2], in_=st[:]), reads=["st"], writes=["mv"])
    P.act(lambda e: e.activation(out=mv[:, 2:3], in_=mv[:, 1:2], func=AF.Sqrt, bias=cst[:, 1:2], scale=1.0),
          reads=["mv", "cst"], writes=["mv2"])
    P.dve(lambda e: e.reciprocal(out=mv[:, 2:3], in_=mv[:, 2:3]), reads=["mv2"], writes=["mv2"])
    P.dve(lambda e: e.scalar_tensor_tensor(out=mv[:, 3:4], in0=mv[:, 0:1], scalar=-1.0, in1=mv[:, 2:3],
                                           op0=ALU.mult, op1=ALU.mult), reads=["mv", "mv2"], writes=["mv3"])
    P.act(lambda e: e.activation(out=z[:], in_=z[:], func=AF.Identity, bias=mv[:, 3:4], scale=mv[:, 2:3]),
          reads=zk + ["mv2", "mv3"], writes=zk)
    if lng is not None:
        P.pool(lambda e: e.tensor_tensor(out=z[:], in0=z[:], in1=lng[:], op=ALU.mult), reads=zk + ["lng"], writes=zk)
        P.pool(lambda e: e.tensor_tensor(out=z[:], in0=z[:], in1=lnb[:], op=ALU.add), reads=zk + ["lnb"], writes=zk)


A_SPECS = [("w_in", [1024, 2048], F32), ("w_out", [1024, 1024], F32), ("w_uq", [256, 1280], F32),
           ("w_kn", [128, 512], F32), ("w_v", [128, 512], F32), ("gA", [128, 4, 128], F32),
           ("gX", [128, 4, 128], F32), ("vec", [128, NV], F32), ("lng", [1, 1024], F32), ("lnb", [1, 1024], F32)]


def prep_A(inp):
    f = np.float32
    w_in = np.asarray(inp["ab_w_in"][0], f)
    kr = w_in[:, 1920:1952]
    kr_sw = np.concatenate([kr[:, 16:32], kr[:, 0:16]], axis=1)
    w_inA = np.concatenate([w_in[:, :1920], kr, kr_sw, np.zeros((1024, 64), f)], axis=1)
    uq = np.asarray(inp["mla_w_uq"][0], f).reshape(256, 8, 96)
    nope = uq[:, :, :64].reshape(256, 512)
    rope = uq[:, :, 64:96]
    rsw = np.concatenate([rope[:, :, 16:32], rope[:, :, 0:16]], axis=2)
    z32 = np.zeros((256, 1, 32), f)
    rope9 = np.concatenate([rope, z32], axis=1).reshape(256, 288)
    rsw9 = np.concatenate([rsw, z32], axis=1).reshape(256, 288)
    pad = np.zeros((256, 96), f)
    w_uqA = np.concatenate([nope, rope9, pad, rsw9, pad], axis=1)
    w_uqA = np.concatenate([nope,
                            np.concatenate([rope9[:, 0:96], np.zeros((256, 32), f)], 1),
                            np.concatenate([rope9[:, 96:192], np.zeros((256, 32), f)], 1),
                            np.concatenate([rope9[:, 192:288], np.zeros((256, 32), f)], 1),
                            np.concatenate([rsw9[:, 0:96], np.zeros((256, 32), f)], 1),
                            np.concatenate([rsw9[:, 96:192], np.zeros((256, 32), f)], 1),
                            np.concatenate([rsw9[:, 192:288], np.zeros((256, 32), f)], 1)], axis=1)
    ukv = np.asarray(inp["mla_w_ukv"][0], f).reshape(128, 8, 128)
    w_kn = np.ascontiguousarray(ukv[:, :, :64]).reshape(128, 512)
    w_v = np.ascontiguousarray(ukv[:, :, 64:]).reshape(128, 512)

    def blockdiag(w):
        w = np.asarray(w[0], f)
        o = np.zeros((128, 4, 128), f)
        for h in range(8):
            r = (h % 2) * 64
            o[r:r + 64, h // 2, r:r + 64] = w[h]
        return o

    vec = np.zeros((128, NV), f)
    cw = np.asarray(inp["ab_conv_w"][0], f)
    for c in range(4):
        for k in range(4):
            vec[:, V_CW + 4 * c + k] = cw[k, c * 128:(c + 1) * 128]
    for nm, col in (("ab_conv_b", V_CB), ("ab_gate_a_b", V_BA), ("ab_gate_x_b", V_BX), ("ab_lambda", V_LAM)):
        vec[:, col:col + 4] = np.asarray(inp[nm][0], f).reshape(4, 128).T
    vec[:, V_QN:V_QN + 2] = np.asarray(inp["mla_q_norm"][0], f).reshape(2, 128).T
    vec[:, V_KVN] = np.asarray(inp["mla_kv_norm"][0], f)
    j = np.arange(128) % 16
    vec[:, V_INVF] = (10000.0 ** (-(2.0 * j) / 32.0)).astype(f)
    vec[:, V_SGN] = np.where((np.arange(128) % 32) < 16, -1.0, 1.0)
    return {"w_in": w_inA, "w_out": np.asarray(inp["ab_w_out"][0], f), "w_uq": w_uqA, "w_kn": w_kn, "w_v": w_v,
            "gA": blockdiag(inp["ab_gate_a_w"]), "gX": blockdiag(inp["ab_gate_x_w"]), "vec": vec,
            "lng": np.asarray(inp["ab_ln_g"], f).reshape(1, 1024), "lnb": np.asarray(inp["ab_ln_b"], f).reshape(1, 1024)}


def declare(nc, specs, pfx):
    return {nm: nc.dram_tensor(pfx + nm, shape, dt, kind="ExternalInput").ap() for nm, shape, dt in specs}


def build_program_A(S):
    nc = bass.Bass("TRN2", target_bir_lowering=False)
    P = Prog(nc)
    d = declare(nc, A_SPECS, "a_")
    d["pos"] = nc.dram_tensor("pos", [1, S], I32, kind="ExternalInput").ap()
    x_in = nc.dram_tensor("x", [S, D], F32, kind="ExternalInput").ap()
    x_out = nc.dram_tensor("x1", [S, D], F32, kind="ExternalOutput").ap()
    build_A(nc, P, S, d, x_in, x_out)
    st = P.emit()
    return nc, st


VB_CW, VB_CB, NVB = 0, 96, 120
RB_DTB, RB_ALOG, RB_D, RB_NW, RB_LNG, RB_LNB, RB_CB, NRB = 0, 32, 64, 96, 2144, 3168, 4192, 7264


def build_B(nc, P, S, d, x_in, x_out):
    T = S // 128
    A = Alloc(nc)
    sb, ps = A.sb, A.ps
    ident, maskT, ones32 = make_consts(nc, P, A, "b_")
    bank = [ps("b_bank%d" % i, [128, 512], F32) for i in range(8)]
    rr_state = [0]

    def rr():
        rr_state[0] = (rr_state[0] + 1) % 8
        return rr_state[0]

    def bk(i):
        return ("bank", i)

    def b3(i):
        return bank[i][:].rearrange("p (a b) -> p a b", a=4)

    tri = sb("b_tri", [128, 128], F32)
    u2 = sb("b_u2", [128, 128], F32)
    m025 = sb("b_m025", [128, 128], F32)
    onesb = sb("b_onesb", [128, 128], BF16)
    cst = sb("b_cst", [128, 8], F32)
    P.pool(lambda e: e.memset(tri[:], 1.0), writes=["tri"])
    P.pool(lambda e: e.affine_select(out=tri[:], in_=tri[:], pattern=[[1, 128]], compare_op=ALU.is_ge, fill=0.0,
                                     base=0, channel_multiplier=-1), reads=["tri"], writes=["tri"])
    P.pool(lambda e: e.memset(u2[:], 1.0), writes=["u2"])
    P.pool(lambda e: e.affine_select(out=u2[:], in_=u2[:], pattern=[[-1, 128]], compare_op=ALU.is_gt, fill=0.0,
                                     base=0, channel_multiplier=1), reads=["u2"], writes=["u2"])
    P.pool(lambda e: e.tensor_scalar(out=m025[:], in0=tri[:], scalar1=0.25, scalar2=None, op0=ALU.mult),
           reads=["tri"], writes=["m025"])
    P.pool(lambda e: e.memset(onesb[:], 1.0), writes=["onesb"])
    for i, v in enumerate([1e-6, 1e-5, 0.0, 1.0, 4e-6]):
        P.pool(lambda e, i=i, v=v: e.memset(cst[:, i:i + 1], v), writes=["cst"])
    w_in = sb("b_w_in", [128, 8, 5152], BF16)
    w_out = sb("b_w_out", [128, 16, 1024], BF16)
    vecb = sb("b_vec", [128, NVB], F32)
    cbrow = sb("b_cbrow", [128, 8, 128], BF16)
    dtb = sb("b_dtb", [128, 32], F32)
    aneg = sb("b_aneg", [128, 32], F32)
    cd = sb("b_cd", [128, 32], F32)
    normw = sb("b_normw", [128, 512], F32)
    dg = sb("b_dg", [128, 96, 128], BF16)
    for c in range(0, 8, 2):
        P.dma("pool", lambda e, c=c: e.dma_start(out=w_in[:, c:c + 2, :],
                                                 in_=d["w_in"][c * 128:(c + 2) * 128, :].rearrange("(c p) n -> p c n", p=128)),
              "wlinb", writes=["w_in"], chain=False)
    cb3 = d["row"][:, RB_CB:RB_CB + 3072].rearrange("o (i r c) -> o i r c", r=3, c=128)
    for r_ in range(3):
        P.dma("pool", lambda e, r_=r_: e.dma_start(out=cbrow[32 * r_:32 * r_ + 1, :, :], in_=cb3[:, :, r_, :]),
              "wlinb", writes=["cbrow"], chain=False)
    for c in range(0, 16, 4):
        P.dma("pool", lambda e, c=c: e.dma_start(out=w_out[:, c:c + 4, :],
                                                 in_=d["w_out"][c * 128:(c + 4) * 128, :].rearrange("(c p) n -> p c n", p=128)),
              "wl", writes=["w_out"], chain=False)
    P.dma("sp", lambda e: e.dma_start(out=vecb[:], in_=d["vec"]), "wl2", writes=["vecb"])
    for tl, off, n, key in ((dtb, RB_DTB, 32, "dtb"), (aneg, RB_ALOG, 32, "aneg"), (cd, RB_D, 32, "cd")):
        P.dma("sp", lambda e, tl=tl, off=off, n=n: e.dma_start(out=tl[:], in_=d["row"][:, off:off + n].partition_broadcast(128)),
              "wl2", writes=[key])
    P.act(lambda e: e.activation(out=aneg[:], in_=aneg[:], func=AF.Exp), reads=["aneg"], writes=["aneg"])
    P.dve(lambda e: e.tensor_scalar(out=aneg[:], in0=aneg[:], scalar1=-1.0, scalar2=None, op0=ALU.mult),
          reads=["aneg"], writes=["aneg"])
    P.dve(lambda e: e.tensor_scalar(out=cd[:], in0=cd[:], scalar1=0.5, scalar2=None, op0=ALU.mult),
          reads=["cd"], writes=["cd"])
    for jk in range(96):
        P.dve(lambda e, jk=jk: e.tensor_scalar(out=dg[:, jk, :], in0=ident[:], scalar1=vecb[:, jk:jk + 1], scalar2=None,
                                               op0=ALU.mult), reads=["ident", "vecb"], writes=["dg"])
    hT = sb("b_hT", [128, 2048], F32)
    hTb = sb("b_hTb", [128, 2048], BF16)
    P.pool(lambda e: e.memset(hT[:], 0.0), writes=["hT"])
    P.pool(lambda e: e.memset(hTb[:], 0.0), writes=["hTb"])
    xrb = sb("b_xrb", [128, 24, 131], BF16)
    P.pool(lambda e: e.memset(xrb[:], 0.0), writes=["xrb"])
    xin2 = [sb("b_xin%d" % i, [128, 1024], F32) for i in range(2)]
    xb = sb("b_xb", [128, 1024], BF16)
    xT = sb("b_xT", [128, 8, 128], BF16)
    sm = sb("b_sm", [128, 12, 32], F32)
    Rg = sb("b_Rg", [128, 4, 128], F32)
    es = sb("b_es", [128, 4, 128], F32)
    MT2 = [sb("b_MT%d" % i, [128, 8, 128], BF16) for i in range(2)]
    szg2 = [sb("b_szg%d" % i, [128, 512], F32) for i in range(2)]
    xs2 = sb("b_xs2", [128, 512], F32)
    xf2 = [sb("b_xf%d" % i, [128, 512], BF16) for i in range(2)]
    xsd2 = [sb("b_xsd%d" % i, [128, 512], BF16) for i in range(2)]
    xfd2 = [sb("b_xfd%d" % i, [128, 512], BF16) for i in range(2)]
    tnh = sb("b_tnh", [128, 512], F32)
    lng = normw
    lnb = tnh
    B2 = sb("b_B2", [128, 512], BF16)
    BCT = sb("b_BCT", [128, 8, 128], BF16)
    GTm = sb("b_GTm", [128, 128], F32)
    yb = sb("b_yb", [128, 512], F32)
    ssq = sb("b_ssq", [128, 4], F32)
    yn = xb[:, 0:512]
    junk = xb[:, 512:1024]
    ynT = sb("b_ynT", [128, 16, 128], BF16)
    st = sb("b_st", [128, 12], F32)
    mv = sb("b_mv", [128, 4], F32)

    def silu2(src_bank_ap, out_ap, key_in, key_out, shape3=None):
        P.act(lambda e: e.activation(out=tnh[:] if shape3 is None else tnh[:].rearrange("p (a b) -> p a b", a=4),
                                     in_=src_bank_ap, func=AF.Tanh, scale=0.5), reads=[key_in], writes=["tnh"])
        P.dve(lambda e: e.scalar_tensor_tensor(out=out_ap, in0=tnh[:] if shape3 is None else tnh[:].rearrange("p (a b) -> p a b", a=4),
                                               scalar=1.0, in1=src_bank_ap, op0=ALU.add, op1=ALU.mult),
              reads=["tnh", key_in], writes=[key_out])

    def silu2g(src_bank_ap, out_ap, key_in, key_out, shape3=None):
        tv = tnh[:] if shape3 is None else tnh[:].rearrange("p (a b) -> p a b", a=4)
        P.act(lambda e: e.activation(out=tv, in_=src_bank_ap, func=AF.Tanh, scale=0.5), reads=[key_in], writes=["tnh"])
        yield
        P.dve(lambda e: e.scalar_tensor_tensor(out=out_ap, in0=tv, scalar=1.0, in1=src_bank_ap, op0=ALU.add, op1=ALU.mult),
              reads=["tnh", key_in], writes=[key_out])
        yield

    def lockstep(*gens):
        gens = [g_ for g_ in gens if g_ is not None]
        while gens:
            for g_ in list(gens):
                try:
                    next(g_)
                except StopIteration:
                    gens.remove(g_)

    def load_x(t):
        if t >= T:
            return
        tsl = slice(t * 128, (t + 1) * 128)
        xi = xin2[t % 2]
        P.dma("sp", lambda e: e.dma_start(out=xi[:], in_=x_in[tsl, :]), "xin%d" % (t % 2), reads=[("x1d", t)],
              writes=[("xin", t % 2)])

    load_x(0)
    for t in range(T):
        ts_ = slice(t * 128, (t + 1) * 128)
        xin = xin2[t % 2]
        kx = ("xin", t % 2)
        P.act(lambda e, xin=xin: e.activation(out=xb[:], in_=xin[:], func=AF.Copy), reads=[kx], writes=["xb"])
        r0 = rr()
        b0 = bank[r0][:].bitcast(BF16)
        for c in range(8):
            P.pe(lambda e, c=c, b0=b0: e.transpose(b0[:, c * 128:(c + 1) * 128], xb[:, c * 128:(c + 1) * 128], ident[:]),
                 reads=["xb", "ident"], writes=[bk(r0)])
        P.dve(lambda e, b0=b0: e.tensor_copy(out=xT[:].rearrange("p a b -> p (a b)"), in_=b0), reads=[bk(r0)], writes=["xT"])
        r = rr()
        for c in range(8):
            P.pe(lambda e, c=c, r=r: e.matmul(bank[r][:, 0:32], lhsT=xT[:, c, :], rhs=w_in[:, c, 5120:5152],
                                              start=(c == 0), stop=(c == 7)), reads=["xT", "w_in"], writes=[bk(r)])
        P.dve(lambda e, r=r: e.tensor_tensor(out=sm[:, 0, :], in0=bank[r][:, 0:32], in1=dtb[:], op=ALU.add),
              reads=[bk(r), "dtb"], writes=["sm0"])
        P.act(lambda e: e.activation(out=sm[:, 1, :], in_=sm[:, 0, :], func=AF.Abs), reads=["sm0"], writes=["sm1"])
        P.act(lambda e: e.activation(out=sm[:, 1, :], in_=sm[:, 1, :], func=AF.Exp, scale=-1.0), reads=["sm1"], writes=["sm1"])
        P.act(lambda e: e.activation(out=sm[:, 1, :], in_=sm[:, 1, :], func=AF.Ln, bias=1.0), reads=["sm1"], writes=["sm1"])
        P.dve(lambda e: e.scalar_tensor_tensor(out=sm[:, 2, :], in0=sm[:, 0, :], scalar=0.0, in1=sm[:, 1, :],
                                               op0=ALU.max, op1=ALU.add), reads=["sm0", "sm1"], writes=["dt"])
        P.dve(lambda e: e.tensor_tensor(out=sm[:, 3, :], in0=sm[:, 2, :], in1=aneg[:], op=ALU.mult),
              reads=["dt", "aneg"], writes=["adt"])
        P.dve(lambda e: e.tensor_scalar(out=sm[:, 4, :], in0=sm[:, 2, :], scalar1=0.5, scalar2=None, op0=ALU.mult),
              reads=["dt"], writes=["cxf"])
        r = rr()
        for i, lh in enumerate((tri, u2, ones32)):
            P.pe(lambda e, i=i, lh=lh, r=r: e.matmul(bank[r][:, 32 * i:32 * i + 32], lhsT=lh[:], rhs=sm[:, 3, :],
                                                     start=True, stop=True), reads=["adt", "tri", "u2", "ones32"],
                 writes=[bk(r)])
        P.act(lambda e, r=r: e.activation(out=sm[:, 7:10, :].rearrange("p a b -> p (a b)"), in_=bank[r][:, 0:96], func=AF.Exp),
              reads=[bk(r)], writes=["e3"])
        P.dve(lambda e: e.scalar_tensor_tensor(out=sm[:, 5, :], in0=sm[:, 4, :], scalar=0.5, in1=sm[:, 8, :],
                                               op0=ALU.mult, op1=ALU.mult), reads=["cxf", "e3"], writes=["cxfd"])
        P.dve(lambda e: e.tensor_scalar(out=sm[:, 6, :], in0=sm[:, 7, :], scalar1=0.5, scalar2=None, op0=ALU.mult),
              reads=["e3"], writes=["eoff"])
        for q in range(6):
            r = rr()
            for jj in range(4):
                j = 4 * q + jj
                for c in range(8):
                    P.pe(lambda e, j=j, jj=jj, c=c, r=r: e.matmul(b3(r)[:, jj, :], lhsT=w_in[:, c, 2048 + j * 128:2048 + (j + 1) * 128],
                                                                  rhs=xT[:, c, :], start=(c == 0), stop=(c == 7)),
                         reads=["w_in", "xT"], writes=[bk(r)])
            P.act(lambda e, q=q, r=r: e.activation(out=xrb[:, 4 * q:4 * q + 4, 3:131], in_=b3(r), func=AF.Copy),
                  reads=[bk(r)], writes=["xrb"])

        def conv_tok(r, j0):
            for jj in range(4):
                j = j0 + jj
                osl = bank[r][:, jj * 128:(jj + 1) * 128]
                for k in range(4):
                    P.pe(lambda e, j=j, k=k, osl=osl: e.matmul(osl, lhsT=xrb[:, j, k:k + 128], rhs=dg[:, 4 * j + k, :],
                                                               start=(k == 0), stop=False), reads=["xrb", "dg"], writes=[bk(r)])
                P.pe(lambda e, j=j, osl=osl: e.matmul(osl, lhsT=onesb[32 * (j % 3):32 * (j % 3) + 1, :], rhs=cbrow[32 * (j % 3):32 * (j % 3) + 1, j // 3, :],
                                                      start=False, stop=True), reads=["onesb", "cbrow"], writes=[bk(r)])

        def conv_feat(r, j0):
            for jj in range(4):
                j = j0 + jj
                osl = bank[r][:, jj * 128:(jj + 1) * 128]
                for k in range(4):
                    P.pe(lambda e, j=j, k=k, osl=osl: e.matmul(osl, lhsT=dg[:, 4 * j + k, :], rhs=xrb[:, j, k:k + 128],
                                                               start=(k == 0), stop=False), reads=["xrb", "dg"], writes=[bk(r)])
                P.pe(lambda e, j=j, osl=osl: e.matmul(osl, lhsT=cbrow[32 * (j % 3):32 * (j % 3) + 1, j // 3, :], rhs=onesb[32 * (j % 3):32 * (j % 3) + 1, :],
                                                      start=False, stop=True), reads=["onesb", "cbrow"], writes=[bk(r)])

        r = rr()
        conv_tok(r, 16)
        silu2(bank[r][:], B2[:], bk(r), "B2")
        for i in range(2):
            r = rr()
            conv_feat(r, 16 + 4 * i)
            silu2(b3(r), BCT[:, 4 * i:4 * i + 4, :], bk(r), "BCT", shape3=True)
        def stage1(g):
            gs = slice(g * 512, (g + 1) * 512)
            hs8 = slice(g * 8, (g + 1) * 8)
            pp = g % 2
            szg, xf, xsd, xfd, MT = szg2[pp], xf2[pp], xsd2[pp], xfd2[pp], MT2[pp]
            tv = tnh[:]

            def bc(i, hs8=hs8):
                return sm[:, i, hs8].unsqueeze(2).broadcast_to([128, 8, 64])

            def mkR(hf):
                h4 = slice(g * 8 + hf * 4, g * 8 + hf * 4 + 4)
                P.pool(lambda e: e.tensor_tensor(out=Rg[:], in0=tri[:].unsqueeze(1).broadcast_to([128, 4, 128]),
                                                 in1=sm[:, 3, h4].unsqueeze(2).broadcast_to([128, 4, 128]), op=ALU.mult),
                       reads=["tri", "adt"], writes=["Rg"])

            def seg_mm(r):
                P.pe(lambda e: e.matmul(bank[r][:], lhsT=u2[:], rhs=Rg[:].rearrange("p a b -> p (a b)"), start=True, stop=True),
                     reads=["u2", "Rg"], writes=[bk(r)])

            def seg_exp(r):
                P.act(lambda e: e.activation(out=es[:].rearrange("p a b -> p (a b)"), in_=bank[r][:], func=AF.Exp),
                      reads=[bk(r)], writes=["es"])

            def mk_mt(hf):
                P.dve(lambda e: e.tensor_tensor(out=MT[:, 4 * hf:4 * hf + 4, :], in0=es[:],
                                                in1=GTm[:].unsqueeze(1).broadcast_to([128, 4, 128]), op=ALU.mult),
                      reads=["es", "GTm"], writes=[("MT", pp, hf)])

            mkR(0)
            rg = rr()
            P.pe(lambda e: e.matmul(bank[rg][:, 0:128], lhsT=BCT[:, g, :], rhs=BCT[:, 4 + g, :], start=True, stop=True),
                 reads=["BCT"], writes=[bk(rg)])
            yield
            rc = rr()
            conv_tok(rc, 4 * g)
            yield
            rs0 = rr()
            seg_mm(rs0)
            P.dve(lambda e: e.tensor_tensor(out=GTm[:], in0=bank[rg][:, 0:128], in1=m025[:], op=ALU.mult),
                  reads=[bk(rg), "m025"], writes=["GTm"])
            yield
            P.act(lambda e: e.activation(out=tv, in_=bank[rc][:], func=AF.Tanh, scale=0.5), reads=[bk(rc)], writes=["tnh"])
            rz = rr()
            for c in range(8):
                P.pe(lambda e, c=c: e.matmul(bank[rz][:], lhsT=xT[:, c, :], rhs=w_in[:, c, gs], start=(c == 0),
                                             stop=(c == 7)), reads=["xT", "w_in"], writes=[bk(rz)])
            yield
            seg_exp(rs0)
            yield
            P.dve(lambda e: e.scalar_tensor_tensor(out=xs2[:], in0=tv, scalar=1.0, in1=bank[rc][:], op0=ALU.add, op1=ALU.mult),
                  reads=["tnh", bk(rc)], writes=["xs2"])
            mkR(1)
            yield
            mk_mt(0)
            rs1 = rr()
            seg_mm(rs1)
            yield
            x3 = xs2[:].rearrange("p (h v) -> p h v", h=8)
            P.act(lambda e: e.activation(out=tv, in_=bank[rz][:], func=AF.Tanh, scale=0.5), reads=[bk(rz)], writes=["tnh"])
            P.dve(lambda e: e.tensor_tensor(out=xf[:].rearrange("p (h v) -> p h v", h=8), in0=x3, in1=bc(4),
                                            op=ALU.mult), reads=["xs2", "cxf"], writes=[("xf", pp)])
            P.pool(lambda e: e.tensor_tensor(out=xsd[:].rearrange("p (h v) -> p h v", h=8), in0=x3,
                                             in1=cd[:, hs8].unsqueeze(2).broadcast_to([128, 8, 64]), op=ALU.mult),
                   reads=["xs2", "cd"], writes=[("xsd", pp)])
            yield
            seg_exp(rs1)
            P.pool(lambda e: e.tensor_tensor(out=xfd[:].rearrange("p (h v) -> p h v", h=8), in0=x3, in1=bc(5),
                                             op=ALU.mult), reads=["xs2", "cxfd"], writes=[("xfd", pp)])
            yield
            P.dve(lambda e: e.scalar_tensor_tensor(out=szg[:], in0=tv, scalar=1.0, in1=bank[rz][:], op0=ALU.add, op1=ALU.mult),
                  reads=["tnh", bk(rz)], writes=[("szg", pp)])
            yield
            mk_mt(1)
            yield

        def stage2(g):
            gs = slice(g * 512, (g + 1) * 512)
            hs8 = slice(g * 8, (g + 1) * 8)
            pp = g % 2
            szg, xf, xsd, xfd, MT = szg2[pp], xf2[pp], xsd2[pp], xfd2[pp], MT2[pp]

            def bc(i, hs8=hs8):
                return sm[:, i, hs8].unsqueeze(2).broadcast_to([128, 8, 64])
            P.dma("sp", lambda e, g=g: e.dma_start(out=normw[:], in_=d["row"][:, RB_NW + g * 512:RB_NW + (g + 1) * 512].partition_broadcast(128)),
                  "nwl", writes=["normw"])
            ry = rr()
            P.pe(lambda e, ry=ry: e.matmul(bank[ry][:], lhsT=ident[:], rhs=xsd[:], start=True, stop=False),
                 reads=["ident", ("xsd", pp)], writes=[bk(ry)])
            yield
            for hh in range(8):
                P.pe(lambda e, ry=ry, hh=hh: e.matmul(bank[ry][:, hh * 64:(hh + 1) * 64], lhsT=MT[:, hh, :],
                                                      rhs=xf[:, hh * 64:(hh + 1) * 64], start=False, stop=(hh == 7)),
                     reads=[("MT", pp, hh // 4), ("xf", pp)], writes=[bk(ry)])
                yield
            ro = rr()
            P.pe(lambda e, ro=ro, g=g, gs=gs: e.matmul(bank[ro][:], lhsT=BCT[:, 4 + g, :], rhs=hTb[:, gs], start=True, stop=True),
                 reads=["BCT", ("hTb", g)], writes=[bk(ro)])
            yield
            P.dve(lambda e, ro=ro, bc=bc: e.tensor_tensor(out=yb[:].rearrange("p (h v) -> p h v", h=8),
                                                          in0=bank[ro][:].rearrange("p (h v) -> p h v", h=8), in1=bc(6), op=ALU.mult),
                  reads=[bk(ro), "eoff"], writes=["yb"])
            yield
            P.dve(lambda e, ry=ry: e.tensor_tensor(out=yb[:], in0=yb[:], in1=bank[ry][:], op=ALU.add),
                  reads=["yb", bk(ry)], writes=["yb"])
            yield
            P.dve(lambda e: e.tensor_tensor(out=yb[:], in0=yb[:], in1=szg[:], op=ALU.mult), reads=["yb", ("szg", pp)], writes=["yb"])
            yield
            P.act(lambda e, g=g: e.activation(out=junk, in_=yb[:], func=AF.Square, accum_out=ssq[:, g:g + 1]),
                  reads=["yb"], writes=["junk", "ssq"])
            yield
            P.act(lambda e, g=g: e.activation(out=ssq[:, g:g + 1], in_=ssq[:, g:g + 1], func=AF.Sqrt, bias=cst[:, 4:5],
                                              scale=1.0 / 512), reads=["ssq", "cst"], writes=["ssq"])
            yield
            P.dve(lambda e, g=g: e.reciprocal(out=ssq[:, g:g + 1], in_=ssq[:, g:g + 1]), reads=["ssq"], writes=["ssq"])
            yield
            P.dve(lambda e, g=g: e.scalar_tensor_tensor(out=yn, in0=yb[:], scalar=ssq[:, g:g + 1], in1=normw[:],
                                                        op0=ALU.mult, op1=ALU.mult), reads=["yb", "ssq", "normw"], writes=["yn"])
            yield
            r = rr()
            bt = bank[r][:].bitcast(BF16)
            for c in range(4):
                P.pe(lambda e, c=c, bt=bt: e.transpose(bt[:, c * 128:(c + 1) * 128], xb[:, c * 128:(c + 1) * 128], ident[:]),
                     reads=["yn", "ident"], writes=[bk(r)])
                yield
            P.act(lambda e, g=g, bt=bt: e.activation(out=ynT[:, 4 * g:4 * g + 4, :].rearrange("p a b -> p (a b)"), in_=bt[:, 0:512],
                                                     func=AF.Copy), reads=[bk(r)], writes=[("ynT", g)])
            yield
            r = rr()
            P.pe(lambda e, r=r, g=g: e.matmul(bank[r][:], lhsT=B2[:, g * 128:(g + 1) * 128], rhs=xfd[:], start=True, stop=True),
                 reads=["B2", ("xfd", pp)], writes=[bk(r)])
            yield
            h3 = hT[:, gs].rearrange("p (h v) -> p h v", h=8)
            P.dve(lambda e, h3=h3, bc=bc: e.tensor_tensor(out=h3, in0=h3, in1=bc(9), op=ALU.mult),
                  reads=[("hT", g), "e3"], writes=[("hT", g)])
            yield
            P.dve(lambda e, r=r, gs=gs: e.tensor_tensor(out=hT[:, gs], in0=hT[:, gs], in1=bank[r][:], op=ALU.add),
                  reads=[("hT", g), bk(r)], writes=[("hT", g)])
            yield
            P.pool(lambda e, gs=gs: e.tensor_copy(out=hTb[:, gs], in_=hT[:, gs]), reads=[("hT", g)], writes=[("hTb", g)])
            yield
        lockstep(stage1(0))
        for g in range(4):
            lockstep(stage1(g + 1) if g + 1 < 4 else None, stage2(g))
        P.pool(lambda e: e.tensor_copy(out=xrb[:, :, 0:3], in_=xrb[:, :, 128:131]), reads=["xrb"], writes=["xrb"])
        load_x(t + 1)
        ynk = [("ynT", g) for g in range(4)]
        rs = [rr(), rr()]
        for hf in range(2):
            for c in range(16):
                P.pe(lambda e, hf=hf, c=c, r=rs[hf]: e.matmul(bank[r][:], lhsT=ynT[:, c, :], rhs=w_out[:, c, hf * 512:(hf + 1) * 512],
                                                              start=(c == 0), stop=(c == 15)), reads=ynk + ["w_out"], writes=[bk(rs[hf])])
        for hf in range(2):
            hs = slice(hf * 512, (hf + 1) * 512)
            P.dve(lambda e, hs=hs, r=rs[hf], xin=xin: e.scalar_tensor_tensor(out=xin[:, hs], in0=xin[:, hs], scalar=ALPHA, in1=bank[r][:],
                                                                             op0=ALU.mult, op1=ALU.add), reads=[kx, bk(rs[hf])], writes=[kx])
            P.dve(lambda e, hf=hf, hs=hs, xin=xin: e.bn_stats(out=st[:, 6 * hf:6 * hf + 6], in_=xin[:, hs]), reads=[kx], writes=["st"])
        layer_norm_tail(P, xin, st, mv, cst, None, None, None, [kx])
        for hf in range(2):
            hs = slice(hf * 512, (hf + 1) * 512)
            P.dma("sp", lambda e, hf=hf: e.dma_start(out=lng[:], in_=d["row"][:, RB_LNG + hf * 512:RB_LNG + (hf + 1) * 512].partition_broadcast(128)),
                  "lgl", writes=["normw"])
            P.dma("sp", lambda e, hf=hf: e.dma_start(out=lnb[:], in_=d["row"][:, RB_LNB + hf * 512:RB_LNB + (hf + 1) * 512].partition_broadcast(128)),
                  "lbl", writes=["tnh"])
            P.pool(lambda e, hs=hs, xin=xin: e.tensor_tensor(out=xin[:, hs], in0=xin[:, hs], in1=lng[:], op=ALU.mult), reads=[kx, "normw"], writes=[kx])
            P.pool(lambda e, hs=hs, xin=xin: e.tensor_tensor(out=xin[:, hs], in0=xin[:, hs], in1=lnb[:], op=ALU.add), reads=[kx, "tnh"], writes=[kx])
        P.dma("sp", lambda e, ts_=ts_, xin=xin: e.dma_start(out=x_out[ts_, :], in_=xin[:]), "xo%d" % (t % 2), reads=[kx], writes=[("outd", t)])
    return A


B_SPECS = [("w_in", [1024, 5152], F32), ("w_out", [2048, 1024], F32), ("vec", [128, NVB], F32), ("row", [1, NRB], F32)]


def prep_B(inp):
    f = np.float32
    vec = np.zeros((128, NVB), f)
    cw = np.asarray(inp["ssd_conv_w"][0], f)
    for j in range(24):
        for k in range(4):
            vec[:, VB_CW + 4 * j + k] = cw[k, j * 128:(j + 1) * 128]
    cb = np.asarray(inp["ssd_conv_b"][0], f)
    vec[:, VB_CB:VB_CB + 24] = cb.reshape(24, 128).T
    row = np.zeros((1, NRB), f)
    row[0, RB_DTB:RB_DTB + 32] = np.asarray(inp["ssd_dt_bias"][0], f)
    row[0, RB_ALOG:RB_ALOG + 32] = np.asarray(inp["ssd_a_log"][0], f)
    row[0, RB_D:RB_D + 32] = np.asarray(inp["ssd_d"][0], f)
    row[0, RB_NW:RB_NW + 2048] = np.asarray(inp["ssd_norm"][0], f)
    row[0, RB_LNG:RB_LNG + 1024] = np.asarray(inp["ssd_ln_g"][0], f)
    row[0, RB_LNB:RB_LNB + 1024] = np.asarray(inp["ssd_ln_b"][0], f)
    row[0, RB_CB:RB_CB + 3072] = cb
    return {"w_in": np.asarray(inp["ssd_w_in"][0], f), "w_out": np.asarray(inp["ssd_w_out"][0], f), "vec": vec, "row": row}


def build_program_B(S):
    nc = bass.Bass("TRN2", target_bir_lowering=False)
    P = Prog(nc)
    d = declare(nc, B_SPECS, "b_")
    x_in = nc.dram_tensor("x1", [S, D], F32, kind="ExternalInput").ap()
    x_out = nc.dram_tensor("out", [S, D], F32, kind="ExternalOutput").ap()
    build_B(nc, P, S, d, x_in, x_out)
    st = P.emit()
    return nc, st


_CACHE = {}


def build_program_fused(S):
    nc = bass.Bass("TRN2", target_bir_lowering=False)
    P = Prog(nc)
    dA = declare(nc, A_SPECS, "a_")
    dA["pos"] = nc.dram_tensor("pos", [1, S], I32, kind="ExternalInput").ap()
    dB = declare(nc, B_SPECS, "b_")
    x_in = nc.dram_tensor("x", [S, D], F32, kind="ExternalInput").ap()
    x1 = nc.dram_tensor("x1_scratch", [S, D], F32).ap()
    x_out = nc.dram_tensor("out", [S, D], F32, kind="ExternalOutput").ap()
    A = build_A(nc, P, S, dA, x_in, x1)
    keep = {k: v for k, v in P.last_writer.items() if isinstance(k, tuple) and k[0] == "x1d"}
    P.barrier()
    P.last_writer.update(keep)
    A.free()
    build_B(nc, P, S, dB, x1, x_out)
    st = P.emit()
    return nc, st


def kernel(**inputs):
    S = SEQ
    x = np.ascontiguousarray(np.asarray(inputs["x"], np.float32))
    pos = np.ascontiguousarray(np.asarray(inputs["positions"], np.int32))
    hpA = prep_A(inputs)
    hpB = prep_B(inputs)
    mode = "fused"
    if mode == "split":
        if "A" not in _CACHE:
            _CACHE["A"] = build_program_A(S)[0]
            _CACHE["B"] = build_program_B(S)[0]
        mapsA = []
        for b in range(NCORES):
            m = {"a_" + k: v for k, v in hpA.items()}
            m["x"] = x[b]
            m["pos"] = pos[b:b + 1]
            mapsA.append(m)
        resA = run_bass_kernel_spmd(_CACHE["A"], mapsA, core_ids=list(range(NCORES)))
        mapsB = []
        for b in range(NCORES):
            m = {"b_" + k: v for k, v in hpB.items()}
            m["x1"] = np.ascontiguousarray(np.asarray(resA.results[b]["x1"], np.float32))
            mapsB.append(m)
        resB = run_bass_kernel_spmd(_CACHE["B"], mapsB, core_ids=list(range(NCORES)))
        return np.stack([np.asarray(resB.results[b]["out"], np.float32) for b in range(NCORES)], axis=0)
    if "F" not in _CACHE:
        _CACHE["F"] = build_program_fused(S)[0]
    maps = []
    for b in range(NCORES):
        m = {"a_" + k: v for k, v in hpA.items()}
        m.update({"b_" + k: v for k, v in hpB.items()})
        m["x"] = x[b]
        m["pos"] = pos[b:b + 1]
        maps.append(m)
    res = run_bass_kernel_spmd(_CACHE["F"], maps, core_ids=list(range(NCORES)))
    return np.stack([np.asarray(res.results[b]["out"], np.float32) for b in range(NCORES)], axis=0)
```

```python
import math
import numpy as np
import concourse.bass as bass
import concourse.mybir as mybir
from concourse.bass_utils import run_bass_kernel_spmd

F32 = mybir.dt.float32
BF16 = mybir.dt.bfloat16
I32 = mybir.dt.int32
AF = mybir.ActivationFunctionType
ALU = mybir.AluOpType

D = 1024
NCORES = 8
SEQ = 4096
ALPHA = 4.0 ** 0.25
MAGIC = 12582912.0
C1 = 6.28125
C2 = 2.0 * math.pi - 6.28125


class Op:
    __slots__ = ("eng", "fn", "deps", "is_dma", "sem", "val", "marked", "dma_wait")

    def __init__(self, eng, fn, is_dma=False, sem=None):
        self.eng = eng
        self.fn = fn
        self.deps = []
        self.is_dma = is_dma
        self.sem = sem
        self.val = None
        self.marked = False
        self.dma_wait = {}


class Prog:
    ENGS = ("pe", "act", "dve", "pool", "sp")

    def __init__(self, nc):
        self.nc = nc
        self.eobj = {"pe": nc.tensor, "act": nc.scalar, "dve": nc.vector,
                     "pool": nc.gpsimd, "sp": nc.sync}
        self.ops = []
        self.last_writer = {}
        self.readers = {}
        self.dma_count = {}
        self.dma_last = {}
        self.last_on = {}
        self.bar = {}

    def _add(self, op, reads, writes):
        deps = op.deps

        def need(p):
            if p is None or p is op:
                return
            if p.is_dma:
                op.dma_wait[p.sem] = self.dma_count[p.sem]
            else:
                deps.append(p)

        b = self.bar.pop(op.eng, None)
        if b is not None:
            for p in b[0]:
                need(p)
            for s, c in b[1].items():
                op.dma_wait[s] = c
        for k in reads:
            w = self.last_writer.get(k)
            if w is not None:
                if (not w.is_dma) and (not op.is_dma) and w.eng == op.eng == "pe":
                    continue
                need(w)
        strict = op.eng != "pe"
        for k in writes:
            w = self.last_writer.get(k)
            if w is not None:
                if w.is_dma or op.is_dma or w.eng != op.eng or strict:
                    need(w)
            for r in self.readers.get(k, ()):
                if r.is_dma or op.is_dma or r.eng != op.eng or strict:
                    need(r)
        for k in writes:
            self.last_writer[k] = op
            self.readers[k] = []
        for k in reads:
            self.readers.setdefault(k, []).append(op)
        self.ops.append(op)
        if not op.is_dma:
            self.last_on[op.eng] = op
        return op

    def op(self, eng, fn, reads=(), writes=()):
        return self._add(Op(eng, fn), reads, writes)

    def dma(self, eng, fn, sem, reads=(), writes=(), chain=True):
        o = Op(eng, fn, is_dma=True, sem=sem)
        if chain and sem in self.dma_last:
            o.dma_wait[sem] = self.dma_count[sem]
        self.dma_count.setdefault(sem, 0)
        self._add(o, reads, writes)
        self.dma_count[sem] += 1
        o.val = self.dma_count[sem]
        self.dma_last[sem] = o
        return o

    def pe(self, fn, reads=(), writes=()):
        return self.op("pe", fn, reads, writes)

    def act(self, fn, reads=(), writes=()):
        return self.op("act", fn, reads, writes)

    def dve(self, fn, reads=(), writes=()):
        return self.op("dve", fn, reads, writes)

    def pool(self, fn, reads=(), writes=()):
        return self.op("pool", fn, reads, writes)

    def barrier(self, exclude=()):
        lasts = list(self.last_on.values())
        dm = {s_: c_ for s_, c_ in self.dma_count.items() if s_ not in exclude}
        for e in self.ENGS:
            self.bar[e] = (lasts, dm)
        keep = {k_: w_ for k_, w_ in self.last_writer.items() if w_.is_dma and w_.sem in exclude}
        self.last_writer = keep
        self.readers = {}

    def emit(self):
        nc = self.nc
        for o in self.ops:
            for d in o.deps:
                d.marked = True
        cnt = {e: 0 for e in self.ENGS}
        for o in self.ops:
            if not o.is_dma and o.marked:
                cnt[o.eng] += 1
                o.val = cnt[o.eng]
        ctr = {e: nc.alloc_semaphore(name="ctr_" + e) for e in self.ENGS}
        dsem = {s: nc.alloc_semaphore(name="dma_%d" % i) for i, s in enumerate(self.dma_count)}
        waited = {e: {} for e in self.ENGS}
        nwait = 0
        for o in self.ops:
            eng = self.eobj[o.eng]
            need = {}
            for d in o.deps:
                key = ("c", d.eng)
                if need.get(key, (None, 0))[1] < d.val:
                    need[key] = (ctr[d.eng], d.val)
            for s, c in o.dma_wait.items():
                key = ("d", s)
                if need.get(key, (None, 0))[1] < 16 * c:
                    need[key] = (dsem[s], 16 * c)
            w = waited[o.eng]
            for key, (h, v) in need.items():
                if w.get(key, 0) >= v:
                    continue
                eng.wait_ge(h, v)
                nwait += 1
                w[key] = v
            ins = o.fn(eng)
            if o.is_dma:
                ins.then_inc(dsem[o.sem], 16)
            elif o.marked:
                ins.then_inc(ctr[o.eng], 1)
        eng = self.eobj["sp"]
        for s, c in self.dma_count.items():
            eng.wait_ge(dsem[s], 16 * c)
        return dict(n_ops=len(self.ops), n_wait=nwait, marked=cnt)


class Alloc:
    def __init__(self, nc):
        self.nc = nc
        self.guards = []

    def sb(self, name, shape, dt=F32):
        g = self.nc.sbuf_tensor("s_" + name, list(shape), dt)
        t = g.__enter__()
        self.guards.append(g)
        return t

    def ps(self, name, shape, dt=F32):
        g = self.nc.psum_tensor("p_" + name, list(shape), dt)
        t = g.__enter__()
        self.guards.append(g)
        return t

    def free(self):
        for g in reversed(self.guards):
            g.__exit__(None, None, None)
        self.guards = []


def make_consts(nc, P, A, pfx):
    ident = A.sb(pfx + "ident", [128, 128], BF16)
    maskT = A.sb(pfx + "maskT", [128, 128], BF16)
    ones32 = A.sb(pfx + "ones32", [128, 128], F32)
    P.pool(lambda e: e.memset(ident[:], 1.0), writes=["ident"])
    P.pool(lambda e: e.affine_select(out=ident[:], in_=ident[:], pattern=[[-1, 128]],
                                     compare_op=ALU.is_equal, fill=0.0, base=0,
                                     channel_multiplier=1), reads=["ident"], writes=["ident"])
    P.pool(lambda e: e.memset(maskT[:], 1.0), writes=["maskT"])
    P.pool(lambda e: e.affine_select(out=maskT[:], in_=maskT[:], pattern=[[1, 128]],
                                     compare_op=ALU.is_ge, fill=0.0, base=0,
                                     channel_multiplier=-1), reads=["maskT"], writes=["maskT"])
    P.pool(lambda e: e.memset(ones32[:], 1.0), writes=["ones32"])
    return ident, maskT, ones32


V_CW, V_CB, V_BA, V_BX, V_LAM, V_QN, V_KVN, V_INVF, V_SGN, NV = 0, 16, 20, 24, 28, 32, 34, 35, 36, 40
ATT_SCALE = 96.0 ** -0.5


def build_A(nc, P, S, d, x_in, x_out):
    T = S // 128
    A = Alloc(nc)
    sb, ps = A.sb, A.ps
    ident, maskT, ones32 = make_consts(nc, P, A, "a_")
    bank = [ps("a_bank%d" % i, [128, 512], F32) for i in range(8)]

    def bk(i):
        return ("bank", i)

    w_in = sb("a_w_in", [128, 8, 2048], BF16)
    w_out = sb("a_w_out", [128, 8, 1024], BF16)
    w_uq = sb("a_w_uq", [128, 2, 1280], BF16)
    w_kn = sb("a_w_kn", [128, 512], BF16)
    w_v = sb("a_w_v", [128, 512], BF16)
    gA = sb("a_gA", [128, 4, 128], BF16)
    gX = sb("a_gX", [128, 4, 128], BF16)
    vec = sb("a_vec", [128, NV], F32)
    lng = sb("a_lng", [128, 512], F32)
    lnb = sb("a_lnb", [128, 512], F32)
    cst = sb("a_cst", [128, 8], F32)
    for i, v in enumerate([1e-6, 1e-5, math.pi / 2, 0.0]):
        P.pool(lambda e, i=i, v=v: e.memset(cst[:, i:i + 1], v), writes=["cst"])
    for c in range(0, 8, 2):
        P.dma("pool", lambda e, c=c: e.dma_start(out=w_in[:, c:c + 2, :],
                                                 in_=d["w_in"][c * 128:(c + 2) * 128, :].rearrange("(c p) n -> p c n", p=128)),
              "wlina", writes=["w_in"], chain=False)
    for c in range(0, 8, 4):
        P.dma("pool", lambda e, c=c: e.dma_start(out=w_out[:, c:c + 4, :],
                                                 in_=d["w_out"][c * 128:(c + 4) * 128, :].rearrange("(c p) n -> p c n", p=128)),
              "wl", writes=["w_out"], chain=False)
    P.dma("pool", lambda e: e.dma_start(out=w_uq[:], in_=d["w_uq"].rearrange("(c p) n -> p c n", p=128)),
          "wl", writes=["w_uq"], chain=False)
    P.dma("pool", lambda e: e.dma_start(out=w_kn[:], in_=d["w_kn"]), "wl", writes=["w_kn"], chain=False)
    P.dma("pool", lambda e: e.dma_start(out=w_v[:], in_=d["w_v"]), "wl", writes=["w_v"], chain=False)
    P.dma("pool", lambda e: e.dma_start(out=gA[:], in_=d["gA"]), "wl", writes=["gA"], chain=False)
    P.dma("pool", lambda e: e.dma_start(out=gX[:], in_=d["gX"]), "wl", writes=["gX"], chain=False)
    P.dma("sp", lambda e: e.dma_start(out=vec[:], in_=d["vec"]), "wl2", writes=["vec"])

    der = sb("a_der", [128, 16], F32)
    tmp4 = sb("a_tmp4", [128, 4], F32)
    P.act(lambda e: e.activation(out=tmp4[:], in_=vec[:, V_LAM:V_LAM + 4], func=AF.Exp, scale=-1.0),
          reads=["vec"], writes=["tmp4"])
    P.act(lambda e: e.activation(out=tmp4[:], in_=tmp4[:], func=AF.Ln, bias=1.0), reads=["tmp4"], writes=["tmp4"])
    P.dve(lambda e: e.tensor_scalar(out=der[:, 0:4], in0=tmp4[:], scalar1=4.0, scalar2=None, op0=ALU.mult),
          reads=["tmp4"], writes=["der"])
    P.dve(lambda e: e.tensor_scalar(out=der[:, 4:8], in0=tmp4[:], scalar1=-4.0, scalar2=None, op0=ALU.mult),
          reads=["tmp4"], writes=["der"])
    P.dve(lambda e: e.tensor_scalar(out=der[:, 8:16], in0=vec[:, V_BA:V_BA + 8], scalar1=0.5, scalar2=None,
                                    op0=ALU.mult), reads=["vec"], writes=["der"])

    trig_d = nc.dram_tensor("a_trig", [128, 2, S], F32).ap()
    CB = min(4096, S)
    A2 = Alloc(nc)
    pi_t = A2.sb("a_pi", [128, CB], I32)
    tA = A2.sb("a_tA", [128, CB], F32)
    tB = A2.sb("a_tB", [128, CB], F32)
    tC = A2.sb("a_tC", [128, 2, CB], F32)
    for blk in range(S // CB):
        cs = slice(blk * CB, (blk + 1) * CB)
        P.dma("sp", lambda e, cs=cs: e.dma_start(out=pi_t[:], in_=d["pos"][:, cs].partition_broadcast(128)),
              "trg", writes=["pi"])
        P.dve(lambda e: e.tensor_copy(out=tA[:], in_=pi_t[:]), reads=["pi"], writes=["tA"])
        P.dve(lambda e: e.tensor_scalar(out=tA[:], in0=tA[:], scalar1=vec[:, V_INVF:V_INVF + 1], scalar2=None,
                                        op0=ALU.mult), reads=["tA", "vec"], writes=["tA"])
        P.dve(lambda e: e.tensor_scalar(out=tB[:], in0=tA[:], scalar1=1.0 / (2 * math.pi), scalar2=MAGIC,
                                        op0=ALU.mult, op1=ALU.add), reads=["tA"], writes=["tB"])
        P.dve(lambda e: e.tensor_scalar(out=tB[:], in0=tB[:], scalar1=-MAGIC, scalar2=None, op0=ALU.add),
              reads=["tB"], writes=["tB"])
        P.dve(lambda e: e.scalar_tensor_tensor(out=tA[:], in0=tB[:], scalar=-C1, in1=tA[:], op0=ALU.mult,
                                               op1=ALU.add), reads=["tA", "tB"], writes=["tA"])
        P.dve(lambda e: e.scalar_tensor_tensor(out=tA[:], in0=tB[:], scalar=-C2, in1=tA[:], op0=ALU.mult,
                                               op1=ALU.add), reads=["tA", "tB"], writes=["tA"])
        P.dve(lambda e: e.tensor_scalar(out=tA[:], in0=tA[:], scalar1=3.1415925, scalar2=-3.1415925,
                                        op0=ALU.min, op1=ALU.max), reads=["tA"], writes=["tA"])
        P.act(lambda e: e.activation(out=tB[:], in_=tA[:], func=AF.Abs), reads=["tA"], writes=["tB"])
        P.act(lambda e: e.activation(out=tC[:, 0, :], in_=tB[:], func=AF.Sin, bias=cst[:, 2:3], scale=-1.0),
              reads=["tB", "cst"], writes=["tC"])
        P.act(lambda e: e.activation(out=tC[:, 1, :], in_=tA[:], func=AF.Sin, scale=vec[:, V_SGN:V_SGN + 1]),
              reads=["tA", "vec"], writes=["tC"])
        P.dma("sp", lambda e, cs=cs: e.dma_start(out=trig_d[:, :, cs], in_=tC[:]), "trg",
              reads=["tC"], writes=["trig_d"])

    keepd = {k: v for k, v in P.last_writer.items() if k == "trig_d"}
    P.barrier(exclude=("wl", "wlina"))
    P.last_writer.update(keepd)
    A2.free()
    KnT = sb("a_KnT", [128, 4, S], BF16)
    KrT = sb("a_KrT", [128, S], BF16)
    Vaug = sb("a_Vaug", [128, T, 8, 96], BF16)
    P.pool(lambda e: e.memset(Vaug[:], 1.0), writes=["Vall"])
    xin2 = [sb("a_xin%d" % i, [128, 1024], F32) for i in range(2)]
    xb = sb("a_xb", [128, 1024], BF16)
    xT = sb("a_xT", [128, 8, 128], BF16)
    xr = sb("a_xr", [128, 4, 131], F32)
    s12 = [sb("a_s1%d" % i, [128, 8, 128], F32) for i in range(2)]
    sq = sb("a_sq", [128, 3, 128], F32)
    sr = sb("a_sr", [128, 2, 128], F32)
    cqn = sb("a_cqn", [128, 3, 128], BF16)
    trg = sb("a_trg", [128, 2, 128], F32)
    kr1 = sb("a_kr1", [128, 128], F32)
    kr2 = sb("a_kr2", [128, 128], F32)
    acc = sb("a_acc", [128, 4, 128], F32)
    xcb = sb("a_xcb", [128, 4, 128], BF16)
    ta = sb("a_ta", [128, 4, 128], F32)
    ti = sb("a_ti", [128, 4, 128], F32)
    aa = sb("a_aa", [128, 4, 128], F32)
    hh2 = [sb("a_hh%d" % i, [128, 4, 128], F32) for i in range(2)]
    hc = sb("a_hc", [128, 4], F32)
    qsb = sb("a_qsb", [128, 6, 128], F32)
    th = qsb[:, 0:4, :]
    QnT2 = [sb("a_QnT%d" % i, [128, 4, 128], BF16) for i in range(2)]
    QrT2 = [sb("a_QrT%d" % i, [128, 3, 128], BF16) for i in range(2)]
    PT = [sb("a_PT%d" % i, [128, 4, 128], BF16) for i in range(2)]
    rl = sb("a_rl", [128, 128], F32)
    ym = sb("a_ym", [128, 4, 128], F32)
    yT = sb("a_yT", [128, 8, 128], BF16)
    st = sb("a_st", [128, 12], F32)
    mv = sb("a_mv", [128, 4], F32)
    P.pool(lambda e: e.memset(xr[:], 0.0), writes=["xr"])
    P.pool(lambda e: e.memset(hc[:], 0.0), writes=["hc"])

    def b3(i):
        return bank[i][:].rearrange("p (a b) -> p a b", a=4)

    def front(t):
        ts_ = slice(t * 128, (t + 1) * 128)
        pp = t % 2
        xin, s1, hh, QnT, QrT = xin2[pp], s12[pp], hh2[pp], QnT2[pp], QrT2[pp]
        kx, ks1, khh, kqn, kqr = ("xin", pp), ("s1", pp), ("hh", pp), ("QnT", pp), ("QrT", pp)
        P.dma("sp", lambda e: e.dma_start(out=xin[:], in_=x_in[ts_, :]), "xin%d" % pp, writes=[kx])
        P.dma("sp", lambda e: e.dma_start(out=trg[:], in_=trig_d[:, :, ts_]), "trl", reads=["trig_d"], writes=["trg"])
        P.act(lambda e: e.activation(out=xb[:], in_=xin[:], func=AF.Copy), reads=[kx], writes=["xb"])
        yield
        b0 = bank[0][:].bitcast(BF16)
        for c in range(8):
            P.pe(lambda e, c=c: e.transpose(b0[:, c * 128:(c + 1) * 128], xb[:, c * 128:(c + 1) * 128], ident[:]),
                 reads=["xb", "ident"], writes=[bk(0)])
        P.dve(lambda e: e.tensor_copy(out=xT[:].rearrange("p a b -> p (a b)"), in_=b0), reads=[bk(0)], writes=["xT"])
        yield
        yield
        ibank = [1, 2, 3, 0]
        for j in range(16):
            bi = ibank[j // 4]
            for c in range(8):
                P.pe(lambda e, j=j, c=c, bi=bi: e.matmul(b3(bi)[:, j % 4, :], lhsT=w_in[:, c, j * 128:(j + 1) * 128],
                                                        rhs=xT[:, c, :], start=(c == 0), stop=(c == 7)),
                     reads=["w_in", "xT"], writes=[bk(bi)])
            if j % 4 == 3:
                yield
        P.act(lambda e: e.activation(out=xr[:, :, 3:131], in_=b3(1), func=AF.Copy), reads=[bk(1)], writes=["xr"])
        yield
        for i in range(2):
            P.act(lambda e, i=i: e.activation(out=s1[:, 4 * i:4 * i + 4, :], in_=b3(2 + i), func=AF.Tanh, scale=0.5),
                  reads=[bk(2 + i)], writes=[ks1])
            P.dve(lambda e, i=i: e.scalar_tensor_tensor(out=s1[:, 4 * i:4 * i + 4, :], in0=s1[:, 4 * i:4 * i + 4, :],
                                                         scalar=1.0, in1=b3(2 + i), op0=ALU.add, op1=ALU.mult),
                  reads=[ks1, bk(2 + i)], writes=[ks1])
        P.act(lambda e: e.activation(out=sq[:], in_=b3(0)[:, 0:3, :], func=AF.Square), reads=[bk(0)], writes=["sq"])
        yield
        b5 = b3(1)
        P.pe(lambda e: e.matmul(b5[:, 0, :], lhsT=ones32[:], rhs=sq[:, 0, :], start=True, stop=False),
             reads=["sq", "ones32"], writes=[bk(1)])
        yield
        P.pe(lambda e: e.matmul(b5[:, 0, :], lhsT=ones32[:], rhs=sq[:, 1, :], start=False, stop=True),
             reads=["sq", "ones32"], writes=[bk(1)])
        yield
        P.pe(lambda e: e.matmul(b5[:, 1, :], lhsT=ones32[:], rhs=sq[:, 2, :], start=True, stop=True),
             reads=["sq", "ones32"], writes=[bk(1)])
        yield
        P.act(lambda e: e.activation(out=sr[:, 0, :], in_=b5[:, 0, :], func=AF.Sqrt, bias=cst[:, 0:1], scale=1.0 / 256),
              reads=[bk(1), "cst"], writes=["sr"])
        yield
        P.act(lambda e: e.activation(out=sr[:, 1, :], in_=b5[:, 1, :], func=AF.Sqrt, bias=cst[:, 0:1], scale=1.0 / 128),
              reads=[bk(1), "cst"], writes=["sr"])
        yield
        P.dve(lambda e: e.reciprocal(out=sr[:], in_=sr[:]), reads=["sr"], writes=["sr"])
        yield
        for c in range(3):
            P.dve(lambda e, c=c: e.scalar_tensor_tensor(out=cqn[:, c, :], in0=b3(0)[:, c, :],
                                                         scalar=vec[:, V_QN + c:V_QN + c + 1],
                                                         in1=sr[:, min(c, 2) // 2, :], op0=ALU.mult, op1=ALU.mult),
                  reads=[bk(0), "vec", "sr"], writes=["cqn"])
        P.dve(lambda e: e.tensor_tensor(out=kr1[0:32, :], in0=b3(0)[0:32, 3, :], in1=trg[0:32, 0, :], op=ALU.mult),
              reads=[bk(0), "trg"], writes=["kr1"])
        yield
        P.dve(lambda e: e.tensor_tensor(out=kr2[0:32, :], in0=b3(0)[32:64, 3, :], in1=trg[32:64, 1, :], op=ALU.mult),
              reads=[bk(0), "trg"], writes=["kr2"])
        yield
        for i in range(3):
            P.pool(lambda e, i=i: e.tensor_tensor(out=KrT[32 * i:32 * i + 32, ts_], in0=kr1[0:32, :],
                                                  in1=kr2[0:32, :], op=ALU.add),
                   reads=["kr1", "kr2"], writes=[("Kr", t)])
        yield
        for c in range(4):
            P.dve(lambda e, c=c: e.tensor_scalar(out=acc[:, c, :], in0=xr[:, c, 3:131],
                                                 scalar1=vec[:, V_CW + 4 * c + 3:V_CW + 4 * c + 4],
                                                 scalar2=vec[:, V_CB + c:V_CB + c + 1], op0=ALU.mult, op1=ALU.add),
                  reads=["xr", "vec"], writes=[("acc", c)])
            for k in range(3):
                P.dve(lambda e, c=c, k=k: e.scalar_tensor_tensor(out=acc[:, c, :], in0=xr[:, c, k:k + 128],
                                                                  scalar=vec[:, V_CW + 4 * c + k:V_CW + 4 * c + k + 1],
                                                                  in1=acc[:, c, :], op0=ALU.mult, op1=ALU.add),
                      reads=["xr", "vec", ("acc", c)], writes=[("acc", c)])
            if c % 2 == 1:
                yield
        P.pool(lambda e: e.tensor_copy(out=xr[:, :, 0:3], in_=xr[:, :, 128:131]), reads=["xr"], writes=["xr"])
        yield
        accs = [("acc", c) for c in range(4)]
        P.pool(lambda e: e.tensor_copy(out=xcb[:], in_=acc[:]), reads=accs, writes=["xcb"])
        yield
        for c in range(4):
            P.pe(lambda e, c=c: e.matmul(b3(2)[:, c, :], lhsT=gA[:, c, :], rhs=xcb[:, c, :], start=True, stop=True),
                 reads=["gA", "xcb"], writes=[bk(2)])
        for c in range(4):
            P.pe(lambda e, c=c: e.matmul(b3(3)[:, c, :], lhsT=gX[:, c, :], rhs=xcb[:, c, :], start=True, stop=True),
                 reads=["gX", "xcb"], writes=[bk(3)])
        for c in range(4):
            P.act(lambda e, c=c: e.activation(out=ta[:, c, :], in_=b3(2)[:, c, :], func=AF.Tanh,
                                              bias=der[:, 8 + c:9 + c], scale=0.5), reads=[bk(2), "der"], writes=["ta"])
            P.act(lambda e, c=c: e.activation(out=ti[:, c, :], in_=b3(3)[:, c, :], func=AF.Tanh,
                                              bias=der[:, 12 + c:13 + c], scale=0.5), reads=[bk(3), "der"], writes=["ti"])
        yield
        for c in range(4):
            P.act(lambda e, c=c: e.activation(out=aa[:, c, :], in_=ta[:, c, :], func=AF.Exp,
                                              bias=der[:, 4 + c:5 + c], scale=der[:, 4 + c:5 + c]),
                  reads=["ta", "der"], writes=["aa"])
            P.act(lambda e, c=c: e.activation(out=th[:, c, :], in_=ta[:, c, :], func=AF.Tanh,
                                              bias=der[:, c:c + 1], scale=der[:, c:c + 1]),
                  reads=["ta", "der"], writes=["qsb"])
        P.dve(lambda e: e.tensor_tensor(out=ta[:], in0=aa[:], in1=aa[:], op=ALU.mult), reads=["aa"], writes=["ta"])
        yield
        P.dve(lambda e: e.scalar_tensor_tensor(out=ta[:], in0=ta[:], scalar=1.0, in1=th, op0=ALU.add, op1=ALU.mult),
              reads=["ta", "qsb"], writes=["ta"])
        yield
        P.act(lambda e: e.activation(out=ta[:], in_=ta[:], func=AF.Sqrt), reads=["ta"], writes=["ta"])
        yield
        P.dve(lambda e: e.scalar_tensor_tensor(out=ti[:], in0=ti[:], scalar=1.0, in1=acc[:], op0=ALU.add, op1=ALU.mult),
              reads=["ti"] + accs, writes=["ti"])
        yield
        P.dve(lambda e: e.scalar_tensor_tensor(out=ti[:], in0=ti[:], scalar=0.25, in1=ta[:], op0=ALU.mult, op1=ALU.mult),
              reads=["ti", "ta"], writes=["ti"])
        yield
        yield
        for c in range(4):
            P.dve(lambda e, c=c: e.tensor_tensor_scan(out=hh[:, c, :], data0=aa[:, c, :], data1=ti[:, c, :],
                                                       initial=hc[:, c:c + 1], op0=ALU.mult, op1=ALU.add),
                  reads=["aa", "ti", "hc"], writes=[khh])
        P.pool(lambda e: e.tensor_copy(out=hc[:], in_=hh[:, :, 127]), reads=[khh], writes=["hc"])
        yield
        yield
        for j in range(10):
            bi, jj = (j // 4, j % 4)
            for c in range(2):
                P.pe(lambda e, j=j, c=c, bi=bi, jj=jj: e.matmul(b3(bi)[:, jj, :], lhsT=w_uq[:, c, j * 128:(j + 1) * 128],
                                                                rhs=cqn[:, c, :], start=(c == 0), stop=(c == 1)),
                     reads=["w_uq", "cqn"], writes=[bk(bi)])
        for j in range(4):
            P.pe(lambda e, j=j: e.matmul(b3(3)[:, j, :], lhsT=w_kn[:, j * 128:(j + 1) * 128], rhs=cqn[:, 2, :],
                                         start=True, stop=True), reads=["w_kn", "cqn"], writes=[bk(3)])
        P.act(lambda e: e.activation(out=QnT[:], in_=b3(0), func=AF.Copy), reads=[bk(0)], writes=[kqn])
        yield
        P.act(lambda e: e.activation(out=qsb[:, 0:4, :], in_=b3(1), func=AF.Copy), reads=[bk(1)], writes=["qsb"])
        yield
        P.act(lambda e: e.activation(out=qsb[:, 4:6, :], in_=b3(2)[:, 0:2, :], func=AF.Copy), reads=[bk(2)], writes=["qsb"])
        yield
        P.pe(lambda e: e.matmul(bank[0][:], lhsT=cqn[:, 2, :], rhs=w_v[:], start=True, stop=True),
             reads=["w_v", "cqn"], writes=[bk(0)])
        yield
        yield
        cosb = trg[:, 0, :].unsqueeze(1).broadcast_to([128, 3, 128])
        sinb = trg[:, 1, :].unsqueeze(1).broadcast_to([128, 3, 128])
        P.pool(lambda e: e.tensor_tensor(out=qsb[:, 0:3, :], in0=qsb[:, 0:3, :], in1=cosb, op=ALU.mult),
               reads=["qsb", "trg"], writes=["qsb"])
        yield
        P.pool(lambda e: e.tensor_tensor(out=qsb[:, 3:6, :], in0=qsb[:, 3:6, :], in1=sinb, op=ALU.mult),
               reads=["qsb", "trg"], writes=["qsb"])
        yield
        P.pool(lambda e: e.tensor_tensor(out=QrT[:], in0=qsb[:, 0:3, :], in1=qsb[:, 3:6, :], op=ALU.add),
               reads=["qsb"], writes=[kqr])
        yield
        P.act(lambda e: e.activation(out=KnT[:, :, ts_], in_=b3(3), func=AF.Copy), reads=[bk(3)],
              writes=[("Kn", t)])
        yield
        P.dve(lambda e: e.tensor_copy(out=Vaug[:, t, :, 0:64], in_=bank[0][:].rearrange("p (h v) -> p h v", h=8)),
              reads=[bk(0), "Vall"], writes=[("V", t)])
        yield
        yield

    def attention(t, gen):
        pp = t % 2
        QnT, QrT = QnT2[pp], QrT2[pp]
        kqn, kqr = ("QnT", pp), ("QrT", pp)
        items = []
        for h in range(8):
            nb = (t + 4) // 4
            for b in range(nb):
                items.append((h, b, list(range(4 * b, min(4 * b + 4, t + 1)))))

        def qk(i):
            h, b, kts = items[i]
            sbk = 4 + (i % 2)
            j2, hr = h // 2, (h % 2) * 64
            c3, r3 = h // 3, (h % 3) * 32
            for jj, kt in enumerate(kts):
                ks = slice(kt * 128, (kt + 1) * 128)
                P.pe(lambda e, jj=jj, ks=ks: e.matmul(b3(sbk)[:, jj, :], lhsT=KnT[hr:hr + 64, j2, ks],
                                                       rhs=QnT[hr:hr + 64, j2, :], start=True, stop=False),
                     reads=[("Kn", kt), kqn], writes=[bk(sbk)])
                P.pe(lambda e, jj=jj, ks=ks: e.matmul(b3(sbk)[:, jj, :], lhsT=KrT[r3:r3 + 32, ks],
                                                       rhs=QrT[r3:r3 + 32, c3, :], start=False, stop=True),
                     reads=[("Kr", kt), kqr], writes=[bk(sbk)])

        def rest(i):
            h, b, kts = items[i]
            sbk = 4 + (i % 2)
            obk = 6 + (h % 2)
            n = len(kts)
            pt = PT[i % 2]
            ptk = ("PT", i % 2)
            P.act(lambda e: e.activation(out=pt[:, 0:n, :], in_=b3(sbk)[:, 0:n, :], func=AF.Exp, scale=ATT_SCALE),
                  reads=[bk(sbk)], writes=[ptk])
            if t in kts:
                jd = t - 4 * b
                P.pool(lambda e: e.tensor_tensor(out=pt[:, jd, :], in0=pt[:, jd, :], in1=maskT[:], op=ALU.mult),
                       reads=[ptk, "maskT"], writes=[ptk])
            for jj, kt in enumerate(kts):
                P.pe(lambda e, jj=jj, kt=kt: e.matmul(bank[obk][0:96, 0:128], lhsT=Vaug[:, kt, h, :], rhs=pt[:, jj, :],
                                                      start=(kt == 0), stop=(kt == t)),
                     reads=[("V", kt), "Vall", ptk], writes=[bk(obk)])
            if kts[-1] == t:
                j2, hr = h // 2, (h % 2) * 64
                for hf in range(2):
                    P.dve(lambda e, hf=hf: e.reciprocal(out=rl[32 * hf:32 * hf + 32, :], in_=bank[obk][64:96, 0:128]),
                          reads=[bk(obk)], writes=["rl"])
                P.dve(lambda e: e.scalar_tensor_tensor(
                    out=ym[hr:hr + 64, j2, :], in0=bank[obk][0:64, 0:128],
                    scalar=0.5, in1=rl[0:64, :], op0=ALU.mult, op1=ALU.mult),
                    reads=[bk(obk), "rl"], writes=["ym"])

        state = {"k": 0}

        def adv():
            while state["k"] < len(gen):
                g_ = gen[state["k"]]
                if g_ is None:
                    state["k"] += 1
                    continue
                try:
                    next(g_)
                    return
                except StopIteration:
                    state["k"] += 1

        for i in range(len(items)):
            qk(i)
            if i > 0:
                rest(i - 1)
            adv()
        rest(len(items) - 1)

    def tail_a(t):
        pp = t % 2
        s1, hh = s12[pp], hh2[pp]
        ks1, khh = ("s1", pp), ("hh", pp)
        P.dve(lambda e: e.tensor_tensor(out=yT[:, 0:4, :], in0=s1[:, 0:4, :], in1=hh[:], op=ALU.mult),
              reads=[ks1, khh], writes=["yT"])
        P.dve(lambda e: e.tensor_tensor(out=yT[:, 4:8, :], in0=s1[:, 4:8, :], in1=ym[:], op=ALU.mult),
              reads=[ks1, "ym"], writes=["yT"])

    def tail(t):
        ts_ = slice(t * 128, (t + 1) * 128)
        pp = t % 2
        xin = xin2[pp]
        kx = ("xin", pp)
        for hf in range(2):
            for c in range(8):
                P.pe(lambda e, hf=hf, c=c: e.matmul(bank[hf][:], lhsT=yT[:, c, :], rhs=w_out[:, c, hf * 512:(hf + 1) * 512],
                                                    start=(c == 0), stop=(c == 7)),
                     reads=["yT", "w_out"], writes=[bk(hf)])
        for hf in range(2):
            hs = slice(hf * 512, (hf + 1) * 512)
            P.dve(lambda e, hf=hf, hs=hs: e.scalar_tensor_tensor(out=xin[:, hs], in0=xin[:, hs], scalar=ALPHA, in1=bank[hf][:],
                                                                  op0=ALU.mult, op1=ALU.add),
                  reads=[kx, bk(hf)], writes=[kx])
            yield
            P.dve(lambda e, hf=hf, hs=hs: e.bn_stats(out=st[:, 6 * hf:6 * hf + 6], in_=xin[:, hs]), reads=[kx],
                  writes=["st"])
            yield
        layer_norm_tail(P, xin, st, mv, cst, None, None, None, [kx])
        yield
        for hf in range(2):
            hs = slice(hf * 512, (hf + 1) * 512)
            P.dma("sp", lambda e, hf=hf: e.dma_start(out=lng[:], in_=d["lng"][:, hf * 512:(hf + 1) * 512].partition_broadcast(128)),
                  "lgl", writes=["lng"])
            P.dma("sp", lambda e, hf=hf: e.dma_start(out=lnb[:], in_=d["lnb"][:, hf * 512:(hf + 1) * 512].partition_broadcast(128)),
                  "lbl", writes=["lnb"])
            P.pool(lambda e, hs=hs: e.tensor_tensor(out=xin[:, hs], in0=xin[:, hs], in1=lng[:], op=ALU.mult),
                   reads=[kx, "lng"], writes=[kx])
            yield
            P.pool(lambda e, hs=hs: e.tensor_tensor(out=xin[:, hs], in0=xin[:, hs], in1=lnb[:], op=ALU.add),
                   reads=[kx, "lnb"], writes=[kx])
            yield
        P.dma("sp", lambda e: e.dma_start(out=x_out[ts_, :], in_=xin[:]), "xo%d" % pp, reads=[kx], writes=[("x1d", t)])

    def exhaust(g_):
        if g_ is not None:
            for _ in g_:
                pass

    exhaust(front(0))
    ptail = None
    for t in range(T):
        gfront = front(t + 1) if t + 1 < T else None
        attention(t, [ptail, gfront])
        exhaust(ptail)
        exhaust(gfront)
        tail_a(t)
        ptail = tail(t)
    exhaust(ptail)
    return A


def layer_norm_tail(P, z, st, mv, cst, lng, lnb, _unused, zk):
    P.dve(lambda e: e.bn_aggr(out=mv[:, 0:2], in_=st[:]), reads=["st"], writes=["mv"])
    P.act(lambda e: e.activation(out=mv[:, 2:3], in_=mv[:, 1:2], func=AF.Sqrt, bias=cst[:, 1:2], scale=1.0),
          reads=["mv", "cst"], writes=["mv2"])
    P.dve(lambda e: e.reciprocal(out=mv[:, 2:3], in_=mv[:, 2:3]), reads=["mv2"], writes=["mv2"])
    P.dve(lambda e: e.scalar_tensor_tensor(out=mv[:, 3:4], in0=mv[:, 0:1], scalar=-1.0, in1=mv[:, 2:3],
                                           op0=ALU.mult, op1=ALU.mult), reads=["mv", "mv2"], writes=["mv3"])
    P.act(lambda e: e.activation(out=z[:], in_=z[:], func=AF.Identity, bias=mv[:, 3:4], scale=mv[:, 2:3]),
          reads=zk + ["mv2", "mv3"], writes=zk)
    if lng is not None:
        P.pool(lambda e: e.tensor_tensor(out=z[:], in0=z[:], in1=lng[:], op=ALU.mult), reads=zk + ["lng"], writes=zk)
        P.pool(lambda e: e.tensor_tensor(out=z[:], in0=z[:], in1=lnb[:], op=ALU.add), reads=zk + ["lnb"], writes=zk)


A_SPECS = [("w_in", [1024, 2048], F32), ("w_out", [1024, 1024], F32), ("w_uq", [256, 1280], F32),
           ("w_kn", [128, 512], F32), ("w_v", [128, 512], F32), ("gA", [128, 4, 128], F32),
           ("gX", [128, 4, 128], F32), ("vec", [128, NV], F32), ("lng", [1, 1024], F32), ("lnb", [1, 1024], F32)]


def prep_A(inp):
    f = np.float32
    w_in = np.asarray(inp["ab_w_in"][0], f)
    kr = w_in[:, 1920:1952]
    kr_sw = np.concatenate([kr[:, 16:32], kr[:, 0:16]], axis=1)
    w_inA = np.concatenate([w_in[:, :1920], kr, kr_sw, np.zeros((1024, 64), f)], axis=1)
    uq = np.asarray(inp["mla_w_uq"][0], f).reshape(256, 8, 96)
    nope = uq[:, :, :64].reshape(256, 512)
    rope = uq[:, :, 64:96]
    rsw = np.concatenate([rope[:, :, 16:32], rope[:, :, 0:16]], axis=2)
    z32 = np.zeros((256, 1, 32), f)
    rope9 = np.concatenate([rope, z32], axis=1).reshape(256, 288)
    rsw9 = np.concatenate([rsw, z32], axis=1).reshape(256, 288)
    pad = np.zeros((256, 96), f)
    w_uqA = np.concatenate([nope, rope9, pad, rsw9, pad], axis=1)
    w_uqA = np.concatenate([nope,
                            np.concatenate([rope9[:, 0:96], np.zeros((256, 32), f)], 1),
                            np.concatenate([rope9[:, 96:192], np.zeros((256, 32), f)], 1),
                            np.concatenate([rope9[:, 192:288], np.zeros((256, 32), f)], 1),
                            np.concatenate([rsw9[:, 0:96], np.zeros((256, 32), f)], 1),
                            np.concatenate([rsw9[:, 96:192], np.zeros((256, 32), f)], 1),
                            np.concatenate([rsw9[:, 192:288], np.zeros((256, 32), f)], 1)], axis=1)
    ukv = np.asarray(inp["mla_w_ukv"][0], f).reshape(128, 8, 128)
    w_kn = np.ascontiguousarray(ukv[:, :, :64]).reshape(128, 512)
    w_v = np.ascontiguousarray(ukv[:, :, 64:]).reshape(128, 512)

    def blockdiag(w):
        w = np.asarray(w[0], f)
        o = np.zeros((128, 4, 128), f)
        for h in range(8):
            r = (h % 2) * 64
            o[r:r + 64, h // 2, r:r + 64] = w[h]
        return o

    vec = np.zeros((128, NV), f)
    cw = np.asarray(inp["ab_conv_w"][0], f)
    for c in range(4):
        for k in range(4):
            vec[:, V_CW + 4 * c + k] = cw[k, c * 128:(c + 1) * 128]
    for nm, col in (("ab_conv_b", V_CB), ("ab_gate_a_b", V_BA), ("ab_gate_x_b", V_BX), ("ab_lambda", V_LAM)):
        vec[:, col:col + 4] = np.asarray(inp[nm][0], f).reshape(4, 128).T
    vec[:, V_QN:V_QN + 2] = np.asarray(inp["mla_q_norm"][0], f).reshape(2, 128).T
    vec[:, V_KVN] = np.asarray(inp["mla_kv_norm"][0], f)
    j = np.arange(128) % 16
    vec[:, V_INVF] = (10000.0 ** (-(2.0 * j) / 32.0)).astype(f)
    vec[:, V_SGN] = np.where((np.arange(128) % 32) < 16, -1.0, 1.0)
    return {"w_in": w_inA, "w_out": np.asarray(inp["ab_w_out"][0], f), "w_uq": w_uqA, "w_kn": w_kn, "w_v": w_v,
            "gA": blockdiag(inp["ab_gate_a_w"]), "gX": blockdiag(inp["ab_gate_x_w"]), "vec": vec,
            "lng": np.asarray(inp["ab_ln_g"], f).reshape(1, 1024), "lnb": np.asarray(inp["ab_ln_b"], f).reshape(1, 1024)}


def declare(nc, specs, pfx):
    return {nm: nc.dram_tensor(pfx + nm, shape, dt, kind="ExternalInput").ap() for nm, shape, dt in specs}


def build_program_A(S):
    nc = bass.Bass("TRN2", target_bir_lowering=False)
    P = Prog(nc)
    d = declare(nc, A_SPECS, "a_")
    d["pos"] = nc.dram_tensor("pos", [1, S], I32, kind="ExternalInput").ap()
    x_in = nc.dram_tensor("x", [S, D], F32, kind="ExternalInput").ap()
    x_out = nc.dram_tensor("x1", [S, D], F32, kind="ExternalOutput").ap()
    build_A(nc, P, S, d, x_in, x_out)
    st = P.emit()
    return nc, st


VB_CW, VB_CB, NVB = 0, 96, 120
RB_DTB, RB_ALOG, RB_D, RB_NW, RB_LNG, RB_LNB, RB_CB, NRB = 0, 32, 64, 96, 2144, 3168, 4192, 7264


def build_B(nc, P, S, d, x_in, x_out):
    T = S // 128
    A = Alloc(nc)
    sb, ps = A.sb, A.ps
    ident, maskT, ones32 = make_consts(nc, P, A, "b_")
    bank = [ps("b_bank%d" % i, [128, 512], F32) for i in range(8)]
    rr_state = [0]

    def rr():
        rr_state[0] = (rr_state[0] + 1) % 8
        return rr_state[0]

    def bk(i):
        return ("bank", i)

    def b3(i):
        return bank[i][:].rearrange("p (a b) -> p a b", a=4)

    tri = sb("b_tri", [128, 128], F32)
    u2 = sb("b_u2", [128, 128], F32)
    m025 = sb("b_m025", [128, 128], F32)
    onesb = sb("b_onesb", [128, 128], BF16)
    cst = sb("b_cst", [128, 8], F32)
    P.pool(lambda e: e.memset(tri[:], 1.0), writes=["tri"])
    P.pool(lambda e: e.affine_select(out=tri[:], in_=tri[:], pattern=[[1, 128]], compare_op=ALU.is_ge, fill=0.0,
                                     base=0, channel_multiplier=-1), reads=["tri"], writes=["tri"])
    P.pool(lambda e: e.memset(u2[:], 1.0), writes=["u2"])
    P.pool(lambda e: e.affine_select(out=u2[:], in_=u2[:], pattern=[[-1, 128]], compare_op=ALU.is_gt, fill=0.0,
                                     base=0, channel_multiplier=1), reads=["u2"], writes=["u2"])
    P.pool(lambda e: e.tensor_scalar(out=m025[:], in0=tri[:], scalar1=0.25, scalar2=None, op0=ALU.mult),
           reads=["tri"], writes=["m025"])
    P.pool(lambda e: e.memset(onesb[:], 1.0), writes=["onesb"])
    for i, v in enumerate([1e-6, 1e-5, 0.0, 1.0, 4e-6]):
        P.pool(lambda e, i=i, v=v: e.memset(cst[:, i:i + 1], v), writes=["cst"])
    w_in = sb("b_w_in", [128, 8, 5152], BF16)
    w_out = sb("b_w_out", [128, 16, 1024], BF16)
    vecb = sb("b_vec", [128, NVB], F32)
    cbrow = sb("b_cbrow", [128, 8, 128], BF16)
    dtb = sb("b_dtb", [128, 32], F32)
    aneg = sb("b_aneg", [128, 32], F32)
    cd = sb("b_cd", [128, 32], F32)
    normw = sb("b_normw", [128, 512], F32)
    dg = sb("b_dg", [128, 96, 128], BF16)
    for c in range(0, 8, 2):
        P.dma("pool", lambda e, c=c: e.dma_start(out=w_in[:, c:c + 2, :],
                                                 in_=d["w_in"][c * 128:(c + 2) * 128, :].rearrange("(c p) n -> p c n", p=128)),
              "wlinb", writes=["w_in"], chain=False)
    cb3 = d["row"][:, RB_CB:RB_CB + 3072].rearrange("o (i r c) -> o i r c", r=3, c=128)
    for r_ in range(3):
        P.dma("pool", lambda e, r_=r_: e.dma_start(out=cbrow[32 * r_:32 * r_ + 1, :, :], in_=cb3[:, :, r_, :]),
              "wlinb", writes=["cbrow"], chain=False)
    for c in range(0, 16, 4):
        P.dma("pool", lambda e, c=c: e.dma_start(out=w_out[:, c:c + 4, :],
                                                 in_=d["w_out"][c * 128:(c + 4) * 128, :].rearrange("(c p) n -> p c n", p=128)),
              "wl", writes=["w_out"], chain=False)
    P.dma("sp", lambda e: e.dma_start(out=vecb[:], in_=d["vec"]), "wl2", writes=["vecb"])
    for tl, off, n, key in ((dtb, RB_DTB, 32, "dtb"), (aneg, RB_ALOG, 32, "aneg"), (cd, RB_D, 32, "cd")):
        P.dma("sp", lambda e, tl=tl, off=off, n=n: e.dma_start(out=tl[:], in_=d["row"][:, off:off + n].partition_broadcast(128)),
              "wl2", writes=[key])
    P.act(lambda e: e.activation(out=aneg[:], in_=aneg[:], func=AF.Exp), reads=["aneg"], writes=["aneg"])
    P.dve(lambda e: e.tensor_scalar(out=aneg[:], in0=aneg[:], scalar1=-1.0, scalar2=None, op0=ALU.mult),
          reads=["aneg"], writes=["aneg"])
    P.dve(lambda e: e.tensor_scalar(out=cd[:], in0=cd[:], scalar1=0.5, scalar2=None, op0=ALU.mult),
          reads=["cd"], writes=["cd"])
    for jk in range(96):
        P.dve(lambda e, jk=jk: e.tensor_scalar(out=dg[:, jk, :], in0=ident[:], scalar1=vecb[:, jk:jk + 1], scalar2=None,
                                               op0=ALU.mult), reads=["ident", "vecb"], writes=["dg"])
    hT = sb("b_hT", [128, 2048], F32)
    hTb = sb("b_hTb", [128, 2048], BF16)
    P.pool(lambda e: e.memset(hT[:], 0.0), writes=["hT"])
    P.pool(lambda e: e.memset(hTb[:], 0.0), writes=["hTb"])
    xrb = sb("b_xrb", [128, 24, 131], BF16)
    P.pool(lambda e: e.memset(xrb[:], 0.0), writes=["xrb"])
    xin2 = [sb("b_xin%d" % i, [128, 1024], F32) for i in range(2)]
    xb = sb("b_xb", [128, 1024], BF16)
    xT = sb("b_xT", [128, 8, 128], BF16)
    sm = sb("b_sm", [128, 12, 32], F32)
    Rg = sb("b_Rg", [128, 4, 128], F32)
    es = sb("b_es", [128, 4, 128], F32)
    MT2 = [sb("b_MT%d" % i, [128, 8, 128], BF16) for i in range(2)]
    szg2 = [sb("b_szg%d" % i, [128, 512], F32) for i in range(2)]
    xs2 = sb("b_xs2", [128, 512], F32)
    xf2 = [sb("b_xf%d" % i, [128, 512], BF16) for i in range(2)]
    xsd2 = [sb("b_xsd%d" % i, [128, 512], BF16) for i in range(2)]
    xfd2 = [sb("b_xfd%d" % i, [128, 512], BF16) for i in range(2)]
    tnh = sb("b_tnh", [128, 512], F32)
    lng = normw
    lnb = tnh
    B2 = sb("b_B2", [128, 512], BF16)
    BCT = sb("b_BCT", [128, 8, 128], BF16)
    GTm = sb("b_GTm", [128, 128], F32)
    yb = sb("b_yb", [128, 512], F32)
    ssq = sb("b_ssq", [128, 4], F32)
    yn = xb[:, 0:512]
    junk = xb[:, 512:1024]
    ynT = sb("b_ynT", [128, 16, 128], BF16)
    st = sb("b_st", [128, 12], F32)
    mv = sb("b_mv", [128, 4], F32)

    def silu2(src_bank_ap, out_ap, key_in, key_out, shape3=None):
        P.act(lambda e: e.activation(out=tnh[:] if shape3 is None else tnh[:].rearrange("p (a b) -> p a b", a=4),
                                     in_=src_bank_ap, func=AF.Tanh, scale=0.5), reads=[key_in], writes=["tnh"])
        P.dve(lambda e: e.scalar_tensor_tensor(out=out_ap, in0=tnh[:] if shape3 is None else tnh[:].rearrange("p (a b) -> p a b", a=4),
                                               scalar=1.0, in1=src_bank_ap, op0=ALU.add, op1=ALU.mult),
              reads=["tnh", key_in], writes=[key_out])

    def silu2g(src_bank_ap, out_ap, key_in, key_out, shape3=None):
        tv = tnh[:] if shape3 is None else tnh[:].rearrange("p (a b) -> p a b", a=4)
        P.act(lambda e: e.activation(out=tv, in_=src_bank_ap, func=AF.Tanh, scale=0.5), reads=[key_in], writes=["tnh"])
        yield
        P.dve(lambda e: e.scalar_tensor_tensor(out=out_ap, in0=tv, scalar=1.0, in1=src_bank_ap, op0=ALU.add, op1=ALU.mult),
              reads=["tnh", key_in], writes=[key_out])
        yield

    def lockstep(*gens):
        gens = [g_ for g_ in gens if g_ is not None]
        while gens:
            for g_ in list(gens):
                try:
                    next(g_)
                except StopIteration:
                    gens.remove(g_)

    def load_x(t):
        if t >= T:
            return
        tsl = slice(t * 128, (t + 1) * 128)
        xi = xin2[t % 2]
        P.dma("sp", lambda e: e.dma_start(out=xi[:], in_=x_in[tsl, :]), "xin%d" % (t % 2), reads=[("x1d", t)],
              writes=[("xin", t % 2)])

    load_x(0)
    for t in range(T):
        ts_ = slice(t * 128, (t + 1) * 128)
        xin = xin2[t % 2]
        kx = ("xin", t % 2)
        P.act(lambda e, xin=xin: e.activation(out=xb[:], in_=xin[:], func=AF.Copy), reads=[kx], writes=["xb"])
        r0 = rr()
        b0 = bank[r0][:].bitcast(BF16)
        for c in range(8):
            P.pe(lambda e, c=c, b0=b0: e.transpose(b0[:, c * 128:(c + 1) * 128], xb[:, c * 128:(c + 1) * 128], ident[:]),
                 reads=["xb", "ident"], writes=[bk(r0)])
        P.dve(lambda e, b0=b0: e.tensor_copy(out=xT[:].rearrange("p a b -> p (a b)"), in_=b0), reads=[bk(r0)], writes=["xT"])
        r = rr()
        for c in range(8):
            P.pe(lambda e, c=c, r=r: e.matmul(bank[r][:, 0:32], lhsT=xT[:, c, :], rhs=w_in[:, c, 5120:5152],
                                              start=(c == 0), stop=(c == 7)), reads=["xT", "w_in"], writes=[bk(r)])
        P.dve(lambda e, r=r: e.tensor_tensor(out=sm[:, 0, :], in0=bank[r][:, 0:32], in1=dtb[:], op=ALU.add),
              reads=[bk(r), "dtb"], writes=["sm0"])
        P.act(lambda e: e.activation(out=sm[:, 1, :], in_=sm[:, 0, :], func=AF.Abs), reads=["sm0"], writes=["sm1"])
        P.act(lambda e: e.activation(out=sm[:, 1, :], in_=sm[:, 1, :], func=AF.Exp, scale=-1.0), reads=["sm1"], writes=["sm1"])
        P.act(lambda e: e.activation(out=sm[:, 1, :], in_=sm[:, 1, :], func=AF.Ln, bias=1.0), reads=["sm1"], writes=["sm1"])
        P.dve(lambda e: e.scalar_tensor_tensor(out=sm[:, 2, :], in0=sm[:, 0, :], scalar=0.0, in1=sm[:, 1, :],
                                               op0=ALU.max, op1=ALU.add), reads=["sm0", "sm1"], writes=["dt"])
        P.dve(lambda e: e.tensor_tensor(out=sm[:, 3, :], in0=sm[:, 2, :], in1=aneg[:], op=ALU.mult),
              reads=["dt", "aneg"], writes=["adt"])
        P.dve(lambda e: e.tensor_scalar(out=sm[:, 4, :], in0=sm[:, 2, :], scalar1=0.5, scalar2=None, op0=ALU.mult),
              reads=["dt"], writes=["cxf"])
        r = rr()
        for i, lh in enumerate((tri, u2, ones32)):
            P.pe(lambda e, i=i, lh=lh, r=r: e.matmul(bank[r][:, 32 * i:32 * i + 32], lhsT=lh[:], rhs=sm[:, 3, :],
                                                     start=True, stop=True), reads=["adt", "tri", "u2", "ones32"],
                 writes=[bk(r)])
        P.act(lambda e, r=r: e.activation(out=sm[:, 7:10, :].rearrange("p a b -> p (a b)"), in_=bank[r][:, 0:96], func=AF.Exp),
              reads=[bk(r)], writes=["e3"])
        P.dve(lambda e: e.scalar_tensor_tensor(out=sm[:, 5, :], in0=sm[:, 4, :], scalar=0.5, in1=sm[:, 8, :],
                                               op0=ALU.mult, op1=ALU.mult), reads=["cxf", "e3"], writes=["cxfd"])
        P.dve(lambda e: e.tensor_scalar(out=sm[:, 6, :], in0=sm[:, 7, :], scalar1=0.5, scalar2=None, op0=ALU.mult),
              reads=["e3"], writes=["eoff"])
        for q in range(6):
            r = rr()
            for jj in range(4):
                j = 4 * q + jj
                for c in range(8):
                    P.pe(lambda e, j=j, jj=jj, c=c, r=r: e.matmul(b3(r)[:, jj, :], lhsT=w_in[:, c, 2048 + j * 128:2048 + (j + 1) * 128],
                                                                  rhs=xT[:, c, :], start=(c == 0), stop=(c == 7)),
                         reads=["w_in", "xT"], writes=[bk(r)])
            P.act(lambda e, q=q, r=r: e.activation(out=xrb[:, 4 * q:4 * q + 4, 3:131], in_=b3(r), func=AF.Copy),
                  reads=[bk(r)], writes=["xrb"])

        def conv_tok(r, j0):
            for jj in range(4):
                j = j0 + jj
                osl = bank[r][:, jj * 128:(jj + 1) * 128]
                for k in range(4):
                    P.pe(lambda e, j=j, k=k, osl=osl: e.matmul(osl, lhsT=xrb[:, j, k:k + 128], rhs=dg[:, 4 * j + k, :],
                                                               start=(k == 0), stop=False), reads=["xrb", "dg"], writes=[bk(r)])
                P.pe(lambda e, j=j, osl=osl: e.matmul(osl, lhsT=onesb[32 * (j % 3):32 * (j % 3) + 1, :], rhs=cbrow[32 * (j % 3):32 * (j % 3) + 1, j // 3, :],
                                                      start=False, stop=True), reads=["onesb", "cbrow"], writes=[bk(r)])

        def conv_feat(r, j0):
            for jj in range(4):
                j = j0 + jj
                osl = bank[r][:, jj * 128:(jj + 1) * 128]
                for k in range(4):
                    P.pe(lambda e, j=j, k=k, osl=osl: e.matmul(osl, lhsT=dg[:, 4 * j + k, :], rhs=xrb[:, j, k:k + 128],
                                                               start=(k == 0), stop=False), reads=["xrb", "dg"], writes=[bk(r)])
                P.pe(lambda e, j=j, osl=osl: e.matmul(osl, lhsT=cbrow[32 * (j % 3):32 * (j % 3) + 1, j // 3, :], rhs=onesb[32 * (j % 3):32 * (j % 3) + 1, :],
                                                      start=False, stop=True), reads=["onesb", "cbrow"], writes=[bk(r)])

        r = rr()
        conv_tok(r, 16)
        silu2(bank[r][:], B2[:], bk(r), "B2")
        for i in range(2):
            r = rr()
            conv_feat(r, 16 + 4 * i)
            silu2(b3(r), BCT[:, 4 * i:4 * i + 4, :], bk(r), "BCT", shape3=True)
        def stage1(g):
            gs = slice(g * 512, (g + 1) * 512)
            hs8 = slice(g * 8, (g + 1) * 8)
            pp = g % 2
            szg, xf, xsd, xfd, MT = szg2[pp], xf2[pp], xsd2[pp], xfd2[pp], MT2[pp]
            tv = tnh[:]

            def bc(i, hs8=hs8):
                return sm[:, i, hs8].unsqueeze(2).broadcast_to([128, 8, 64])

            def mkR(hf):
                h4 = slice(g * 8 + hf * 4, g * 8 + hf * 4 + 4)
                P.pool(lambda e: e.tensor_tensor(out=Rg[:], in0=tri[:].unsqueeze(1).broadcast_to([128, 4, 128]),
                                                 in1=sm[:, 3, h4].unsqueeze(2).broadcast_to([128, 4, 128]), op=ALU.mult),
                       reads=["tri", "adt"], writes=["Rg"])

            def seg_mm(r):
                P.pe(lambda e: e.matmul(bank[r][:], lhsT=u2[:], rhs=Rg[:].rearrange("p a b -> p (a b)"), start=True, stop=True),
                     reads=["u2", "Rg"], writes=[bk(r)])

            def seg_exp(r):
                P.act(lambda e: e.activation(out=es[:].rearrange("p a b -> p (a b)"), in_=bank[r][:], func=AF.Exp),
                      reads=[bk(r)], writes=["es"])

            def mk_mt(hf):
                P.dve(lambda e: e.tensor_tensor(out=MT[:, 4 * hf:4 * hf + 4, :], in0=es[:],
                                                in1=GTm[:].unsqueeze(1).broadcast_to([128, 4, 128]), op=ALU.mult),
                      reads=["es", "GTm"], writes=[("MT", pp, hf)])

            mkR(0)
            rg = rr()
            P.pe(lambda e: e.matmul(bank[rg][:, 0:128], lhsT=BCT[:, g, :], rhs=BCT[:, 4 + g, :], start=True, stop=True),
                 reads=["BCT"], writes=[bk(rg)])
            yield
            rc = rr()
            conv_tok(rc, 4 * g)
            yield
            rs0 = rr()
            seg_mm(rs0)
            P.dve(lambda e: e.tensor_tensor(out=GTm[:], in0=bank[rg][:, 0:128], in1=m025[:], op=ALU.mult),
                  reads=[bk(rg), "m025"], writes=["GTm"])
            yield
            P.act(lambda e: e.activation(out=tv, in_=bank[rc][:], func=AF.Tanh, scale=0.5), reads=[bk(rc)], writes=["tnh"])
            rz = rr()
            for c in range(8):
                P.pe(lambda e, c=c: e.matmul(bank[rz][:], lhsT=xT[:, c, :], rhs=w_in[:, c, gs], start=(c == 0),
                                             stop=(c == 7)), reads=["xT", "w_in"], writes=[bk(rz)])
            yield
            seg_exp(rs0)
            yield
            P.dve(lambda e: e.scalar_tensor_tensor(out=xs2[:], in0=tv, scalar=1.0, in1=bank[rc][:], op0=ALU.add, op1=ALU.mult),
                  reads=["tnh", bk(rc)], writes=["xs2"])
            mkR(1)
            yield
            mk_mt(0)
            rs1 = rr()
            seg_mm(rs1)
            yield
            x3 = xs2[:].rearrange("p (h v) -> p h v", h=8)
            P.act(lambda e: e.activation(out=tv, in_=bank[rz][:], func=AF.Tanh, scale=0.5), reads=[bk(rz)], writes=["tnh"])
            P.dve(lambda e: e.tensor_tensor(out=xf[:].rearrange("p (h v) -> p h v", h=8), in0=x3, in1=bc(4),
                                            op=ALU.mult), reads=["xs2", "cxf"], writes=[("xf", pp)])
            P.pool(lambda e: e.tensor_tensor(out=xsd[:].rearrange("p (h v) -> p h v", h=8), in0=x3,
                                             in1=cd[:, hs8].unsqueeze(2).broadcast_to([128, 8, 64]), op=ALU.mult),
                   reads=["xs2", "cd"], writes=[("xsd", pp)])
            yield
            seg_exp(rs1)
            P.pool(lambda e: e.tensor_tensor(out=xfd[:].rearrange("p (h v) -> p h v", h=8), in0=x3, in1=bc(5),
                                             op=ALU.mult), reads=["xs2", "cxfd"], writes=[("xfd", pp)])
            yield
            P.dve(lambda e: e.scalar_tensor_tensor(out=szg[:], in0=tv, scalar=1.0, in1=bank[rz][:], op0=ALU.add, op1=ALU.mult),
                  reads=["tnh", bk(rz)], writes=[("szg", pp)])
            yield
            mk_mt(1)
            yield

        def stage2(g):
            gs = slice(g * 512, (g + 1) * 512)
            hs8 = slice(g * 8, (g + 1) * 8)
            pp = g % 2
            szg, xf, xsd, xfd, MT = szg2[pp], xf2[pp], xsd2[pp], xfd2[pp], MT2[pp]

            def bc(i, hs8=hs8):
                return sm[:, i, hs8].unsqueeze(2).broadcast_to([128, 8, 64])
            P.dma("sp", lambda e, g=g: e.dma_start(out=normw[:], in_=d["row"][:, RB_NW + g * 512:RB_NW + (g + 1) * 512].partition_broadcast(128)),
                  "nwl", writes=["normw"])
            ry = rr()
            P.pe(lambda e, ry=ry: e.matmul(bank[ry][:], lhsT=ident[:], rhs=xsd[:], start=True, stop=False),
                 reads=["ident", ("xsd", pp)], writes=[bk(ry)])
            yield
            for hh in range(8):
                P.pe(lambda e, ry=ry, hh=hh: e.matmul(bank[ry][:, hh * 64:(hh + 1) * 64], lhsT=MT[:, hh, :],
                                                      rhs=xf[:, hh * 64:(hh + 1) * 64], start=False, stop=(hh == 7)),
                     reads=[("MT", pp, hh // 4), ("xf", pp)], writes=[bk(ry)])
                yield
            ro = rr()
            P.pe(lambda e, ro=ro, g=g, gs=gs: e.matmul(bank[ro][:], lhsT=BCT[:, 4 + g, :], rhs=hTb[:, gs], start=True, stop=True),
                 reads=["BCT", ("hTb", g)], writes=[bk(ro)])
            yield
            P.dve(lambda e, ro=ro, bc=bc: e.tensor_tensor(out=yb[:].rearrange("p (h v) -> p h v", h=8),
                                                          in0=bank[ro][:].rearrange("p (h v) -> p h v", h=8), in1=bc(6), op=ALU.mult),
                  reads=[bk(ro), "eoff"], writes=["yb"])
            yield
            P.dve(lambda e, ry=ry: e.tensor_tensor(out=yb[:], in0=yb[:], in1=bank[ry][:], op=ALU.add),
                  reads=["yb", bk(ry)], writes=["yb"])
            yield
            P.dve(lambda e: e.tensor_tensor(out=yb[:], in0=yb[:], in1=szg[:], op=ALU.mult), reads=["yb", ("szg", pp)], writes=["yb"])
            yield
            P.act(lambda e, g=g: e.activation(out=junk, in_=yb[:], func=AF.Square, accum_out=ssq[:, g:g + 1]),
                  reads=["yb"], writes=["junk", "ssq"])
            yield
            P.act(lambda e, g=g: e.activation(out=ssq[:, g:g + 1], in_=ssq[:, g:g + 1], func=AF.Sqrt, bias=cst[:, 4:5],
                                              scale=1.0 / 512), reads=["ssq", "cst"], writes=["ssq"])
            yield
            P.dve(lambda e, g=g: e.reciprocal(out=ssq[:, g:g + 1], in_=ssq[:, g:g + 1]), reads=["ssq"], writes=["ssq"])
            yield
            P.dve(lambda e, g=g: e.scalar_tensor_tensor(out=yn, in0=yb[:], scalar=ssq[:, g:g + 1], in1=normw[:],
                                                        op0=ALU.mult, op1=ALU.mult), reads=["yb", "ssq", "normw"], writes=["yn"])
            yield
            r = rr()
            bt = bank[r][:].bitcast(BF16)
            for c in range(4):
                P.pe(lambda e, c=c, bt=bt: e.transpose(bt[:, c * 128:(c + 1) * 128], xb[:, c * 128:(c + 1) * 128], ident[:]),
                     reads=["yn", "ident"], writes=[bk(r)])
                yield
            P.act(lambda e, g=g, bt=bt: e.activation(out=ynT[:, 4 * g:4 * g + 4, :].rearrange("p a b -> p (a b)"), in_=bt[:, 0:512],
                                                     func=AF.Copy), reads=[bk(r)], writes=[("ynT", g)])
            yield
            r = rr()
            P.pe(lambda e, r=r, g=g: e.matmul(bank[r][:], lhsT=B2[:, g * 128:(g + 1) * 128], rhs=xfd[:], start=True, stop=True),
                 reads=["B2", ("xfd", pp)], writes=[bk(r)])
            yield
            h3 = hT[:, gs].rearrange("p (h v) -> p h v", h=8)
            P.dve(lambda e, h3=h3, bc=bc: e.tensor_tensor(out=h3, in0=h3, in1=bc(9), op=ALU.mult),
                  reads=[("hT", g), "e3"], writes=[("hT", g)])
            yield
            P.dve(lambda e, r=r, gs=gs: e.tensor_tensor(out=hT[:, gs], in0=hT[:, gs], in1=bank[r][:], op=ALU.add),
                  reads=[("hT", g), bk(r)], writes=[("hT", g)])
            yield
            P.pool(lambda e, gs=gs: e.tensor_copy(out=hTb[:, gs], in_=hT[:, gs]), reads=[("hT", g)], writes=[("hTb", g)])
            yield
        lockstep(stage1(0))
        for g in range(4):
            lockstep(stage1(g + 1) if g + 1 < 4 else None, stage2(g))
        P.pool(lambda e: e.tensor_copy(out=xrb[:, :, 0:3], in_=xrb[:, :, 128:131]), reads=["xrb"], writes=["xrb"])
        load_x(t + 1)
        ynk = [("ynT", g) for g in range(4)]
        rs = [rr(), rr()]
        for hf in range(2):
            for c in range(16):
                P.pe(lambda e, hf=hf, c=c, r=rs[hf]: e.matmul(bank[r][:], lhsT=ynT[:, c, :], rhs=w_out[:, c, hf * 512:(hf + 1) * 512],
                                                              start=(c == 0), stop=(c == 15)), reads=ynk + ["w_out"], writes=[bk(rs[hf])])
        for hf in range(2):
            hs = slice(hf * 512, (hf + 1) * 512)
            P.dve(lambda e, hs=hs, r=rs[hf], xin=xin: e.scalar_tensor_tensor(out=xin[:, hs], in0=xin[:, hs], scalar=ALPHA, in1=bank[r][:],
                                                                             op0=ALU.mult, op1=ALU.add), reads=[kx, bk(rs[hf])], writes=[kx])
            P.dve(lambda e, hf=hf, hs=hs, xin=xin: e.bn_stats(out=st[:, 6 * hf:6 * hf + 6], in_=xin[:, hs]), reads=[kx], writes=["st"])
        layer_norm_tail(P, xin, st, mv, cst, None, None, None, [kx])
        for hf in range(2):
            hs = slice(hf * 512, (hf + 1) * 512)
            P.dma("sp", lambda e, hf=hf: e.dma_start(out=lng[:], in_=d["row"][:, RB_LNG + hf * 512:RB_LNG + (hf + 1) * 512].partition_broadcast(128)),
                  "lgl", writes=["normw"])
            P.dma("sp", lambda e, hf=hf: e.dma_start(out=lnb[:], in_=d["row"][:, RB_LNB + hf * 512:RB_LNB + (hf + 1) * 512].partition_broadcast(128)),
                  "lbl", writes=["tnh"])
            P.pool(lambda e, hs=hs, xin=xin: e.tensor_tensor(out=xin[:, hs], in0=xin[:, hs], in1=lng[:], op=ALU.mult), reads=[kx, "normw"], writes=[kx])
            P.pool(lambda e, hs=hs, xin=xin: e.tensor_tensor(out=xin[:, hs], in0=xin[:, hs], in1=lnb[:], op=ALU.add), reads=[kx, "tnh"], writes=[kx])
        P.dma("sp", lambda e, ts_=ts_, xin=xin: e.dma_start(out=x_out[ts_, :], in_=xin[:]), "xo%d" % (t % 2), reads=[kx], writes=[("outd", t)])
    return A


B_SPECS = [("w_in", [1024, 5152], F32), ("w_out", [2048, 1024], F32), ("vec", [128, NVB], F32), ("row", [1, NRB], F32)]


def prep_B(inp):
    f = np.float32
    vec = np.zeros((128, NVB), f)
    cw = np.asarray(inp["ssd_conv_w"][0], f)
    for j in range(24):
        for k in range(4):
            vec[:, VB_CW + 4 * j + k] = cw[k, j * 128:(j + 1) * 128]
    cb = np.asarray(inp["ssd_conv_b"][0], f)
    vec[:, VB_CB:VB_CB + 24] = cb.reshape(24, 128).T
    row = np.zeros((1, NRB), f)
    row[0, RB_DTB:RB_DTB + 32] = np.asarray(inp["ssd_dt_bias"][0], f)
    row[0, RB_ALOG:RB_ALOG + 32] = np.asarray(inp["ssd_a_log"][0], f)
    row[0, RB_D:RB_D + 32] = np.asarray(inp["ssd_d"][0], f)
    row[0, RB_NW:RB_NW + 2048] = np.asarray(inp["ssd_norm"][0], f)
    row[0, RB_LNG:RB_LNG + 1024] = np.asarray(inp["ssd_ln_g"][0], f)
    row[0, RB_LNB:RB_LNB + 1024] = np.asarray(inp["ssd_ln_b"][0], f)
    row[0, RB_CB:RB_CB + 3072] = cb
    return {"w_in": np.asarray(inp["ssd_w_in"][0], f), "w_out": np.asarray(inp["ssd_w_out"][0], f), "vec": vec, "row": row}


def build_program_B(S):
    nc = bass.Bass("TRN2", target_bir_lowering=False)
    P = Prog(nc)
    d = declare(nc, B_SPECS, "b_")
    x_in = nc.dram_tensor("x1", [S, D], F32, kind="ExternalInput").ap()
    x_out = nc.dram_tensor("out", [S, D], F32, kind="ExternalOutput").ap()
    build_B(nc, P, S, d, x_in, x_out)
    st = P.emit()
    return nc, st


_CACHE = {}


def build_program_fused(S):
    nc = bass.Bass("TRN2", target_bir_lowering=False)
    P = Prog(nc)
    dA = declare(nc, A_SPECS, "a_")
    dA["pos"] = nc.dram_tensor("pos", [1, S], I32, kind="ExternalInput").ap()
    dB = declare(nc, B_SPECS, "b_")
    x_in = nc.dram_tensor("x", [S, D], F32, kind="ExternalInput").ap()
    x1 = nc.dram_tensor("x1_scratch", [S, D], F32).ap()
    x_out = nc.dram_tensor("out", [S, D], F32, kind="ExternalOutput").ap()
    A = build_A(nc, P, S, dA, x_in, x1)
    keep = {k: v for k, v in P.last_writer.items() if isinstance(k, tuple) and k[0] == "x1d"}
    P.barrier()
    P.last_writer.update(keep)
    A.free()
    build_B(nc, P, S, dB, x1, x_out)
    st = P.emit()
    return nc, st


def kernel(**inputs):
    S = SEQ
    x = np.ascontiguousarray(np.asarray(inputs["x"], np.float32))
    pos = np.ascontiguousarray(np.asarray(inputs["positions"], np.int32))
    hpA = prep_A(inputs)
    hpB = prep_B(inputs)
    mode = "fused"
    if mode == "split":
        if "A" not in _CACHE:
            _CACHE["A"] = build_program_A(S)[0]
            _CACHE["B"] = build_program_B(S)[0]
        mapsA = []
        for b in range(NCORES):
            m = {"a_" + k: v for k, v in hpA.items()}
            m["x"] = x[b]
            m["pos"] = pos[b:b + 1]
            mapsA.append(m)
        resA = run_bass_kernel_spmd(_CACHE["A"], mapsA, core_ids=list(range(NCORES)))
        mapsB = []
        for b in range(NCORES):
            m = {"b_" + k: v for k, v in hpB.items()}
            m["x1"] = np.ascontiguousarray(np.asarray(resA.results[b]["x1"], np.float32))
            mapsB.append(m)
        resB = run_bass_kernel_spmd(_CACHE["B"], mapsB, core_ids=list(range(NCORES)))
        return np.stack([np.asarray(resB.results[b]["out"], np.float32) for b in range(NCORES)], axis=0)
    if "F" not in _CACHE:
        _CACHE["F"] = build_program_fused(S)[0]
    maps = []
    for b in range(NCORES):
        m = {"a_" + k: v for k, v in hpA.items()}
        m.update({"b_" + k: v for k, v in hpB.items()})
        m["x"] = x[b]
        m["pos"] = pos[b:b + 1]
        maps.append(m)
    res = run_bass_kernel_spmd(_CACHE["F"], maps, core_ids=list(range(NCORES)))
    return np.stack([np.asarray(res.results[b]["out"], np.float32) for b in range(NCORES)], axis=0)
```

```python
import math
import numpy as np
import concourse.bass as bass
import concourse.mybir as mybir
from concourse.bass_utils import run_bass_kernel_spmd

F32 = mybir.dt.float32
BF16 = mybir.dt.bfloat16
I32 = mybir.dt.int32
AF = mybir.ActivationFunctionType
ALU = mybir.AluOpType

D = 1024
NCORES = 8
SEQ = 4096
ALPHA = 4.0 ** 0.25
MAGIC = 12582912.0
C1 = 6.28125
C2 = 2.0 * math.pi - 6.28125


class Op:
    __slots__ = ("eng", "fn", "deps", "is_dma", "sem", "val", "marked", "dma_wait")

    def __init__(self, eng, fn, is_dma=False, sem=None):
        self.eng = eng
        self.fn = fn
        self.deps = []
        self.is_dma = is_dma
        self.sem = sem
        self.val = None
        self.marked = False
        self.dma_wait = {}


class Prog:
    ENGS = ("pe", "act", "dve", "pool", "sp")

    def __init__(self, nc):
        self.nc = nc
        self.eobj = {"pe": nc.tensor, "act": nc.scalar, "dve": nc.vector,
                     "pool": nc.gpsimd, "sp": nc.sync}
        self.ops = []
        self.last_writer = {}
        self.readers = {}
        self.dma_count = {}
        self.dma_last = {}
        self.last_on = {}
        self.bar = {}

    def _add(self, op, reads, writes):
        deps = op.deps

        def need(p):
            if p is None or p is op:
                return
            if p.is_dma:
                op.dma_wait[p.sem] = self.dma_count[p.sem]
            else:
                deps.append(p)

        b = self.bar.pop(op.eng, None)
        if b is not None:
            for p in b[0]:
                need(p)
            for s, c in b[1].items():
                op.dma_wait[s] = c
        for k in reads:
            w = self.last_writer.get(k)
            if w is not None:
                if (not w.is_dma) and (not op.is_dma) and w.eng == op.eng == "pe":
                    continue
                need(w)
        strict = op.eng != "pe"
        for k in writes:
            w = self.last_writer.get(k)
            if w is not None:
                if w.is_dma or op.is_dma or w.eng != op.eng or strict:
                    need(w)
            for r in self.readers.get(k, ()):
                if r.is_dma or op.is_dma or r.eng != op.eng or strict:
                    need(r)
        for k in writes:
            self.last_writer[k] = op
            self.readers[k] = []
        for k in reads:
            self.readers.setdefault(k, []).append(op)
        self.ops.append(op)
        if not op.is_dma:
            self.last_on[op.eng] = op
        return op

    def op(self, eng, fn, reads=(), writes=()):
        return self._add(Op(eng, fn), reads, writes)

    def dma(self, eng, fn, sem, reads=(), writes=(), chain=True):
        o = Op(eng, fn, is_dma=True, sem=sem)
        if chain and sem in self.dma_last:
            o.dma_wait[sem] = self.dma_count[sem]
        self.dma_count.setdefault(sem, 0)
        self._add(o, reads, writes)
        self.dma_count[sem] += 1
        o.val = self.dma_count[sem]
        self.dma_last[sem] = o
        return o

    def pe(self, fn, reads=(), writes=()):
        return self.op("pe", fn, reads, writes)

    def act(self, fn, reads=(), writes=()):
        return self.op("act", fn, reads, writes)

    def dve(self, fn, reads=(), writes=()):
        return self.op("dve", fn, reads, writes)

    def pool(self, fn, reads=(), writes=()):
        return self.op("pool", fn, reads, writes)

    def barrier(self):
        lasts = list(self.last_on.values())
        dm = dict(self.dma_count)
        for e in self.ENGS:
            self.bar[e] = (lasts, dm)
        self.last_writer = {}
        self.readers = {}

    def emit(self):
        nc = self.nc
        for o in self.ops:
            for d in o.deps:
                d.marked = True
        cnt = {e: 0 for e in self.ENGS}
        for o in self.ops:
            if not o.is_dma and o.marked:
                cnt[o.eng] += 1
                o.val = cnt[o.eng]
        ctr = {e: nc.alloc_semaphore(name="ctr_" + e) for e in self.ENGS}
        dsem = {s: nc.alloc_semaphore(name="dma_%d" % i) for i, s in enumerate(self.dma_count)}
        waited = {e: {} for e in self.ENGS}
        nwait = 0
        for o in self.ops:
            eng = self.eobj[o.eng]
            need = {}
            for d in o.deps:
                key = ("c", d.eng)
                if need.get(key, (None, 0))[1] < d.val:
                    need[key] = (ctr[d.eng], d.val)
            for s, c in o.dma_wait.items():
                key = ("d", s)
                if need.get(key, (None, 0))[1] < 16 * c:
                    need[key] = (dsem[s], 16 * c)
            w = waited[o.eng]
            for key, (h, v) in need.items():
                if w.get(key, 0) >= v:
                    continue
                eng.wait_ge(h, v)
                nwait += 1
                w[key] = v
            ins = o.fn(eng)
            if o.is_dma:
                ins.then_inc(dsem[o.sem], 16)
            elif o.marked:
                ins.then_inc(ctr[o.eng], 1)
        eng = self.eobj["sp"]
        for s, c in self.dma_count.items():
            eng.wait_ge(dsem[s], 16 * c)
        return dict(n_ops=len(self.ops), n_wait=nwait, marked=cnt)


class Alloc:
    def __init__(self, nc):
        self.nc = nc
        self.guards = []

    def sb(self, name, shape, dt=F32):
        g = self.nc.sbuf_tensor("s_" + name, list(shape), dt)
        t = g.__enter__()
        self.guards.append(g)
        return t

    def ps(self, name, shape, dt=F32):
        g = self.nc.psum_tensor("p_" + name, list(shape), dt)
        t = g.__enter__()
        self.guards.append(g)
        return t

    def free(self):
        for g in reversed(self.guards):
            g.__exit__(None, None, None)
        self.guards = []


def make_consts(nc, P, A, pfx):
    ident = A.sb(pfx + "ident", [128, 128], BF16)
    maskT = A.sb(pfx + "maskT", [128, 128], BF16)
    ones32 = A.sb(pfx + "ones32", [128, 128], F32)
    P.pool(lambda e: e.memset(ident[:], 1.0), writes=["ident"])
    P.pool(lambda e: e.affine_select(out=ident[:], in_=ident[:], pattern=[[-1, 128]],
                                     compare_op=ALU.is_equal, fill=0.0, base=0,
                                     channel_multiplier=1), reads=["ident"], writes=["ident"])
    P.pool(lambda e: e.memset(maskT[:], 1.0), writes=["maskT"])
    P.pool(lambda e: e.affine_select(out=maskT[:], in_=maskT[:], pattern=[[1, 128]],
                                     compare_op=ALU.is_ge, fill=0.0, base=0,
                                     channel_multiplier=-1), reads=["maskT"], writes=["maskT"])
    P.pool(lambda e: e.memset(ones32[:], 1.0), writes=["ones32"])
    return ident, maskT, ones32


V_CW, V_CB, V_BA, V_BX, V_LAM, V_QN, V_KVN, V_INVF, V_SGN, NV = 0, 16, 20, 24, 28, 32, 34, 35, 36, 40
ATT_SCALE = 96.0 ** -0.5


def build_A(nc, P, S, d, x_in, x_out):
    T = S // 128
    A = Alloc(nc)
    sb, ps = A.sb, A.ps
    ident, maskT, ones32 = make_consts(nc, P, A, "a_")
    bank = [ps("a_bank%d" % i, [128, 512], F32) for i in range(8)]

    def bk(i):
        return ("bank", i)

    w_in = sb("a_w_in", [128, 8, 2048], BF16)
    w_out = sb("a_w_out", [128, 8, 1024], BF16)
    w_uq = sb("a_w_uq", [128, 2, 1280], BF16)
    w_kn = sb("a_w_kn", [128, 512], BF16)
    w_v = sb("a_w_v", [128, 512], BF16)
    gA = sb("a_gA", [128, 4, 128], BF16)
    gX = sb("a_gX", [128, 4, 128], BF16)
    vec = sb("a_vec", [128, NV], F32)
    lng = sb("a_lng", [128, 512], F32)
    lnb = sb("a_lnb", [128, 512], F32)
    cst = sb("a_cst", [128, 8], F32)
    for i, v in enumerate([1e-6, 1e-5, math.pi / 2, 0.0]):
        P.pool(lambda e, i=i, v=v: e.memset(cst[:, i:i + 1], v), writes=["cst"])
    for c in range(0, 8, 2):
        P.dma("pool", lambda e, c=c: e.dma_start(out=w_in[:, c:c + 2, :],
                                                 in_=d["w_in"][c * 128:(c + 2) * 128, :].rearrange("(c p) n -> p c n", p=128)),
              "wl", writes=["w_in"], chain=False)
    for c in range(0, 8, 4):
        P.dma("pool", lambda e, c=c: e.dma_start(out=w_out[:, c:c + 4, :],
                                                 in_=d["w_out"][c * 128:(c + 4) * 128, :].rearrange("(c p) n -> p c n", p=128)),
              "wl", writes=["w_out"], chain=False)
    P.dma("pool", lambda e: e.dma_start(out=w_uq[:], in_=d["w_uq"].rearrange("(c p) n -> p c n", p=128)),
          "wl", writes=["w_uq"], chain=False)
    P.dma("pool", lambda e: e.dma_start(out=w_kn[:], in_=d["w_kn"]), "wl", writes=["w_kn"], chain=False)
    P.dma("pool", lambda e: e.dma_start(out=w_v[:], in_=d["w_v"]), "wl", writes=["w_v"], chain=False)
    P.dma("pool", lambda e: e.dma_start(out=gA[:], in_=d["gA"]), "wl", writes=["gA"], chain=False)
    P.dma("pool", lambda e: e.dma_start(out=gX[:], in_=d["gX"]), "wl", writes=["gX"], chain=False)
    P.dma("sp", lambda e: e.dma_start(out=vec[:], in_=d["vec"]), "wl2", writes=["vec"])

    der = sb("a_der", [128, 16], F32)
    tmp4 = sb("a_tmp4", [128, 4], F32)
    P.act(lambda e: e.activation(out=tmp4[:], in_=vec[:, V_LAM:V_LAM + 4], func=AF.Exp, scale=-1.0),
          reads=["vec"], writes=["tmp4"])
    P.act(lambda e: e.activation(out=tmp4[:], in_=tmp4[:], func=AF.Ln, bias=1.0), reads=["tmp4"], writes=["tmp4"])
    P.dve(lambda e: e.tensor_scalar(out=der[:, 0:4], in0=tmp4[:], scalar1=4.0, scalar2=None, op0=ALU.mult),
          reads=["tmp4"], writes=["der"])
    P.dve(lambda e: e.tensor_scalar(out=der[:, 4:8], in0=tmp4[:], scalar1=-4.0, scalar2=None, op0=ALU.mult),
          reads=["tmp4"], writes=["der"])
    P.dve(lambda e: e.tensor_scalar(out=der[:, 8:16], in0=vec[:, V_BA:V_BA + 8], scalar1=0.5, scalar2=None,
                                    op0=ALU.mult), reads=["vec"], writes=["der"])

    trig_d = nc.dram_tensor("a_trig", [128, 2, S], F32).ap()
    CB = min(4096, S)
    A2 = Alloc(nc)
    pi_t = A2.sb("a_pi", [128, CB], I32)
    tA = A2.sb("a_tA", [128, CB], F32)
    tB = A2.sb("a_tB", [128, CB], F32)
    tC = A2.sb("a_tC", [128, 2, CB], F32)
    for blk in range(S // CB):
        cs = slice(blk * CB, (blk + 1) * CB)
        P.dma("sp", lambda e, cs=cs: e.dma_start(out=pi_t[:], in_=d["pos"][:, cs].partition_broadcast(128)),
              "trg", writes=["pi"])
        P.dve(lambda e: e.tensor_copy(out=tA[:], in_=pi_t[:]), reads=["pi"], writes=["tA"])
        P.dve(lambda e: e.tensor_scalar(out=tA[:], in0=tA[:], scalar1=vec[:, V_INVF:V_INVF + 1], scalar2=None,
                                        op0=ALU.mult), reads=["tA", "vec"], writes=["tA"])
        P.dve(lambda e: e.tensor_scalar(out=tB[:], in0=tA[:], scalar1=1.0 / (2 * math.pi), scalar2=MAGIC,
                                        op0=ALU.mult, op1=ALU.add), reads=["tA"], writes=["tB"])
        P.dve(lambda e: e.tensor_scalar(out=tB[:], in0=tB[:], scalar1=-MAGIC, scalar2=None, op0=ALU.add),
              reads=["tB"], writes=["tB"])
        P.dve(lambda e: e.scalar_tensor_tensor(out=tA[:], in0=tB[:], scalar=-C1, in1=tA[:], op0=ALU.mult,
                                               op1=ALU.add), reads=["tA", "tB"], writes=["tA"])
        P.dve(lambda e: e.scalar_tensor_tensor(out=tA[:], in0=tB[:], scalar=-C2, in1=tA[:], op0=ALU.mult,
                                               op1=ALU.add), reads=["tA", "tB"], writes=["tA"])
        P.dve(lambda e: e.tensor_scalar(out=tA[:], in0=tA[:], scalar1=3.1415925, scalar2=-3.1415925,
                                        op0=ALU.min, op1=ALU.max), reads=["tA"], writes=["tA"])
        P.act(lambda e: e.activation(out=tB[:], in_=tA[:], func=AF.Abs), reads=["tA"], writes=["tB"])
        P.act(lambda e: e.activation(out=tC[:, 0, :], in_=tB[:], func=AF.Sin, bias=cst[:, 2:3], scale=-1.0),
              reads=["tB", "cst"], writes=["tC"])
        P.act(lambda e: e.activation(out=tC[:, 1, :], in_=tA[:], func=AF.Sin, scale=vec[:, V_SGN:V_SGN + 1]),
              reads=["tA", "vec"], writes=["tC"])
        P.dma("sp", lambda e, cs=cs: e.dma_start(out=trig_d[:, :, cs], in_=tC[:]), "trg",
              reads=["tC"], writes=["trig_d"])

    keepd = {k: v for k, v in P.last_writer.items() if k == "trig_d"}
    P.barrier()
    P.last_writer.update(keepd)
    A2.free()
    KnT = sb("a_KnT", [128, 4, S], BF16)
    KrT = sb("a_KrT", [128, S], BF16)
    Vaug = sb("a_Vaug", [128, T, 8, 96], BF16)
    P.pool(lambda e: e.memset(Vaug[:], 1.0), writes=["Vall"])
    xin2 = [sb("a_xin%d" % i, [128, 1024], F32) for i in range(2)]
    xb = sb("a_xb", [128, 1024], BF16)
    xT = sb("a_xT", [128, 8, 128], BF16)
    xr = sb("a_xr", [128, 4, 131], F32)
    s12 = [sb("a_s1%d" % i, [128, 8, 128], F32) for i in range(2)]
    sq = sb("a_sq", [128, 3, 128], F32)
    sr = sb("a_sr", [128, 2, 128], F32)
    cqn = sb("a_cqn", [128, 3, 128], BF16)
    trg = sb("a_trg", [128, 2, 128], F32)
    kr1 = sb("a_kr1", [128, 128], F32)
    kr2 = sb("a_kr2", [128, 128], F32)
    acc = sb("a_acc", [128, 4, 128], F32)
    xcb = sb("a_xcb", [128, 4, 128], BF16)
    ta = sb("a_ta", [128, 4, 128], F32)
    ti = sb("a_ti", [128, 4, 128], F32)
    aa = sb("a_aa", [128, 4, 128], F32)
    hh2 = [sb("a_hh%d" % i, [128, 4, 128], F32) for i in range(2)]
    hc = sb("a_hc", [128, 4], F32)
    qsb = sb("a_qsb", [128, 6, 128], F32)
    th = qsb[:, 0:4, :]
    QnT2 = [sb("a_QnT%d" % i, [128, 4, 128], BF16) for i in range(2)]
    QrT2 = [sb("a_QrT%d" % i, [128, 3, 128], BF16) for i in range(2)]
    PT = [sb("a_PT%d" % i, [128, 4, 128], BF16) for i in range(2)]
    rl = sb("a_rl", [128, 128], F32)
    ym = sb("a_ym", [128, 4, 128], F32)
    yT = sb("a_yT", [128, 8, 128], BF16)
    st = sb("a_st", [128, 12], F32)
    mv = sb("a_mv", [128, 4], F32)
    P.pool(lambda e: e.memset(xr[:], 0.0), writes=["xr"])
    P.pool(lambda e: e.memset(hc[:], 0.0), writes=["hc"])

    def b3(i):
        return bank[i][:].rearrange("p (a b) -> p a b", a=4)

    def front(t):
        ts_ = slice(t * 128, (t + 1) * 128)
        pp = t % 2
        xin, s1, hh, QnT, QrT = xin2[pp], s12[pp], hh2[pp], QnT2[pp], QrT2[pp]
        kx, ks1, khh, kqn, kqr = ("xin", pp), ("s1", pp), ("hh", pp), ("QnT", pp), ("QrT", pp)
        P.dma("sp", lambda e: e.dma_start(out=xin[:], in_=x_in[ts_, :]), "xin%d" % pp, writes=[kx])
        P.dma("sp", lambda e: e.dma_start(out=trg[:], in_=trig_d[:, :, ts_]), "trl", reads=["trig_d"], writes=["trg"])
        P.act(lambda e: e.activation(out=xb[:], in_=xin[:], func=AF.Copy), reads=[kx], writes=["xb"])
        yield
        b0 = bank[0][:].bitcast(BF16)
        for c in range(8):
            P.pe(lambda e, c=c: e.transpose(b0[:, c * 128:(c + 1) * 128], xb[:, c * 128:(c + 1) * 128], ident[:]),
                 reads=["xb", "ident"], writes=[bk(0)])
        P.dve(lambda e: e.tensor_copy(out=xT[:].rearrange("p a b -> p (a b)"), in_=b0), reads=[bk(0)], writes=["xT"])
        yield
        yield
        ibank = [1, 2, 3, 0]
        for j in range(16):
            bi = ibank[j // 4]
            for c in range(8):
                P.pe(lambda e, j=j, c=c, bi=bi: e.matmul(b3(bi)[:, j % 4, :], lhsT=w_in[:, c, j * 128:(j + 1) * 128],
                                                        rhs=xT[:, c, :], start=(c == 0), stop=(c == 7)),
                     reads=["w_in", "xT"], writes=[bk(bi)])
            if j % 4 == 3:
                yield
        P.act(lambda e: e.activation(out=xr[:, :, 3:131], in_=b3(1), func=AF.Copy), reads=[bk(1)], writes=["xr"])
        yield
        for i in range(2):
            P.act(lambda e, i=i: e.activation(out=s1[:, 4 * i:4 * i + 4, :], in_=b3(2 + i), func=AF.Tanh, scale=0.5),
                  reads=[bk(2 + i)], writes=[ks1])
            P.dve(lambda e, i=i: e.scalar_tensor_tensor(out=s1[:, 4 * i:4 * i + 4, :], in0=s1[:, 4 * i:4 * i + 4, :],
                                                         scalar=1.0, in1=b3(2 + i), op0=ALU.add, op1=ALU.mult),
                  reads=[ks1, bk(2 + i)], writes=[ks1])
        P.act(lambda e: e.activation(out=sq[:], in_=b3(0)[:, 0:3, :], func=AF.Square), reads=[bk(0)], writes=["sq"])
        yield
        b5 = b3(1)
        P.pe(lambda e: e.matmul(b5[:, 0, :], lhsT=ones32[:], rhs=sq[:, 0, :], start=True, stop=False),
             reads=["sq", "ones32"], writes=[bk(1)])
        yield
        P.pe(lambda e: e.matmul(b5[:, 0, :], lhsT=ones32[:], rhs=sq[:, 1, :], start=False, stop=True),
             reads=["sq", "ones32"], writes=[bk(1)])
        yield
        P.pe(lambda e: e.matmul(b5[:, 1, :], lhsT=ones32[:], rhs=sq[:, 2, :], start=True, stop=True),
             reads=["sq", "ones32"], writes=[bk(1)])
        yield
        P.act(lambda e: e.activation(out=sr[:, 0, :], in_=b5[:, 0, :], func=AF.Sqrt, bias=cst[:, 0:1], scale=1.0 / 256),
              reads=[bk(1), "cst"], writes=["sr"])
        yield
        P.act(lambda e: e.activation(out=sr[:, 1, :], in_=b5[:, 1, :], func=AF.Sqrt, bias=cst[:, 0:1], scale=1.0 / 128),
              reads=[bk(1), "cst"], writes=["sr"])
        yield
        P.dve(lambda e: e.reciprocal(out=sr[:], in_=sr[:]), reads=["sr"], writes=["sr"])
        yield
        for c in range(3):
            P.dve(lambda e, c=c: e.scalar_tensor_tensor(out=cqn[:, c, :], in0=b3(0)[:, c, :],
                                                         scalar=vec[:, V_QN + c:V_QN + c + 1],
                                                         in1=sr[:, min(c, 2) // 2, :], op0=ALU.mult, op1=ALU.mult),
                  reads=[bk(0), "vec", "sr"], writes=["cqn"])
        P.dve(lambda e: e.tensor_tensor(out=kr1[0:32, :], in0=b3(0)[0:32, 3, :], in1=trg[0:32, 0, :], op=ALU.mult),
              reads=[bk(0), "trg"], writes=["kr1"])
        yield
        P.dve(lambda e: e.tensor_tensor(out=kr2[0:32, :], in0=b3(0)[32:64, 3, :], in1=trg[32:64, 1, :], op=ALU.mult),
              reads=[bk(0), "trg"], writes=["kr2"])
        yield
        for i in range(3):
            P.pool(lambda e, i=i: e.tensor_tensor(out=KrT[32 * i:32 * i + 32, ts_], in0=kr1[0:32, :],
                                                  in1=kr2[0:32, :], op=ALU.add),
                   reads=["kr1", "kr2"], writes=[("Kr", t)])
        yield
        for c in range(4):
            P.dve(lambda e, c=c: e.tensor_scalar(out=acc[:, c, :], in0=xr[:, c, 3:131],
                                                 scalar1=vec[:, V_CW + 4 * c + 3:V_CW + 4 * c + 4],
                                                 scalar2=vec[:, V_CB + c:V_CB + c + 1], op0=ALU.mult, op1=ALU.add),
                  reads=["xr", "vec"], writes=[("acc", c)])
            for k in range(3):
                P.dve(lambda e, c=c, k=k: e.scalar_tensor_tensor(out=acc[:, c, :], in0=xr[:, c, k:k + 128],
                                                                  scalar=vec[:, V_CW + 4 * c + k:V_CW + 4 * c + k + 1],
                                                                  in1=acc[:, c, :], op0=ALU.mult, op1=ALU.add),
                      reads=["xr", "vec", ("acc", c)], writes=[("acc", c)])
            if c % 2 == 1:
                yield
        P.pool(lambda e: e.tensor_copy(out=xr[:, :, 0:3], in_=xr[:, :, 128:131]), reads=["xr"], writes=["xr"])
        yield
        accs = [("acc", c) for c in range(4)]
        P.pool(lambda e: e.tensor_copy(out=xcb[:], in_=acc[:]), reads=accs, writes=["xcb"])
        yield
        for c in range(4):
            P.pe(lambda e, c=c: e.matmul(b3(2)[:, c, :], lhsT=gA[:, c, :], rhs=xcb[:, c, :], start=True, stop=True),
                 reads=["gA", "xcb"], writes=[bk(2)])
        for c in range(4):
            P.pe(lambda e, c=c: e.matmul(b3(3)[:, c, :], lhsT=gX[:, c, :], rhs=xcb[:, c, :], start=True, stop=True),
                 reads=["gX", "xcb"], writes=[bk(3)])
        for c in range(4):
            P.act(lambda e, c=c: e.activation(out=ta[:, c, :], in_=b3(2)[:, c, :], func=AF.Tanh,
                                              bias=der[:, 8 + c:9 + c], scale=0.5), reads=[bk(2), "der"], writes=["ta"])
            P.act(lambda e, c=c: e.activation(out=ti[:, c, :], in_=b3(3)[:, c, :], func=AF.Tanh,
                                              bias=der[:, 12 + c:13 + c], scale=0.5), reads=[bk(3), "der"], writes=["ti"])
        yield
        for c in range(4):
            P.act(lambda e, c=c: e.activation(out=aa[:, c, :], in_=ta[:, c, :], func=AF.Exp,
                                              bias=der[:, 4 + c:5 + c], scale=der[:, 4 + c:5 + c]),
                  reads=["ta", "der"], writes=["aa"])
            P.act(lambda e, c=c: e.activation(out=th[:, c, :], in_=ta[:, c, :], func=AF.Tanh,
                                              bias=der[:, c:c + 1], scale=der[:, c:c + 1]),
                  reads=["ta", "der"], writes=["qsb"])
        P.dve(lambda e: e.tensor_tensor(out=ta[:], in0=aa[:], in1=aa[:], op=ALU.mult), reads=["aa"], writes=["ta"])
        yield
        P.dve(lambda e: e.scalar_tensor_tensor(out=ta[:], in0=ta[:], scalar=1.0, in1=th, op0=ALU.add, op1=ALU.mult),
              reads=["ta", "qsb"], writes=["ta"])
        yield
        P.act(lambda e: e.activation(out=ta[:], in_=ta[:], func=AF.Sqrt), reads=["ta"], writes=["ta"])
        yield
        P.dve(lambda e: e.scalar_tensor_tensor(out=ti[:], in0=ti[:], scalar=1.0, in1=acc[:], op0=ALU.add, op1=ALU.mult),
              reads=["ti"] + accs, writes=["ti"])
        yield
        P.dve(lambda e: e.scalar_tensor_tensor(out=ti[:], in0=ti[:], scalar=0.25, in1=ta[:], op0=ALU.mult, op1=ALU.mult),
              reads=["ti", "ta"], writes=["ti"])
        yield
        yield
        for c in range(4):
            P.dve(lambda e, c=c: e.tensor_tensor_scan(out=hh[:, c, :], data0=aa[:, c, :], data1=ti[:, c, :],
                                                       initial=hc[:, c:c + 1], op0=ALU.mult, op1=ALU.add),
                  reads=["aa", "ti", "hc"], writes=[khh])
        P.pool(lambda e: e.tensor_copy(out=hc[:], in_=hh[:, :, 127]), reads=[khh], writes=["hc"])
        yield
        yield
        for j in range(10):
            bi, jj = (j // 4, j % 4)
            for c in range(2):
                P.pe(lambda e, j=j, c=c, bi=bi, jj=jj: e.matmul(b3(bi)[:, jj, :], lhsT=w_uq[:, c, j * 128:(j + 1) * 128],
                                                                rhs=cqn[:, c, :], start=(c == 0), stop=(c == 1)),
                     reads=["w_uq", "cqn"], writes=[bk(bi)])
        for j in range(4):
            P.pe(lambda e, j=j: e.matmul(b3(3)[:, j, :], lhsT=w_kn[:, j * 128:(j + 1) * 128], rhs=cqn[:, 2, :],
                                         start=True, stop=True), reads=["w_kn", "cqn"], writes=[bk(3)])
        P.act(lambda e: e.activation(out=QnT[:], in_=b3(0), func=AF.Copy), reads=[bk(0)], writes=[kqn])
        yield
        P.act(lambda e: e.activation(out=qsb[:, 0:4, :], in_=b3(1), func=AF.Copy), reads=[bk(1)], writes=["qsb"])
        yield
        P.act(lambda e: e.activation(out=qsb[:, 4:6, :], in_=b3(2)[:, 0:2, :], func=AF.Copy), reads=[bk(2)], writes=["qsb"])
        yield
        P.pe(lambda e: e.matmul(bank[0][:], lhsT=cqn[:, 2, :], rhs=w_v[:], start=True, stop=True),
             reads=["w_v", "cqn"], writes=[bk(0)])
        yield
        yield
        cosb = trg[:, 0, :].unsqueeze(1).broadcast_to([128, 3, 128])
        sinb = trg[:, 1, :].unsqueeze(1).broadcast_to([128, 3, 128])
        P.pool(lambda e: e.tensor_tensor(out=qsb[:, 0:3, :], in0=qsb[:, 0:3, :], in1=cosb, op=ALU.mult),
               reads=["qsb", "trg"], writes=["qsb"])
        yield
        P.pool(lambda e: e.tensor_tensor(out=qsb[:, 3:6, :], in0=qsb[:, 3:6, :], in1=sinb, op=ALU.mult),
               reads=["qsb", "trg"], writes=["qsb"])
        yield
        P.pool(lambda e: e.tensor_tensor(out=QrT[:], in0=qsb[:, 0:3, :], in1=qsb[:, 3:6, :], op=ALU.add),
               reads=["qsb"], writes=[kqr])
        yield
        P.act(lambda e: e.activation(out=KnT[:, :, ts_], in_=b3(3), func=AF.Copy), reads=[bk(3)],
              writes=[("Kn", t)])
        yield
        P.dve(lambda e: e.tensor_copy(out=Vaug[:, t, :, 0:64], in_=bank[0][:].rearrange("p (h v) -> p h v", h=8)),
              reads=[bk(0), "Vall"], writes=[("V", t)])
        yield
        yield

    def attention(t, gen):
        pp = t % 2
        QnT, QrT = QnT2[pp], QrT2[pp]
        kqn, kqr = ("QnT", pp), ("QrT", pp)
        items = []
        for h in range(8):
            nb = (t + 4) // 4
            for b in range(nb):
                items.append((h, b, list(range(4 * b, min(4 * b + 4, t + 1)))))

        def qk(i):
            h, b, kts = items[i]
            sbk = 4 + (i % 2)
            j2, hr = h // 2, (h % 2) * 64
            c3, r3 = h // 3, (h % 3) * 32
            for jj, kt in enumerate(kts):
                ks = slice(kt * 128, (kt + 1) * 128)
                P.pe(lambda e, jj=jj, ks=ks: e.matmul(b3(sbk)[:, jj, :], lhsT=KnT[hr:hr + 64, j2, ks],
                                                       rhs=QnT[hr:hr + 64, j2, :], start=True, stop=False),
                     reads=[("Kn", kt), kqn], writes=[bk(sbk)])
                P.pe(lambda e, jj=jj, ks=ks: e.matmul(b3(sbk)[:, jj, :], lhsT=KrT[r3:r3 + 32, ks],
                                                       rhs=QrT[r3:r3 + 32, c3, :], start=False, stop=True),
                     reads=[("Kr", kt), kqr], writes=[bk(sbk)])

        def rest(i):
            h, b, kts = items[i]
            sbk = 4 + (i % 2)
            obk = 6 + (h % 2)
            n = len(kts)
            pt = PT[i % 2]
            ptk = ("PT", i % 2)
            P.act(lambda e: e.activation(out=pt[:, 0:n, :], in_=b3(sbk)[:, 0:n, :], func=AF.Exp, scale=ATT_SCALE),
                  reads=[bk(sbk)], writes=[ptk])
            if t in kts:
                jd = t - 4 * b
                P.pool(lambda e: e.tensor_tensor(out=pt[:, jd, :], in0=pt[:, jd, :], in1=maskT[:], op=ALU.mult),
                       reads=[ptk, "maskT"], writes=[ptk])
            for jj, kt in enumerate(kts):
                P.pe(lambda e, jj=jj, kt=kt: e.matmul(bank[obk][0:96, 0:128], lhsT=Vaug[:, kt, h, :], rhs=pt[:, jj, :],
                                                      start=(kt == 0), stop=(kt == t)),
                     reads=[("V", kt), "Vall", ptk], writes=[bk(obk)])
            if kts[-1] == t:
                j2, hr = h // 2, (h % 2) * 64
                for hf in range(2):
                    P.dve(lambda e, hf=hf: e.reciprocal(out=rl[32 * hf:32 * hf + 32, :], in_=bank[obk][64:96, 0:128]),
                          reads=[bk(obk)], writes=["rl"])
                P.dve(lambda e: e.scalar_tensor_tensor(
                    out=ym[hr:hr + 64, j2, :], in0=bank[obk][0:64, 0:128],
                    scalar=0.5, in1=rl[0:64, :], op0=ALU.mult, op1=ALU.mult),
                    reads=[bk(obk), "rl"], writes=["ym"])

        state = {"k": 0}

        def adv():
            while state["k"] < len(gen):
                g_ = gen[state["k"]]
                if g_ is None:
                    state["k"] += 1
                    continue
                try:
                    next(g_)
                    return
                except StopIteration:
                    state["k"] += 1

        for i in range(len(items)):
            qk(i)
            if i > 0:
                rest(i - 1)
            adv()
        rest(len(items) - 1)

    def tail_a(t):
        pp = t % 2
        s1, hh = s12[pp], hh2[pp]
        ks1, khh = ("s1", pp), ("hh", pp)
        P.dve(lambda e: e.tensor_tensor(out=yT[:, 0:4, :], in0=s1[:, 0:4, :], in1=hh[:], op=ALU.mult),
              reads=[ks1, khh], writes=["yT"])
        P.dve(lambda e: e.tensor_tensor(out=yT[:, 4:8, :], in0=s1[:, 4:8, :], in1=ym[:], op=ALU.mult),
              reads=[ks1, "ym"], writes=["yT"])

    def tail(t):
        ts_ = slice(t * 128, (t + 1) * 128)
        pp = t % 2
        xin = xin2[pp]
        kx = ("xin", pp)
        for hf in range(2):
            for c in range(8):
                P.pe(lambda e, hf=hf, c=c: e.matmul(bank[hf][:], lhsT=yT[:, c, :], rhs=w_out[:, c, hf * 512:(hf + 1) * 512],
                                                    start=(c == 0), stop=(c == 7)),
                     reads=["yT", "w_out"], writes=[bk(hf)])
        for hf in range(2):
            hs = slice(hf * 512, (hf + 1) * 512)
            P.dve(lambda e, hf=hf, hs=hs: e.scalar_tensor_tensor(out=xin[:, hs], in0=xin[:, hs], scalar=ALPHA, in1=bank[hf][:],
                                                                  op0=ALU.mult, op1=ALU.add),
                  reads=[kx, bk(hf)], writes=[kx])
            yield
            P.dve(lambda e, hf=hf, hs=hs: e.bn_stats(out=st[:, 6 * hf:6 * hf + 6], in_=xin[:, hs]), reads=[kx],
                  writes=["st"])
            yield
        layer_norm_tail(P, xin, st, mv, cst, None, None, None, [kx])
        yield
        for hf in range(2):
            hs = slice(hf * 512, (hf + 1) * 512)
            P.dma("sp", lambda e, hf=hf: e.dma_start(out=lng[:], in_=d["lng"][:, hf * 512:(hf + 1) * 512].partition_broadcast(128)),
                  "lgl", writes=["lng"])
            P.dma("sp", lambda e, hf=hf: e.dma_start(out=lnb[:], in_=d["lnb"][:, hf * 512:(hf + 1) * 512].partition_broadcast(128)),
                  "lbl", writes=["lnb"])
            P.pool(lambda e, hs=hs: e.tensor_tensor(out=xin[:, hs], in0=xin[:, hs], in1=lng[:], op=ALU.mult),
                   reads=[kx, "lng"], writes=[kx])
            yield
            P.pool(lambda e, hs=hs: e.tensor_tensor(out=xin[:, hs], in0=xin[:, hs], in1=lnb[:], op=ALU.add),
                   reads=[kx, "lnb"], writes=[kx])
            yield
        P.dma("sp", lambda e: e.dma_start(out=x_out[ts_, :], in_=xin[:]), "xo%d" % pp, reads=[kx], writes=[("x1d", t)])

    def exhaust(g_):
        if g_ is not None:
            for _ in g_:
                pass

    exhaust(front(0))
    ptail = None
    for t in range(T):
        gfront = front(t + 1) if t + 1 < T else None
        attention(t, [ptail, gfront])
        exhaust(ptail)
        exhaust(gfront)
        tail_a(t)
        ptail = tail(t)
    exhaust(ptail)
    return A


def layer_norm_tail(P, z, st, mv, cst, lng, lnb, _unused, zk):
    P.dve(lambda e: e.bn_aggr(out=mv[:, 0:2], in_=st[:]), reads=["st"], writes=["mv"])
    P.act(lambda e: e.activation(out=mv[:, 2:3], in_=mv[:, 1:2], func=AF.Sqrt, bias=cst[:, 1:2], scale=1.0),
          reads=["mv", "cst"], writes=["mv2"])
    P.dve(lambda e: e.reciprocal(out=mv[:, 2:3], in_=mv[:, 2:3]), reads=["mv2"], writes=["mv2"])
    P.dve(lambda e: e.scalar_tensor_tensor(out=mv[:, 3:4], in0=mv[:, 0:1], scalar=-1.0, in1=mv[:, 2:3],
                                           op0=ALU.mult, op1=ALU.mult), reads=["mv", "mv2"], writes=["mv3"])
    P.act(lambda e: e.activation(out=z[:], in_=z[:], func=AF.Identity, bias=mv[:, 3:4], scale=mv[:, 2:3]),
          reads=zk + ["mv2", "mv3"], writes=zk)
    if lng is not None:
        P.pool(lambda e: e.tensor_tensor(out=z[:], in0=z[:], in1=lng[:], op=ALU.mult), reads=zk + ["lng"], writes=zk)
        P.pool(lambda e: e.tensor_tensor(out=z[:], in0=z[:], in1=lnb[:], op=ALU.add), reads=zk + ["lnb"], writes=zk)


A_SPECS = [("w_in", [1024, 2048], F32), ("w_out", [1024, 1024], F32), ("w_uq", [256, 1280], F32),
           ("w_kn", [128, 512], F32), ("w_v", [128, 512], F32), ("gA", [128, 4, 128], F32),
           ("gX", [128, 4, 128], F32), ("vec", [128, NV], F32), ("lng", [1, 1024], F32), ("lnb", [1, 1024], F32)]


def prep_A(inp):
    f = np.float32
    w_in = np.asarray(inp["ab_w_in"][0], f)
    kr = w_in[:, 1920:1952]
    kr_sw = np.concatenate([kr[:, 16:32], kr[:, 0:16]], axis=1)
    w_inA = np.concatenate([w_in[:, :1920], kr, kr_sw, np.zeros((1024, 64), f)], axis=1)
    uq = np.asarray(inp["mla_w_uq"][0], f).reshape(256, 8, 96)
    nope = uq[:, :, :64].reshape(256, 512)
    rope = uq[:, :, 64:96]
    rsw = np.concatenate([rope[:, :, 16:32], rope[:, :, 0:16]], axis=2)
    z32 = np.zeros((256, 1, 32), f)
    rope9 = np.concatenate([rope, z32], axis=1).reshape(256, 288)
    rsw9 = np.concatenate([rsw, z32], axis=1).reshape(256, 288)
    pad = np.zeros((256, 96), f)
    w_uqA = np.concatenate([nope, rope9, pad, rsw9, pad], axis=1)
    w_uqA = np.concatenate([nope,
                            np.concatenate([rope9[:, 0:96], np.zeros((256, 32), f)], 1),
                            np.concatenate([rope9[:, 96:192], np.zeros((256, 32), f)], 1),
                            np.concatenate([rope9[:, 192:288], np.zeros((256, 32), f)], 1),
                            np.concatenate([rsw9[:, 0:96], np.zeros((256, 32), f)], 1),
                            np.concatenate([rsw9[:, 96:192], np.zeros((256, 32), f)], 1),
                            np.concatenate([rsw9[:, 192:288], np.zeros((256, 32), f)], 1)], axis=1)
    ukv = np.asarray(inp["mla_w_ukv"][0], f).reshape(128, 8, 128)
    w_kn = np.ascontiguousarray(ukv[:, :, :64]).reshape(128, 512)
    w_v = np.ascontiguousarray(ukv[:, :, 64:]).reshape(128, 512)

    def blockdiag(w):
        w = np.asarray(w[0], f)
        o = np.zeros((128, 4, 128), f)
        for h in range(8):
            r = (h % 2) * 64
            o[r:r + 64, h // 2, r:r + 64] = w[h]
        return o

    vec = np.zeros((128, NV), f)
    cw = np.asarray(inp["ab_conv_w"][0], f)
    for c in range(4):
        for k in range(4):
            vec[:, V_CW + 4 * c + k] = cw[k, c * 128:(c + 1) * 128]
    for nm, col in (("ab_conv_b", V_CB), ("ab_gate_a_b", V_BA), ("ab_gate_x_b", V_BX), ("ab_lambda", V_LAM)):
        vec[:, col:col + 4] = np.asarray(inp[nm][0], f).reshape(4, 128).T
    vec[:, V_QN:V_QN + 2] = np.asarray(inp["mla_q_norm"][0], f).reshape(2, 128).T
    vec[:, V_KVN] = np.asarray(inp["mla_kv_norm"][0], f)
    j = np.arange(128) % 16
    vec[:, V_INVF] = (10000.0 ** (-(2.0 * j) / 32.0)).astype(f)
    vec[:, V_SGN] = np.where((np.arange(128) % 32) < 16, -1.0, 1.0)
    return {"w_in": w_inA, "w_out": np.asarray(inp["ab_w_out"][0], f), "w_uq": w_uqA, "w_kn": w_kn, "w_v": w_v,
            "gA": blockdiag(inp["ab_gate_a_w"]), "gX": blockdiag(inp["ab_gate_x_w"]), "vec": vec,
            "lng": np.asarray(inp["ab_ln_g"], f).reshape(1, 1024), "lnb": np.asarray(inp["ab_ln_b"], f).reshape(1, 1024)}


def declare(nc, specs, pfx):
    return {nm: nc.dram_tensor(pfx + nm, shape, dt, kind="ExternalInput").ap() for nm, shape, dt in specs}


def build_program_A(S):
    nc = bass.Bass("TRN2", target_bir_lowering=False)
    P = Prog(nc)
    d = declare(nc, A_SPECS, "a_")
    d["pos"] = nc.dram_tensor("pos", [1, S], I32, kind="ExternalInput").ap()
    x_in = nc.dram_tensor("x", [S, D], F32, kind="ExternalInput").ap()
    x_out = nc.dram_tensor("x1", [S, D], F32, kind="ExternalOutput").ap()
    build_A(nc, P, S, d, x_in, x_out)
    st = P.emit()
    return nc, st


VB_CW, VB_CB, NVB = 0, 96, 120
RB_DTB, RB_ALOG, RB_D, RB_NW, RB_LNG, RB_LNB, RB_CB, NRB = 0, 32, 64, 96, 2144, 3168, 4192, 7264


def build_B(nc, P, S, d, x_in, x_out):
    T = S // 128
    A = Alloc(nc)
    sb, ps = A.sb, A.ps
    ident, maskT, ones32 = make_consts(nc, P, A, "b_")
    bank = [ps("b_bank%d" % i, [128, 512], F32) for i in range(8)]
    rr_state = [0]

    def rr():
        rr_state[0] = (rr_state[0] + 1) % 8
        return rr_state[0]

    def bk(i):
        return ("bank", i)

    def b3(i):
        return bank[i][:].rearrange("p (a b) -> p a b", a=4)

    tri = sb("b_tri", [128, 128], F32)
    u2 = sb("b_u2", [128, 128], F32)
    m025 = sb("b_m025", [128, 128], F32)
    onesb = sb("b_onesb", [128, 128], BF16)
    cst = sb("b_cst", [128, 8], F32)
    P.pool(lambda e: e.memset(tri[:], 1.0), writes=["tri"])
    P.pool(lambda e: e.affine_select(out=tri[:], in_=tri[:], pattern=[[1, 128]], compare_op=ALU.is_ge, fill=0.0,
                                     base=0, channel_multiplier=-1), reads=["tri"], writes=["tri"])
    P.pool(lambda e: e.memset(u2[:], 1.0), writes=["u2"])
    P.pool(lambda e: e.affine_select(out=u2[:], in_=u2[:], pattern=[[-1, 128]], compare_op=ALU.is_gt, fill=0.0,
                                     base=0, channel_multiplier=1), reads=["u2"], writes=["u2"])
    P.pool(lambda e: e.tensor_scalar(out=m025[:], in0=tri[:], scalar1=0.25, scalar2=None, op0=ALU.mult),
           reads=["tri"], writes=["m025"])
    P.pool(lambda e: e.memset(onesb[:], 1.0), writes=["onesb"])
    for i, v in enumerate([1e-6, 1e-5, 0.0, 1.0, 4e-6]):
        P.pool(lambda e, i=i, v=v: e.memset(cst[:, i:i + 1], v), writes=["cst"])
    w_in = sb("b_w_in", [128, 8, 5152], BF16)
    w_out = sb("b_w_out", [128, 16, 1024], BF16)
    vecb = sb("b_vec", [128, NVB], F32)
    cbrow = sb("b_cbrow", [128, 8, 128], BF16)
    dtb = sb("b_dtb", [128, 32], F32)
    aneg = sb("b_aneg", [128, 32], F32)
    cd = sb("b_cd", [128, 32], F32)
    normw = sb("b_normw", [128, 512], F32)
    dg = sb("b_dg", [128, 96, 128], BF16)
    for c in range(0, 8, 2):
        P.dma("pool", lambda e, c=c: e.dma_start(out=w_in[:, c:c + 2, :],
                                                 in_=d["w_in"][c * 128:(c + 2) * 128, :].rearrange("(c p) n -> p c n", p=128)),
              "wlinb", writes=["w_in"], chain=False)
    cb3 = d["row"][:, RB_CB:RB_CB + 3072].rearrange("o (i r c) -> o i r c", r=3, c=128)
    for r_ in range(3):
        P.dma("pool", lambda e, r_=r_: e.dma_start(out=cbrow[32 * r_:32 * r_ + 1, :, :], in_=cb3[:, :, r_, :]),
              "wlinb", writes=["cbrow"], chain=False)
    for c in range(0, 16, 4):
        P.dma("pool", lambda e, c=c: e.dma_start(out=w_out[:, c:c + 4, :],
                                                 in_=d["w_out"][c * 128:(c + 4) * 128, :].rearrange("(c p) n -> p c n", p=128)),
              "wl", writes=["w_out"], chain=False)
    P.dma("sp", lambda e: e.dma_start(out=vecb[:], in_=d["vec"]), "wl2", writes=["vecb"])
    for tl, off, n, key in ((dtb, RB_DTB, 32, "dtb"), (aneg, RB_ALOG, 32, "aneg"), (cd, RB_D, 32, "cd")):
        P.dma("sp", lambda e, tl=tl, off=off, n=n: e.dma_start(out=tl[:], in_=d["row"][:, off:off + n].partition_broadcast(128)),
              "wl2", writes=[key])
    P.act(lambda e: e.activation(out=aneg[:], in_=aneg[:], func=AF.Exp), reads=["aneg"], writes=["aneg"])
    P.dve(lambda e: e.tensor_scalar(out=aneg[:], in0=aneg[:], scalar1=-1.0, scalar2=None, op0=ALU.mult),
          reads=["aneg"], writes=["aneg"])
    P.dve(lambda e: e.tensor_scalar(out=cd[:], in0=cd[:], scalar1=0.5, scalar2=None, op0=ALU.mult),
          reads=["cd"], writes=["cd"])
    for jk in range(96):
        P.dve(lambda e, jk=jk: e.tensor_scalar(out=dg[:, jk, :], in0=ident[:], scalar1=vecb[:, jk:jk + 1], scalar2=None,
                                               op0=ALU.mult), reads=["ident", "vecb"], writes=["dg"])
    hT = sb("b_hT", [128, 2048], F32)
    hTb = sb("b_hTb", [128, 2048], BF16)
    P.pool(lambda e: e.memset(hT[:], 0.0), writes=["hT"])
    P.pool(lambda e: e.memset(hTb[:], 0.0), writes=["hTb"])
    xrb = sb("b_xrb", [128, 24, 131], BF16)
    P.pool(lambda e: e.memset(xrb[:], 0.0), writes=["xrb"])
    xin2 = [sb("b_xin%d" % i, [128, 1024], F32) for i in range(2)]
    xb = sb("b_xb", [128, 1024], BF16)
    xT = sb("b_xT", [128, 8, 128], BF16)
    sm = sb("b_sm", [128, 12, 32], F32)
    Rg = sb("b_Rg", [128, 4, 128], F32)
    es = sb("b_es", [128, 4, 128], F32)
    MT2 = [sb("b_MT%d" % i, [128, 8, 128], BF16) for i in range(2)]
    szg2 = [sb("b_szg%d" % i, [128, 512], F32) for i in range(2)]
    xs2 = sb("b_xs2", [128, 512], F32)
    xf2 = [sb("b_xf%d" % i, [128, 512], BF16) for i in range(2)]
    xsd2 = [sb("b_xsd%d" % i, [128, 512], BF16) for i in range(2)]
    xfd2 = [sb("b_xfd%d" % i, [128, 512], BF16) for i in range(2)]
    tnh = sb("b_tnh", [128, 512], F32)
    lng = normw
    lnb = tnh
    B2 = sb("b_B2", [128, 512], BF16)
    BCT = sb("b_BCT", [128, 8, 128], BF16)
    GTm = sb("b_GTm", [128, 128], F32)
    yb = sb("b_yb", [128, 512], F32)
    ssq = sb("b_ssq", [128, 4], F32)
    yn = xb[:, 0:512]
    junk = xb[:, 512:1024]
    ynT = sb("b_ynT", [128, 16, 128], BF16)
    st = sb("b_st", [128, 12], F32)
    mv = sb("b_mv", [128, 4], F32)

    def silu2(src_bank_ap, out_ap, key_in, key_out, shape3=None):
        P.act(lambda e: e.activation(out=tnh[:] if shape3 is None else tnh[:].rearrange("p (a b) -> p a b", a=4),
                                     in_=src_bank_ap, func=AF.Tanh, scale=0.5), reads=[key_in], writes=["tnh"])
        P.dve(lambda e: e.scalar_tensor_tensor(out=out_ap, in0=tnh[:] if shape3 is None else tnh[:].rearrange("p (a b) -> p a b", a=4),
                                               scalar=1.0, in1=src_bank_ap, op0=ALU.add, op1=ALU.mult),
              reads=["tnh", key_in], writes=[key_out])

    def silu2g(src_bank_ap, out_ap, key_in, key_out, shape3=None):
        tv = tnh[:] if shape3 is None else tnh[:].rearrange("p (a b) -> p a b", a=4)
        P.act(lambda e: e.activation(out=tv, in_=src_bank_ap, func=AF.Tanh, scale=0.5), reads=[key_in], writes=["tnh"])
        yield
        P.dve(lambda e: e.scalar_tensor_tensor(out=out_ap, in0=tv, scalar=1.0, in1=src_bank_ap, op0=ALU.add, op1=ALU.mult),
              reads=["tnh", key_in], writes=[key_out])
        yield

    def lockstep(*gens):
        gens = [g_ for g_ in gens if g_ is not None]
        while gens:
            for g_ in list(gens):
                try:
                    next(g_)
                except StopIteration:
                    gens.remove(g_)

    def load_x(t):
        if t >= T:
            return
        tsl = slice(t * 128, (t + 1) * 128)
        xi = xin2[t % 2]
        P.dma("sp", lambda e: e.dma_start(out=xi[:], in_=x_in[tsl, :]), "xin%d" % (t % 2), reads=[("x1d", t)],
              writes=[("xin", t % 2)])

    load_x(0)
    for t in range(T):
        ts_ = slice(t * 128, (t + 1) * 128)
        xin = xin2[t % 2]
        kx = ("xin", t % 2)
        P.act(lambda e, xin=xin: e.activation(out=xb[:], in_=xin[:], func=AF.Copy), reads=[kx], writes=["xb"])
        r0 = rr()
        b0 = bank[r0][:].bitcast(BF16)
        for c in range(8):
            P.pe(lambda e, c=c, b0=b0: e.transpose(b0[:, c * 128:(c + 1) * 128], xb[:, c * 128:(c + 1) * 128], ident[:]),
                 reads=["xb", "ident"], writes=[bk(r0)])
        P.dve(lambda e, b0=b0: e.tensor_copy(out=xT[:].rearrange("p a b -> p (a b)"), in_=b0), reads=[bk(r0)], writes=["xT"])
        r = rr()
        for c in range(8):
            P.pe(lambda e, c=c, r=r: e.matmul(bank[r][:, 0:32], lhsT=xT[:, c, :], rhs=w_in[:, c, 5120:5152],
                                              start=(c == 0), stop=(c == 7)), reads=["xT", "w_in"], writes=[bk(r)])
        P.dve(lambda e, r=r: e.tensor_tensor(out=sm[:, 0, :], in0=bank[r][:, 0:32], in1=dtb[:], op=ALU.add),
              reads=[bk(r), "dtb"], writes=["sm0"])
        P.act(lambda e: e.activation(out=sm[:, 1, :], in_=sm[:, 0, :], func=AF.Abs), reads=["sm0"], writes=["sm1"])
        P.act(lambda e: e.activation(out=sm[:, 1, :], in_=sm[:, 1, :], func=AF.Exp, scale=-1.0), reads=["sm1"], writes=["sm1"])
        P.act(lambda e: e.activation(out=sm[:, 1, :], in_=sm[:, 1, :], func=AF.Ln, bias=1.0), reads=["sm1"], writes=["sm1"])
        P.dve(lambda e: e.scalar_tensor_tensor(out=sm[:, 2, :], in0=sm[:, 0, :], scalar=0.0, in1=sm[:, 1, :],
                                               op0=ALU.max, op1=ALU.add), reads=["sm0", "sm1"], writes=["dt"])
        P.dve(lambda e: e.tensor_tensor(out=sm[:, 3, :], in0=sm[:, 2, :], in1=aneg[:], op=ALU.mult),
              reads=["dt", "aneg"], writes=["adt"])
        P.dve(lambda e: e.tensor_scalar(out=sm[:, 4, :], in0=sm[:, 2, :], scalar1=0.5, scalar2=None, op0=ALU.mult),
              reads=["dt"], writes=["cxf"])
        for q in range(6):
            r = rr()
            for jj in range(4):
                j = 4 * q + jj
                for c in range(8):
                    P.pe(lambda e, j=j, jj=jj, c=c, r=r: e.matmul(b3(r)[:, jj, :], lhsT=w_in[:, c, 2048 + j * 128:2048 + (j + 1) * 128],
                                                                  rhs=xT[:, c, :], start=(c == 0), stop=(c == 7)),
                         reads=["w_in", "xT"], writes=[bk(r)])
            P.act(lambda e, q=q, r=r: e.activation(out=xrb[:, 4 * q:4 * q + 4, 3:131], in_=b3(r), func=AF.Copy),
                  reads=[bk(r)], writes=["xrb"])

        r = rr()
        for i, lh in enumerate((tri, u2, ones32)):
            P.pe(lambda e, i=i, lh=lh, r=r: e.matmul(bank[r][:, 32 * i:32 * i + 32], lhsT=lh[:], rhs=sm[:, 3, :],
                                                     start=True, stop=True), reads=["adt", "tri", "u2", "ones32"],
                 writes=[bk(r)])
        P.act(lambda e, r=r: e.activation(out=sm[:, 7:10, :].rearrange("p a b -> p (a b)"), in_=bank[r][:, 0:96], func=AF.Exp),
              reads=[bk(r)], writes=["e3"])
        P.dve(lambda e: e.scalar_tensor_tensor(out=sm[:, 5, :], in0=sm[:, 4, :], scalar=0.5, in1=sm[:, 8, :],
                                               op0=ALU.mult, op1=ALU.mult), reads=["cxf", "e3"], writes=["cxfd"])
        P.dve(lambda e: e.tensor_scalar(out=sm[:, 6, :], in0=sm[:, 7, :], scalar1=0.5, scalar2=None, op0=ALU.mult),
              reads=["e3"], writes=["eoff"])
        def conv_tok(r, j0):
            for jj in range(4):
                j = j0 + jj
                osl = bank[r][:, jj * 128:(jj + 1) * 128]
                for k in range(4):
                    P.pe(lambda e, j=j, k=k, osl=osl: e.matmul(osl, lhsT=xrb[:, j, k:k + 128], rhs=dg[:, 4 * j + k, :],
                                                               start=(k == 0), stop=False), reads=["xrb", "dg"], writes=[bk(r)])
                P.pe(lambda e, j=j, osl=osl: e.matmul(osl, lhsT=onesb[32 * (j % 3):32 * (j % 3) + 1, :], rhs=cbrow[32 * (j % 3):32 * (j % 3) + 1, j // 3, :],
                                                      start=False, stop=True), reads=["onesb", "cbrow"], writes=[bk(r)])

        def conv_feat(r, j0):
            for jj in range(4):
                j = j0 + jj
                osl = bank[r][:, jj * 128:(jj + 1) * 128]
                for k in range(4):
                    P.pe(lambda e, j=j, k=k, osl=osl: e.matmul(osl, lhsT=dg[:, 4 * j + k, :], rhs=xrb[:, j, k:k + 128],
                                                               start=(k == 0), stop=False), reads=["xrb", "dg"], writes=[bk(r)])
                P.pe(lambda e, j=j, osl=osl: e.matmul(osl, lhsT=cbrow[32 * (j % 3):32 * (j % 3) + 1, j // 3, :], rhs=onesb[32 * (j % 3):32 * (j % 3) + 1, :],
                                                      start=False, stop=True), reads=["onesb", "cbrow"], writes=[bk(r)])

        r = rr()
        conv_tok(r, 16)
        silu2(bank[r][:], B2[:], bk(r), "B2")
        for i in range(2):
            r = rr()
            conv_feat(r, 16 + 4 * i)
            silu2(b3(r), BCT[:, 4 * i:4 * i + 4, :], bk(r), "BCT", shape3=True)
        def stage1(g):
            gs = slice(g * 512, (g + 1) * 512)
            hs8 = slice(g * 8, (g + 1) * 8)
            pp = g % 2
            szg, xf, xsd, xfd, MT = szg2[pp], xf2[pp], xsd2[pp], xfd2[pp], MT2[pp]
            tv = tnh[:]

            def bc(i, hs8=hs8):
                return sm[:, i, hs8].unsqueeze(2).broadcast_to([128, 8, 64])

            def mkR(hf):
                h4 = slice(g * 8 + hf * 4, g * 8 + hf * 4 + 4)
                P.pool(lambda e: e.tensor_tensor(out=Rg[:], in0=tri[:].unsqueeze(1).broadcast_to([128, 4, 128]),
                                                 in1=sm[:, 3, h4].unsqueeze(2).broadcast_to([128, 4, 128]), op=ALU.mult),
                       reads=["tri", "adt"], writes=["Rg"])

            def seg_mm(r):
                P.pe(lambda e: e.matmul(bank[r][:], lhsT=u2[:], rhs=Rg[:].rearrange("p a b -> p (a b)"), start=True, stop=True),
                     reads=["u2", "Rg"], writes=[bk(r)])

            def seg_exp(r):
                P.act(lambda e: e.activation(out=es[:].rearrange("p a b -> p (a b)"), in_=bank[r][:], func=AF.Exp),
                      reads=[bk(r)], writes=["es"])

            def mk_mt(hf):
                P.dve(lambda e: e.tensor_tensor(out=MT[:, 4 * hf:4 * hf + 4, :], in0=es[:],
                                                in1=GTm[:].unsqueeze(1).broadcast_to([128, 4, 128]), op=ALU.mult),
                      reads=["es", "GTm"], writes=[("MT", pp, hf)])

            mkR(0)
            rg = rr()
            P.pe(lambda e: e.matmul(bank[rg][:, 0:128], lhsT=BCT[:, g, :], rhs=BCT[:, 4 + g, :], start=True, stop=True),
                 reads=["BCT"], writes=[bk(rg)])
            yield
            rc = rr()
            conv_tok(rc, 4 * g)
            yield
            rs0 = rr()
            seg_mm(rs0)
            P.dve(lambda e: e.tensor_tensor(out=GTm[:], in0=bank[rg][:, 0:128], in1=m025[:], op=ALU.mult),
                  reads=[bk(rg), "m025"], writes=["GTm"])
            yield
            P.act(lambda e: e.activation(out=tv, in_=bank[rc][:], func=AF.Tanh, scale=0.5), reads=[bk(rc)], writes=["tnh"])
            rz = rr()
            for c in range(8):
                P.pe(lambda e, c=c: e.matmul(bank[rz][:], lhsT=xT[:, c, :], rhs=w_in[:, c, gs], start=(c == 0),
                                             stop=(c == 7)), reads=["xT", "w_in"], writes=[bk(rz)])
            yield
            seg_exp(rs0)
            yield
            P.dve(lambda e: e.scalar_tensor_tensor(out=xs2[:], in0=tv, scalar=1.0, in1=bank[rc][:], op0=ALU.add, op1=ALU.mult),
                  reads=["tnh", bk(rc)], writes=["xs2"])
            mkR(1)
            yield
            mk_mt(0)
            rs1 = rr()
            seg_mm(rs1)
            yield
            x3 = xs2[:].rearrange("p (h v) -> p h v", h=8)
            P.act(lambda e: e.activation(out=tv, in_=bank[rz][:], func=AF.Tanh, scale=0.5), reads=[bk(rz)], writes=["tnh"])
            P.dve(lambda e: e.tensor_tensor(out=xf[:].rearrange("p (h v) -> p h v", h=8), in0=x3, in1=bc(4),
                                            op=ALU.mult), reads=["xs2", "cxf"], writes=[("xf", pp)])
            P.pool(lambda e: e.tensor_tensor(out=xsd[:].rearrange("p (h v) -> p h v", h=8), in0=x3,
                                             in1=cd[:, hs8].unsqueeze(2).broadcast_to([128, 8, 64]), op=ALU.mult),
                   reads=["xs2", "cd"], writes=[("xsd", pp)])
            yield
            seg_exp(rs1)
            P.pool(lambda e: e.tensor_tensor(out=xfd[:].rearrange("p (h v) -> p h v", h=8), in0=x3, in1=bc(5),
                                             op=ALU.mult), reads=["xs2", "cxfd"], writes=[("xfd", pp)])
            yield
            P.dve(lambda e: e.scalar_tensor_tensor(out=szg[:], in0=tv, scalar=1.0, in1=bank[rz][:], op0=ALU.add, op1=ALU.mult),
                  reads=["tnh", bk(rz)], writes=[("szg", pp)])
            yield
            mk_mt(1)
            yield

        def stage2(g):
            gs = slice(g * 512, (g + 1) * 512)
            hs8 = slice(g * 8, (g + 1) * 8)
            pp = g % 2
            szg, xf, xsd, xfd, MT = szg2[pp], xf2[pp], xsd2[pp], xfd2[pp], MT2[pp]

            def bc(i, hs8=hs8):
                return sm[:, i, hs8].unsqueeze(2).broadcast_to([128, 8, 64])
            P.dma("sp", lambda e, g=g: e.dma_start(out=normw[:], in_=d["row"][:, RB_NW + g * 512:RB_NW + (g + 1) * 512].partition_broadcast(128)),
                  "nwl", writes=["normw"])
            ry = rr()
            P.pe(lambda e, ry=ry: e.matmul(bank[ry][:], lhsT=ident[:], rhs=xsd[:], start=True, stop=False),
                 reads=["ident", ("xsd", pp)], writes=[bk(ry)])
            yield
            for hh in range(8):
                P.pe(lambda e, ry=ry, hh=hh: e.matmul(bank[ry][:, hh * 64:(hh + 1) * 64], lhsT=MT[:, hh, :],
                                                      rhs=xf[:, hh * 64:(hh + 1) * 64], start=False, stop=(hh == 7)),
                     reads=[("MT", pp, hh // 4), ("xf", pp)], writes=[bk(ry)])
                yield
            ro = rr()
            P.pe(lambda e, ro=ro, g=g, gs=gs: e.matmul(bank[ro][:], lhsT=BCT[:, 4 + g, :], rhs=hTb[:, gs], start=True, stop=True),
                 reads=["BCT", ("hTb", g)], writes=[bk(ro)])
            yield
            P.dve(lambda e, ro=ro, bc=bc: e.tensor_tensor(out=yb[:].rearrange("p (h v) -> p h v", h=8),
                                                          in0=bank[ro][:].rearrange("p (h v) -> p h v", h=8), in1=bc(6), op=ALU.mult),
                  reads=[bk(ro), "eoff"], writes=["yb"])
            yield
            P.dve(lambda e, ry=ry: e.tensor_tensor(out=yb[:], in0=yb[:], in1=bank[ry][:], op=ALU.add),
                  reads=["yb", bk(ry)], writes=["yb"])
            yield
            P.dve(lambda e: e.tensor_tensor(out=yb[:], in0=yb[:], in1=szg[:], op=ALU.mult), reads=["yb", ("szg", pp)], writes=["yb"])
            yield
            P.act(lambda e, g=g: e.activation(out=junk, in_=yb[:], func=AF.Square, accum_out=ssq[:, g:g + 1]),
                  reads=["yb"], writes=["junk", "ssq"])
            yield
            P.act(lambda e, g=g: e.activation(out=ssq[:, g:g + 1], in_=ssq[:, g:g + 1], func=AF.Sqrt, bias=cst[:, 4:5],
                                              scale=1.0 / 512), reads=["ssq", "cst"], writes=["ssq"])
            yield
            P.dve(lambda e, g=g: e.reciprocal(out=ssq[:, g:g + 1], in_=ssq[:, g:g + 1]), reads=["ssq"], writes=["ssq"])
            yield
            P.dve(lambda e, g=g: e.scalar_tensor_tensor(out=yn, in0=yb[:], scalar=ssq[:, g:g + 1], in1=normw[:],
                                                        op0=ALU.mult, op1=ALU.mult), reads=["yb", "ssq", "normw"], writes=["yn"])
            yield
            r = rr()
            bt = bank[r][:].bitcast(BF16)
            for c in range(4):
                P.pe(lambda e, c=c, bt=bt: e.transpose(bt[:, c * 128:(c + 1) * 128], xb[:, c * 128:(c + 1) * 128], ident[:]),
                     reads=["yn", "ident"], writes=[bk(r)])
                yield
            P.act(lambda e, g=g, bt=bt: e.activation(out=ynT[:, 4 * g:4 * g + 4, :].rearrange("p a b -> p (a b)"), in_=bt[:, 0:512],
                                                     func=AF.Copy), reads=[bk(r)], writes=[("ynT", g)])
            yield
            r = rr()
            P.pe(lambda e, r=r, g=g: e.matmul(bank[r][:], lhsT=B2[:, g * 128:(g + 1) * 128], rhs=xfd[:], start=True, stop=True),
                 reads=["B2", ("xfd", pp)], writes=[bk(r)])
            yield
            h3 = hT[:, gs].rearrange("p (h v) -> p h v", h=8)
            P.dve(lambda e, h3=h3, bc=bc: e.tensor_tensor(out=h3, in0=h3, in1=bc(9), op=ALU.mult),
                  reads=[("hT", g), "e3"], writes=[("hT", g)])
            yield
            P.dve(lambda e, r=r, gs=gs: e.tensor_tensor(out=hT[:, gs], in0=hT[:, gs], in1=bank[r][:], op=ALU.add),
                  reads=[("hT", g), bk(r)], writes=[("hT", g)])
            yield
            P.pool(lambda e, gs=gs: e.tensor_copy(out=hTb[:, gs], in_=hT[:, gs]), reads=[("hT", g)], writes=[("hTb", g)])
            yield
        lockstep(stage1(0))
        for g in range(4):
            lockstep(stage1(g + 1) if g + 1 < 4 else None, stage2(g))
        P.pool(lambda e: e.tensor_copy(out=xrb[:, :, 0:3], in_=xrb[:, :, 128:131]), reads=["xrb"], writes=["xrb"])
        load_x(t + 1)
        ynk = [("ynT", g) for g in range(4)]
        rs = [rr(), rr()]
        for hf in range(2):
            for c in range(16):
                P.pe(lambda e, hf=hf, c=c, r=rs[hf]: e.matmul(bank[r][:], lhsT=ynT[:, c, :], rhs=w_out[:, c, hf * 512:(hf + 1) * 512],
                                                              start=(c == 0), stop=(c == 15)), reads=ynk + ["w_out"], writes=[bk(rs[hf])])
        for hf in range(2):
            hs = slice(hf * 512, (hf + 1) * 512)
            P.dve(lambda e, hs=hs, r=rs[hf], xin=xin: e.scalar_tensor_tensor(out=xin[:, hs], in0=xin[:, hs], scalar=ALPHA, in1=bank[r][:],
                                                                             op0=ALU.mult, op1=ALU.add), reads=[kx, bk(rs[hf])], writes=[kx])
            P.dve(lambda e, hf=hf, hs=hs, xin=xin: e.bn_stats(out=st[:, 6 * hf:6 * hf + 6], in_=xin[:, hs]), reads=[kx], writes=["st"])
        layer_norm_tail(P, xin, st, mv, cst, None, None, None, [kx])
        for hf in range(2):
            hs = slice(hf * 512, (hf + 1) * 512)
            P.dma("sp", lambda e, hf=hf: e.dma_start(out=lng[:], in_=d["row"][:, RB_LNG + hf * 512:RB_LNG + (hf + 1) * 512].partition_broadcast(128)),
                  "lgl", writes=["normw"])
            P.dma("sp", lambda e, hf=hf: e.dma_start(out=lnb[:], in_=d["row"][:, RB_LNB + hf * 512:RB_LNB + (hf + 1) * 512].partition_broadcast(128)),
                  "lbl", writes=["tnh"])
            P.pool(lambda e, hs=hs, xin=xin: e.tensor_tensor(out=xin[:, hs], in0=xin[:, hs], in1=lng[:], op=ALU.mult), reads=[kx, "normw"], writes=[kx])
            P.pool(lambda e, hs=hs, xin=xin: e.tensor_tensor(out=xin[:, hs], in0=xin[:, hs], in1=lnb[:], op=ALU.add), reads=[kx, "tnh"], writes=[kx])
        P.dma("sp", lambda e, ts_=ts_, xin=xin: e.dma_start(out=x_out[ts_, :], in_=xin[:]), "xo%d" % (t % 2), reads=[kx], writes=[("outd", t)])
    return A


B_SPECS = [("w_in", [1024, 5152], F32), ("w_out", [2048, 1024], F32), ("vec", [128, NVB], F32), ("row", [1, NRB], F32)]


def prep_B(inp):
    f = np.float32
    vec = np.zeros((128, NVB), f)
    cw = np.asarray(inp["ssd_conv_w"][0], f)
    for j in range(24):
        for k in range(4):
            vec[:, VB_CW + 4 * j + k] = cw[k, j * 128:(j + 1) * 128]
    cb = np.asarray(inp["ssd_conv_b"][0], f)
    vec[:, VB_CB:VB_CB + 24] = cb.reshape(24, 128).T
    row = np.zeros((1, NRB), f)
    row[0, RB_DTB:RB_DTB + 32] = np.asarray(inp["ssd_dt_bias"][0], f)
    row[0, RB_ALOG:RB_ALOG + 32] = np.asarray(inp["ssd_a_log"][0], f)
    row[0, RB_D:RB_D + 32] = np.asarray(inp["ssd_d"][0], f)
    row[0, RB_NW:RB_NW + 2048] = np.asarray(inp["ssd_norm"][0], f)
    row[0, RB_LNG:RB_LNG + 1024] = np.asarray(inp["ssd_ln_g"][0], f)
    row[0, RB_LNB:RB_LNB + 1024] = np.asarray(inp["ssd_ln_b"][0], f)
    row[0, RB_CB:RB_CB + 3072] = cb
    return {"w_in": np.asarray(inp["ssd_w_in"][0], f), "w_out": np.asarray(inp["ssd_w_out"][0], f), "vec": vec, "row": row}


def build_program_B(S):
    nc = bass.Bass("TRN2", target_bir_lowering=False)
    P = Prog(nc)
    d = declare(nc, B_SPECS, "b_")
    x_in = nc.dram_tensor("x1", [S, D], F32, kind="ExternalInput").ap()
    x_out = nc.dram_tensor("out", [S, D], F32, kind="ExternalOutput").ap()
    build_B(nc, P, S, d, x_in, x_out)
    st = P.emit()
    return nc, st


_CACHE = {}


def build_program_fused(S):
    nc = bass.Bass("TRN2", target_bir_lowering=False)
    P = Prog(nc)
    dA = declare(nc, A_SPECS, "a_")
    dA["pos"] = nc.dram_tensor("pos", [1, S], I32, kind="ExternalInput").ap()
    dB = declare(nc, B_SPECS, "b_")
    x_in = nc.dram_tensor("x", [S, D], F32, kind="ExternalInput").ap()
    x1 = nc.dram_tensor("x1_scratch", [S, D], F32).ap()
    x_out = nc.dram_tensor("out", [S, D], F32, kind="ExternalOutput").ap()
    A = build_A(nc, P, S, dA, x_in, x1)
    keep = {k: v for k, v in P.last_writer.items() if isinstance(k, tuple) and k[0] == "x1d"}
    P.barrier()
    P.last_writer.update(keep)
    A.free()
    build_B(nc, P, S, dB, x1, x_out)
    st = P.emit()
    return nc, st


def kernel(**inputs):
    S = SEQ
    x = np.ascontiguousarray(np.asarray(inputs["x"], np.float32))
    pos = np.ascontiguousarray(np.asarray(inputs["positions"], np.int32))
    hpA = prep_A(inputs)
    hpB = prep_B(inputs)
    mode = "fused"
    if mode == "split":
        if "A" not in _CACHE:
            _CACHE["A"] = build_program_A(S)[0]
            _CACHE["B"] = build_program_B(S)[0]
        mapsA = []
        for b in range(NCORES):
            m = {"a_" + k: v for k, v in hpA.items()}
            m["x"] = x[b]
            m["pos"] = pos[b:b + 1]
            mapsA.append(m)
        resA = run_bass_kernel_spmd(_CACHE["A"], mapsA, core_ids=list(range(NCORES)))
        mapsB = []
        for b in range(NCORES):
            m = {"b_" + k: v for k, v in hpB.items()}
            m["x1"] = np.ascontiguousarray(np.asarray(resA.results[b]["x1"], np.float32))
            mapsB.append(m)
        resB = run_bass_kernel_spmd(_CACHE["B"], mapsB, core_ids=list(range(NCORES)))
        return np.stack([np.asarray(resB.results[b]["out"], np.float32) for b in range(NCORES)], axis=0)
    if "F" not in _CACHE:
        _CACHE["F"] = build_program_fused(S)[0]
    maps = []
    for b in range(NCORES):
        m = {"a_" + k: v for k, v in hpA.items()}
        m.update({"b_" + k: v for k, v in hpB.items()})
        m["x"] = x[b]
        m["pos"] = pos[b:b + 1]
        maps.append(m)
    res = run_bass_kernel_spmd(_CACHE["F"], maps, core_ids=list(range(NCORES)))
    return np.stack([np.asarray(res.results[b]["out"], np.float32) for b in range(NCORES)], axis=0)
```

```python
import math
import numpy as np
import concourse.bass as bass
import concourse.mybir as mybir
from concourse.bass_utils import run_bass_kernel_spmd

F32 = mybir.dt.float32
BF16 = mybir.dt.bfloat16
I32 = mybir.dt.int32
AF = mybir.ActivationFunctionType
ALU = mybir.AluOpType

D = 1024
NCORES = 8
SEQ = 4096
ALPHA = 4.0 ** 0.25
MAGIC = 12582912.0
C1 = 6.28125
C2 = 2.0 * math.pi - 6.28125


class Op:
    __slots__ = ("eng", "fn", "deps", "is_dma", "sem", "val", "marked", "dma_wait")

    def __init__(self, eng, fn, is_dma=False, sem=None):
        self.eng = eng
        self.fn = fn
        self.deps = []
        self.is_dma = is_dma
        self.sem = sem
        self.val = None
        self.marked = False
        self.dma_wait = {}


class Prog:
    ENGS = ("pe", "act", "dve", "pool", "sp")

    def __init__(self, nc):
        self.nc = nc
        self.eobj = {"pe": nc.tensor, "act": nc.scalar, "dve": nc.vector,
                     "pool": nc.gpsimd, "sp": nc.sync}
        self.ops = []
        self.last_writer = {}
        self.readers = {}
        self.dma_count = {}
        self.dma_last = {}
        self.last_on = {}
        self.bar = {}

    def _add(self, op, reads, writes):
        deps = op.deps

        def need(p):
            if p is None or p is op:
                return
            if p.is_dma:
                op.dma_wait[p.sem] = self.dma_count[p.sem]
            else:
                deps.append(p)

        b = self.bar.pop(op.eng, None)
        if b is not None:
            for p in b[0]:
                need(p)
            for s, c in b[1].items():
                op.dma_wait[s] = c
        for k in reads:
            w = self.last_writer.get(k)
            if w is not None:
                if (not w.is_dma) and (not op.is_dma) and w.eng == op.eng == "pe":
                    continue
                need(w)
        strict = op.eng != "pe"
        for k in writes:
            w = self.last_writer.get(k)
            if w is not None:
                if w.is_dma or op.is_dma or w.eng != op.eng or strict:
                    need(w)
            for r in self.readers.get(k, ()):
                if r.is_dma or op.is_dma or r.eng != op.eng or strict:
                    need(r)
        for k in writes:
            self.last_writer[k] = op
            self.readers[k] = []
        for k in reads:
            self.readers.setdefault(k, []).append(op)
        self.ops.append(op)
        if not op.is_dma:
            self.last_on[op.eng] = op
        return op

    def op(self, eng, fn, reads=(), writes=()):
        return self._add(Op(eng, fn), reads, writes)

    def dma(self, eng, fn, sem, reads=(), writes=(), chain=True):
        o = Op(eng, fn, is_dma=True, sem=sem)
        if chain and sem in self.dma_last:
            o.dma_wait[sem] = self.dma_count[sem]
        self.dma_count.setdefault(sem, 0)
        self._add(o, reads, writes)
        self.dma_count[sem] += 1
        o.val = self.dma_count[sem]
        self.dma_last[sem] = o
        return o

    def pe(self, fn, reads=(), writes=()):
        return self.op("pe", fn, reads, writes)

    def act(self, fn, reads=(), writes=()):
        return self.op("act", fn, reads, writes)

    def dve(self, fn, reads=(), writes=()):
        return self.op("dve", fn, reads, writes)

    def pool(self, fn, reads=(), writes=()):
        return self.op("pool", fn, reads, writes)

    def barrier(self):
        lasts = list(self.last_on.values())
        dm = dict(self.dma_count)
        for e in self.ENGS:
            self.bar[e] = (lasts, dm)
        self.last_writer = {}
        self.readers = {}

    def emit(self):
        nc = self.nc
        for o in self.ops:
            for d in o.deps:
                d.marked = True
        cnt = {e: 0 for e in self.ENGS}
        for o in self.ops:
            if not o.is_dma and o.marked:
                cnt[o.eng] += 1
                o.val = cnt[o.eng]
        ctr = {e: nc.alloc_semaphore(name="ctr_" + e) for e in self.ENGS}
        dsem = {s: nc.alloc_semaphore(name="dma_%d" % i) for i, s in enumerate(self.dma_count)}
        waited = {e: {} for e in self.ENGS}
        nwait = 0
        for o in self.ops:
            eng = self.eobj[o.eng]
            need = {}
            for d in o.deps:
                key = ("c", d.eng)
                if need.get(key, (None, 0))[1] < d.val:
                    need[key] = (ctr[d.eng], d.val)
            for s, c in o.dma_wait.items():
                key = ("d", s)
                if need.get(key, (None, 0))[1] < 16 * c:
                    need[key] = (dsem[s], 16 * c)
            w = waited[o.eng]
            for key, (h, v) in need.items():
                if w.get(key, 0) >= v:
                    continue
                eng.wait_ge(h, v)
                nwait += 1
                w[key] = v
            ins = o.fn(eng)
            if o.is_dma:
                ins.then_inc(dsem[o.sem], 16)
            elif o.marked:
                ins.then_inc(ctr[o.eng], 1)
        eng = self.eobj["sp"]
        for s, c in self.dma_count.items():
            eng.wait_ge(dsem[s], 16 * c)
        return dict(n_ops=len(self.ops), n_wait=nwait, marked=cnt)


class Alloc:
    def __init__(self, nc):
        self.nc = nc
        self.guards = []

    def sb(self, name, shape, dt=F32):
        g = self.nc.sbuf_tensor("s_" + name, list(shape), dt)
        t = g.__enter__()
        self.guards.append(g)
        return t

    def ps(self, name, shape, dt=F32):
        g = self.nc.psum_tensor("p_" + name, list(shape), dt)
        t = g.__enter__()
        self.guards.append(g)
        return t

    def free(self):
        for g in reversed(self.guards):
            g.__exit__(None, None, None)
        self.guards = []


def make_consts(nc, P, A, pfx):
    ident = A.sb(pfx + "ident", [128, 128], BF16)
    maskT = A.sb(pfx + "maskT", [128, 128], BF16)
    ones32 = A.sb(pfx + "ones32", [128, 128], F32)
    P.pool(lambda e: e.memset(ident[:], 1.0), writes=["ident"])
    P.pool(lambda e: e.affine_select(out=ident[:], in_=ident[:], pattern=[[-1, 128]],
                                     compare_op=ALU.is_equal, fill=0.0, base=0,
                                     channel_multiplier=1), reads=["ident"], writes=["ident"])
    P.pool(lambda e: e.memset(maskT[:], 1.0), writes=["maskT"])
    P.pool(lambda e: e.affine_select(out=maskT[:], in_=maskT[:], pattern=[[1, 128]],
                                     compare_op=ALU.is_ge, fill=0.0, base=0,
                                     channel_multiplier=-1), reads=["maskT"], writes=["maskT"])
    P.pool(lambda e: e.memset(ones32[:], 1.0), writes=["ones32"])
    return ident, maskT, ones32


V_CW, V_CB, V_BA, V_BX, V_LAM, V_QN, V_KVN, V_INVF, V_SGN, NV = 0, 16, 20, 24, 28, 32, 34, 35, 36, 40
ATT_SCALE = 96.0 ** -0.5


def build_A(nc, P, S, d, x_in, x_out):
    T = S // 128
    A = Alloc(nc)
    sb, ps = A.sb, A.ps
    ident, maskT, ones32 = make_consts(nc, P, A, "a_")
    bank = [ps("a_bank%d" % i, [128, 512], F32) for i in range(8)]

    def bk(i):
        return ("bank", i)

    w_in = sb("a_w_in", [128, 8, 2048], BF16)
    w_out = sb("a_w_out", [128, 8, 1024], BF16)
    w_uq = sb("a_w_uq", [128, 2, 1280], BF16)
    w_kn = sb("a_w_kn", [128, 512], BF16)
    w_v = sb("a_w_v", [128, 512], BF16)
    gA = sb("a_gA", [128, 4, 128], BF16)
    gX = sb("a_gX", [128, 4, 128], BF16)
    vec = sb("a_vec", [128, NV], F32)
    lng = sb("a_lng", [128, 512], F32)
    lnb = sb("a_lnb", [128, 512], F32)
    cst = sb("a_cst", [128, 8], F32)
    for i, v in enumerate([1e-6, 1e-5, math.pi / 2, 0.0]):
        P.pool(lambda e, i=i, v=v: e.memset(cst[:, i:i + 1], v), writes=["cst"])
    for c in range(0, 8, 2):
        P.dma("pool", lambda e, c=c: e.dma_start(out=w_in[:, c:c + 2, :],
                                                 in_=d["w_in"][c * 128:(c + 2) * 128, :].rearrange("(c p) n -> p c n", p=128)),
              "wl", writes=["w_in"], chain=False)
    for c in range(0, 8, 4):
        P.dma("pool", lambda e, c=c: e.dma_start(out=w_out[:, c:c + 4, :],
                                                 in_=d["w_out"][c * 128:(c + 4) * 128, :].rearrange("(c p) n -> p c n", p=128)),
              "wl", writes=["w_out"], chain=False)
    P.dma("pool", lambda e: e.dma_start(out=w_uq[:], in_=d["w_uq"].rearrange("(c p) n -> p c n", p=128)),
          "wl", writes=["w_uq"], chain=False)
    P.dma("pool", lambda e: e.dma_start(out=w_kn[:], in_=d["w_kn"]), "wl", writes=["w_kn"], chain=False)
    P.dma("pool", lambda e: e.dma_start(out=w_v[:], in_=d["w_v"]), "wl", writes=["w_v"], chain=False)
    P.dma("pool", lambda e: e.dma_start(out=gA[:], in_=d["gA"]), "wl", writes=["gA"], chain=False)
    P.dma("pool", lambda e: e.dma_start(out=gX[:], in_=d["gX"]), "wl", writes=["gX"], chain=False)
    P.dma("sp", lambda e: e.dma_start(out=vec[:], in_=d["vec"]), "wl2", writes=["vec"])

    der = sb("a_der", [128, 16], F32)
    tmp4 = sb("a_tmp4", [128, 4], F32)
    P.act(lambda e: e.activation(out=tmp4[:], in_=vec[:, V_LAM:V_LAM + 4], func=AF.Exp, scale=-1.0),
          reads=["vec"], writes=["tmp4"])
    P.act(lambda e: e.activation(out=tmp4[:], in_=tmp4[:], func=AF.Ln, bias=1.0), reads=["tmp4"], writes=["tmp4"])
    P.dve(lambda e: e.tensor_scalar(out=der[:, 0:4], in0=tmp4[:], scalar1=4.0, scalar2=None, op0=ALU.mult),
          reads=["tmp4"], writes=["der"])
    P.dve(lambda e: e.tensor_scalar(out=der[:, 4:8], in0=tmp4[:], scalar1=-4.0, scalar2=None, op0=ALU.mult),
          reads=["tmp4"], writes=["der"])
    P.dve(lambda e: e.tensor_scalar(out=der[:, 8:16], in0=vec[:, V_BA:V_BA + 8], scalar1=0.5, scalar2=None,
                                    op0=ALU.mult), reads=["vec"], writes=["der"])

    trig_d = nc.dram_tensor("a_trig", [128, 2, S], F32).ap()
    CB = min(4096, S)
    A2 = Alloc(nc)
    pi_t = A2.sb("a_pi", [128, CB], I32)
    tA = A2.sb("a_tA", [128, CB], F32)
    tB = A2.sb("a_tB", [128, CB], F32)
    tC = A2.sb("a_tC", [128, 2, CB], F32)
    for blk in range(S // CB):
        cs = slice(blk * CB, (blk + 1) * CB)
        P.dma("sp", lambda e, cs=cs: e.dma_start(out=pi_t[:], in_=d["pos"][:, cs].partition_broadcast(128)),
              "trg", writes=["pi"])
        P.dve(lambda e: e.tensor_copy(out=tA[:], in_=pi_t[:]), reads=["pi"], writes=["tA"])
        P.dve(lambda e: e.tensor_scalar(out=tA[:], in0=tA[:], scalar1=vec[:, V_INVF:V_INVF + 1], scalar2=None,
                                        op0=ALU.mult), reads=["tA", "vec"], writes=["tA"])
        P.dve(lambda e: e.tensor_scalar(out=tB[:], in0=tA[:], scalar1=1.0 / (2 * math.pi), scalar2=MAGIC,
                                        op0=ALU.mult, op1=ALU.add), reads=["tA"], writes=["tB"])
        P.dve(lambda e: e.tensor_scalar(out=tB[:], in0=tB[:], scalar1=-MAGIC, scalar2=None, op0=ALU.add),
              reads=["tB"], writes=["tB"])
        P.dve(lambda e: e.scalar_tensor_tensor(out=tA[:], in0=tB[:], scalar=-C1, in1=tA[:], op0=ALU.mult,
                                               op1=ALU.add), reads=["tA", "tB"], writes=["tA"])
        P.dve(lambda e: e.scalar_tensor_tensor(out=tA[:], in0=tB[:], scalar=-C2, in1=tA[:], op0=ALU.mult,
                                               op1=ALU.add), reads=["tA", "tB"], writes=["tA"])
        P.dve(lambda e: e.tensor_scalar(out=tA[:], in0=tA[:], scalar1=3.1415925, scalar2=-3.1415925,
                                        op0=ALU.min, op1=ALU.max), reads=["tA"], writes=["tA"])
        P.act(lambda e: e.activation(out=tB[:], in_=tA[:], func=AF.Abs), reads=["tA"], writes=["tB"])
        P.act(lambda e: e.activation(out=tC[:, 0, :], in_=tB[:], func=AF.Sin, bias=cst[:, 2:3], scale=-1.0),
              reads=["tB", "cst"], writes=["tC"])
        P.act(lambda e: e.activation(out=tC[:, 1, :], in_=tA[:], func=AF.Sin, scale=vec[:, V_SGN:V_SGN + 1]),
              reads=["tA", "vec"], writes=["tC"])
        P.dma("sp", lambda e, cs=cs: e.dma_start(out=trig_d[:, :, cs], in_=tC[:]), "trg",
              reads=["tC"], writes=["trig_d"])

    keepd = {k: v for k, v in P.last_writer.items() if k == "trig_d"}
    P.barrier()
    P.last_writer.update(keepd)
    A2.free()
    KnT = sb("a_KnT", [128, 4, S], BF16)
    KrT = sb("a_KrT", [128, S], BF16)
    Vaug = sb("a_Vaug", [128, T, 8, 96], BF16)
    P.pool(lambda e: e.memset(Vaug[:], 1.0), writes=["Vall"])
    xin2 = [sb("a_xin%d" % i, [128, 1024], F32) for i in range(2)]
    xb = sb("a_xb", [128, 1024], BF16)
    xT = sb("a_xT", [128, 8, 128], BF16)
    xr = sb("a_xr", [128, 4, 131], F32)
    s12 = [sb("a_s1%d" % i, [128, 8, 128], F32) for i in range(2)]
    sq = sb("a_sq", [128, 3, 128], F32)
    sr = sb("a_sr", [128, 2, 128], F32)
    cqn = sb("a_cqn", [128, 3, 128], BF16)
    trg = sb("a_trg", [128, 2, 128], F32)
    kr1 = sb("a_kr1", [128, 128], F32)
    kr2 = sb("a_kr2", [128, 128], F32)
    acc = sb("a_acc", [128, 4, 128], F32)
    xcb = sb("a_xcb", [128, 4, 128], BF16)
    ta = sb("a_ta", [128, 4, 128], F32)
    ti = sb("a_ti", [128, 4, 128], F32)
    aa = sb("a_aa", [128, 4, 128], F32)
    hh2 = [sb("a_hh%d" % i, [128, 4, 128], F32) for i in range(2)]
    hc = sb("a_hc", [128, 4], F32)
    qsb = sb("a_qsb", [128, 6, 128], F32)
    th = qsb[:, 0:4, :]
    QnT2 = [sb("a_QnT%d" % i, [128, 4, 128], BF16) for i in range(2)]
    QrT2 = [sb("a_QrT%d" % i, [128, 3, 128], BF16) for i in range(2)]
    PT = [sb("a_PT%d" % i, [128, 4, 128], BF16) for i in range(2)]
    rl = sb("a_rl", [128, 128], F32)
    ym = sb("a_ym", [128, 4, 128], F32)
    yT = sb("a_yT", [128, 8, 128], BF16)
    st = sb("a_st", [128, 12], F32)
    mv = sb("a_mv", [128, 4], F32)
    P.pool(lambda e: e.memset(xr[:], 0.0), writes=["xr"])
    P.pool(lambda e: e.memset(hc[:], 0.0), writes=["hc"])

    def b3(i):
        return bank[i][:].rearrange("p (a b) -> p a b", a=4)

    def front(t):
        ts_ = slice(t * 128, (t + 1) * 128)
        pp = t % 2
        xin, s1, hh, QnT, QrT = xin2[pp], s12[pp], hh2[pp], QnT2[pp], QrT2[pp]
        kx, ks1, khh, kqn, kqr = ("xin", pp), ("s1", pp), ("hh", pp), ("QnT", pp), ("QrT", pp)
        P.dma("sp", lambda e: e.dma_start(out=xin[:], in_=x_in[ts_, :]), "xin%d" % pp, writes=[kx])
        P.dma("sp", lambda e: e.dma_start(out=trg[:], in_=trig_d[:, :, ts_]), "trl", reads=["trig_d"], writes=["trg"])
        P.act(lambda e: e.activation(out=xb[:], in_=xin[:], func=AF.Copy), reads=[kx], writes=["xb"])
        yield
        b0 = bank[0][:].bitcast(BF16)
        for c in range(8):
            P.pe(lambda e, c=c: e.transpose(b0[:, c * 128:(c + 1) * 128], xb[:, c * 128:(c + 1) * 128], ident[:]),
                 reads=["xb", "ident"], writes=[bk(0)])
        P.dve(lambda e: e.tensor_copy(out=xT[:].rearrange("p a b -> p (a b)"), in_=b0), reads=[bk(0)], writes=["xT"])
        yield
        yield
        ibank = [1, 2, 3, 0]
        for j in range(16):
            bi = ibank[j // 4]
            for c in range(8):
                P.pe(lambda e, j=j, c=c, bi=bi: e.matmul(b3(bi)[:, j % 4, :], lhsT=w_in[:, c, j * 128:(j + 1) * 128],
                                                        rhs=xT[:, c, :], start=(c == 0), stop=(c == 7)),
                     reads=["w_in", "xT"], writes=[bk(bi)])
            if j % 4 == 3:
                yield
        P.act(lambda e: e.activation(out=xr[:, :, 3:131], in_=b3(1), func=AF.Copy), reads=[bk(1)], writes=["xr"])
        yield
        for i in range(2):
            P.act(lambda e, i=i: e.activation(out=s1[:, 4 * i:4 * i + 4, :], in_=b3(2 + i), func=AF.Tanh, scale=0.5),
                  reads=[bk(2 + i)], writes=[ks1])
            P.dve(lambda e, i=i: e.scalar_tensor_tensor(out=s1[:, 4 * i:4 * i + 4, :], in0=s1[:, 4 * i:4 * i + 4, :],
                                                         scalar=1.0, in1=b3(2 + i), op0=ALU.add, op1=ALU.mult),
                  reads=[ks1, bk(2 + i)], writes=[ks1])
        P.act(lambda e: e.activation(out=sq[:], in_=b3(0)[:, 0:3, :], func=AF.Square), reads=[bk(0)], writes=["sq"])
        yield
        b5 = b3(1)
        P.pe(lambda e: e.matmul(b5[:, 0, :], lhsT=ones32[:], rhs=sq[:, 0, :], start=True, stop=False),
             reads=["sq", "ones32"], writes=[bk(1)])
        yield
        P.pe(lambda e: e.matmul(b5[:, 0, :], lhsT=ones32[:], rhs=sq[:, 1, :], start=False, stop=True),
             reads=["sq", "ones32"], writes=[bk(1)])
        yield
        P.pe(lambda e: e.matmul(b5[:, 1, :], lhsT=ones32[:], rhs=sq[:, 2, :], start=True, stop=True),
             reads=["sq", "ones32"], writes=[bk(1)])
        yield
        P.act(lambda e: e.activation(out=sr[:, 0, :], in_=b5[:, 0, :], func=AF.Sqrt, bias=cst[:, 0:1], scale=1.0 / 256),
              reads=[bk(1), "cst"], writes=["sr"])
        yield
        P.act(lambda e: e.activation(out=sr[:, 1, :], in_=b5[:, 1, :], func=AF.Sqrt, bias=cst[:, 0:1], scale=1.0 / 128),
              reads=[bk(1), "cst"], writes=["sr"])
        yield
        P.dve(lambda e: e.reciprocal(out=sr[:], in_=sr[:]), reads=["sr"], writes=["sr"])
        yield
        for c in range(3):
            P.dve(lambda e, c=c: e.scalar_tensor_tensor(out=cqn[:, c, :], in0=b3(0)[:, c, :],
                                                         scalar=vec[:, V_QN + c:V_QN + c + 1],
                                                         in1=sr[:, min(c, 2) // 2, :], op0=ALU.mult, op1=ALU.mult),
                  reads=[bk(0), "vec", "sr"], writes=["cqn"])
        P.dve(lambda e: e.tensor_tensor(out=kr1[0:32, :], in0=b3(0)[0:32, 3, :], in1=trg[0:32, 0, :], op=ALU.mult),
              reads=[bk(0), "trg"], writes=["kr1"])
        yield
        P.dve(lambda e: e.tensor_tensor(out=kr2[0:32, :], in0=b3(0)[32:64, 3, :], in1=trg[32:64, 1, :], op=ALU.mult),
              reads=[bk(0), "trg"], writes=["kr2"])
        yield
        for i in range(3):
            P.pool(lambda e, i=i: e.tensor_tensor(out=KrT[32 * i:32 * i + 32, ts_], in0=kr1[0:32, :],
                                                  in1=kr2[0:32, :], op=ALU.add),
                   reads=["kr1", "kr2"], writes=[("Kr", t)])
        yield
        for c in range(4):
            P.dve(lambda e, c=c: e.tensor_scalar(out=acc[:, c, :], in0=xr[:, c, 3:131],
                                                 scalar1=vec[:, V_CW + 4 * c + 3:V_CW + 4 * c + 4],
                                                 scalar2=vec[:, V_CB + c:V_CB + c + 1], op0=ALU.mult, op1=ALU.add),
                  reads=["xr", "vec"], writes=[("acc", c)])
            for k in range(3):
                P.dve(lambda e, c=c, k=k: e.scalar_tensor_tensor(out=acc[:, c, :], in0=xr[:, c, k:k + 128],
                                                                  scalar=vec[:, V_CW + 4 * c + k:V_CW + 4 * c + k + 1],
                                                                  in1=acc[:, c, :], op0=ALU.mult, op1=ALU.add),
                      reads=["xr", "vec", ("acc", c)], writes=[("acc", c)])
            if c % 2 == 1:
                yield
        P.pool(lambda e: e.tensor_copy(out=xr[:, :, 0:3], in_=xr[:, :, 128:131]), reads=["xr"], writes=["xr"])
        yield
        accs = [("acc", c) for c in range(4)]
        P.pool(lambda e: e.tensor_copy(out=xcb[:], in_=acc[:]), reads=accs, writes=["xcb"])
        yield
        for c in range(4):
            P.pe(lambda e, c=c: e.matmul(b3(2)[:, c, :], lhsT=gA[:, c, :], rhs=xcb[:, c, :], start=True, stop=True),
                 reads=["gA", "xcb"], writes=[bk(2)])
        for c in range(4):
            P.pe(lambda e, c=c: e.matmul(b3(3)[:, c, :], lhsT=gX[:, c, :], rhs=xcb[:, c, :], start=True, stop=True),
                 reads=["gX", "xcb"], writes=[bk(3)])
        for c in range(4):
            P.act(lambda e, c=c: e.activation(out=ta[:, c, :], in_=b3(2)[:, c, :], func=AF.Tanh,
                                              bias=der[:, 8 + c:9 + c], scale=0.5), reads=[bk(2), "der"], writes=["ta"])
            P.act(lambda e, c=c: e.activation(out=ti[:, c, :], in_=b3(3)[:, c, :], func=AF.Tanh,
                                              bias=der[:, 12 + c:13 + c], scale=0.5), reads=[bk(3), "der"], writes=["ti"])
        yield
        for c in range(4):
            P.act(lambda e, c=c: e.activation(out=aa[:, c, :], in_=ta[:, c, :], func=AF.Exp,
                                              bias=der[:, 4 + c:5 + c], scale=der[:, 4 + c:5 + c]),
                  reads=["ta", "der"], writes=["aa"])
            P.act(lambda e, c=c: e.activation(out=th[:, c, :], in_=ta[:, c, :], func=AF.Tanh,
                                              bias=der[:, c:c + 1], scale=der[:, c:c + 1]),
                  reads=["ta", "der"], writes=["qsb"])
        P.dve(lambda e: e.tensor_tensor(out=ta[:], in0=aa[:], in1=aa[:], op=ALU.mult), reads=["aa"], writes=["ta"])
        yield
        P.dve(lambda e: e.scalar_tensor_tensor(out=ta[:], in0=ta[:], scalar=1.0, in1=th, op0=ALU.add, op1=ALU.mult),
              reads=["ta", "qsb"], writes=["ta"])
        yield
        P.act(lambda e: e.activation(out=ta[:], in_=ta[:], func=AF.Sqrt), reads=["ta"], writes=["ta"])
        yield
        P.dve(lambda e: e.scalar_tensor_tensor(out=ti[:], in0=ti[:], scalar=1.0, in1=acc[:], op0=ALU.add, op1=ALU.mult),
              reads=["ti"] + accs, writes=["ti"])
        yield
        P.dve(lambda e: e.scalar_tensor_tensor(out=ti[:], in0=ti[:], scalar=0.25, in1=ta[:], op0=ALU.mult, op1=ALU.mult),
              reads=["ti", "ta"], writes=["ti"])
        yield
        yield
        for c in range(4):
            P.dve(lambda e, c=c: e.tensor_tensor_scan(out=hh[:, c, :], data0=aa[:, c, :], data1=ti[:, c, :],
                                                       initial=hc[:, c:c + 1], op0=ALU.mult, op1=ALU.add),
                  reads=["aa", "ti", "hc"], writes=[khh])
        P.pool(lambda e: e.tensor_copy(out=hc[:], in_=hh[:, :, 127]), reads=[khh], writes=["hc"])
        yield
        yield
        for j in range(10):
            bi, jj = (j // 4, j % 4)
            for c in range(2):
                P.pe(lambda e, j=j, c=c, bi=bi, jj=jj: e.matmul(b3(bi)[:, jj, :], lhsT=w_uq[:, c, j * 128:(j + 1) * 128],
                                                                rhs=cqn[:, c, :], start=(c == 0), stop=(c == 1)),
                     reads=["w_uq", "cqn"], writes=[bk(bi)])
        for j in range(4):
            P.pe(lambda e, j=j: e.matmul(b3(3)[:, j, :], lhsT=w_kn[:, j * 128:(j + 1) * 128], rhs=cqn[:, 2, :],
                                         start=True, stop=True), reads=["w_kn", "cqn"], writes=[bk(3)])
        P.act(lambda e: e.activation(out=QnT[:], in_=b3(0), func=AF.Copy), reads=[bk(0)], writes=[kqn])
        yield
        P.act(lambda e: e.activation(out=qsb[:, 0:4, :], in_=b3(1), func=AF.Copy), reads=[bk(1)], writes=["qsb"])
        yield
        P.act(lambda e: e.activation(out=qsb[:, 4:6, :], in_=b3(2)[:, 0:2, :], func=AF.Copy), reads=[bk(2)], writes=["qsb"])
        yield
        P.pe(lambda e: e.matmul(bank[0][:], lhsT=cqn[:, 2, :], rhs=w_v[:], start=True, stop=True),
             reads=["w_v", "cqn"], writes=[bk(0)])
        yield
        yield
        cosb = trg[:, 0, :].unsqueeze(1).broadcast_to([128, 3, 128])
        sinb = trg[:, 1, :].unsqueeze(1).broadcast_to([128, 3, 128])
        P.pool(lambda e: e.tensor_tensor(out=qsb[:, 0:3, :], in0=qsb[:, 0:3, :], in1=cosb, op=ALU.mult),
               reads=["qsb", "trg"], writes=["qsb"])
        yield
        P.pool(lambda e: e.tensor_tensor(out=qsb[:, 3:6, :], in0=qsb[:, 3:6, :], in1=sinb, op=ALU.mult),
               reads=["qsb", "trg"], writes=["qsb"])
        yield
        P.pool(lambda e: e.tensor_tensor(out=QrT[:], in0=qsb[:, 0:3, :], in1=qsb[:, 3:6, :], op=ALU.add),
               reads=["qsb"], writes=[kqr])
        yield
        P.act(lambda e: e.activation(out=KnT[:, :, ts_], in_=b3(3), func=AF.Copy), reads=[bk(3)],
              writes=[("Kn", t)])
        yield
        P.dve(lambda e: e.tensor_copy(out=Vaug[:, t, :, 0:64], in_=bank[0][:].rearrange("p (h v) -> p h v", h=8)),
              reads=[bk(0), "Vall"], writes=[("V", t)])
        yield
        yield

    def attention(t, gen):
        pp = t % 2
        QnT, QrT = QnT2[pp], QrT2[pp]
        kqn, kqr = ("QnT", pp), ("QrT", pp)
        items = []
        for h in range(8):
            nb = (t + 4) // 4
            for b in range(nb):
                items.append((h, b, list(range(4 * b, min(4 * b + 4, t + 1)))))

        def qk(i):
            h, b, kts = items[i]
            sbk = 4 + (i % 2)
            j2, hr = h // 2, (h % 2) * 64
            c3, r3 = h // 3, (h % 3) * 32
            for jj, kt in enumerate(kts):
                ks = slice(kt * 128, (kt + 1) * 128)
                P.pe(lambda e, jj=jj, ks=ks: e.matmul(b3(sbk)[:, jj, :], lhsT=KnT[hr:hr + 64, j2, ks],
                                                       rhs=QnT[hr:hr + 64, j2, :], start=True, stop=False),
                     reads=[("Kn", kt), kqn], writes=[bk(sbk)])
                P.pe(lambda e, jj=jj, ks=ks: e.matmul(b3(sbk)[:, jj, :], lhsT=KrT[r3:r3 + 32, ks],
                                                       rhs=QrT[r3:r3 + 32, c3, :], start=False, stop=True),
                     reads=[("Kr", kt), kqr], writes=[bk(sbk)])

        def rest(i):
            h, b, kts = items[i]
            sbk = 4 + (i % 2)
            obk = 6 + (h % 2)
            n = len(kts)
            pt = PT[i % 2]
            ptk = ("PT", i % 2)
            P.act(lambda e: e.activation(out=pt[:, 0:n, :], in_=b3(sbk)[:, 0:n, :], func=AF.Exp, scale=ATT_SCALE),
                  reads=[bk(sbk)], writes=[ptk])
            if t in kts:
                jd = t - 4 * b
                P.pool(lambda e: e.tensor_tensor(out=pt[:, jd, :], in0=pt[:, jd, :], in1=maskT[:], op=ALU.mult),
                       reads=[ptk, "maskT"], writes=[ptk])
            for jj, kt in enumerate(kts):
                P.pe(lambda e, jj=jj, kt=kt: e.matmul(bank[obk][0:96, 0:128], lhsT=Vaug[:, kt, h, :], rhs=pt[:, jj, :],
                                                      start=(kt == 0), stop=(kt == t)),
                     reads=[("V", kt), "Vall", ptk], writes=[bk(obk)])
            if kts[-1] == t:
                j2, hr = h // 2, (h % 2) * 64
                for hf in range(2):
                    P.dve(lambda e, hf=hf: e.reciprocal(out=rl[32 * hf:32 * hf + 32, :], in_=bank[obk][64:96, 0:128]),
                          reads=[bk(obk)], writes=["rl"])
                P.dve(lambda e: e.scalar_tensor_tensor(
                    out=ym[hr:hr + 64, j2, :], in0=bank[obk][0:64, 0:128],
                    scalar=0.5, in1=rl[0:64, :], op0=ALU.mult, op1=ALU.mult),
                    reads=[bk(obk), "rl"], writes=["ym"])

        state = {"k": 0}

        def adv():
            while state["k"] < len(gen):
                g_ = gen[state["k"]]
                if g_ is None:
                    state["k"] += 1
                    continue
                try:
                    next(g_)
                    return
                except StopIteration:
                    state["k"] += 1

        for i in range(len(items)):
            qk(i)
            if i > 0:
                rest(i - 1)
            adv()
        rest(len(items) - 1)

    def tail_a(t):
        pp = t % 2
        s1, hh = s12[pp], hh2[pp]
        ks1, khh = ("s1", pp), ("hh", pp)
        P.dve(lambda e: e.tensor_tensor(out=yT[:, 0:4, :], in0=s1[:, 0:4, :], in1=hh[:], op=ALU.mult),
              reads=[ks1, khh], writes=["yT"])
        P.dve(lambda e: e.tensor_tensor(out=yT[:, 4:8, :], in0=s1[:, 4:8, :], in1=ym[:], op=ALU.mult),
              reads=[ks1, "ym"], writes=["yT"])

    def tail(t):
        ts_ = slice(t * 128, (t + 1) * 128)
        pp = t % 2
        xin = xin2[pp]
        kx = ("xin", pp)
        for hf in range(2):
            for c in range(8):
                P.pe(lambda e, hf=hf, c=c: e.matmul(bank[hf][:], lhsT=yT[:, c, :], rhs=w_out[:, c, hf * 512:(hf + 1) * 512],
                                                    start=(c == 0), stop=(c == 7)),
                     reads=["yT", "w_out"], writes=[bk(hf)])
        for hf in range(2):
            hs = slice(hf * 512, (hf + 1) * 512)
            P.dve(lambda e, hf=hf, hs=hs: e.scalar_tensor_tensor(out=xin[:, hs], in0=xin[:, hs], scalar=ALPHA, in1=bank[hf][:],
                                                                  op0=ALU.mult, op1=ALU.add),
                  reads=[kx, bk(hf)], writes=[kx])
            yield
            P.dve(lambda e, hf=hf, hs=hs: e.bn_stats(out=st[:, 6 * hf:6 * hf + 6], in_=xin[:, hs]), reads=[kx],
                  writes=["st"])
            yield
        layer_norm_tail(P, xin, st, mv, cst, None, None, None, [kx])
        yield
        for hf in range(2):
            hs = slice(hf * 512, (hf + 1) * 512)
            P.dma("sp", lambda e, hf=hf: e.dma_start(out=lng[:], in_=d["lng"][:, hf * 512:(hf + 1) * 512].partition_broadcast(128)),
                  "lgl", writes=["lng"])
            P.dma("sp", lambda e, hf=hf: e.dma_start(out=lnb[:], in_=d["lnb"][:, hf * 512:(hf + 1) * 512].partition_broadcast(128)),
                  "lbl", writes=["lnb"])
            P.pool(lambda e, hs=hs: e.tensor_tensor(out=xin[:, hs], in0=xin[:, hs], in1=lng[:], op=ALU.mult),
                   reads=[kx, "lng"], writes=[kx])
            yield
            P.pool(lambda e, hs=hs: e.tensor_tensor(out=xin[:, hs], in0=xin[:, hs], in1=lnb[:], op=ALU.add),
                   reads=[kx, "lnb"], writes=[kx])
            yield
        P.dma("sp", lambda e: e.dma_start(out=x_out[ts_, :], in_=xin[:]), "xo%d" % pp, reads=[kx], writes=[("x1d", t)])

    def exhaust(g_):
        if g_ is not None:
            for _ in g_:
                pass

    exhaust(front(0))
    ptail = None
    for t in range(T):
        gfront = front(t + 1) if t + 1 < T else None
        attention(t, [ptail, gfront])
        exhaust(ptail)
        exhaust(gfront)
        tail_a(t)
        ptail = tail(t)
    exhaust(ptail)
    return A


def layer_norm_tail(P, z, st, mv, cst, lng, lnb, _unused, zk):
    P.dve(lambda e: e.bn_aggr(out=mv[:, 0:2], in_=st[:]), reads=["st"], writes=["mv"])
    P.act(lambda e: e.activation(out=mv[:, 2:3], in_=mv[:, 1:2], func=AF.Sqrt, bias=cst[:, 1:2], scale=1.0),
          reads=["mv", "cst"], writes=["mv2"])
    P.dve(lambda e: e.reciprocal(out=mv[:, 2:3], in_=mv[:, 2:3]), reads=["mv2"], writes=["mv2"])
    P.dve(lambda e: e.scalar_tensor_tensor(out=mv[:, 3:4], in0=mv[:, 0:1], scalar=-1.0, in1=mv[:, 2:3],
                                           op0=ALU.mult, op1=ALU.mult), reads=["mv", "mv2"], writes=["mv3"])
    P.act(lambda e: e.activation(out=z[:], in_=z[:], func=AF.Identity, bias=mv[:, 3:4], scale=mv[:, 2:3]),
          reads=zk + ["mv2", "mv3"], writes=zk)
    if lng is not None:
        P.pool(lambda e: e.tensor_tensor(out=z[:], in0=z[:], in1=lng[:], op=ALU.mult), reads=zk + ["lng"], writes=zk)
        P.pool(lambda e: e.tensor_tensor(out=z[:], in0=z[:], in1=lnb[:], op=ALU.add), reads=zk + ["lnb"], writes=zk)


A_SPECS = [("w_in", [1024, 2048], F32), ("w_out", [1024, 1024], F32), ("w_uq", [256, 1280], F32),
           ("w_kn", [128, 512], F32), ("w_v", [128, 512], F32), ("gA", [128, 4, 128], F32),
           ("gX", [128, 4, 128], F32), ("vec", [128, NV], F32), ("lng", [1, 1024], F32), ("lnb", [1, 1024], F32)]


def prep_A(inp):
    f = np.float32
    w_in = np.asarray(inp["ab_w_in"][0], f)
    kr = w_in[:, 1920:1952]
    kr_sw = np.concatenate([kr[:, 16:32], kr[:, 0:16]], axis=1)
    w_inA = np.concatenate([w_in[:, :1920], kr, kr_sw, np.zeros((1024, 64), f)], axis=1)
    uq = np.asarray(inp["mla_w_uq"][0], f).reshape(256, 8, 96)
    nope = uq[:, :, :64].reshape(256, 512)
    rope = uq[:, :, 64:96]
    rsw = np.concatenate([rope[:, :, 16:32], rope[:, :, 0:16]], axis=2)
    z32 = np.zeros((256, 1, 32), f)
    rope9 = np.concatenate([rope, z32], axis=1).reshape(256, 288)
    rsw9 = np.concatenate([rsw, z32], axis=1).reshape(256, 288)
    pad = np.zeros((256, 96), f)
    w_uqA = np.concatenate([nope, rope9, pad, rsw9, pad], axis=1)
    w_uqA = np.concatenate([nope,
                            np.concatenate([rope9[:, 0:96], np.zeros((256, 32), f)], 1),
                            np.concatenate([rope9[:, 96:192], np.zeros((256, 32), f)], 1),
                            np.concatenate([rope9[:, 192:288], np.zeros((256, 32), f)], 1),
                            np.concatenate([rsw9[:, 0:96], np.zeros((256, 32), f)], 1),
                            np.concatenate([rsw9[:, 96:192], np.zeros((256, 32), f)], 1),
                            np.concatenate([rsw9[:, 192:288], np.zeros((256, 32), f)], 1)], axis=1)
    ukv = np.asarray(inp["mla_w_ukv"][0], f).reshape(128, 8, 128)
    w_kn = np.ascontiguousarray(ukv[:, :, :64]).reshape(128, 512)
    w_v = np.ascontiguousarray(ukv[:, :, 64:]).reshape(128, 512)

    def blockdiag(w):
        w = np.asarray(w[0], f)
        o = np.zeros((128, 4, 128), f)
        for h in range(8):
            r = (h % 2) * 64
            o[r:r + 64, h // 2, r:r + 64] = w[h]
        return o

    vec = np.zeros((128, NV), f)
    cw = np.asarray(inp["ab_conv_w"][0], f)
    for c in range(4):
        for k in range(4):
            vec[:, V_CW + 4 * c + k] = cw[k, c * 128:(c + 1) * 128]
    for nm, col in (("ab_conv_b", V_CB), ("ab_gate_a_b", V_BA), ("ab_gate_x_b", V_BX), ("ab_lambda", V_LAM)):
        vec[:, col:col + 4] = np.asarray(inp[nm][0], f).reshape(4, 128).T
    vec[:, V_QN:V_QN + 2] = np.asarray(inp["mla_q_norm"][0], f).reshape(2, 128).T
    vec[:, V_KVN] = np.asarray(inp["mla_kv_norm"][0], f)
    j = np.arange(128) % 16
    vec[:, V_INVF] = (10000.0 ** (-(2.0 * j) / 32.0)).astype(f)
    vec[:, V_SGN] = np.where((np.arange(128) % 32) < 16, -1.0, 1.0)
    return {"w_in": w_inA, "w_out": np.asarray(inp["ab_w_out"][0], f), "w_uq": w_uqA, "w_kn": w_kn, "w_v": w_v,
            "gA": blockdiag(inp["ab_gate_a_w"]), "gX": blockdiag(inp["ab_gate_x_w"]), "vec": vec,
            "lng": np.asarray(inp["ab_ln_g"], f).reshape(1, 1024), "lnb": np.asarray(inp["ab_ln_b"], f).reshape(1, 1024)}


def declare(nc, specs, pfx):
    return {nm: nc.dram_tensor(pfx + nm, shape, dt, kind="ExternalInput").ap() for nm, shape, dt in specs}


def build_program_A(S):
    nc = bass.Bass("TRN2", target_bir_lowering=False)
    P = Prog(nc)
    d = declare(nc, A_SPECS, "a_")
    d["pos"] = nc.dram_tensor("pos", [1, S], I32, kind="ExternalInput").ap()
    x_in = nc.dram_tensor("x", [S, D], F32, kind="ExternalInput").ap()
    x_out = nc.dram_tensor("x1", [S, D], F32, kind="ExternalOutput").ap()
    build_A(nc, P, S, d, x_in, x_out)
    st = P.emit()
    return nc, st


VB_CW, VB_CB, NVB = 0, 96, 120
RB_DTB, RB_ALOG, RB_D, RB_NW, RB_LNG, RB_LNB, RB_CB, NRB = 0, 32, 64, 96, 2144, 3168, 4192, 7264


def build_B(nc, P, S, d, x_in, x_out):
    T = S // 128
    A = Alloc(nc)
    sb, ps = A.sb, A.ps
    ident, maskT, ones32 = make_consts(nc, P, A, "b_")
    bank = [ps("b_bank%d" % i, [128, 512], F32) for i in range(8)]
    rr_state = [0]

    def rr():
        rr_state[0] = (rr_state[0] + 1) % 8
        return rr_state[0]

    def bk(i):
        return ("bank", i)

    def b3(i):
        return bank[i][:].rearrange("p (a b) -> p a b", a=4)

    tri = sb("b_tri", [128, 128], F32)
    u2 = sb("b_u2", [128, 128], F32)
    m025 = sb("b_m025", [128, 128], F32)
    onesb = sb("b_onesb", [128, 128], BF16)
    cst = sb("b_cst", [128, 8], F32)
    P.pool(lambda e: e.memset(tri[:], 1.0), writes=["tri"])
    P.pool(lambda e: e.affine_select(out=tri[:], in_=tri[:], pattern=[[1, 128]], compare_op=ALU.is_ge, fill=0.0,
                                     base=0, channel_multiplier=-1), reads=["tri"], writes=["tri"])
    P.pool(lambda e: e.memset(u2[:], 1.0), writes=["u2"])
    P.pool(lambda e: e.affine_select(out=u2[:], in_=u2[:], pattern=[[-1, 128]], compare_op=ALU.is_gt, fill=0.0,
                                     base=0, channel_multiplier=1), reads=["u2"], writes=["u2"])
    P.pool(lambda e: e.tensor_scalar(out=m025[:], in0=tri[:], scalar1=0.25, scalar2=None, op0=ALU.mult),
           reads=["tri"], writes=["m025"])
    P.pool(lambda e: e.memset(onesb[:], 1.0), writes=["onesb"])
    for i, v in enumerate([1e-6, 1e-5, 0.0, 1.0, 4e-6]):
        P.pool(lambda e, i=i, v=v: e.memset(cst[:, i:i + 1], v), writes=["cst"])
    w_in = sb("b_w_in", [128, 8, 5152], BF16)
    w_out = sb("b_w_out", [128, 16, 1024], BF16)
    vecb = sb("b_vec", [128, NVB], F32)
    cbrow = sb("b_cbrow", [128, 8, 128], BF16)
    dtb = sb("b_dtb", [128, 32], F32)
    aneg = sb("b_aneg", [128, 32], F32)
    cd = sb("b_cd", [128, 32], F32)
    normw = sb("b_normw", [128, 512], F32)
    dg = sb("b_dg", [128, 96, 128], BF16)
    for c in range(0, 8, 2):
        P.dma("pool", lambda e, c=c: e.dma_start(out=w_in[:, c:c + 2, :],
                                                 in_=d["w_in"][c * 128:(c + 2) * 128, :].rearrange("(c p) n -> p c n", p=128)),
              "wlinb", writes=["w_in"], chain=False)
    cb3 = d["row"][:, RB_CB:RB_CB + 3072].rearrange("o (i r c) -> o i r c", r=3, c=128)
    for r_ in range(3):
        P.dma("pool", lambda e, r_=r_: e.dma_start(out=cbrow[32 * r_:32 * r_ + 1, :, :], in_=cb3[:, :, r_, :]),
              "wlinb", writes=["cbrow"], chain=False)
    for c in range(0, 16, 4):
        P.dma("pool", lambda e, c=c: e.dma_start(out=w_out[:, c:c + 4, :],
                                                 in_=d["w_out"][c * 128:(c + 4) * 128, :].rearrange("(c p) n -> p c n", p=128)),
              "wl", writes=["w_out"], chain=False)
    P.dma("sp", lambda e: e.dma_start(out=vecb[:], in_=d["vec"]), "wl2", writes=["vecb"])
    for tl, off, n, key in ((dtb, RB_DTB, 32, "dtb"), (aneg, RB_ALOG, 32, "aneg"), (cd, RB_D, 32, "cd")):
        P.dma("sp", lambda e, tl=tl, off=off, n=n: e.dma_start(out=tl[:], in_=d["row"][:, off:off + n].partition_broadcast(128)),
              "wl2", writes=[key])
    P.act(lambda e: e.activation(out=aneg[:], in_=aneg[:], func=AF.Exp), reads=["aneg"], writes=["aneg"])
    P.dve(lambda e: e.tensor_scalar(out=aneg[:], in0=aneg[:], scalar1=-1.0, scalar2=None, op0=ALU.mult),
          reads=["aneg"], writes=["aneg"])
    P.dve(lambda e: e.tensor_scalar(out=cd[:], in0=cd[:], scalar1=0.5, scalar2=None, op0=ALU.mult),
          reads=["cd"], writes=["cd"])
    for jk in range(96):
        P.dve(lambda e, jk=jk: e.tensor_scalar(out=dg[:, jk, :], in0=ident[:], scalar1=vecb[:, jk:jk + 1], scalar2=None,
                                               op0=ALU.mult), reads=["ident", "vecb"], writes=["dg"])
    hT = sb("b_hT", [128, 2048], F32)
    hTb = sb("b_hTb", [128, 2048], BF16)
    P.pool(lambda e: e.memset(hT[:], 0.0), writes=["hT"])
    P.pool(lambda e: e.memset(hTb[:], 0.0), writes=["hTb"])
    xrb = sb("b_xrb", [128, 24, 131], BF16)
    P.pool(lambda e: e.memset(xrb[:], 0.0), writes=["xrb"])
    xin2 = [sb("b_xin%d" % i, [128, 1024], F32) for i in range(2)]
    xb = sb("b_xb", [128, 1024], BF16)
    xT = sb("b_xT", [128, 8, 128], BF16)
    sm = sb("b_sm", [128, 12, 32], F32)
    Rg = sb("b_Rg", [128, 4, 128], F32)
    es = sb("b_es", [128, 4, 128], F32)
    MT2 = [sb("b_MT%d" % i, [128, 8, 128], BF16) for i in range(2)]
    szg2 = [sb("b_szg%d" % i, [128, 512], F32) for i in range(2)]
    xs2 = sb("b_xs2", [128, 512], F32)
    xf2 = [sb("b_xf%d" % i, [128, 512], BF16) for i in range(2)]
    xsd2 = [sb("b_xsd%d" % i, [128, 512], BF16) for i in range(2)]
    xfd2 = [sb("b_xfd%d" % i, [128, 512], BF16) for i in range(2)]
    tnh = sb("b_tnh", [128, 512], F32)
    lng = normw
    lnb = tnh
    B2 = sb("b_B2", [128, 512], BF16)
    BCT = sb("b_BCT", [128, 8, 128], BF16)
    GTm = sb("b_GTm", [128, 128], F32)
    yb = sb("b_yb", [128, 512], F32)
    ssq = sb("b_ssq", [128, 4], F32)
    yn = xb[:, 0:512]
    junk = xb[:, 512:1024]
    ynT = sb("b_ynT", [128, 16, 128], BF16)
    st = sb("b_st", [128, 12], F32)
    mv = sb("b_mv", [128, 4], F32)

    def silu2(src_bank_ap, out_ap, key_in, key_out, shape3=None):
        P.act(lambda e: e.activation(out=tnh[:] if shape3 is None else tnh[:].rearrange("p (a b) -> p a b", a=4),
                                     in_=src_bank_ap, func=AF.Tanh, scale=0.5), reads=[key_in], writes=["tnh"])
        P.dve(lambda e: e.scalar_tensor_tensor(out=out_ap, in0=tnh[:] if shape3 is None else tnh[:].rearrange("p (a b) -> p a b", a=4),
                                               scalar=1.0, in1=src_bank_ap, op0=ALU.add, op1=ALU.mult),
              reads=["tnh", key_in], writes=[key_out])

    def silu2g(src_bank_ap, out_ap, key_in, key_out, shape3=None):
        tv = tnh[:] if shape3 is None else tnh[:].rearrange("p (a b) -> p a b", a=4)
        P.act(lambda e: e.activation(out=tv, in_=src_bank_ap, func=AF.Tanh, scale=0.5), reads=[key_in], writes=["tnh"])
        yield
        P.dve(lambda e: e.scalar_tensor_tensor(out=out_ap, in0=tv, scalar=1.0, in1=src_bank_ap, op0=ALU.add, op1=ALU.mult),
              reads=["tnh", key_in], writes=[key_out])
        yield

    def lockstep(*gens):
        gens = [g_ for g_ in gens if g_ is not None]
        while gens:
            for g_ in list(gens):
                try:
                    next(g_)
                except StopIteration:
                    gens.remove(g_)

    def load_x(t):
        if t >= T:
            return
        tsl = slice(t * 128, (t + 1) * 128)
        xi = xin2[t % 2]
        P.dma("sp", lambda e: e.dma_start(out=xi[:], in_=x_in[tsl, :]), "xin%d" % (t % 2), reads=[("x1d", t)],
              writes=[("xin", t % 2)])

    def pre_front(t):
        if t >= T:
            return
        xi = xin2[t % 2]
        P.act(lambda e: e.activation(out=xb[:], in_=xi[:], func=AF.Copy), reads=[("xin", t % 2)], writes=["xb"])
        r0 = rr()
        b0 = bank[r0][:].bitcast(BF16)
        for c in range(8):
            P.pe(lambda e, c=c: e.transpose(b0[:, c * 128:(c + 1) * 128], xb[:, c * 128:(c + 1) * 128], ident[:]),
                 reads=["xb", "ident"], writes=[bk(r0)])
        P.dve(lambda e: e.tensor_copy(out=xT[:].rearrange("p a b -> p (a b)"), in_=b0), reads=[bk(r0)], writes=["xT"])

    load_x(0)
    pre_front(0)
    for t in range(T):
        ts_ = slice(t * 128, (t + 1) * 128)
        xin = xin2[t % 2]
        kx = ("xin", t % 2)
        r = rr()
        for c in range(8):
            P.pe(lambda e, c=c, r=r: e.matmul(bank[r][:, 0:32], lhsT=xT[:, c, :], rhs=w_in[:, c, 5120:5152],
                                              start=(c == 0), stop=(c == 7)), reads=["xT", "w_in"], writes=[bk(r)])
        P.dve(lambda e, r=r: e.tensor_tensor(out=sm[:, 0, :], in0=bank[r][:, 0:32], in1=dtb[:], op=ALU.add),
              reads=[bk(r), "dtb"], writes=["sm0"])
        P.act(lambda e: e.activation(out=sm[:, 1, :], in_=sm[:, 0, :], func=AF.Abs), reads=["sm0"], writes=["sm1"])
        P.act(lambda e: e.activation(out=sm[:, 1, :], in_=sm[:, 1, :], func=AF.Exp, scale=-1.0), reads=["sm1"], writes=["sm1"])
        P.act(lambda e: e.activation(out=sm[:, 1, :], in_=sm[:, 1, :], func=AF.Ln, bias=1.0), reads=["sm1"], writes=["sm1"])
        P.dve(lambda e: e.scalar_tensor_tensor(out=sm[:, 2, :], in0=sm[:, 0, :], scalar=0.0, in1=sm[:, 1, :],
                                               op0=ALU.max, op1=ALU.add), reads=["sm0", "sm1"], writes=["dt"])
        P.dve(lambda e: e.tensor_tensor(out=sm[:, 3, :], in0=sm[:, 2, :], in1=aneg[:], op=ALU.mult),
              reads=["dt", "aneg"], writes=["adt"])
        P.dve(lambda e: e.tensor_scalar(out=sm[:, 4, :], in0=sm[:, 2, :], scalar1=0.5, scalar2=None, op0=ALU.mult),
              reads=["dt"], writes=["cxf"])
        for q in range(6):
            r = rr()
            for jj in range(4):
                j = 4 * q + jj
                for c in range(8):
                    P.pe(lambda e, j=j, jj=jj, c=c, r=r: e.matmul(b3(r)[:, jj, :], lhsT=w_in[:, c, 2048 + j * 128:2048 + (j + 1) * 128],
                                                                  rhs=xT[:, c, :], start=(c == 0), stop=(c == 7)),
                         reads=["w_in", "xT"], writes=[bk(r)])
            P.act(lambda e, q=q, r=r: e.activation(out=xrb[:, 4 * q:4 * q + 4, 3:131], in_=b3(r), func=AF.Copy),
                  reads=[bk(r)], writes=["xrb"])

        r = rr()
        for i, lh in enumerate((tri, u2, ones32)):
            P.pe(lambda e, i=i, lh=lh, r=r: e.matmul(bank[r][:, 32 * i:32 * i + 32], lhsT=lh[:], rhs=sm[:, 3, :],
                                                     start=True, stop=True), reads=["adt", "tri", "u2", "ones32"],
                 writes=[bk(r)])
        P.act(lambda e, r=r: e.activation(out=sm[:, 7:10, :].rearrange("p a b -> p (a b)"), in_=bank[r][:, 0:96], func=AF.Exp),
              reads=[bk(r)], writes=["e3"])
        P.dve(lambda e: e.scalar_tensor_tensor(out=sm[:, 5, :], in0=sm[:, 4, :], scalar=0.5, in1=sm[:, 8, :],
                                               op0=ALU.mult, op1=ALU.mult), reads=["cxf", "e3"], writes=["cxfd"])
        P.dve(lambda e: e.tensor_scalar(out=sm[:, 6, :], in0=sm[:, 7, :], scalar1=0.5, scalar2=None, op0=ALU.mult),
              reads=["e3"], writes=["eoff"])
        def conv_tok(r, j0):
            for jj in range(4):
                j = j0 + jj
                osl = bank[r][:, jj * 128:(jj + 1) * 128]
                for k in range(4):
                    P.pe(lambda e, j=j, k=k, osl=osl: e.matmul(osl, lhsT=xrb[:, j, k:k + 128], rhs=dg[:, 4 * j + k, :],
                                                               start=(k == 0), stop=False), reads=["xrb", "dg"], writes=[bk(r)])
                P.pe(lambda e, j=j, osl=osl: e.matmul(osl, lhsT=onesb[32 * (j % 3):32 * (j % 3) + 1, :], rhs=cbrow[32 * (j % 3):32 * (j % 3) + 1, j // 3, :],
                                                      start=False, stop=True), reads=["onesb", "cbrow"], writes=[bk(r)])

        def conv_feat(r, j0):
            for jj in range(4):
                j = j0 + jj
                osl = bank[r][:, jj * 128:(jj + 1) * 128]
                for k in range(4):
                    P.pe(lambda e, j=j, k=k, osl=osl: e.matmul(osl, lhsT=dg[:, 4 * j + k, :], rhs=xrb[:, j, k:k + 128],
                                                               start=(k == 0), stop=False), reads=["xrb", "dg"], writes=[bk(r)])
                P.pe(lambda e, j=j, osl=osl: e.matmul(osl, lhsT=cbrow[32 * (j % 3):32 * (j % 3) + 1, j // 3, :], rhs=onesb[32 * (j % 3):32 * (j % 3) + 1, :],
                                                      start=False, stop=True), reads=["onesb", "cbrow"], writes=[bk(r)])

        r = rr()
        conv_tok(r, 16)
        silu2(bank[r][:], B2[:], bk(r), "B2")
        for i in range(2):
            r = rr()
            conv_feat(r, 16 + 4 * i)
            silu2(b3(r), BCT[:, 4 * i:4 * i + 4, :], bk(r), "BCT", shape3=True)
        def stage1(g):
            gs = slice(g * 512, (g + 1) * 512)
            hs8 = slice(g * 8, (g + 1) * 8)
            pp = g % 2
            szg, xf, xsd, xfd, MT = szg2[pp], xf2[pp], xsd2[pp], xfd2[pp], MT2[pp]
            tv = tnh[:]

            def bc(i, hs8=hs8):
                return sm[:, i, hs8].unsqueeze(2).broadcast_to([128, 8, 64])

            def mkR(hf):
                h4 = slice(g * 8 + hf * 4, g * 8 + hf * 4 + 4)
                P.pool(lambda e: e.tensor_tensor(out=Rg[:], in0=tri[:].unsqueeze(1).broadcast_to([128, 4, 128]),
                                                 in1=sm[:, 3, h4].unsqueeze(2).broadcast_to([128, 4, 128]), op=ALU.mult),
                       reads=["tri", "adt"], writes=["Rg"])

            def seg_mm(r):
                P.pe(lambda e: e.matmul(bank[r][:], lhsT=u2[:], rhs=Rg[:].rearrange("p a b -> p (a b)"), start=True, stop=True),
                     reads=["u2", "Rg"], writes=[bk(r)])

            def seg_exp(r):
                P.act(lambda e: e.activation(out=es[:].rearrange("p a b -> p (a b)"), in_=bank[r][:], func=AF.Exp),
                      reads=[bk(r)], writes=["es"])

            def mk_mt(hf):
                P.dve(lambda e: e.tensor_tensor(out=MT[:, 4 * hf:4 * hf + 4, :], in0=es[:],
                                                in1=GTm[:].unsqueeze(1).broadcast_to([128, 4, 128]), op=ALU.mult),
                      reads=["es", "GTm"], writes=[("MT", pp, hf)])

            mkR(0)
            rg = rr()
            P.pe(lambda e: e.matmul(bank[rg][:, 0:128], lhsT=BCT[:, g, :], rhs=BCT[:, 4 + g, :], start=True, stop=True),
                 reads=["BCT"], writes=[bk(rg)])
            yield
            rc = rr()
            conv_tok(rc, 4 * g)
            yield
            rs0 = rr()
            seg_mm(rs0)
            P.dve(lambda e: e.tensor_tensor(out=GTm[:], in0=bank[rg][:, 0:128], in1=m025[:], op=ALU.mult),
                  reads=[bk(rg), "m025"], writes=["GTm"])
            yield
            P.act(lambda e: e.activation(out=tv, in_=bank[rc][:], func=AF.Tanh, scale=0.5), reads=[bk(rc)], writes=["tnh"])
            rz = rr()
            for c in range(8):
                P.pe(lambda e, c=c: e.matmul(bank[rz][:], lhsT=xT[:, c, :], rhs=w_in[:, c, gs], start=(c == 0),
                                             stop=(c == 7)), reads=["xT", "w_in"], writes=[bk(rz)])
            yield
            seg_exp(rs0)
            yield
            P.dve(lambda e: e.scalar_tensor_tensor(out=xs2[:], in0=tv, scalar=1.0, in1=bank[rc][:], op0=ALU.add, op1=ALU.mult),
                  reads=["tnh", bk(rc)], writes=["xs2"])
            mkR(1)
            yield
            mk_mt(0)
            rs1 = rr()
            seg_mm(rs1)
            yield
            x3 = xs2[:].rearrange("p (h v) -> p h v", h=8)
            P.act(lambda e: e.activation(out=tv, in_=bank[rz][:], func=AF.Tanh, scale=0.5), reads=[bk(rz)], writes=["tnh"])
            P.dve(lambda e: e.tensor_tensor(out=xf[:].rearrange("p (h v) -> p h v", h=8), in0=x3, in1=bc(4),
                                            op=ALU.mult), reads=["xs2", "cxf"], writes=[("xf", pp)])
            P.pool(lambda e: e.tensor_tensor(out=xsd[:].rearrange("p (h v) -> p h v", h=8), in0=x3,
                                             in1=cd[:, hs8].unsqueeze(2).broadcast_to([128, 8, 64]), op=ALU.mult),
                   reads=["xs2", "cd"], writes=[("xsd", pp)])
            yield
            seg_exp(rs1)
            P.pool(lambda e: e.tensor_tensor(out=xfd[:].rearrange("p (h v) -> p h v", h=8), in0=x3, in1=bc(5),
                                             op=ALU.mult), reads=["xs2", "cxfd"], writes=[("xfd", pp)])
            yield
            P.dve(lambda e: e.scalar_tensor_tensor(out=szg[:], in0=tv, scalar=1.0, in1=bank[rz][:], op0=ALU.add, op1=ALU.mult),
                  reads=["tnh", bk(rz)], writes=[("szg", pp)])
            yield
            mk_mt(1)
            yield

        def stage2(g):
            gs = slice(g * 512, (g + 1) * 512)
            hs8 = slice(g * 8, (g + 1) * 8)
            pp = g % 2
            szg, xf, xsd, xfd, MT = szg2[pp], xf2[pp], xsd2[pp], xfd2[pp], MT2[pp]

            def bc(i, hs8=hs8):
                return sm[:, i, hs8].unsqueeze(2).broadcast_to([128, 8, 64])
            P.dma("sp", lambda e, g=g: e.dma_start(out=normw[:], in_=d["row"][:, RB_NW + g * 512:RB_NW + (g + 1) * 512].partition_broadcast(128)),
                  "nwl", writes=["normw"])
            ry = rr()
            P.pe(lambda e, ry=ry: e.matmul(bank[ry][:], lhsT=ident[:], rhs=xsd[:], start=True, stop=False),
                 reads=["ident", ("xsd", pp)], writes=[bk(ry)])
            yield
            for hh in range(8):
                P.pe(lambda e, ry=ry, hh=hh: e.matmul(bank[ry][:, hh * 64:(hh + 1) * 64], lhsT=MT[:, hh, :],
                                                      rhs=xf[:, hh * 64:(hh + 1) * 64], start=False, stop=(hh == 7)),
                     reads=[("MT", pp, hh // 4), ("xf", pp)], writes=[bk(ry)])
                yield
            ro = rr()
            P.pe(lambda e, ro=ro, g=g, gs=gs: e.matmul(bank[ro][:], lhsT=BCT[:, 4 + g, :], rhs=hTb[:, gs], start=True, stop=True),
                 reads=["BCT", ("hTb", g)], writes=[bk(ro)])
            yield
            P.dve(lambda e, ro=ro, bc=bc: e.tensor_tensor(out=yb[:].rearrange("p (h v) -> p h v", h=8),
                                                          in0=bank[ro][:].rearrange("p (h v) -> p h v", h=8), in1=bc(6), op=ALU.mult),
                  reads=[bk(ro), "eoff"], writes=["yb"])
            yield
            P.dve(lambda e, ry=ry: e.tensor_tensor(out=yb[:], in0=yb[:], in1=bank[ry][:], op=ALU.add),
                  reads=["yb", bk(ry)], writes=["yb"])
            yield
            P.dve(lambda e: e.tensor_tensor(out=yb[:], in0=yb[:], in1=szg[:], op=ALU.mult), reads=["yb", ("szg", pp)], writes=["yb"])
            yield
            P.act(lambda e, g=g: e.activation(out=junk, in_=yb[:], func=AF.Square, accum_out=ssq[:, g:g + 1]),
                  reads=["yb"], writes=["junk", "ssq"])
            yield
            P.act(lambda e, g=g: e.activation(out=ssq[:, g:g + 1], in_=ssq[:, g:g + 1], func=AF.Sqrt, bias=cst[:, 4:5],
                                              scale=1.0 / 512), reads=["ssq", "cst"], writes=["ssq"])
            yield
            P.dve(lambda e, g=g: e.reciprocal(out=ssq[:, g:g + 1], in_=ssq[:, g:g + 1]), reads=["ssq"], writes=["ssq"])
            yield
            P.dve(lambda e, g=g: e.scalar_tensor_tensor(out=yn, in0=yb[:], scalar=ssq[:, g:g + 1], in1=normw[:],
                                                        op0=ALU.mult, op1=ALU.mult), reads=["yb", "ssq", "normw"], writes=["yn"])
            yield
            r = rr()
            bt = bank[r][:].bitcast(BF16)
            for c in range(4):
                P.pe(lambda e, c=c, bt=bt: e.transpose(bt[:, c * 128:(c + 1) * 128], xb[:, c * 128:(c + 1) * 128], ident[:]),
                     reads=["yn", "ident"], writes=[bk(r)])
                yield
            P.act(lambda e, g=g, bt=bt: e.activation(out=ynT[:, 4 * g:4 * g + 4, :].rearrange("p a b -> p (a b)"), in_=bt[:, 0:512],
                                                     func=AF.Copy), reads=[bk(r)], writes=[("ynT", g)])
            yield
            r = rr()
            P.pe(lambda e, r=r, g=g: e.matmul(bank[r][:], lhsT=B2[:, g * 128:(g + 1) * 128], rhs=xfd[:], start=True, stop=True),
                 reads=["B2", ("xfd", pp)], writes=[bk(r)])
            yield
            h3 = hT[:, gs].rearrange("p (h v) -> p h v", h=8)
            P.dve(lambda e, h3=h3, bc=bc: e.tensor_tensor(out=h3, in0=h3, in1=bc(9), op=ALU.mult),
                  reads=[("hT", g), "e3"], writes=[("hT", g)])
            yield
            P.dve(lambda e, r=r, gs=gs: e.tensor_tensor(out=hT[:, gs], in0=hT[:, gs], in1=bank[r][:], op=ALU.add),
                  reads=[("hT", g), bk(r)], writes=[("hT", g)])
            yield
            P.pool(lambda e, gs=gs: e.tensor_copy(out=hTb[:, gs], in_=hT[:, gs]), reads=[("hT", g)], writes=[("hTb", g)])
            yield
        lockstep(stage1(0))
        for g in range(4):
            lockstep(stage1(g + 1) if g + 1 < 4 else None, stage2(g))
        P.pool(lambda e: e.tensor_copy(out=xrb[:, :, 0:3], in_=xrb[:, :, 128:131]), reads=["xrb"], writes=["xrb"])
        load_x(t + 1)
        ynk = [("ynT", g) for g in range(4)]
        rs = [rr(), rr()]
        for hf in range(2):
            for c in range(16):
                P.pe(lambda e, hf=hf, c=c, r=rs[hf]: e.matmul(bank[r][:], lhsT=ynT[:, c, :], rhs=w_out[:, c, hf * 512:(hf + 1) * 512],
                                                              start=(c == 0), stop=(c == 15)), reads=ynk + ["w_out"], writes=[bk(rs[hf])])
        pre_front(t + 1)
        for hf in range(2):
            hs = slice(hf * 512, (hf + 1) * 512)
            P.dve(lambda e, hs=hs, r=rs[hf], xin=xin: e.scalar_tensor_tensor(out=xin[:, hs], in0=xin[:, hs], scalar=ALPHA, in1=bank[r][:],
                                                                             op0=ALU.mult, op1=ALU.add), reads=[kx, bk(rs[hf])], writes=[kx])
            P.dve(lambda e, hf=hf, hs=hs, xin=xin: e.bn_stats(out=st[:, 6 * hf:6 * hf + 6], in_=xin[:, hs]), reads=[kx], writes=["st"])
        layer_norm_tail(P, xin, st, mv, cst, None, None, None, [kx])
        for hf in range(2):
            hs = slice(hf * 512, (hf + 1) * 512)
            P.dma("sp", lambda e, hf=hf: e.dma_start(out=lng[:], in_=d["row"][:, RB_LNG + hf * 512:RB_LNG + (hf + 1) * 512].partition_broadcast(128)),
                  "lgl", writes=["normw"])
            P.dma("sp", lambda e, hf=hf: e.dma_start(out=lnb[:], in_=d["row"][:, RB_LNB + hf * 512:RB_LNB + (hf + 1) * 512].partition_broadcast(128)),
                  "lbl", writes=["tnh"])
            P.pool(lambda e, hs=hs, xin=xin: e.tensor_tensor(out=xin[:, hs], in0=xin[:, hs], in1=lng[:], op=ALU.mult), reads=[kx, "normw"], writes=[kx])
            P.pool(lambda e, hs=hs, xin=xin: e.tensor_tensor(out=xin[:, hs], in0=xin[:, hs], in1=lnb[:], op=ALU.add), reads=[kx, "tnh"], writes=[kx])
        P.dma("sp", lambda e, ts_=ts_, xin=xin: e.dma_start(out=x_out[ts_, :], in_=xin[:]), "xo%d" % (t % 2), reads=[kx], writes=[("outd", t)])
    return A


B_SPECS = [("w_in", [1024, 5152], F32), ("w_out", [2048, 1024], F32), ("vec", [128, NVB], F32), ("row", [1, NRB], F32)]


def prep_B(inp):
    f = np.float32
    vec = np.zeros((128, NVB), f)
    cw = np.asarray(inp["ssd_conv_w"][0], f)
    for j in range(24):
        for k in range(4):
            vec[:, VB_CW + 4 * j + k] = cw[k, j * 128:(j + 1) * 128]
    cb = np.asarray(inp["ssd_conv_b"][0], f)
    vec[:, VB_CB:VB_CB + 24] = cb.reshape(24, 128).T
    row = np.zeros((1, NRB), f)
    row[0, RB_DTB:RB_DTB + 32] = np.asarray(inp["ssd_dt_bias"][0], f)
    row[0, RB_ALOG:RB_ALOG + 32] = np.asarray(inp["ssd_a_log"][0], f)
    row[0, RB_D:RB_D + 32] = np.asarray(inp["ssd_d"][0], f)
    row[0, RB_NW:RB_NW + 2048] = np.asarray(inp["ssd_norm"][0], f)
    row[0, RB_LNG:RB_LNG + 1024] = np.asarray(inp["ssd_ln_g"][0], f)
    row[0, RB_LNB:RB_LNB + 1024] = np.asarray(inp["ssd_ln_b"][0], f)
    row[0, RB_CB:RB_CB + 3072] = cb
    return {"w_in": np.asarray(inp["ssd_w_in"][0], f), "w_out": np.asarray(inp["ssd_w_out"][0], f), "vec": vec, "row": row}


def build_program_B(S):
    nc = bass.Bass("TRN2", target_bir_lowering=False)
    P = Prog(nc)
    d = declare(nc, B_SPECS, "b_")
    x_in = nc.dram_tensor("x1", [S, D], F32, kind="ExternalInput").ap()
    x_out = nc.dram_tensor("out", [S, D], F32, kind="ExternalOutput").ap()
    build_B(nc, P, S, d, x_in, x_out)
    st = P.emit()
    return nc, st


_CACHE = {}


def build_program_fused(S):
    nc = bass.Bass("TRN2", target_bir_lowering=False)
    P = Prog(nc)
    dA = declare(nc, A_SPECS, "a_")
    dA["pos"] = nc.dram_tensor("pos", [1, S], I32, kind="ExternalInput").ap()
    dB = declare(nc, B_SPECS, "b_")
    x_in = nc.dram_tensor("x", [S, D], F32, kind="ExternalInput").ap()
    x1 = nc.dram_tensor("x1_scratch", [S, D], F32).ap()
    x_out = nc.dram_tensor("out", [S, D], F32, kind="ExternalOutput").ap()
    A = build_A(nc, P, S, dA, x_in, x1)
    keep = {k: v for k, v in P.last_writer.items() if isinstance(k, tuple) and k[0] == "x1d"}
    P.barrier()
    P.last_writer.update(keep)
    A.free()
    build_B(nc, P, S, dB, x1, x_out)
    st = P.emit()
    return nc, st


def kernel(**inputs):
    S = SEQ
    x = np.ascontiguousarray(np.asarray(inputs["x"], np.float32))
    pos = np.ascontiguousarray(np.asarray(inputs["positions"], np.int32))
    hpA = prep_A(inputs)
    hpB = prep_B(inputs)
    mode = "fused"
    if mode == "split":
        if "A" not in _CACHE:
            _CACHE["A"] = build_program_A(S)[0]
            _CACHE["B"] = build_program_B(S)[0]
        mapsA = []
        for b in range(NCORES):
            m = {"a_" + k: v for k, v in hpA.items()}
            m["x"] = x[b]
            m["pos"] = pos[b:b + 1]
            mapsA.append(m)
        resA = run_bass_kernel_spmd(_CACHE["A"], mapsA, core_ids=list(range(NCORES)))
        mapsB = []
        for b in range(NCORES):
            m = {"b_" + k: v for k, v in hpB.items()}
            m["x1"] = np.ascontiguousarray(np.asarray(resA.results[b]["x1"], np.float32))
            mapsB.append(m)
        resB = run_bass_kernel_spmd(_CACHE["B"], mapsB, core_ids=list(range(NCORES)))
        return np.stack([np.asarray(resB.results[b]["out"], np.float32) for b in range(NCORES)], axis=0)
    if "F" not in _CACHE:
        _CACHE["F"] = build_program_fused(S)[0]
    maps = []
    for b in range(NCORES):
        m = {"a_" + k: v for k, v in hpA.items()}
        m.update({"b_" + k: v for k, v in hpB.items()})
        m["x"] = x[b]
        m["pos"] = pos[b:b + 1]
        maps.append(m)
    res = run_bass_kernel_spmd(_CACHE["F"], maps, core_ids=list(range(NCORES)))
    return np.stack([np.asarray(res.results[b]["out"], np.float32) for b in range(NCORES)], axis=0)
```
